# Optimizing a Trainium2 kernel written in Bass

```python
import math
import numpy as np
import jax
import jax.numpy as jnp
from jax import lax

D_MODEL = 1024
BATCH = 32
SEQ = 256
DEPTH = 2
DEC_BATCH = 4
DEC_SEQ = 4096
PAST_LEN = 256

GRID_W = 64
BLOCK = 128
A_HEADS = 4
A_DQK = 32
A_DV = 64
B_HEADS = 8
B_KV_HEADS = 2
B_GROUP = B_HEADS // B_KV_HEADS
B_DH = 64
WINDOW = 128
C_HEADS = 4
C_Q_RANK = 256
C_KV_RANK = 128
C_NOPE = 64
C_ROPE = 32
C_DV = 64
MIX_WIDTH = A_HEADS * A_DV + B_HEADS * B_DH + C_HEADS * C_DV
IN_WIDTHS = (A_HEADS * 2 * A_DQK, A_HEADS * 2 * A_DQK, A_HEADS * A_DV,
             B_HEADS * B_DH, B_KV_HEADS * B_DH, B_KV_HEADS * B_DH,
             C_Q_RANK, C_KV_RANK, C_ROPE)
IN_WIDTH = sum(IN_WIDTHS)
D_FF = 2816
N_MOD = 9
ROPE_BASE = 10000.0
LN_EPS = 1e-5
RMS_EPS = 1e-6
DN_ALPHA = (2 * DEPTH) ** 0.25
DN_BETA = (8 * DEPTH) ** -0.25
FFN_RES = 0.5

kernel_name = 'hybrid_diff_prefix_trunk_step'


def _layernorm(x, g, b):
    xf = x.astype(jnp.float32)
    mu = jnp.mean(xf, axis=-1, keepdims=True)
    var = jnp.mean(jnp.square(xf - mu), axis=-1, keepdims=True)
    y = (xf - mu) * lax.rsqrt(var + LN_EPS)
    return (y * g.astype(jnp.float32) + b.astype(jnp.float32)).astype(x.dtype)


def _rmsnorm(x, g):
    xf = x.astype(jnp.float32)
    y = xf * lax.rsqrt(jnp.mean(jnp.square(xf), axis=-1, keepdims=True) + RMS_EPS)
    return (y * g.astype(jnp.float32)).astype(x.dtype)


def _modulation(cond, w, b):
    return (jax.nn.silu(cond) @ w + b).reshape(cond.shape[0], N_MOD, D_MODEL)


def _modulate(x, shift, scale):
    return x * (1.0 + scale[:, None, :]) + shift[:, None, :]


def _residual_post_norm(x, f, gate, weight, g, b):
    return _layernorm(DN_ALPHA * x + weight * gate[:, None, :] * f, g, b)


def _ffn_half(x, mod, slot, w1, w3, w2, g, b):
    h = _modulate(x, mod[:, 3 * slot], mod[:, 3 * slot + 1])
    f = (jax.nn.silu(h @ w1) * (h @ w3)) @ w2
    return _residual_post_norm(x, f, mod[:, 3 * slot + 2], FFN_RES, g, b)


def _axial_rope_tables(rows, dim):
    row = jnp.repeat(jnp.arange(rows, dtype=jnp.float32), GRID_W)
    col = jnp.tile(jnp.arange(GRID_W, dtype=jnp.float32), rows)
    a = dim // 2
    inv = jnp.power(ROPE_BASE, -jnp.arange(0, a, 2, dtype=jnp.float32) / a)
    ar = row[:, None] * inv[None, :]
    ac = col[:, None] * inv[None, :]
    ang = jnp.concatenate([ar, ar, ac, ac], axis=-1)
    return jnp.cos(ang), jnp.sin(ang)


def _rope(x, cos, sin):
    dim = x.shape[-1]
    a = dim // 2
    q = a // 2
    xr, xc = x[..., :a], x[..., a:]
    rot = jnp.concatenate([-xr[..., q:], xr[..., :q], -xc[..., q:], xc[..., :q]], axis=-1)
    return x * cos[:, None, :].astype(x.dtype) + rot * sin[:, None, :].astype(x.dtype)


def _sweep_query_blocks(fn, *qs):
    b, n = qs[0].shape[:2]
    nb = n // BLOCK
    blocks = tuple(jnp.swapaxes(q.reshape(b, nb, BLOCK, *q.shape[2:]), 0, 1) for q in qs)
    out = lax.map(lambda qb: fn(*qb), blocks)
    out = jnp.swapaxes(out, 0, 1)
    return out.reshape(b, n, *out.shape[3:])


def _sink_softmax(s, sink):
    m = jnp.maximum(jnp.max(s, axis=-1, keepdims=True), sink)
    e = jnp.exp(s - m)
    return e / (jnp.sum(e, axis=-1, keepdims=True) + jnp.exp(sink - m))


def _diff_lambda(lam_params, lam_init):
    lp = lam_params.astype(jnp.float32)
    return jnp.exp(jnp.sum(lp[0] * lp[1])) - jnp.exp(jnp.sum(lp[2] * lp[3])) + lam_init


def _diff_attn(q, k, v, lam, subln_g, lam_init):
    scale = A_DQK ** -0.5

    def one_block(qb):
        s = jnp.einsum('bqhjd,bkhjd->bhjqk', qb, k).astype(jnp.float32) * scale
        p = jax.nn.softmax(s, axis=-1)
        w = p[:, :, 0] - lam * p[:, :, 1]
        return jnp.einsum('bhqk,bkhd->bqhd', w.astype(v.dtype), v)

    o = _sweep_query_blocks(one_block, q)
    return _rmsnorm(o, subln_g) * (1.0 - lam_init)


def _sink_attn_dense(q, k, v, sink):
    scale = B_DH ** -0.5
    sk = sink.astype(jnp.float32).reshape(1, B_KV_HEADS, B_GROUP, 1, 1)

    def one_block(qb):
        s = jnp.einsum('bqhgd,bkhd->bhgqk', qb, k).astype(jnp.float32) * scale
        p = _sink_softmax(s, sk)
        return jnp.einsum('bhgqk,bkhd->bqhgd', p.astype(v.dtype), v)

    return _sweep_query_blocks(one_block, q)


def _band_blocks(t):
    b, n = t.shape[:2]
    nb = n // BLOCK
    tb = t.reshape(b, nb, BLOCK, *t.shape[2:])
    tp = jnp.pad(tb, ((0, 0), (1, 1)) + ((0, 0),) * (tb.ndim - 2))
    return jnp.concatenate([tp[:, :-2], tp[:, 1:-1], tp[:, 2:]], axis=2)


def _sink_attn_banded(q, k, v, k_ctx, v_ctx, sink):
    b, n = q.shape[:2]
    nb = n // BLOCK
    n_ctx = k_ctx.shape[1]
    scale = B_DH ** -0.5
    qb = q.reshape(b, nb, BLOCK, B_KV_HEADS, B_GROUP, B_DH)
    kw = _band_blocks(k)
    vw = _band_blocks(v)
    s_loc = jnp.einsum('bnqhgd,bnkhd->bnhgqk', qb, kw).astype(jnp.float32) * scale
    s_ctx = jnp.einsum('bnqhgd,bkhd->bnhgqk', qb, k_ctx).astype(jnp.float32) * scale
    blk = jnp.arange(nb)[:, None, None]
    qi = jnp.arange(BLOCK)[None, :, None]
    kj = jnp.arange(3 * BLOCK)[None, None, :]
    kpos = (blk - 1) * BLOCK + kj
    qpos = blk * BLOCK + qi
    valid = (jnp.abs(kpos - qpos) <= WINDOW) & (kpos >= 0) & (kpos < n)
    s_loc = jnp.where(valid[None, :, None, None], s_loc, -jnp.inf)
    sk = sink.astype(jnp.float32).reshape(1, 1, B_KV_HEADS, B_GROUP, 1, 1)
    p = _sink_softmax(jnp.concatenate([s_ctx, s_loc], axis=-1), sk)
    p_ctx = p[..., :n_ctx].astype(v.dtype)
    p_loc = p[..., n_ctx:].astype(v.dtype)
    o = (jnp.einsum('bnhgqk,bkhd->bnqhgd', p_ctx, v_ctx)
         + jnp.einsum('bnhgqk,bnkhd->bnqhgd', p_loc, vw))
    return o.reshape(b, n, B_KV_HEADS, B_GROUP, B_DH)


def _mla(q_nope, q_pe, c_kv, k_pe, w_kv_up):
    w_uk = w_kv_up[..., :C_NOPE]
    w_uv = w_kv_up[..., C_NOPE:]
    q_lat = jnp.einsum('bqhd,chd->bqhc', q_nope, w_uk)
    scale = (C_NOPE + C_ROPE) ** -0.5

    def one_block(ql, qp):
        s = (jnp.einsum('bqhc,bkc->bhqk', ql, c_kv)
             + jnp.einsum('bqhr,bkr->bhqk', qp, k_pe)).astype(jnp.float32) * scale
        p = jax.nn.softmax(s, axis=-1)
        return jnp.einsum('bhqk,bkc->bqhc', p.astype(c_kv.dtype), c_kv)

    o_lat = _sweep_query_blocks(one_block, q_lat, q_pe)
    return jnp.einsum('bqhc,chd->bqhd', o_lat, w_uv)


def _project(h, w_in, q_norm_g, w_q_up, kv_norm_g, ropes):
    b, n, _ = h.shape
    offs = np.cumsum(IN_WIDTHS)[:-1].tolist()
    a_q, a_k, a_v, b_q, b_k, b_v, c_qd, c_kvd, c_kpe = jnp.split(h @ w_in, offs, axis=-1)
    a_q = a_q.reshape(b, n, 2 * A_HEADS, A_DQK)
    a_k = a_k.reshape(b, n, 2 * A_HEADS, A_DQK)
    a_v = a_v.reshape(b, n, A_HEADS, A_DV)
    b_q = b_q.reshape(b, n, B_HEADS, B_DH)
    b_k = b_k.reshape(b, n, B_KV_HEADS, B_DH)
    b_v = b_v.reshape(b, n, B_KV_HEADS, B_DH)
    c_q = (_rmsnorm(c_qd, q_norm_g) @ w_q_up).reshape(b, n, C_HEADS, C_NOPE + C_ROPE)
    c_qn, c_qp = c_q[..., :C_NOPE], c_q[..., C_NOPE:]
    c_kv = _rmsnorm(c_kvd, kv_norm_g)
    c_kpe = c_kpe[:, :, None, :]
    if ropes is not None:
        (ca, sa), (cb, sb), (cc, sc) = ropes
        a_q = _rope(a_q, ca, sa)
        a_k = _rope(a_k, ca, sa)
        b_q = _rope(b_q, cb, sb)
        b_k = _rope(b_k, cb, sb)
        c_qp = _rope(c_qp, cc, sc)
        c_kpe = _rope(c_kpe, cc, sc)
    return (a_q.reshape(b, n, A_HEADS, 2, A_DQK), a_k.reshape(b, n, A_HEADS, 2, A_DQK), a_v,
            b_q.reshape(b, n, B_KV_HEADS, B_GROUP, B_DH), b_k, b_v,
            c_qn, c_qp, c_kv, c_kpe[:, :, 0])


def _context_state(t):
    a_q, a_k, a_v, b_q, b_k, b_v, c_qn, c_qp, c_kv, c_kpe = t
    b, n = a_k.shape[:2]
    return (a_k.reshape(b, n, A_HEADS, 2 * A_DQK), a_v, b_k, b_v, c_kv, c_kpe)


def _mix(t, ctx, lam, lam_init, subln_g, sink, w_kv_up, w_o):
    a_q, a_k, a_v, b_q, b_k, b_v, c_qn, c_qp, c_kv, c_kpe = t
    b, n = a_q.shape[:2]
    if ctx is None:
        o_a = _diff_attn(a_q, a_k, a_v, lam, subln_g, lam_init)
        o_b = _sink_attn_dense(b_q, b_k, b_v, sink)
        o_c = _mla(c_qn, c_qp, c_kv, c_kpe, w_kv_up)
    else:
        ctx_ak, ctx_av, ctx_bk, ctx_bv, ctx_ckv, ctx_ckpe = ctx
        n_ctx = ctx_ak.shape[1]
        ak_all = jnp.concatenate([ctx_ak.reshape(b, n_ctx, A_HEADS, 2, A_DQK), a_k], axis=1)
        av_all = jnp.concatenate([ctx_av, a_v], axis=1)
        o_a = _diff_attn(a_q, ak_all, av_all, lam, subln_g, lam_init)
        o_b = _sink_attn_banded(b_q, b_k, b_v, ctx_bk, ctx_bv, sink)
        o_c = _mla(c_qn, c_qp, jnp.concatenate([ctx_ckv, c_kv], axis=1),
                   jnp.concatenate([ctx_ckpe, c_kpe], axis=1), w_kv_up)
    o = jnp.concatenate([o_a.reshape(b, n, -1), o_b.reshape(b, n, -1), o_c.reshape(b, n, -1)], axis=-1)
    return o @ w_o


def setup_inputs(seed: int = 0) -> dict:
    key = jax.random.key(seed)
    ks = jax.random.split(key, 32)

    def nrm(k, shape, s):
        return jax.random.normal(k, shape, jnp.float32) * s

    return {
        'x_prompt': nrm(ks[0], (BATCH, SEQ, D_MODEL), 1.0),
        'x_sample': nrm(ks[1], (DEC_BATCH, DEC_SEQ, D_MODEL), 1.0),
        'cache_a_k': nrm(ks[2], (DEC_BATCH, DEPTH, PAST_LEN, A_HEADS, 2 * A_DQK), 1.0),
        'cache_a_v': nrm(ks[3], (DEC_BATCH, DEPTH, PAST_LEN, A_HEADS, A_DV), 1.0),
        'cache_b_k': nrm(ks[4], (DEC_BATCH, DEPTH, PAST_LEN, B_KV_HEADS, B_DH), 1.0),
        'cache_b_v': nrm(ks[5], (DEC_BATCH, DEPTH, PAST_LEN, B_KV_HEADS, B_DH), 1.0),
        'cache_c_kv': nrm(ks[6], (DEC_BATCH, DEPTH, PAST_LEN, C_KV_RANK), 1.0),
        'cache_c_kpe': nrm(ks[7], (DEC_BATCH, DEPTH, PAST_LEN, C_ROPE), 1.0),
        'c': nrm(ks[8], (DEC_BATCH, D_MODEL), 1.0),
        'c_ctx': nrm(ks[9], (D_MODEL,), 1.0),
        'w_mod': nrm(ks[10], (DEPTH, D_MODEL, N_MOD * D_MODEL), D_MODEL ** -0.5),
        'b_mod': nrm(ks[11], (DEPTH, N_MOD * D_MODEL), 0.02),
        'ln_g': 1.0 + nrm(ks[12], (DEPTH, 3, D_MODEL), 0.02),
        'ln_b': nrm(ks[13], (DEPTH, 3, D_MODEL), 0.02),
        'ffn_w1': nrm(ks[14], (DEPTH, 2, D_MODEL, D_FF), D_MODEL ** -0.5),
        'ffn_w3': nrm(ks[15], (DEPTH, 2, D_MODEL, D_FF), D_MODEL ** -0.5),
        'ffn_w2': nrm(ks[16], (DEPTH, 2, D_FF, D_MODEL), D_FF ** -0.5 * DN_BETA),
        'w_in': nrm(ks[17], (DEPTH, D_MODEL, IN_WIDTH), D_MODEL ** -0.5),
        'w_o': nrm(ks[18], (DEPTH, MIX_WIDTH, D_MODEL), MIX_WIDTH ** -0.5 * DN_BETA),
        'a_lambda': nrm(ks[19], (DEPTH, 4, A_DQK), 0.1),
        'a_subln_g': 1.0 + nrm(ks[20], (DEPTH, A_DV), 0.02),
        'b_sink': nrm(ks[21], (DEPTH, B_HEADS), 0.5),
        'c_q_norm_g': 1.0 + nrm(ks[22], (DEPTH, C_Q_RANK), 0.02),
        'c_w_q_up': nrm(ks[23], (DEPTH, C_Q_RANK, C_HEADS * (C_NOPE + C_ROPE)), C_Q_RANK ** -0.5),
        'c_kv_norm_g': 1.0 + nrm(ks[24], (DEPTH, C_KV_RANK), 0.02),
        'c_w_kv_up': nrm(ks[25], (DEPTH, C_KV_RANK, C_HEADS, C_NOPE + C_DV), C_KV_RANK ** -0.5),
    }


def reference(x_prompt, x_sample, cache_a_k, cache_a_v, cache_b_k, cache_b_v, cache_c_kv, cache_c_kpe,
              c, c_ctx, w_mod, b_mod, ln_g, ln_b, ffn_w1, ffn_w3, ffn_w2, w_in, w_o,
              a_lambda, a_subln_g, b_sink, c_q_norm_g, c_w_q_up, c_kv_norm_g, c_w_kv_up):
    n_lat = x_sample.shape[1]
    rows = n_lat // GRID_W
    ropes = (_axial_rope_tables(rows, A_DQK), _axial_rope_tables(rows, B_DH), _axial_rope_tables(rows, C_ROPE))
    xp = x_prompt
    xs = x_sample
    st_ak, st_av, st_bk, st_bv, st_ckv, st_ckpe = [], [], [], [], [], []
    for l in range(DEPTH):
        lam_init = 0.8 - 0.6 * math.exp(-0.3 * l)
        lam = _diff_lambda(a_lambda[l], lam_init)
        mod_p = _modulation(c_ctx[None, :], w_mod[l], b_mod[l])
        mod_s = _modulation(c, w_mod[l], b_mod[l])
        proj_w = (w_in[l], c_q_norm_g[l], c_w_q_up[l], c_kv_norm_g[l])
        mix_w = (lam, lam_init, a_subln_g[l], b_sink[l], c_w_kv_up[l], w_o[l])
        ffn1 = (ffn_w1[l, 0], ffn_w3[l, 0], ffn_w2[l, 0], ln_g[l, 0], ln_b[l, 0])
        ffn2 = (ffn_w1[l, 1], ffn_w3[l, 1], ffn_w2[l, 1], ln_g[l, 2], ln_b[l, 2])

        xp = _ffn_half(xp, mod_p, 0, *ffn1)
        tp = _project(_modulate(xp, mod_p[:, 3], mod_p[:, 4]), *proj_w, None)
        xp = _residual_post_norm(xp, _mix(tp, None, *mix_w), mod_p[:, 5], 1.0, ln_g[l, 1], ln_b[l, 1])
        s_ak, s_av, s_bk, s_bv, s_ckv, s_ckpe = _context_state(tp)
        st_ak.append(s_ak)
        st_av.append(s_av)
        st_bk.append(s_bk)
        st_bv.append(s_bv)
        st_ckv.append(s_ckv)
        st_ckpe.append(s_ckpe)
        xp = _ffn_half(xp, mod_p, 2, *ffn2)

        xs = _ffn_half(xs, mod_s, 0, *ffn1)
        ctx = (cache_a_k[:, l], cache_a_v[:, l], cache_b_k[:, l], cache_b_v[:, l],
               cache_c_kv[:, l], cache_c_kpe[:, l])
        ts = _project(_modulate(xs, mod_s[:, 3], mod_s[:, 4]), *proj_w, ropes)
        xs = _residual_post_norm(xs, _mix(ts, ctx, *mix_w), mod_s[:, 5], 1.0, ln_g[l, 1], ln_b[l, 1])
        xs = _ffn_half(xs, mod_s, 2, *ffn2)

    new_a_k = jnp.stack(st_ak, axis=1)
    new_a_v = jnp.stack(st_av, axis=1)
    new_b_k = jnp.stack(st_bk, axis=1)
    new_b_v = jnp.stack(st_bv, axis=1)
    new_c_kv = jnp.stack(st_ckv, axis=1)
    new_c_kpe = jnp.stack(st_ckpe, axis=1)
    return (xp, xs, new_a_k, new_a_v, new_b_k, new_b_v, new_c_kv, new_c_kpe)
```

```python
import os
import numpy as np
from contextlib import ExitStack
import concourse.bass as bass
import concourse.mybir as mybir
from concourse.bass_utils import run_bass_kernel_spmd

F32 = mybir.dt.float32
BF16 = mybir.dt.bfloat16
ALU = mybir.AluOpType
AF = mybir.ActivationFunctionType
AX = mybir.AxisListType

D = 1024
DFF = 2816
NJ = DFF // 128
DEPTH = 2
INW = 1952
LN_EPS = 1e-5
RMS_EPS = 1e-6
DN_ALPHA = float((2 * DEPTH) ** 0.25)
NEG = -30000.0
NPB = 8
NSB = 16
NB = NPB + NSB
XK_ROWS = 544
XV_W = 390
X_ROWS = XK_ROWS + XV_W
QS_ROWS = 2560
XG_PIECES = ((0, 256), (256, 512), (512, 739), (739, 934))


class Res:
    __slots__ = ("name", "w", "r", "dsem", "dcnt", "excl")

    def __init__(self, name, excl=False):
        self.name = name
        self.excl = excl
        self.w = None
        self.r = {}
        self.dsem = None
        self.dcnt = 0

    def inherit(self, other):
        if other.w is not None:
            k, v = other.w
            if self.r.get(k, 0) < v:
                self.r[k] = v
        for k, v in other.r.items():
            if self.r.get(k, 0) < v:
                self.r[k] = v


class Sched:
    ENG = ("pe", "act", "dve", "pool", "sp")

    def __init__(self):
        self.q = {e: [] for e in self.ENG}
        self.cnt = {e: 0 for e in self.ENG}
        self.waited = {e: {} for e in self.ENG}
        self.ndsem = 0
        self.semmap = {}

    def _dep(self, eng, k, v):
        if eng == "pe" and k == ("e", "pe"):
            return
        if self.waited[eng].get(k, 0) >= v:
            return
        self.waited[eng][k] = v
        self.q[eng].append(("wait", k, v))

    def op(self, eng, fn, reads=(), writes=(), dma=None, ndma=1, inc=16):
        for r in reads:
            if r.w is not None:
                self._dep(eng, *r.w)
            if r.excl:
                for k, v in r.r.items():
                    if k != ("e", eng):
                        self._dep(eng, k, v)
        for w in writes:
            if w.w is not None:
                self._dep(eng, *w.w)
            for k, v in w.r.items():
                self._dep(eng, k, v)
        if dma is None:
            self.cnt[eng] += 1
            tok = (("e", eng), self.cnt[eng])
            self.q[eng].append(("op", fn, tok[0], 1))
        else:
            ent = self.semmap.get(dma.name)
            if ent is None:
                ent = [("d", self.ndsem), 0, None]
                self.ndsem += 1
                self.semmap[dma.name] = ent
            if ent[2] is not dma:
                if ent[1] > 0:
                    self._dep(eng, ent[0], ent[1])
                ent[2] = dma
            ent[1] += inc * ndma
            tok = (ent[0], ent[1])
            self.q[eng].append(("op", fn, tok[0], inc))
        for r in reads:
            if r.r.get(tok[0], 0) < tok[1]:
                r.r[tok[0]] = tok[1]
        for w in writes:
            w.w = tok
            w.r = {}
        return tok

    def finish(self, eng="sp"):
        for ent in self.semmap.values():
            self._dep(eng, ent[0], ent[1])
        for e in self.ENG:
            if e != eng and self.cnt[e] > 0:
                self._dep(eng, ("e", e), self.cnt[e])

    def emit(self, nc, stack):
        sems = {}
        for e in self.ENG:
            sems[("e", e)] = stack.enter_context(nc.semaphore("s_" + e))
        for i in range(self.ndsem):
            sems[("d", i)] = stack.enter_context(nc.semaphore("d_%d" % i))
        block = stack.enter_context(nc.Block())
        q = self.q

        def run(ename):
            def body(h):
                for it in q[ename]:
                    if it[0] == "wait":
                        h.wait_ge(sems[it[1]], it[2])
                    else:
                        ins = it[1](h)
                        if not isinstance(ins, (list, tuple)):
                            ins = [ins]
                        for i_ in ins:
                            i_.then_inc(sems[it[2]], it[3])
            return body

        block.tensor(run("pe"))
        block.scalar(run("act"))
        block.vector(run("dve"))
        block.gpsimd(run("pool"))
        block.sync(run("sp"))


class T:
    __slots__ = ("ap", "res")

    def __init__(self, ap, res):
        self.ap = ap
        self.res = res


class Builder:
    def __init__(self):
        self.nc = bass.Bass("TRN2", target_bir_lowering=False)
        self.S = Sched()
        self.d = {}
        self.dres = {}
        self.in_names = []

    def din(self, name, shape, dt=F32):
        if os.environ.get("KSTAGE", "full") in ("projonly", "mixonly", "mixprompt") and name.startswith("ffn_w"):
            return
        self.d[name] = self.nc.dram_tensor(name, list(shape), dt, kind="ExternalInput").ap()
        self.dres[name] = Res(name)
        self.in_names.append(name)

    def dout(self, name, shape, dt=F32):
        self.d[name] = self.nc.dram_tensor(name, list(shape), dt, kind="ExternalOutput").ap()
        self.dres[name] = Res(name)

    def dscr(self, name, shape, dt=BF16):
        self.d[name] = self.nc.dram_tensor(name, list(shape), dt).ap()
        self.dres[name] = Res(name)

    def ptile(self, name, free, dt):
        t = self.stack.enter_context(self.nc.sbuf_tensor(name, [128] + list(free), dt))
        return T(t[:], Res(name))

    def arena_reset(self):
        self.acur = 0

    def atile(self, name, free, dt):
        n = int(np.prod(free))
        esz = 4 if dt == F32 else 2
        nb = (n * esz + 63) // 64 * 64
        off = self.acur
        assert off + nb <= self.ABYTES, (name, off, nb, self.ABYTES)
        self.acur += nb
        ap = self.arena[:, off // 2: off // 2 + n * esz // 2]
        if dt == F32:
            ap = ap.bitcast(F32)
        if len(free) == 2:
            ap = ap.rearrange("p (a b) -> p a b", a=free[0])
        elif len(free) == 3:
            ap = ap.rearrange("p (a b c) -> p a b c", a=free[0], b=free[1])
        elif len(free) == 4:
            ap = ap.rearrange("p (a b c d) -> p a b c d", a=free[0], b=free[1], c=free[2])
        res = Res(name)
        keep = []
        for (o, s, r) in self.alive:
            if o < off + nb and off < o + s:
                res.inherit(r)
                if not (off <= o and o + s <= off + nb):
                    keep.append((o, s, r))
            else:
                keep.append((o, s, r))
        keep.append((off, nb, res))
        self.alive = keep
        return T(ap, res)

    def mm(self, out, lhsT, rhs, start, stop, R, W, skip=False):
        self.S.op("pe", lambda e: e.matmul(out, lhsT, rhs, start=start, stop=stop, skip_group_check=skip),
                  reads=R, writes=W)

    def tr(self, out, in_, ident, R, W):
        self.S.op("pe", lambda e: e.transpose(out, in_, ident), reads=R, writes=W)

    def act(self, out, in_, func, R, W, bias=None, scale=None):
        kw = {}
        if bias is not None:
            kw["bias"] = bias
        if scale is not None:
            kw["scale"] = scale
        self.S.op("act", lambda e: e.activation(out, in_, func, **kw), reads=R, writes=W)

    def tt(self, out, in0, in1, op, R, W, eng="dve"):
        self.S.op(eng, lambda e: e.tensor_tensor(out, in0, in1, op), reads=R, writes=W)

    def ts(self, out, in0, s1, s2, op0, op1, R, W, eng="dve"):
        if op1 is None:
            self.S.op(eng, lambda e: e.tensor_scalar(out, in0, s1, None, op0), reads=R, writes=W)
        else:
            self.S.op(eng, lambda e: e.tensor_scalar(out, in0, s1, s2, op0, op1), reads=R, writes=W)

    def stt(self, out, in0, scalar, in1, op0, op1, R, W, eng="dve"):
        self.S.op(eng, lambda e: e.scalar_tensor_tensor(out, in0, scalar, in1, op0, op1), reads=R, writes=W)

    def cp(self, out, in_, R, W, eng="dve"):
        if eng == "act":
            self.S.op("act", lambda e: e.activation(out, in_, AF.Copy), reads=R, writes=W)
        else:
            self.S.op(eng, lambda e: e.tensor_copy(out, in_), reads=R, writes=W)

    def dma(self, q, out, in_, R, W, sem, slow=False):
        if slow:
            self.S.op(q, lambda e: e.dma_start(out=out, in_=in_, allow_slow_non_contiguous=True), reads=R, writes=W, dma=sem)
        else:
            self.S.op(q, lambda e: e.dma_start(out=out, in_=in_), reads=R, writes=W, dma=sem)

    def dmas(self, q, pairs, R, W, sem):
        pairs = list(pairs)
        self.S.op(q, lambda e: [e.dma_start(out=o, in_=i) for (o, i) in pairs], reads=R, writes=W, dma=sem,
                  ndma=len(pairs))

    def build(self):
        nc = self.nc
        self.din("xin", [NB * 128, D])
        self.din("condT", [128, 8, 2])
        self.din("bmT", [128, DEPTH, 72])
        self.din("ck_a", [DEPTH, 256, 256]); self.din("cv_a", [DEPTH, 256, 256])
        self.din("ck_b", [DEPTH, 256, 128]); self.din("cv_b", [DEPTH, 256, 128])
        self.din("c_ckv", [DEPTH, 256, 128]); self.din("c_kpe", [DEPTH, 256, 32])
        self.din("w_mod", [DEPTH, D, 9 * D])
        self.din("ln_g", [DEPTH, 3, D]); self.din("ln_b", [DEPTH, 3, D])
        self.din("ffn_w1", [DEPTH, 2, D, DFF]); self.din("ffn_w3", [DEPTH, 2, D, DFF])
        self.din("ffn_w2", [DEPTH, 2, DFF, D])
        self.din("w_in", [DEPTH, D, INW]); self.din("w_o", [DEPTH, D, D])
        self.din("a_lambda", [DEPTH, 128]); self.din("a_subln_g", [DEPTH, 64]); self.din("b_sink", [DEPTH, 8])
        self.din("c_q_norm_g", [DEPTH, 256]); self.din("c_w_q_up", [DEPTH, 256, 384])
        self.din("c_kv_norm_g", [DEPTH, 128]); self.din("c_w_kv_up", [DEPTH, 128, 512])
        self.din("ident", [128, 128])
        self.din("rope32", [128, 2, NSB, 32]); self.din("rope64", [128, 2, NSB, 64])
        self.din("bmask", [128, 4, 128])
        self.din("rmask", [128, 6])
        self.dout("y", [NB * 128, D])
        self.dout("st", [4, DEPTH, 256, 928])
        self.dscr("XB", [X_ROWS, 2048])
        self.dres["XG"] = Res("XG")
        for pi, (r0, r1) in enumerate(XG_PIECES):
            self.dscr("XG%d" % pi, [2 * (r1 - r0), 2048])
        self.dscr("PB", [X_ROWS, 1024]); self.dscr("QS", [QS_ROWS, NB * 128])
        self.QSr = [Res("QS%d" % i) for i in range(NB)]
        self.XBr = [Res("XB%d" % i) for i in range(NSB)]
        self.PBr = [Res("PB%d" % i) for i in range(NPB)]

        with ExitStack() as st:
            self.stack = st
            self.xres = self.ptile("xres", [NB, D], F32)
            self.xblk = [Res("xblk%d" % i) for i in range(NB)]
            self.identF = self.ptile("identF", [128], F32)
            self.identB = self.ptile("identB", [128], BF16)
            self.onesF = self.ptile("onesF", [128], F32)
            self.csT = self.ptile("csT", [8, 2], F32)
            self.bmT = self.ptile("bmT_s", [DEPTH, 72], F32)
            self.modT = self.ptile("modT", [24, 2], F32)
            self.gate_bc = self.ptile("gate_bc", [2, D], F32)
            self.lng = self.ptile("lng", [D], F32)
            self.lnb = self.ptile("lnb", [D], F32)
            self.bmask = self.ptile("bmask_s", [4, 128], BF16)
            self.rmask = self.ptile("rmask_s", [6], F32)
            self.small = self.ptile("small", [64], F32)
            self.small_r = [Res("small%d" % i) for i in range(8)]
            self.ABYTES = 88 * 1024
            arena_t = st.enter_context(nc.sbuf_tensor("arena", [128, self.ABYTES // 2], BF16))
            self.arena = arena_t[:]
            self.alive = []
            self.acur = 0
            self.ps = []
            for i in range(8):
                p = st.enter_context(nc.psum_tensor("ps%d" % i, [128, 512], F32))
                self.ps.append(T(p[:], Res("ps%d" % i, excl=True)))

            self.prologue()
            for l in range(DEPTH):
                self.layer(l)
            self.epilogue()
            self.S.finish()
            self.S.emit(nc, st)
        return nc

    def prologue(self):
        d = self.d
        xv = d["xin"].rearrange("(b p) f -> p b f", p=128)
        for i in range(3):
            blks = list(range(i * 8, (i + 1) * 8))
            self.dma("sp", self.xres.ap[:, i * 8:(i + 1) * 8, :], xv[:, i * 8:(i + 1) * 8, :], [],
                     [self.xblk[b] for b in blks], self.xblk[blks[0]])
        self.dma("sp", self.identF.ap, d["ident"], [], [self.identF.res], self.identF.res)
        self.dma("pool", self.identB.ap, d["ident"], [], [self.identB.res], self.identB.res)
        self.dma("pool", self.bmask.ap, d["bmask"], [], [self.bmask.res], self.bmask.res)
        self.dma("sp", self.rmask.ap, d["rmask"], [], [self.rmask.res], self.rmask.res)
        self.dma("sp", self.bmT.ap, d["bmT"], [], [self.bmT.res], self.bmT.res)
        self.dma("sp", self.csT.ap, d["condT"], [], [self.csT.res], self.csT.res)
        self.S.op("dve", lambda e: e.memset(self.onesF.ap, 1.0), writes=[self.onesF.res])
        self.act(self.csT.ap, self.csT.ap, AF.Silu, [], [self.csT.res])

    def epilogue(self):
        yv = self.d["y"].rearrange("(b p) f -> p b f", p=128)
        for i in range(6):
            blks = list(range(i * 4, (i + 1) * 4))
            self.dma("sp", yv[:, i * 4:(i + 1) * 4, :], self.xres.ap[:, i * 4:(i + 1) * 4, :],
                     [self.xblk[b] for b in blks], [], self.xblk[blks[0]])

    def layer(self, l):
        tiles = [2, 3, 4, 5, 0, 1]
        if os.environ.get("KSTAGE", "full") in ("projonly", "mixonly", "mixprompt"):
            if l == 0:
                self.mixer_phase(l)
            return
        self.ffn_phase(l, 0, 0, tiles)
        self.mixer_phase(l)
        self.ffn_phase(l, 1, 2, tiles)

    def mods(self, l, slot, weight):
        d = self.d
        ps = self.ps[6]
        wsrc = d["w_mod"][l].rearrange("(k p) n -> p k n", p=128)
        base = slot * 3 * D
        ring = [self.atile("wm%d" % i, [8, 256], F32) for i in range(2)]
        for jb in range(12):
            w = ring[jb % 2]
            self.dma("sp", w.ap, wsrc[:, :, base + jb * 256: base + (jb + 1) * 256], [], [w.res], w.res)
            for cc in range(2):
                n = jb * 2 + cc
                for k in range(8):
                    self.mm(ps.ap[:, 2 * n:2 * n + 2], w.ap[:, k, cc * 128:(cc + 1) * 128], self.csT.ap[:, k, :],
                            k == 0, k == 7, [w.res, self.csT.res], [ps.res])
        psv = ps.ap[:, 0:48].rearrange("p (n c) -> p n c", c=2)
        for c in range(2):
            self.tt(self.modT.ap[:, :, c], psv[:, :, c], self.bmT.ap[:, l, slot * 24:(slot + 1) * 24], ALU.add,
                    [ps.res, self.bmT.res], [self.modT.res])
        self.ts(self.modT.ap[:, 8:16, :], self.modT.ap[:, 8:16, :], 1.0, None, ALU.add, None, [], [self.modT.res])
        self.ts(self.modT.ap[:, 16:24, :], self.modT.ap[:, 16:24, :], float(weight), None, ALU.mult, None, [],
                [self.modT.res])
        dg = [self.atile("dg%d" % i, [128], F32) for i in range(2)]
        pb = [self.ps[4], self.ps[5]]
        i = 0
        for c in range(2):
            for hf in range(2):
                p = pb[(c * 2 + hf) % 2]
                for kk in range(4):
                    k = hf * 4 + kk
                    g = dg[i % 2]
                    i += 1
                    self.ts(g.ap, self.identF.ap, self.modT.ap[:, 16 + k, c:c + 1], None, ALU.mult, None,
                            [self.identF.res, self.modT.res], [g.res])
                    self.mm(p.ap[:, kk * 128:(kk + 1) * 128], self.onesF.ap, g.ap, True, True,
                            [self.onesF.res, g.res], [p.res])
                self.cp(self.gate_bc.ap[:, c, hf * 512:(hf + 1) * 512], p.ap, [p.res], [self.gate_bc.res], eng="act")

    def load_ln(self, l, idx):
        d = self.d
        self.dma("sp", self.lng.ap, d["ln_g"][l, idx].partition_broadcast(128), [], [self.lng.res], self.lng.res)
        self.dma("sp", self.lnb.ap, d["ln_b"][l, idx].partition_broadcast(128), [], [self.lnb.res], self.lnb.res)

    def pre(self, tile, hT, banks):
        cond = 1 if tile < 2 else 0
        for k in range(8):
            p = banks[k % len(banks)]
            for tb in range(4):
                blk = tile * 4 + tb
                self.tr(p.ap[:, tb * 128:(tb + 1) * 128], self.xres.ap[:, blk, k * 128:(k + 1) * 128], self.identF.ap,
                        [self.xblk[blk], self.identF.res], [p.res])
            self.act(hT.ap[:, k, :], p.ap, AF.Identity, [p.res, self.modT.res], [hT.res],
                     bias=self.modT.ap[:, k, cond:cond + 1], scale=self.modT.ap[:, 8 + k, cond:cond + 1])

    def post(self, blk, cond, ybuf):
        x = self.xres.ap[:, blk, :]
        xr = self.xblk[blk]
        sm = self.small.ap
        sr = self.small_r[0]
        self.stt(ybuf.ap, x, DN_ALPHA, ybuf.ap, ALU.mult, ALU.add, [xr], [ybuf.res])
        st6 = sm[:, 0:12].rearrange("p (a b) -> p a b", a=2)
        for hf in range(2):
            self.S.op("dve", lambda e, hf=hf: e.bn_stats(st6[:, hf, :], ybuf.ap[:, hf * 512:(hf + 1) * 512]),
                      reads=[ybuf.res], writes=[sr])
        self.S.op("dve", lambda e: e.bn_aggr(sm[:, 12:14], st6), reads=[], writes=[sr])
        self.ts(sm[:, 14:15], sm[:, 13:14], LN_EPS, None, ALU.add, None, [], [sr])
        self.act(sm[:, 14:15], sm[:, 14:15], AF.Sqrt, [], [sr])
        self.S.op("dve", lambda e: e.reciprocal(sm[:, 14:15], sm[:, 14:15]), reads=[], writes=[sr])
        self.stt(sm[:, 15:16], sm[:, 12:13], -1.0, sm[:, 14:15], ALU.mult, ALU.mult, [], [sr])
        self.act(ybuf.ap, ybuf.ap, AF.Identity, [sr], [ybuf.res], bias=sm[:, 15:16], scale=sm[:, 14:15])
        self.tt(ybuf.ap, ybuf.ap, self.lng.ap, ALU.mult, [self.lng.res], [ybuf.res])
        self.tt(x, ybuf.ap, self.lnb.ap, ALU.add, [ybuf.res, self.lnb.res], [xr])

    def ffn_phase(self, l, half, slot, tiles):
        d = self.d
        self.arena_reset()
        self.mods(l, slot, 0.5)
        self.load_ln(l, 0 if slot == 0 else 2)
        self.arena_reset()
        hT = self.atile("hT", [8, 512], BF16)
        aT = self.atile("aT", [NJ, 512], BF16)
        su = [self.atile("su%d" % i, [512], F32) for i in range(2)]
        w13 = [self.atile("w13_%d" % i, [2, 8, 256], BF16) for i in range(2)]
        w2r = [self.atile("w2_%d" % i, [4, 512], BF16) for i in range(2)]
        yb = [self.atile("yb%d" % i, [D], F32) for i in range(4)]
        w1s = d["ffn_w1"][l, half].rearrange("(k p) n -> p k n", p=128)
        w3s = d["ffn_w3"][l, half].rearrange("(k p) n -> p k n", p=128)
        w2s = d["ffn_w2"][l, half].rearrange("(j p) n -> p j n", p=128)
        state = {"i13": 0, "n13": 0, "i2": 0, "n2": 0}
        total13 = len(tiles) * 11
        total2 = len(tiles) * 12

        def issue13(upto):
            while state["i13"] < min(upto, total13):
                n = state["i13"]
                jb = n % 11
                w = w13[n % 2]
                self.dmas("pool", [(w.ap[:, 0], w1s[:, :, jb * 256:(jb + 1) * 256]),
                                   (w.ap[:, 1], w3s[:, :, jb * 256:(jb + 1) * 256])], [], [w.res], w.res)
                state["i13"] += 1

        def issue2(upto):
            while state["i2"] < min(upto, total2):
                n = state["i2"]
                r = n % 12
                hf, jg = r // 6, r % 6
                nj = 4 if jg < 5 else 2
                w = w2r[n % 2]
                self.dma("pool", w.ap[:, 0:nj, :], w2s[:, jg * 4:jg * 4 + nj, hf * 512:(hf + 1) * 512], [], [w.res], w.res)
                state["i2"] += 1

        issue13(2)
        self.pre(tiles[0], hT, [self.ps[4], self.ps[5]])
        for ti, tile in enumerate(tiles):
            cond = 1 if tile < 2 else 0
            for jb in range(11):
                n = ti * 11 + jb
                issue13(n + 2)
                w = w13[n % 2]
                for cc in range(2):
                    j = jb * 2 + cc
                    pu = self.ps[j % 2]
                    pg = self.ps[2 + j % 2]
                    for k in range(8):
                        self.mm(pu.ap, w.ap[:, 0, k, cc * 128:(cc + 1) * 128], hT.ap[:, k, :], k == 0, k == 7,
                                [w.res, hT.res], [pu.res])
                    for k in range(8):
                        self.mm(pg.ap, w.ap[:, 1, k, cc * 128:(cc + 1) * 128], hT.ap[:, k, :], k == 0, k == 7,
                                [w.res, hT.res], [pg.res])
                    s_ = su[j % 2]
                    self.act(s_.ap, pu.ap, AF.Silu, [pu.res], [s_.res])
                    self.tt(aT.ap[:, j, :], s_.ap, pg.ap, ALU.mult, [s_.res, pg.res], [aT.res])
                if jb == 9:
                    issue2(ti * 12 + 2)
            if ti + 1 < len(tiles):
                self.pre(tiles[ti + 1], hT, [self.ps[4], self.ps[5]])
                issue13((ti + 1) * 11 + 2)
            for hf in range(2):
                banks = [self.ps[4 + tb] for tb in range(4)] if hf == 0 else [self.ps[tb] for tb in range(4)]
                for jg in range(6):
                    n = ti * 12 + hf * 6 + jg
                    issue2(n + 2)
                    w = w2r[n % 2]
                    nj = 4 if jg < 5 else 2
                    for jj in range(nj):
                        j = jg * 4 + jj
                        for tb in range(4):
                            self.mm(banks[tb].ap, aT.ap[:, j, tb * 128:(tb + 1) * 128], w.ap[:, jj, :], j == 0, j == NJ - 1,
                                    [aT.res, w.res], [banks[tb].res])
                for tb in range(4):
                    self.tt(yb[tb].ap[:, hf * 512:(hf + 1) * 512], banks[tb].ap,
                            self.gate_bc.ap[:, cond, hf * 512:(hf + 1) * 512], ALU.mult,
                            [banks[tb].res, self.gate_bc.res], [yb[tb].res])
            for tb in range(4):
                self.post(tile * 4 + tb, cond, yb[tb])

    def mixer_phase(self, l):
        self.arena_reset()
        self.mods(l, 1, 1.0)
        self.load_ln(l, 1)
        self.arena_reset()
        stage = os.environ.get("KSTAGE", "full")
        self.project(l)
        if stage in ("proj", "projonly"):
            return
        self.gather()
        if stage == "gather":
            return
        self.arena_reset()
        self.layer_consts(l)
        mark = self.acur
        o_p = self.atile("o_p", [NPB, D], BF16)
        m2 = self.acur
        for s_ in range(4):
            self.acur = m2
            segs = [("PB", s_ * 256, 256)]
            self.attn_A(l, segs, False, [(s_ * 256, 256)], o_p, s_ * 2, 0)
            self.acur = m2
            self.attn_C(l, segs, False, [(s_ * 256, 256)], o_p, s_ * 2, 0)
            self.acur = m2
            self.attn_B(l, False, s_, o_p, s_ * 2)
        self.acur = m2
        self.wo_phase(l, [0, 1], o_p, 0)
        if stage in ("prompt", "mixprompt"):
            return
        self.acur = mark
        o_s = self.atile("o_s", [NSB, D], BF16)
        m2 = self.acur
        segs = [("XG", i * 1024, 1024) for i in range(4)]
        qt = [(1024 + i * 512, 512) for i in range(4)]
        self.attn_A(l, segs, True, qt, o_s, 0, 1024)
        self.acur = m2
        self.attn_C(l, segs, True, qt, o_s, 0, 1024)
        self.acur = m2
        self.attn_B(l, True, 0, o_s, 0)
        self.acur = m2
        self.wo_phase(l, [2, 3, 4, 5], o_s, 8)

    def xrows(self, name, tok0, ntok, r0, nr):
        if name == "XG":
            rk = tok0 // 2048
            t = tok0 % 2048
            for pi, (p0, p1) in enumerate(XG_PIECES):
                if p0 <= r0 and r0 + nr <= p1:
                    a = self.d["XG%d" % pi]
                    n = p1 - p0
                    return a[rk * n + r0 - p0: rk * n + r0 - p0 + nr, t:t + ntok]
            raise AssertionError((r0, nr))
        a = self.d[name]
        return a[r0:r0 + nr, tok0:tok0 + ntok]

    def xv(self, name, tok0, ntok):
        if name == "XG":
            rk = tok0 // 2048
            t = tok0 % 2048
            assert (t // 1024) == ((t + ntok - 1) // 1024)
            if t < 1024:
                a = self.d["XG2"]
                v = a[rk * 227 + 32: rk * 227 + 227, :]
            else:
                a = self.d["XG3"]
                v = a[rk * 195: rk * 195 + 195, :]
                t -= 1024
            v = v.rearrange("r t -> (r t)").rearrange("(t c) -> t c", c=XV_W)
            return v[t:t + ntok, :]
        a = self.d[name]
        v = a[XK_ROWS:X_ROWS, :]
        v = v.rearrange("r t -> (r t)").rearrange("(t c) -> t c", c=XV_W)
        return v[tok0:tok0 + ntok, :]

    def xres_of(self, name, tok0, ntok):
        if name == "XG":
            return [self.dres["XG"]]
        lst = self.XBr if name == "XB" else self.PBr
        return [lst[b] for b in range(tok0 // 128, (tok0 + ntok) // 128)]

    def rope(self, dst, src, H, Q, tbl, sb, R, W, tmp):
        dim = 4 * Q
        xs = src.rearrange("p (h a w q) -> p h a w q", h=H, a=2, w=2)
        xd = dst.rearrange("p (h a w q) -> p h a w q", h=H, a=2, w=2)
        xt = tmp[:, 0:H * dim].rearrange("p (h a w q) -> p h a w q", h=H, a=2, w=2)
        cs = tbl.ap[:, 0, sb, :].rearrange("p (a w q) -> p a w q", a=2, w=2)
        ss = tbl.ap[:, 1, sb, :].rearrange("p (a w q) -> p a w q", a=2, w=2)
        for a in range(2):
            cb = cs[:, a].unsqueeze(1).to_broadcast([128, H, 2, Q])
            self.tt(xd[:, :, a], xs[:, :, a], cb, ALU.mult, R + [tbl.res], W)
            for w in range(2):
                sbb = ss[:, a, w].unsqueeze(1).to_broadcast([128, H, Q])
                self.tt(xt[:, :, a, w], xs[:, :, a, 1 - w], sbb, ALU.mult, R + [tbl.res], [self.tmp_res])
            self.tt(xd[:, :, a], xd[:, :, a], xt[:, :, a], ALU.add, [self.tmp_res], W)

    def rms(self, dst, src, n, gam, R, W, sidx):
        sm = self.small.ap
        sr = self.small_r[sidx]
        c0 = 16 + sidx * 4
        self.tt(self.junk.ap[:, 0:n], src, src, ALU.mult, R, [self.junk.res])
        self.S.op("dve", lambda e: e.reduce_sum(sm[:, c0:c0 + 1], self.junk.ap[:, 0:n], axis=AX.X),
                  reads=[self.junk.res], writes=[sr])
        self.ts(sm[:, c0:c0 + 1], sm[:, c0:c0 + 1], 1.0 / n, RMS_EPS, ALU.mult, ALU.add, [], [sr])
        self.act(sm[:, c0:c0 + 1], sm[:, c0:c0 + 1], AF.Sqrt, [], [sr])
        self.S.op("dve", lambda e: e.reciprocal(sm[:, c0:c0 + 1], sm[:, c0:c0 + 1]), reads=[], writes=[sr])
        self.stt(dst, src, sm[:, c0:c0 + 1], gam.ap, ALU.mult, ALU.mult, R + [sr, gam.res], W)

    def project(self, l):
        d = self.d
        hT = self.atile("hTm", [8, 512], BF16)
        w_in = self.atile("w_in", [8, INW], BF16)
        wq = self.atile("wq", [2, 384], BF16)
        wkv = self.atile("wkv", [512], BF16)
        gq = self.atile("gq", [256], F32)
        gkv = self.atile("gkv", [128], F32)
        r32 = self.atile("r32", [2, NSB, 32], F32)
        r64 = self.atile("r64", [2, NSB, 64], F32)
        tp = self.atile("tp", [INW], F32)
        rq = self.atile("rq", [1184], F32)
        tmp = self.atile("tmp", [640], F32)
        self.tmp_res = tmp.res
        self.junk = self.atile("junk", [256], F32)
        nq = self.atile("nq", [256], BF16)
        nqT = self.atile("nqT", [2, 128], BF16)
        cq = self.atile("cq", [384], F32)
        ckv = self.atile("ckv", [128], F32)
        kpe96 = self.atile("kpe96", [96], F32)
        qA = self.atile("qA", [8, 128], BF16)
        qB = self.atile("qB", [8, 128], BF16)
        qC = self.atile("qC", [4, 128], BF16)
        kst = self.atile("kst", [5, 128], BF16)
        vst = self.atile("vst", [6, 65], BF16)
        wsrc = d["w_in"][l].rearrange("(k p) n -> p k n", p=128)
        self.dmas("pool", [(w_in.ap[:, :, c0:c1], wsrc[:, :, c0:c1]) for (c0, c1) in ((0, 512), (512, 1024), (1024, 1536), (1536, INW))],
                  [], [w_in.res], w_in.res)
        self.dma("pool", wq.ap, d["c_w_q_up"][l].rearrange("(k p) n -> p k n", p=128), [], [wq.res], wq.res)
        self.dma("pool", wkv.ap, d["c_w_kv_up"][l], [], [wkv.res], wkv.res)
        self.dma("sp", gq.ap, d["c_q_norm_g"][l].partition_broadcast(128), [], [gq.res], gq.res)
        self.dma("sp", gkv.ap, d["c_kv_norm_g"][l].partition_broadcast(128), [], [gkv.res], gkv.res)
        self.dma("sp", r32.ap, d["rope32"], [], [r32.res], r32.res)
        self.dma("sp", r64.ap, d["rope64"], [], [r64.res], r64.res)
        self.S.op("dve", lambda e: e.memset(kpe96.ap, 0.0), writes=[kpe96.res])
        self.S.op("dve", lambda e: e.memset(vst.ap, 1.0), writes=[vst.res])
        self.S.op("dve", lambda e: e.memset(qC.ap, 0.0), writes=[qC.res])
        self.S.op("dve", lambda e: e.memset(kst.ap, 0.0), writes=[kst.res])
        self.wkv_t = None
        groups = ((0, 512), (512, 1024), (1024, 1536), (1536, INW))
        rm = self.rmask
        lvl = int(os.environ.get("KLVL", "9"))
        skip = set(os.environ.get("KSKIP", "").split(","))
        stv = d["st"]
        for tile in [2, 3, 4, 5, 0, 1]:
            samp = tile >= 2
            self.pre(tile, hT, [self.ps[4], self.ps[5]])
            for tb in range(4):
                blk = tile * 4 + tb
                sb = blk - NPB
                for gi, (c0, c1) in enumerate(groups):
                    p = self.ps[gi]
                    for k in range(8):
                        self.mm(p.ap[:, 0:c1 - c0], hT.ap[:, k, tb * 128:(tb + 1) * 128], w_in.ap[:, k, c0:c1], k == 0, k == 7,
                                [hT.res, w_in.res], [p.res])
                    self.cp(tp.ap[:, c0:c1], p.ap[:, 0:c1 - c0], [p.res], [tp.res], eng="act")
                if lvl < 2:
                    continue
                if samp:
                    self.rope(rq.ap[:, 0:512], tp.ap[:, 0:512], 16, 8, r32, sb, [tp.res], [rq.res], tmp.ap)
                    self.rope(rq.ap[:, 512:1152], tp.ap[:, 768:1408], 10, 16, r64, sb, [tp.res], [rq.res], tmp.ap)
                    self.rope(rq.ap[:, 1152:1184], tp.ap[:, 1920:1952], 1, 8, r32, sb, [tp.res], [rq.res], tmp.ap)
                    s_aq, s_ak, s_bq, s_bk, s_kpe = (rq.ap[:, 0:256], rq.ap[:, 256:512], rq.ap[:, 512:1024],
                                                     rq.ap[:, 1024:1152], rq.ap[:, 1152:1184])
                    sres = [rq.res]
                else:
                    s_aq, s_ak, s_bq, s_bk, s_kpe = (tp.ap[:, 0:256], tp.ap[:, 256:512], tp.ap[:, 768:1280],
                                                     tp.ap[:, 1280:1408], tp.ap[:, 1920:1952])
                    sres = [tp.res]
                if lvl < 3:
                    continue
                self.rms(nq.ap, tp.ap[:, 1536:1792], 256, gq, [tp.res], [nq.res], 1)
                self.rms(ckv.ap, tp.ap[:, 1792:1920], 128, gkv, [tp.res], [ckv.res], 2)
                if not samp:
                    seq, t0 = blk // 2, (blk % 2) * 128
                    self.dmas("sp", [(stv[seq, l, t0:t0 + 128, 0:512], tp.ap[:, 256:768]),
                                     (stv[seq, l, t0:t0 + 128, 512:768], tp.ap[:, 1280:1536]),
                                     (stv[seq, l, t0:t0 + 128, 896:928], tp.ap[:, 1920:1952]),
                                     (stv[seq, l, t0:t0 + 128, 768:896], ckv.ap)],
                              [tp.res, ckv.res], [], tp.res)
                if lvl < 4:
                    continue
                p4, p5, p6, p7 = self.ps[4], self.ps[5], self.ps[6], self.ps[7]
                p4b = p4.ap[:, 0:128].bitcast(BF16)
                for kk in range(2):
                    self.tr(p4b[:, kk * 128:(kk + 1) * 128], nq.ap[:, kk * 128:(kk + 1) * 128], self.identB.ap,
                            [nq.res, self.identB.res], [p4.res])
                self.cp(nqT.ap, p4b.rearrange("p (a b) -> p a b", a=2), [p4.res], [nqT.res])
                for kk in range(2):
                    self.mm(p5.ap[:, 0:384], nqT.ap[:, kk, :], wq.ap[:, kk, :], kk == 0, kk == 1, [nqT.res, wq.res], [p5.res])
                self.cp(cq.ap, p5.ap[:, 0:384], [p5.res], [cq.res], eng="act")
                if samp:
                    cq4 = cq.ap.rearrange("p (h e) -> p h e", e=96)
                    p54 = p5.ap[:, 0:384].rearrange("p (h e) -> p h e", e=96)
                    self.rope_strided(cq4[:, :, 64:96], p54[:, :, 64:96], 4, 8, r32, sb, [p5.res], [cq.res], tmp.ap)
                if lvl < 5:
                    continue
                if "A" not in skip:
                    for c in range(2):
                        self.tr(p6.ap[:, c * 128:(c + 1) * 128], s_aq[:, c * 128:(c + 1) * 128], self.identF.ap, sres + [self.identF.res], [p6.res])
                        self.tr(p6.ap[:, 256 + c * 128:256 + (c + 1) * 128], s_ak[:, c * 128:(c + 1) * 128], self.identF.ap, sres + [self.identF.res], [p6.res])
                    for c in range(2):
                        for i in range(4):
                            self.ts(qA.ap[:, c * 4 + i, :], p6.ap[:, c * 128:(c + 1) * 128], rm.ap[:, i:i + 1], None, ALU.mult, None,
                                    [p6.res, rm.res], [qA.res])
                    self.cp(kst.ap[:, 0:2, :], p6.ap[:, 256:512].rearrange("p (a b) -> p a b", a=2), [p6.res], [kst.res], eng="act")
                if "B" not in skip:
                    for c in range(4):
                        self.tr(p7.ap[:, c * 128:(c + 1) * 128], s_bq[:, c * 128:(c + 1) * 128], self.identF.ap, sres + [self.identF.res], [p7.res])
                    for hq in range(8):
                        self.ts(qB.ap[:, hq, :], p7.ap[:, (hq // 2) * 128:(hq // 2 + 1) * 128], rm.ap[:, 4 + hq % 2:5 + hq % 2], None,
                                ALU.mult, None, [p7.res, rm.res], [qB.res])
                if "P4" not in skip:
                    self.cp(kpe96.ap[:, 64:96], s_kpe, sres, [kpe96.res])
                    self.tr(p4.ap[:, 0:128], s_bk, self.identF.ap, sres + [self.identF.res], [p4.res])
                    self.tr(p4.ap[:, 128:256], ckv.ap, self.identF.ap, [ckv.res, self.identF.res], [p4.res])
                    self.tr(p4.ap[0:96, 256:384], kpe96.ap, self.identF.ap, [kpe96.res, self.identF.res], [p4.res])
                    self.cp(kst.ap[:, 2:4, :], p4.ap[:, 0:256].rearrange("p (a b) -> p a b", a=2), [p4.res], [kst.res], eng="act")
                    self.cp(kst.ap[64:96, 4, :], p4.ap[64:96, 256:384], [p4.res], [kst.res])
                if "QC" not in skip:
                    for h in range(4):
                        self.tr(p5.ap[0:96, h * 128:(h + 1) * 128], cq.ap[:, h * 96:(h + 1) * 96], self.identF.ap, [cq.res, self.identF.res], [p5.res])
                    self.cp(qC.ap[0:96, :, :], p5.ap[0:96, :].rearrange("p (a b) -> p a b", a=4), [p5.res], [qC.res])
                if "V" not in skip:
                    self.cp(vst.ap[:, 0:4, 0:64], tp.ap[:, 512:768].rearrange("p (h e) -> p h e", e=64), [tp.res], [vst.res])
                    self.cp(vst.ap[:, 4:6, 0:64], tp.ap[:, 1408:1536].rearrange("p (h e) -> p h e", e=64), [tp.res], [vst.res])
                if lvl < 6:
                    continue
                tok = blk * 128
                QS = d["QS"]
                name = "XB" if samp else "PB"
                xt0 = (blk - NPB) * 128 if samp else blk * 128
                pairs = [
                    (QS[0:1024, tok:tok + 128].rearrange("(i p) t -> p i t", p=128), qA.ap),
                    (QS[1024:2048, tok:tok + 128].rearrange("(i p) t -> p i t", p=128), qB.ap),
                    (QS[2048:2560, tok:tok + 128].rearrange("(i p) t -> p i t", p=128), qC.ap),
                    (self.xrows(name, xt0, 128, 0, 256).rearrange("(c p) t -> p c t", p=128), kst.ap[:, 0:2, :]),
                    (self.xrows(name, xt0, 128, 256, 256).rearrange("(c p) t -> p c t", p=128), kst.ap[:, 2:4, :]),
                    (self.xrows(name, xt0, 128, 512, 32), kst.ap[64:96, 4, :]),
                    (self.xv(name, xt0, 128), vst.ap.rearrange("p h e -> p (h e)")),
                ]
                wr = [self.QSr[blk], (self.XBr[blk - NPB] if samp else self.PBr[blk])]
                self.dmas("sp", pairs, [qA.res, qB.res, qC.res, kst.res, vst.res], wr, qA.res)

    def rope_strided(self, dst, src, H, Q, tbl, sb, R, W, tmp):
        dim = 4 * Q
        xs = src.rearrange("p h (a w q) -> p h a w q", a=2, w=2)
        xd = dst.rearrange("p h (a w q) -> p h a w q", a=2, w=2)
        xt = tmp[:, 0:H * dim].rearrange("p (h a w q) -> p h a w q", h=H, a=2, w=2)
        cs = tbl.ap[:, 0, sb, :].rearrange("p (a w q) -> p a w q", a=2, w=2)
        ss = tbl.ap[:, 1, sb, :].rearrange("p (a w q) -> p a w q", a=2, w=2)
        for a in range(2):
            cb = cs[:, a].unsqueeze(1).to_broadcast([128, H, 2, Q])
            self.tt(xd[:, :, a], xs[:, :, a], cb, ALU.mult, R + [tbl.res], W)
            for w in range(2):
                sbb = ss[:, a, w].unsqueeze(1).to_broadcast([128, H, Q])
                self.tt(xt[:, :, a, w], xs[:, :, a, 1 - w], sbb, ALU.mult, R + [tbl.res], [self.tmp_res])
            self.tt(xd[:, :, a], xd[:, :, a], xt[:, :, a], ALU.add, [self.tmp_res], W)

    def gather(self):
        d = self.d
        for pi, (r0, r1) in enumerate(XG_PIECES):
            src = d["XB"][r0:r1, :]
            dst = d["XG%d" % pi]
            self.S.op("pool", lambda e, src=src, dst=dst: e.collective_compute(
                "AllGather", ALU.bypass, replica_groups=[[0, 1], [2, 3], [4, 5], [6, 7]], ins=[src], outs=[dst]),
                reads=list(self.XBr), writes=[self.dres["XG"]], dma=self.dres["XG"], inc=1)

    def layer_consts(self, l):
        d = self.d
        lam_init = 0.8 - 0.6 * float(np.exp(-0.3 * l))
        al = self.atile("al", [128], F32)
        self.gsub = self.atile("gsub", [64], F32)
        self.es = self.atile("es", [8], F32)
        self.lam = self.atile("lamc", [4], F32)
        self.dma("sp", al.ap, d["a_lambda"][l].partition_broadcast(128), [], [al.res], al.res)
        self.dma("sp", self.gsub.ap, d["a_subln_g"][l].partition_broadcast(128), [], [self.gsub.res], self.gsub.res)
        self.dma("sp", self.es.ap, d["b_sink"][l].partition_broadcast(128), [], [self.es.res], self.es.res)
        self.act(self.es.ap, self.es.ap, AF.Exp, [], [self.es.res])
        self.ts(self.gsub.ap, self.gsub.ap, 1.0 - lam_init, None, ALU.mult, None, [], [self.gsub.res])
        lm = self.lam
        for i in range(2):
            self.tt(al.ap[:, i * 64:i * 64 + 32], al.ap[:, i * 64:i * 64 + 32], al.ap[:, i * 64 + 32:i * 64 + 64], ALU.mult, [], [al.res])
            self.S.op("dve", lambda e, i=i: e.reduce_sum(lm.ap[:, i:i + 1], al.ap[:, i * 64:i * 64 + 32], axis=AX.X),
                      reads=[al.res], writes=[lm.res])
        self.act(lm.ap[:, 0:2], lm.ap[:, 0:2], AF.Exp, [], [lm.res])
        self.tt(lm.ap[:, 2:3], lm.ap[:, 0:1], lm.ap[:, 1:2], ALU.subtract, [], [lm.res])
        self.ts(lm.ap[:, 3:4], lm.ap[:, 2:3], lam_init, -1.0, ALU.add, ALU.mult, [], [lm.res])
        self.pt = [self.atile("pt%d" % i, [512], BF16) for i in range(3)]
        self.pti = 0
        self.sci = 0
        self.rec = self.atile("rec", [8], F32)
        self.ctmp = self.atile("ctmp", [2, 256], F32)

    def attend(self, q_ap, q_res, kblocks, nq, scale, obank):
        nqs = nq // 128
        first = True
        nkb = len(kblocks)
        for i, (kT, kres, V, vres, mask) in enumerate(kblocks):
            sc = self.ps[self.sci % 2]
            self.sci += 1
            self.mm(sc.ap[:, 0:nq], kT, q_ap, True, mask is None, kres + [q_res], [sc.res])
            if mask is not None:
                self.mm(sc.ap[:, 0:nq], self.identB.ap, mask, False, True, [self.identB.res, self.bmask.res], [sc.res])
            pt = self.pt[self.pti % 3]
            self.pti += 1
            self.act(pt.ap[:, 0:nq], sc.ap[:, 0:nq], AF.Exp, [sc.res], [pt.res], scale=float(scale))
            for qs in range(nqs):
                self.mm(obank.ap[:, qs * 65:(qs + 1) * 65], pt.ap[:, qs * 128:(qs + 1) * 128], V, first, i == nkb - 1,
                        [pt.res] + vres, [obank.res], skip=True)
                first = False

    def load_ctxT(self, src_dram, width, dst_fn, dres):
        ct = self.ctmp
        self.dma("sp", ct.ap[:, :, 0:width], src_dram.rearrange("(b p) f -> p b f", p=128), [], [ct.res], ct.res)
        p = self.ps[2]
        nch = width // 128
        for kb in range(2):
            for c in range(nch):
                self.tr(p.ap[:, (kb * nch + c) * 128:(kb * nch + c + 1) * 128], ct.ap[:, kb, c * 128:(c + 1) * 128], self.identF.ap,
                        [ct.res, self.identF.res], [p.res])
        for kb in range(2):
            for c in range(nch):
                self.cp(dst_fn(c, kb), p.ap[:, (kb * nch + c) * 128:(kb * nch + c + 1) * 128], [p.res], [dres])

    def attn_A(self, l, segs, ctx, qtiles, o_t, oblk0, qtok_unused):
        d = self.d
        nk = (256 if ctx else 0) + sum(s[2] for s in segs)
        nkb = nk // 128
        kT = self.atile("kT_A", [2, nk], BF16)
        V = self.atile("V_A", [nkb, 4, 65], BF16)
        qm = [self.atile("qmA%d" % i, [512], BF16) for i in range(2)]
        oa = self.atile("oa", [4, 4, 64], F32)
        o1 = self.atile("o1", [4, 64], F32)
        o2 = self.atile("o2", [64], F32)
        ssq = self.atile("ssq", [16], F32)
        sq = self.atile("sqA", [4, 4, 64], F32)
        koff = 0
        if ctx:
            self.load_ctxT(d["ck_a"][l], 256, lambda c, kb: kT.ap[:, c, kb * 128:(kb + 1) * 128], kT.res)
            self.dmas("pool", [(V.ap[:, b_, :, 0:64], d["cv_a"][l][b_ * 128:(b_ + 1) * 128, :].rearrange("p (h e) -> p h e", e=64))
                               for b_ in range(2)], [], [V.res], V.res)
            self.S.op("dve", lambda e: e.memset(V.ap[:, 0:2, :, 64:65], 1.0), writes=[V.res])
            koff = 256
        pairs = []
        rd = []
        for (name, t0, nt) in segs:
            pairs.append((kT.ap[:, :, koff:koff + nt], self.xrows(name, t0, nt, 0, 256).rearrange("(c p) t -> p c t", p=128)))
            pairs.append((V.ap[:, koff // 128:(koff + nt) // 128, :, :],
                          self.xv(name, t0, nt)[:, 0:260].rearrange("(b p) (h e) -> p b h e", p=128, e=65)))
            rd += self.xres_of(name, t0, nt)
            koff += nt
        self.dmas("sp", pairs, rd, [kT.res, V.res], kT.res)
        QS = d["QS"]
        scale = 32 ** -0.5
        qi = 0
        for ti, (qt0, nq) in enumerate(qtiles):
            nqs = nq // 128
            qres = [self.QSr[b] for b in range(qt0 // 128, (qt0 + nq) // 128)]
            for i in range(8):
                c, h, j = i // 4, (i // 4) * 2 + (i % 4) // 2, i % 2
                q = qm[qi % 2]
                qi += 1
                self.dma("sp", q.ap[:, 0:nq], QS[i * 128:(i + 1) * 128, qt0:qt0 + nq], qres, [q.res], q.res)
                ob = self.ps[4 + (i % 2)]
                kbl = [(kT.ap[:, c, kb * 128:(kb + 1) * 128], [kT.res], V.ap[:, kb, h, :], [V.res], None) for kb in range(nkb)]
                self.attend(q.ap[:, 0:nq], q.res, kbl, nq, scale, ob)
                for qs in range(nqs):
                    self.S.op("dve", lambda e, qs=qs, ob=ob: e.reciprocal(self.rec.ap[:, qs:qs + 1], ob.ap[:, qs * 65 + 64:qs * 65 + 65]),
                              reads=[ob.res], writes=[self.rec.res])
                    if j == 0:
                        self.ts(o1.ap[:, qs, :], ob.ap[:, qs * 65:qs * 65 + 64], self.rec.ap[:, qs:qs + 1], None, ALU.mult, None,
                                [ob.res, self.rec.res], [o1.res])
                    else:
                        self.ts(o2.ap, ob.ap[:, qs * 65:qs * 65 + 64], self.rec.ap[:, qs:qs + 1], None, ALU.mult, None,
                                [ob.res, self.rec.res], [o2.res])
                        self.stt(oa.ap[:, qs, h, :], o2.ap, self.lam.ap[:, 3:4], o1.ap[:, qs, :], ALU.mult, ALU.add,
                                 [o2.res, o1.res, self.lam.res], [oa.res])
            n16 = nqs * 4
            self.tt(sq.ap[:, 0:nqs], oa.ap[:, 0:nqs], oa.ap[:, 0:nqs], ALU.mult, [oa.res], [sq.res])
            self.S.op("dve", lambda e, nqs=nqs, n16=n16: e.reduce_sum(ssq.ap[:, 0:n16], sq.ap[:, 0:nqs].rearrange("p a b c -> p (a b) c"), axis=AX.X),
                      reads=[sq.res], writes=[ssq.res])
            self.ts(ssq.ap[:, 0:n16], ssq.ap[:, 0:n16], 1.0 / 64, RMS_EPS, ALU.mult, ALU.add, [], [ssq.res])
            self.act(ssq.ap[:, 0:n16], ssq.ap[:, 0:n16], AF.Sqrt, [], [ssq.res])
            self.S.op("dve", lambda e, n16=n16: e.reciprocal(ssq.ap[:, 0:n16], ssq.ap[:, 0:n16]), reads=[], writes=[ssq.res])
            for qs in range(nqs):
                for h in range(4):
                    ob_ = oblk0 + ti * nqs + qs
                    self.stt(o_t.ap[:, ob_, h * 64:(h + 1) * 64], oa.ap[:, qs, h, :], ssq.ap[:, qs * 4 + h:qs * 4 + h + 1], self.gsub.ap,
                             ALU.mult, ALU.mult, [oa.res, ssq.res, self.gsub.res], [o_t.res])

    def attn_C(self, l, segs, ctx, qtiles, o_t, oblk0, qtok_unused):
        d = self.d
        nk = (256 if ctx else 0) + sum(s[2] for s in segs)
        nkb = nk // 128
        wkv = self.atile("wkvC", [512], BF16)
        self.dma("pool", wkv.ap, d["c_w_kv_up"][l], [], [wkv.res], wkv.res)
        wkv4 = wkv.ap.rearrange("p (h e) -> p h e", e=128)
        ckT = [self.atile("ckT%d" % i, [512], BF16) for i in range(2)]
        cpe = self.atile("cpe", [2, 96], F32)
        qm = [self.atile("qmC%d" % i, [512], BF16) for i in range(2)]
        kT = self.atile("kT_C", [2, nk], BF16)
        V = self.atile("V_C", [nkb, 2, 65], BF16)
        QS = d["QS"]
        scale = 96 ** -0.5
        qi = 0
        ci = 0
        for hp in range(2):
            self.S.op("dve", lambda e: e.memset(V.ap[:, :, :, 64:65], 1.0), writes=[V.res])
            ktiles = []
            koff = 0
            if ctx:
                ktiles.append((None, 0, 256, 0))
                koff = 256
            for (name, t0, nt) in segs:
                for s0 in range(0, nt, 512):
                    n = min(512, nt - s0)
                    ktiles.append((name, t0 + s0, n, koff))
                    koff += n
            if ctx:
                self.S.op("dve", lambda e: e.memset(cpe.ap, 0.0), writes=[cpe.res])
                self.dma("sp", cpe.ap[:, :, 64:96], d["c_kpe"][l].rearrange("(b p) f -> p b f", p=128), [], [cpe.res], cpe.res)
                p = self.ps[2]
                for kb in range(2):
                    self.tr(p.ap[0:96, kb * 128:(kb + 1) * 128], cpe.ap[:, kb, :], self.identF.ap, [cpe.res, self.identF.res], [p.res])
                for hh in range(2):
                    self.cp(kT.ap[64:96, hh, 0:256], p.ap[64:96, 0:256], [p.res], [kT.res])
            pairs = []
            rd = []
            ko2 = 256 if ctx else 0
            for (name, t0, nt) in segs:
                for hh in range(2):
                    pairs.append((kT.ap[64:96, hh, ko2:ko2 + nt], self.xrows(name, t0, nt, 512, 32)))
                rd += self.xres_of(name, t0, nt)
                ko2 += nt
            self.dmas("sp", pairs, rd, [kT.res], kT.res)
            for (name, t0, n, ko) in ktiles:
                ck = ckT[ci % 2]
                ci += 1
                if name is None:
                    self.load_ctxT(d["c_ckv"][l], 128, lambda c, kb: ck.ap[:, kb * 128:(kb + 1) * 128], ck.res)
                else:
                    self.dma("sp", ck.ap[:, 0:n], self.xrows(name, t0, n, 384, 128), self.xres_of(name, t0, n), [ck.res], ck.res)
                for hh in range(2):
                    h = hp * 2 + hh
                    p = self.ps[2 + hh]
                    self.mm(p.ap[:, 0:n], wkv.ap[:, h * 128:(h + 1) * 128], ck.ap[:, 0:n], True, True, [wkv.res, ck.res], [p.res])
                    self.cp(kT.ap[0:64, hh, ko:ko + n], p.ap[0:64, 0:n], [p.res], [kT.res], eng=("act" if hh else "dve"))
                p = self.ps[6]
                for b in range(n // 128):
                    self.mm(p.ap[:, b * 128:(b + 1) * 128], ck.ap[:, b * 128:(b + 1) * 128], wkv4[:, hp * 2:hp * 2 + 2, 64:128], True, True,
                            [ck.res, wkv.res], [p.res])
                self.cp(V.ap[:, ko // 128:(ko + n) // 128, :, 0:64],
                        p.ap[:, 0:n].rearrange("p (b h e) -> p b h e", h=2, e=64), [p.res], [V.res])
            for ti, (qt0, nq) in enumerate(qtiles):
                nqs = nq // 128
                qres = [self.QSr[b] for b in range(qt0 // 128, (qt0 + nq) // 128)]
                for hh in range(2):
                    h = hp * 2 + hh
                    q = qm[qi % 2]
                    qi += 1
                    self.dma("sp", q.ap[0:96, 0:nq], QS[2048 + h * 128:2048 + h * 128 + 96, qt0:qt0 + nq], qres, [q.res], q.res)
                    ob = self.ps[4 + (qi % 2)]
                    kbl = [(kT.ap[0:96, hh, kb * 128:(kb + 1) * 128], [kT.res], V.ap[:, kb, hh, :], [V.res], None) for kb in range(nkb)]
                    self.attend(q.ap[0:96, 0:nq], q.res, kbl, nq, scale, ob)
                    for qs in range(nqs):
                        ob_ = oblk0 + ti * nqs + qs
                        self.S.op("dve", lambda e, qs=qs, ob=ob: e.reciprocal(self.rec.ap[:, qs:qs + 1], ob.ap[:, qs * 65 + 64:qs * 65 + 65]),
                                  reads=[ob.res], writes=[self.rec.res])
                        self.ts(o_t.ap[:, ob_, 768 + h * 64:768 + (h + 1) * 64], ob.ap[:, qs * 65:qs * 65 + 64], self.rec.ap[:, qs:qs + 1], None,
                                ALU.mult, None, [ob.res, self.rec.res], [o_t.res])

    def attn_B(self, l, samp, seq, o_t, oblk0):
        d = self.d
        QS = d["QS"]
        scale = 64 ** -0.5
        if samp:
            nkb = 20
        else:
            nkb = 2
        nk = nkb * 128
        kT = self.atile("kT_B", [2, nk], BF16)
        V = self.atile("V_B", [nkb, 2, 65], BF16)
        qm = [self.atile("qmB%d" % i, [8, 512], BF16) for i in range(2)]
        dup = self.atile("dupB", [2, 64], F32)
        pairs = []
        rd = []
        if samp:
            ct = self.ctmp
            self.dma("sp", ct.ap[:, :, 0:128], d["ck_b"][l].rearrange("(b p) f -> p b f", p=128), [], [ct.res], ct.res)
            p = self.ps[2]
            for kvh in range(2):
                for kb in range(2):
                    for r in range(2):
                        self.cp(dup.ap[:, r, :], ct.ap[:, kb, kvh * 64:(kvh + 1) * 64], [ct.res], [dup.res])
                    self.tr(p.ap[:, (kvh * 2 + kb) * 128:(kvh * 2 + kb + 1) * 128], dup.ap.rearrange("p a b -> p (a b)"), self.identF.ap,
                            [dup.res, self.identF.res], [p.res])
            for kvh in range(2):
                self.cp(kT.ap[:, kvh, 0:256], p.ap[:, kvh * 256:(kvh + 1) * 256], [p.res], [kT.res])
            self.dmas("pool", [(V.ap[:, b_, :, 0:64], d["cv_b"][l][b_ * 128:(b_ + 1) * 128, :].rearrange("p (h e) -> p h e", e=64))
                               for b_ in range(2)], [], [V.res], V.res)
            self.S.op("dve", lambda e: e.memset(V.ap[:, 0:2, :, 64:65], 1.0), writes=[V.res])
            srcs = [("XB", 0, 2048, 256), ("XG", 15 * 128, 128, 18 * 128), ("XG", 16 * 128, 128, 19 * 128)]
        else:
            srcs = [("PB", seq * 256, 256, 0)]
        for (name, t0, nt, ko) in srcs:
            for kvh in range(2):
                for r in range(2):
                    pairs.append((kT.ap[r * 64:(r + 1) * 64, kvh, ko:ko + nt], self.xrows(name, t0, nt, 256 + kvh * 64, 64)))
            pairs.append((V.ap[:, ko // 128:(ko + nt) // 128, :, :],
                          self.xv(name, t0, nt)[:, 260:390].rearrange("(b p) (h e) -> p b h e", p=128, e=65)))
            rd += self.xres_of(name, t0, nt)
        self.dmas("sp", pairs, rd, [kT.res, V.res], kT.res)
        nblk = 16 if samp else 2
        ntile = 4 if samp else 1
        per = nblk // ntile
        for ti in range(ntile):
            tok0 = (1024 + ti * 512) if samp else seq * 256
            nq = per * 128
            q = qm[ti % 2]
            qres = [self.QSr[b] for b in range(tok0 // 128, (tok0 + nq) // 128)]
            self.dma("sp", q.ap[:, :, 0:nq], QS[1024:2048, tok0:tok0 + nq].rearrange("(i p) t -> p i t", p=128), qres, [q.res], q.res)
            for bi in range(per):
                i = ti * per + bi
                for hq in range(8):
                    kvh = hq // 4
                    ob = self.ps[4 + (hq % 2)]

                    def kb_(idx, mask=None):
                        return (kT.ap[:, kvh, idx * 128:(idx + 1) * 128], [kT.res], V.ap[:, idx, kvh, :], [V.res], mask)
                    bm = self.bmask.ap
                    if samp:
                        kbl = [kb_(0), kb_(1)]
                        kbl.append(kb_(2 + i - 1, bm[:, 0, :]) if i > 0 else kb_(18, bm[:, 2, :]))
                        kbl.append(kb_(2 + i))
                        kbl.append(kb_(2 + i + 1, bm[:, 1, :]) if i < 15 else kb_(19, bm[:, 3, :]))
                    else:
                        kbl = [kb_(0), kb_(1)]
                    self.attend(q.ap[:, hq, bi * 128:(bi + 1) * 128], q.res, kbl, 128, scale, ob)
                    self.ts(self.rec.ap[:, 0:1], ob.ap[:, 64:65], self.es.ap[:, hq:hq + 1], None, ALU.add, None,
                            [ob.res, self.es.res], [self.rec.res])
                    self.S.op("dve", lambda e: e.reciprocal(self.rec.ap[:, 0:1], self.rec.ap[:, 0:1]), reads=[], writes=[self.rec.res])
                    self.ts(o_t.ap[:, oblk0 + i, 256 + hq * 64:256 + (hq + 1) * 64], ob.ap[:, 0:64], self.rec.ap[:, 0:1], None,
                            ALU.mult, None, [ob.res, self.rec.res], [o_t.res])

    def wo_phase(self, l, tiles, o_t, blk0):
        d = self.d
        w_o = self.atile("w_o", [8, D], BF16)
        oT = self.atile("oT", [8, 512], BF16)
        yb = [self.atile("ybm%d" % i, [D], F32) for i in range(4)]
        wsrc = d["w_o"][l].rearrange("(k p) n -> p k n", p=128)
        self.dmas("pool", [(w_o.ap[:, :, 0:512], wsrc[:, :, 0:512]), (w_o.ap[:, :, 512:1024], wsrc[:, :, 512:1024])], [], [w_o.res], w_o.res)
        for tile in tiles:
            cond = 1 if tile < 2 else 0
            for k in range(8):
                p = self.ps[k % 2]
                pb = p.ap[:, 0:256].bitcast(BF16)
                for tb in range(4):
                    ob_ = tile * 4 + tb - blk0
                    self.tr(pb[:, tb * 128:(tb + 1) * 128], o_t.ap[:, ob_, k * 128:(k + 1) * 128], self.identB.ap,
                            [o_t.res, self.identB.res], [p.res])
                self.cp(oT.ap[:, k, :], pb, [p.res], [oT.res], eng=("act" if k % 2 else "dve"))
            for hf in range(2):
                for tb in range(4):
                    p = self.ps[4 + tb] if hf == 0 else self.ps[2 + (tb % 2)]
                    for k in range(8):
                        self.mm(p.ap, oT.ap[:, k, tb * 128:(tb + 1) * 128], w_o.ap[:, k, hf * 512:(hf + 1) * 512], k == 0, k == 7,
                                [oT.res, w_o.res], [p.res])
                    self.tt(yb[tb].ap[:, hf * 512:(hf + 1) * 512], p.ap, self.gate_bc.ap[:, cond, hf * 512:(hf + 1) * 512], ALU.mult,
                            [p.res, self.gate_bc.res], [yb[tb].res])
            for tb in range(4):
                self.post(tile * 4 + tb, cond, yb[tb])


def build_nc():
    b = Builder()
    nc = b.build()
    return nc, list(b.in_names)


def _rope_tables(dim, pos0, n):
    t = np.arange(pos0, pos0 + n)
    row = (t // 64).astype(np.float32)
    col = (t % 64).astype(np.float32)
    a = dim // 2
    inv = np.power(np.float32(10000.0), -np.arange(0, a, 2, dtype=np.float32) / np.float32(a)).astype(np.float32)
    ar = row[:, None] * inv[None, :]
    ac = col[:, None] * inv[None, :]
    ang = np.concatenate([ar, ar, ac, ac], axis=-1).astype(np.float32)
    cos = np.cos(ang).astype(np.float32)
    sin = np.sin(ang).astype(np.float32)
    q = a // 2
    sgn = np.ones(dim, np.float32)
    sgn[0:q] = -1.0
    sgn[a:a + q] = -1.0
    ss = sin * sgn[None, :]
    out = np.stack([cos, ss], 0).reshape(2, n // 128, 128, dim).transpose(2, 0, 1, 3)
    return np.ascontiguousarray(out, dtype=np.float32)


def _consts(core):
    half = core % 2
    ident = np.eye(128, dtype=np.float32)
    k = np.arange(128)[:, None]
    q = np.arange(128)[None, :]
    mL = np.where(k >= q, 0.0, NEG).astype(np.float32)
    mR = np.where(k <= q, 0.0, NEG).astype(np.float32)
    full = np.full((128, 128), NEG, np.float32)
    mLe = full if half == 0 else mL
    mRe = full if half == 1 else mR
    bmask = np.ascontiguousarray(np.stack([mL, mR, mLe, mRe], 1))
    rmask = np.zeros((128, 6), np.float32)
    for i in range(4):
        rmask[32 * i:32 * i + 32, i] = 1.0
    rmask[0:64, 4] = 1.0
    rmask[64:128, 5] = 1.0
    return ident, bmask, rmask


_NC_CACHE = {}


def kernel(**inp):
    f = lambda a: np.ascontiguousarray(np.asarray(a, dtype=np.float32))
    x_prompt = f(inp["x_prompt"]); x_sample = f(inp["x_sample"])
    if "nc" not in _NC_CACHE:
        _NC_CACHE["nc"] = build_nc()
    nc, in_names = _NC_CACHE["nc"]
    shared = {
        "w_mod": f(inp["w_mod"]),
        "bmT": np.ascontiguousarray(f(inp["b_mod"]).reshape(DEPTH, 72, 128).transpose(2, 0, 1)),
        "ln_g": f(inp["ln_g"]), "ln_b": f(inp["ln_b"]),
        "ffn_w1": f(inp["ffn_w1"]), "ffn_w3": f(inp["ffn_w3"]), "ffn_w2": f(inp["ffn_w2"]),
        "w_in": f(inp["w_in"]), "w_o": f(inp["w_o"]),
        "a_lambda": f(inp["a_lambda"]).reshape(DEPTH, 128), "a_subln_g": f(inp["a_subln_g"]),
        "b_sink": f(inp["b_sink"]), "c_q_norm_g": f(inp["c_q_norm_g"]), "c_w_q_up": f(inp["c_w_q_up"]),
        "c_kv_norm_g": f(inp["c_kv_norm_g"]), "c_w_kv_up": f(inp["c_w_kv_up"]).reshape(DEPTH, 128, 512),
    }
    c = f(inp["c"]); c_ctx = f(inp["c_ctx"])
    in_maps = []
    for core in range(8):
        b = core // 2
        half = core % 2
        m = dict(shared)
        m["xin"] = np.ascontiguousarray(np.concatenate(
            [x_prompt[4 * core:4 * core + 4].reshape(1024, D), x_sample[b, half * 2048:(half + 1) * 2048]], 0))
        cond = np.stack([c[b], c_ctx], 0)
        m["condT"] = np.ascontiguousarray(cond.reshape(2, 8, 128).transpose(2, 1, 0))
        m["ck_a"] = f(inp["cache_a_k"])[b].reshape(DEPTH, 256, 256)
        m["cv_a"] = f(inp["cache_a_v"])[b].reshape(DEPTH, 256, 256)
        m["ck_b"] = f(inp["cache_b_k"])[b].reshape(DEPTH, 256, 128)
        m["cv_b"] = f(inp["cache_b_v"])[b].reshape(DEPTH, 256, 128)
        m["c_ckv"] = f(inp["cache_c_kv"])[b]
        m["c_kpe"] = f(inp["cache_c_kpe"])[b]
        ident, bmask, rmask = _consts(core)
        m["ident"] = ident; m["bmask"] = bmask; m["rmask"] = rmask
        m["rope32"] = _rope_tables(32, half * 2048, 2048)
        m["rope64"] = _rope_tables(64, half * 2048, 2048)
        in_maps.append({k: np.ascontiguousarray(m[k]) for k in in_names})
    res = run_bass_kernel_spmd(nc, in_maps, core_ids=list(range(8)))
    ys = [np.asarray(r["y"]) for r in res.results]
    sts = [np.asarray(r["st"]) for r in res.results]
    y_prompt = np.concatenate([y[:1024].reshape(4, 256, D) for y in ys], 0)
    y_sample = np.stack([np.concatenate([ys[2 * b][1024:], ys[2 * b + 1][1024:]], 0) for b in range(4)], 0)
    stt = np.concatenate(sts, 0)
    new_a_k = stt[..., 0:256].reshape(32, DEPTH, 256, 4, 64)
    new_a_v = stt[..., 256:512].reshape(32, DEPTH, 256, 4, 64)
    new_b_k = stt[..., 512:640].reshape(32, DEPTH, 256, 2, 64)
    new_b_v = stt[..., 640:768].reshape(32, DEPTH, 256, 2, 64)
    new_c_kv = stt[..., 768:896]
    new_c_kpe = stt[..., 896:928]
    outs = (y_prompt, y_sample, new_a_k, new_a_v, new_b_k, new_b_v, new_c_kv, new_c_kpe)
    return tuple(np.ascontiguousarray(o, dtype=np.float32) for o in outs)
```

```python
import os
import numpy as np
from contextlib import ExitStack
import concourse.bass as bass
import concourse.mybir as mybir
from concourse.bass_utils import run_bass_kernel_spmd

F32 = mybir.dt.float32
BF16 = mybir.dt.bfloat16
ALU = mybir.AluOpType
AF = mybir.ActivationFunctionType
AX = mybir.AxisListType

D = 1024
DFF = 2816
NJ = DFF // 128
DEPTH = 2
INW = 1952
LN_EPS = 1e-5
RMS_EPS = 1e-6
DN_ALPHA = float((2 * DEPTH) ** 0.25)
NEG = -30000.0
NPB = 8
NSB = 16
NB = NPB + NSB
XK_ROWS = 544
XV_W = 390
X_ROWS = XK_ROWS + XV_W
QS_ROWS = 2560
XG_PIECES = ((0, 256), (256, 512), (512, 739), (739, 934))


class Res:
    __slots__ = ("name", "w", "r", "dsem", "dcnt", "excl")

    def __init__(self, name, excl=False):
        self.name = name
        self.excl = excl
        self.w = None
        self.r = {}
        self.dsem = None
        self.dcnt = 0

    def inherit(self, other):
        if other.w is not None:
            k, v = other.w
            if self.r.get(k, 0) < v:
                self.r[k] = v
        for k, v in other.r.items():
            if self.r.get(k, 0) < v:
                self.r[k] = v


class Sched:
    ENG = ("pe", "act", "dve", "pool", "sp")

    def __init__(self):
        self.q = {e: [] for e in self.ENG}
        self.cnt = {e: 0 for e in self.ENG}
        self.waited = {e: {} for e in self.ENG}
        self.ndsem = 0
        self.semmap = {}

    def _dep(self, eng, k, v):
        if eng == "pe" and k == ("e", "pe"):
            return
        if self.waited[eng].get(k, 0) >= v:
            return
        self.waited[eng][k] = v
        self.q[eng].append(("wait", k, v))

    def op(self, eng, fn, reads=(), writes=(), dma=None, ndma=1, inc=16):
        for r in reads:
            if r.w is not None:
                self._dep(eng, *r.w)
            if r.excl:
                for k, v in r.r.items():
                    if k != ("e", eng):
                        self._dep(eng, k, v)
        for w in writes:
            if w.w is not None:
                self._dep(eng, *w.w)
            for k, v in w.r.items():
                self._dep(eng, k, v)
        if dma is None:
            self.cnt[eng] += 1
            tok = (("e", eng), self.cnt[eng])
            self.q[eng].append(("op", fn, tok[0], 1))
        else:
            ent = self.semmap.get(dma.name)
            if ent is None:
                ent = [("d", self.ndsem), 0, None]
                self.ndsem += 1
                self.semmap[dma.name] = ent
            if ent[2] is not dma:
                if ent[1] > 0:
                    self._dep(eng, ent[0], ent[1])
                ent[2] = dma
            ent[1] += inc * ndma
            tok = (ent[0], ent[1])
            self.q[eng].append(("op", fn, tok[0], inc))
        for r in reads:
            if r.r.get(tok[0], 0) < tok[1]:
                r.r[tok[0]] = tok[1]
        for w in writes:
            w.w = tok
            w.r = {}
        return tok

    def finish(self, eng="sp"):
        for ent in self.semmap.values():
            self._dep(eng, ent[0], ent[1])
        for e in self.ENG:
            if e != eng and self.cnt[e] > 0:
                self._dep(eng, ("e", e), self.cnt[e])

    def emit(self, nc, stack):
        sems = {}
        for e in self.ENG:
            sems[("e", e)] = stack.enter_context(nc.semaphore("s_" + e))
        for i in range(self.ndsem):
            sems[("d", i)] = stack.enter_context(nc.semaphore("d_%d" % i))
        block = stack.enter_context(nc.Block())
        q = self.q

        def run(ename):
            def body(h):
                for it in q[ename]:
                    if it[0] == "wait":
                        h.wait_ge(sems[it[1]], it[2])
                    else:
                        ins = it[1](h)
                        if not isinstance(ins, (list, tuple)):
                            ins = [ins]
                        for i_ in ins:
                            i_.then_inc(sems[it[2]], it[3])
            return body

        block.tensor(run("pe"))
        block.scalar(run("act"))
        block.vector(run("dve"))
        block.gpsimd(run("pool"))
        block.sync(run("sp"))


class T:
    __slots__ = ("ap", "res")

    def __init__(self, ap, res):
        self.ap = ap
        self.res = res


class Builder:
    def __init__(self):
        self.nc = bass.Bass("TRN2", target_bir_lowering=False)
        self.S = Sched()
        self.d = {}
        self.dres = {}
        self.in_names = []

    def din(self, name, shape, dt=F32):
        if os.environ.get("KSTAGE", "full") in ("projonly", "mixonly", "mixprompt") and name.startswith("ffn_w"):
            return
        self.d[name] = self.nc.dram_tensor(name, list(shape), dt, kind="ExternalInput").ap()
        self.dres[name] = Res(name)
        self.in_names.append(name)

    def dout(self, name, shape, dt=F32):
        self.d[name] = self.nc.dram_tensor(name, list(shape), dt, kind="ExternalOutput").ap()
        self.dres[name] = Res(name)

    def dscr(self, name, shape, dt=BF16):
        self.d[name] = self.nc.dram_tensor(name, list(shape), dt).ap()
        self.dres[name] = Res(name)

    def ptile(self, name, free, dt):
        t = self.stack.enter_context(self.nc.sbuf_tensor(name, [128] + list(free), dt))
        return T(t[:], Res(name))

    def arena_reset(self):
        self.acur = 0

    def atile(self, name, free, dt):
        n = int(np.prod(free))
        esz = 4 if dt == F32 else 2
        nb = (n * esz + 63) // 64 * 64
        off = self.acur
        assert off + nb <= self.ABYTES, (name, off, nb, self.ABYTES)
        self.acur += nb
        ap = self.arena[:, off // 2: off // 2 + n * esz // 2]
        if dt == F32:
            ap = ap.bitcast(F32)
        if len(free) == 2:
            ap = ap.rearrange("p (a b) -> p a b", a=free[0])
        elif len(free) == 3:
            ap = ap.rearrange("p (a b c) -> p a b c", a=free[0], b=free[1])
        elif len(free) == 4:
            ap = ap.rearrange("p (a b c d) -> p a b c d", a=free[0], b=free[1], c=free[2])
        res = Res(name)
        keep = []
        for (o, s, r) in self.alive:
            if o < off + nb and off < o + s:
                res.inherit(r)
                if not (off <= o and o + s <= off + nb):
                    keep.append((o, s, r))
            else:
                keep.append((o, s, r))
        keep.append((off, nb, res))
        self.alive = keep
        return T(ap, res)

    def mm(self, out, lhsT, rhs, start, stop, R, W, skip=False):
        self.S.op("pe", lambda e: e.matmul(out, lhsT, rhs, start=start, stop=stop, skip_group_check=skip),
                  reads=R, writes=W)

    def tr(self, out, in_, ident, R, W):
        self.S.op("pe", lambda e: e.transpose(out, in_, ident), reads=R, writes=W)

    def act(self, out, in_, func, R, W, bias=None, scale=None):
        kw = {}
        if bias is not None:
            kw["bias"] = bias
        if scale is not None:
            kw["scale"] = scale
        self.S.op("act", lambda e: e.activation(out, in_, func, **kw), reads=R, writes=W)

    def tt(self, out, in0, in1, op, R, W, eng="dve"):
        self.S.op(eng, lambda e: e.tensor_tensor(out, in0, in1, op), reads=R, writes=W)

    def ts(self, out, in0, s1, s2, op0, op1, R, W, eng="dve"):
        if op1 is None:
            self.S.op(eng, lambda e: e.tensor_scalar(out, in0, s1, None, op0), reads=R, writes=W)
        else:
            self.S.op(eng, lambda e: e.tensor_scalar(out, in0, s1, s2, op0, op1), reads=R, writes=W)

    def stt(self, out, in0, scalar, in1, op0, op1, R, W, eng="dve"):
        self.S.op(eng, lambda e: e.scalar_tensor_tensor(out, in0, scalar, in1, op0, op1), reads=R, writes=W)

    def cp(self, out, in_, R, W, eng="dve"):
        if eng == "act":
            self.S.op("act", lambda e: e.activation(out, in_, AF.Copy), reads=R, writes=W)
        else:
            self.S.op(eng, lambda e: e.tensor_copy(out, in_), reads=R, writes=W)

    def dma(self, q, out, in_, R, W, sem, slow=False):
        if slow:
            self.S.op(q, lambda e: e.dma_start(out=out, in_=in_, allow_slow_non_contiguous=True), reads=R, writes=W, dma=sem)
        else:
            self.S.op(q, lambda e: e.dma_start(out=out, in_=in_), reads=R, writes=W, dma=sem)

    def dmas(self, q, pairs, R, W, sem):
        pairs = list(pairs)
        self.S.op(q, lambda e: [e.dma_start(out=o, in_=i) for (o, i) in pairs], reads=R, writes=W, dma=sem,
                  ndma=len(pairs))

    def build(self):
        nc = self.nc
        self.din("xin", [NB * 128, D])
        self.din("condT", [128, 8, 2])
        self.din("bmT", [128, DEPTH, 72])
        self.din("ck_a", [DEPTH, 256, 256]); self.din("cv_a", [DEPTH, 256, 256])
        self.din("ck_b", [DEPTH, 256, 128]); self.din("cv_b", [DEPTH, 256, 128])
        self.din("c_ckv", [DEPTH, 256, 128]); self.din("c_kpe", [DEPTH, 256, 32])
        self.din("w_mod", [DEPTH, D, 9 * D])
        self.din("ln_g", [DEPTH, 3, D]); self.din("ln_b", [DEPTH, 3, D])
        self.din("ffn_w1", [DEPTH, 2, D, DFF]); self.din("ffn_w3", [DEPTH, 2, D, DFF])
        self.din("ffn_w2", [DEPTH, 2, DFF, D])
        self.din("w_in", [DEPTH, D, INW]); self.din("w_o", [DEPTH, D, D])
        self.din("a_lambda", [DEPTH, 128]); self.din("a_subln_g", [DEPTH, 64]); self.din("b_sink", [DEPTH, 8])
        self.din("c_q_norm_g", [DEPTH, 256]); self.din("c_w_q_up", [DEPTH, 256, 384])
        self.din("c_kv_norm_g", [DEPTH, 128]); self.din("c_w_kv_up", [DEPTH, 128, 512])
        self.din("ident", [128, 128])
        self.din("rope32", [128, 2, NSB, 32]); self.din("rope64", [128, 2, NSB, 64])
        self.din("bmask", [128, 4, 128])
        self.din("rmask", [128, 6])
        self.dout("y", [NB * 128, D])
        self.dout("st", [4, DEPTH, 256, 928])
        self.dscr("XB", [X_ROWS, 2048])
        self.dres["XG"] = Res("XG")
        for pi, (r0, r1) in enumerate(XG_PIECES):
            self.dscr("XG%d" % pi, [2 * (r1 - r0), 2048])
        self.dscr("PB", [X_ROWS, 1024]); self.dscr("QS", [QS_ROWS, NB * 128])
        self.QSr = [Res("QS%d" % i) for i in range(NB)]
        self.XBr = [Res("XB%d" % i) for i in range(NSB)]
        self.PBr = [Res("PB%d" % i) for i in range(NPB)]

        with ExitStack() as st:
            self.stack = st
            self.xres = self.ptile("xres", [NB, D], F32)
            self.xblk = [Res("xblk%d" % i) for i in range(NB)]
            self.identF = self.ptile("identF", [128], F32)
            self.identB = self.ptile("identB", [128], BF16)
            self.onesF = self.ptile("onesF", [128], F32)
            self.csT = self.ptile("csT", [8, 2], F32)
            self.bmT = self.ptile("bmT_s", [DEPTH, 72], F32)
            self.modT = self.ptile("modT", [24, 2], F32)
            self.gate_bc = self.ptile("gate_bc", [2, D], F32)
            self.lng = self.ptile("lng", [D], F32)
            self.lnb = self.ptile("lnb", [D], F32)
            self.bmask = self.ptile("bmask_s", [4, 128], BF16)
            self.rmask = self.ptile("rmask_s", [6], F32)
            self.small = self.ptile("small", [64], F32)
            self.small_r = [Res("small%d" % i) for i in range(8)]
            self.ABYTES = 88 * 1024
            arena_t = st.enter_context(nc.sbuf_tensor("arena", [128, self.ABYTES // 2], BF16))
            self.arena = arena_t[:]
            self.alive = []
            self.acur = 0
            self.ps = []
            for i in range(8):
                p = st.enter_context(nc.psum_tensor("ps%d" % i, [128, 512], F32))
                self.ps.append(T(p[:], Res("ps%d" % i, excl=True)))

            self.prologue()
            for l in range(DEPTH):
                self.layer(l)
            self.epilogue()
            self.S.finish()
            self.S.emit(nc, st)
        return nc

    def prologue(self):
        d = self.d
        xv = d["xin"].rearrange("(b p) f -> p b f", p=128)
        for i in range(3):
            blks = list(range(i * 8, (i + 1) * 8))
            self.dma("sp", self.xres.ap[:, i * 8:(i + 1) * 8, :], xv[:, i * 8:(i + 1) * 8, :], [],
                     [self.xblk[b] for b in blks], self.xblk[blks[0]])
        self.dma("sp", self.identF.ap, d["ident"], [], [self.identF.res], self.identF.res)
        self.dma("pool", self.identB.ap, d["ident"], [], [self.identB.res], self.identB.res)
        self.dma("pool", self.bmask.ap, d["bmask"], [], [self.bmask.res], self.bmask.res)
        self.dma("sp", self.rmask.ap, d["rmask"], [], [self.rmask.res], self.rmask.res)
        self.dma("sp", self.bmT.ap, d["bmT"], [], [self.bmT.res], self.bmT.res)
        self.dma("sp", self.csT.ap, d["condT"], [], [self.csT.res], self.csT.res)
        self.S.op("dve", lambda e: e.memset(self.onesF.ap, 1.0), writes=[self.onesF.res])
        self.act(self.csT.ap, self.csT.ap, AF.Silu, [], [self.csT.res])

    def epilogue(self):
        yv = self.d["y"].rearrange("(b p) f -> p b f", p=128)
        for i in range(6):
            blks = list(range(i * 4, (i + 1) * 4))
            self.dma("sp", yv[:, i * 4:(i + 1) * 4, :], self.xres.ap[:, i * 4:(i + 1) * 4, :],
                     [self.xblk[b] for b in blks], [], self.xblk[blks[0]])

    def layer(self, l):
        tiles = [2, 3, 4, 5, 0, 1]
        if os.environ.get("KSTAGE", "full") in ("projonly", "mixonly", "mixprompt"):
            if l == 0:
                self.mixer_phase(l)
            return
        self.ffn_phase(l, 0, 0, tiles)
        self.mixer_phase(l)
        self.ffn_phase(l, 1, 2, tiles)

    def mods(self, l, slot, weight):
        d = self.d
        ps = self.ps[6]
        wsrc = d["w_mod"][l].rearrange("(k p) n -> p k n", p=128)
        base = slot * 3 * D
        ring = [self.atile("wm%d" % i, [8, 256], F32) for i in range(2)]
        for jb in range(12):
            w = ring[jb % 2]
            self.dma("sp", w.ap, wsrc[:, :, base + jb * 256: base + (jb + 1) * 256], [], [w.res], w.res)
            for cc in range(2):
                n = jb * 2 + cc
                for k in range(8):
                    self.mm(ps.ap[:, 2 * n:2 * n + 2], w.ap[:, k, cc * 128:(cc + 1) * 128], self.csT.ap[:, k, :],
                            k == 0, k == 7, [w.res, self.csT.res], [ps.res])
        psv = ps.ap[:, 0:48].rearrange("p (n c) -> p n c", c=2)
        for c in range(2):
            self.tt(self.modT.ap[:, :, c], psv[:, :, c], self.bmT.ap[:, l, slot * 24:(slot + 1) * 24], ALU.add,
                    [ps.res, self.bmT.res], [self.modT.res])
        self.ts(self.modT.ap[:, 8:16, :], self.modT.ap[:, 8:16, :], 1.0, None, ALU.add, None, [], [self.modT.res])
        self.ts(self.modT.ap[:, 16:24, :], self.modT.ap[:, 16:24, :], float(weight), None, ALU.mult, None, [],
                [self.modT.res])
        dg = [self.atile("dg%d" % i, [128], F32) for i in range(2)]
        pb = [self.ps[4], self.ps[5]]
        i = 0
        for c in range(2):
            for hf in range(2):
                p = pb[(c * 2 + hf) % 2]
                for kk in range(4):
                    k = hf * 4 + kk
                    g = dg[i % 2]
                    i += 1
                    self.ts(g.ap, self.identF.ap, self.modT.ap[:, 16 + k, c:c + 1], None, ALU.mult, None,
                            [self.identF.res, self.modT.res], [g.res])
                    self.mm(p.ap[:, kk * 128:(kk + 1) * 128], self.onesF.ap, g.ap, True, True,
                            [self.onesF.res, g.res], [p.res])
                self.cp(self.gate_bc.ap[:, c, hf * 512:(hf + 1) * 512], p.ap, [p.res], [self.gate_bc.res], eng="act")

    def load_ln(self, l, idx):
        d = self.d
        self.dma("sp", self.lng.ap, d["ln_g"][l, idx].partition_broadcast(128), [], [self.lng.res], self.lng.res)
        self.dma("sp", self.lnb.ap, d["ln_b"][l, idx].partition_broadcast(128), [], [self.lnb.res], self.lnb.res)

    def pre(self, tile, hT, banks):
        cond = 1 if tile < 2 else 0
        for k in range(8):
            p = banks[k % len(banks)]
            for tb in range(4):
                blk = tile * 4 + tb
                self.tr(p.ap[:, tb * 128:(tb + 1) * 128], self.xres.ap[:, blk, k * 128:(k + 1) * 128], self.identF.ap,
                        [self.xblk[blk], self.identF.res], [p.res])
            self.act(hT.ap[:, k, :], p.ap, AF.Identity, [p.res, self.modT.res], [hT.res],
                     bias=self.modT.ap[:, k, cond:cond + 1], scale=self.modT.ap[:, 8 + k, cond:cond + 1])

    def post(self, blk, cond, ybuf):
        x = self.xres.ap[:, blk, :]
        xr = self.xblk[blk]
        sm = self.small.ap
        sr = self.small_r[0]
        self.stt(ybuf.ap, x, DN_ALPHA, ybuf.ap, ALU.mult, ALU.add, [xr], [ybuf.res])
        st6 = sm[:, 0:12].rearrange("p (a b) -> p a b", a=2)
        for hf in range(2):
            self.S.op("dve", lambda e, hf=hf: e.bn_stats(st6[:, hf, :], ybuf.ap[:, hf * 512:(hf + 1) * 512]),
                      reads=[ybuf.res], writes=[sr])
        self.S.op("dve", lambda e: e.bn_aggr(sm[:, 12:14], st6), reads=[], writes=[sr])
        self.ts(sm[:, 14:15], sm[:, 13:14], LN_EPS, None, ALU.add, None, [], [sr])
        self.act(sm[:, 14:15], sm[:, 14:15], AF.Sqrt, [], [sr])
        self.S.op("dve", lambda e: e.reciprocal(sm[:, 14:15], sm[:, 14:15]), reads=[], writes=[sr])
        self.stt(sm[:, 15:16], sm[:, 12:13], -1.0, sm[:, 14:15], ALU.mult, ALU.mult, [], [sr])
        self.act(ybuf.ap, ybuf.ap, AF.Identity, [sr], [ybuf.res], bias=sm[:, 15:16], scale=sm[:, 14:15])
        self.tt(ybuf.ap, ybuf.ap, self.lng.ap, ALU.mult, [self.lng.res], [ybuf.res])
        self.tt(x, ybuf.ap, self.lnb.ap, ALU.add, [ybuf.res, self.lnb.res], [xr])

    def ffn_phase(self, l, half, slot, tiles):
        d = self.d
        self.arena_reset()
        self.mods(l, slot, 0.5)
        self.load_ln(l, 0 if slot == 0 else 2)
        self.arena_reset()
        hT = self.atile("hT", [8, 512], BF16)
        aT = self.atile("aT", [NJ, 512], BF16)
        su = [self.atile("su%d" % i, [512], F32) for i in range(2)]
        w13 = [self.atile("w13_%d" % i, [2, 8, 256], BF16) for i in range(2)]
        w2r = [self.atile("w2_%d" % i, [4, 512], BF16) for i in range(2)]
        yb = [self.atile("yb%d" % i, [D], F32) for i in range(4)]
        w1s = d["ffn_w1"][l, half].rearrange("(k p) n -> p k n", p=128)
        w3s = d["ffn_w3"][l, half].rearrange("(k p) n -> p k n", p=128)
        w2s = d["ffn_w2"][l, half].rearrange("(j p) n -> p j n", p=128)
        state = {"i13": 0, "n13": 0, "i2": 0, "n2": 0}
        total13 = len(tiles) * 11
        total2 = len(tiles) * 12

        def issue13(upto):
            while state["i13"] < min(upto, total13):
                n = state["i13"]
                jb = n % 11
                w = w13[n % 2]
                self.dmas("pool", [(w.ap[:, 0], w1s[:, :, jb * 256:(jb + 1) * 256]),
                                   (w.ap[:, 1], w3s[:, :, jb * 256:(jb + 1) * 256])], [], [w.res], w.res)
                state["i13"] += 1

        def issue2(upto):
            while state["i2"] < min(upto, total2):
                n = state["i2"]
                r = n % 12
                hf, jg = r // 6, r % 6
                nj = 4 if jg < 5 else 2
                w = w2r[n % 2]
                self.dma("pool", w.ap[:, 0:nj, :], w2s[:, jg * 4:jg * 4 + nj, hf * 512:(hf + 1) * 512], [], [w.res], w.res)
                state["i2"] += 1

        issue13(2)
        self.pre(tiles[0], hT, [self.ps[4], self.ps[5]])
        for ti, tile in enumerate(tiles):
            cond = 1 if tile < 2 else 0
            for jb in range(11):
                n = ti * 11 + jb
                issue13(n + 2)
                w = w13[n % 2]
                for cc in range(2):
                    j = jb * 2 + cc
                    pu = self.ps[j % 2]
                    pg = self.ps[2 + j % 2]
                    for k in range(8):
                        self.mm(pu.ap, w.ap[:, 0, k, cc * 128:(cc + 1) * 128], hT.ap[:, k, :], k == 0, k == 7,
                                [w.res, hT.res], [pu.res])
                    for k in range(8):
                        self.mm(pg.ap, w.ap[:, 1, k, cc * 128:(cc + 1) * 128], hT.ap[:, k, :], k == 0, k == 7,
                                [w.res, hT.res], [pg.res])
                    s_ = su[j % 2]
                    self.act(s_.ap, pu.ap, AF.Silu, [pu.res], [s_.res])
                    self.tt(aT.ap[:, j, :], s_.ap, pg.ap, ALU.mult, [s_.res, pg.res], [aT.res])
                if jb == 9:
                    issue2(ti * 12 + 2)
            if ti + 1 < len(tiles):
                self.pre(tiles[ti + 1], hT, [self.ps[4], self.ps[5]])
                issue13((ti + 1) * 11 + 2)
            for hf in range(2):
                banks = [self.ps[4 + tb] for tb in range(4)] if hf == 0 else [self.ps[tb] for tb in range(4)]
                for jg in range(6):
                    n = ti * 12 + hf * 6 + jg
                    issue2(n + 2)
                    w = w2r[n % 2]
                    nj = 4 if jg < 5 else 2
                    for jj in range(nj):
                        j = jg * 4 + jj
                        for tb in range(4):
                            self.mm(banks[tb].ap, aT.ap[:, j, tb * 128:(tb + 1) * 128], w.ap[:, jj, :], j == 0, j == NJ - 1,
                                    [aT.res, w.res], [banks[tb].res])
                for tb in range(4):
                    self.tt(yb[tb].ap[:, hf * 512:(hf + 1) * 512], banks[tb].ap,
                            self.gate_bc.ap[:, cond, hf * 512:(hf + 1) * 512], ALU.mult,
                            [banks[tb].res, self.gate_bc.res], [yb[tb].res])
            for tb in range(4):
                self.post(tile * 4 + tb, cond, yb[tb])

    def mixer_phase(self, l):
        self.arena_reset()
        self.mods(l, 1, 1.0)
        self.load_ln(l, 1)
        self.arena_reset()
        stage = os.environ.get("KSTAGE", "full")
        self.project(l)
        if stage in ("proj", "projonly"):
            return
        self.gather()
        if stage == "gather":
            return
        self.arena_reset()
        self.layer_consts(l)
        mark = self.acur
        o_p = self.atile("o_p", [NPB, D], BF16)
        m2 = self.acur
        for s_ in range(4):
            self.acur = m2
            segs = [("PB", s_ * 256, 256)]
            self.attn_A(l, segs, False, [(s_ * 256, 256)], o_p, s_ * 2, 0)
            self.acur = m2
            self.attn_C(l, segs, False, [(s_ * 256, 256)], o_p, s_ * 2, 0)
            self.acur = m2
            self.attn_B(l, False, s_, o_p, s_ * 2)
        self.acur = m2
        self.wo_phase(l, [0, 1], o_p, 0)
        if stage in ("prompt", "mixprompt"):
            return
        self.acur = mark
        o_s = self.atile("o_s", [NSB, D], BF16)
        m2 = self.acur
        segs = [("XG", i * 1024, 1024) for i in range(4)]
        qt = [(1024 + i * 512, 512) for i in range(4)]
        self.attn_A(l, segs, True, qt, o_s, 0, 1024)
        self.acur = m2
        self.attn_C(l, segs, True, qt, o_s, 0, 1024)
        self.acur = m2
        self.attn_B(l, True, 0, o_s, 0)
        self.acur = m2
        self.wo_phase(l, [2, 3, 4, 5], o_s, 8)

    def xrows(self, name, tok0, ntok, r0, nr):
        if name == "XG":
            rk = tok0 // 2048
            t = tok0 % 2048
            for pi, (p0, p1) in enumerate(XG_PIECES):
                if p0 <= r0 and r0 + nr <= p1:
                    a = self.d["XG%d" % pi]
                    n = p1 - p0
                    return a[rk * n + r0 - p0: rk * n + r0 - p0 + nr, t:t + ntok]
            raise AssertionError((r0, nr))
        a = self.d[name]
        return a[r0:r0 + nr, tok0:tok0 + ntok]

    def xv(self, name, tok0, ntok):
        if name == "XG":
            rk = tok0 // 2048
            t = tok0 % 2048
            assert (t // 1024) == ((t + ntok - 1) // 1024)
            if t < 1024:
                a = self.d["XG2"]
                v = a[rk * 227 + 32: rk * 227 + 227, :]
            else:
                a = self.d["XG3"]
                v = a[rk * 195: rk * 195 + 195, :]
                t -= 1024
            v = v.rearrange("r t -> (r t)").rearrange("(t c) -> t c", c=XV_W)
            return v[t:t + ntok, :]
        a = self.d[name]
        v = a[XK_ROWS:X_ROWS, :]
        v = v.rearrange("r t -> (r t)").rearrange("(t c) -> t c", c=XV_W)
        return v[tok0:tok0 + ntok, :]

    def xres_of(self, name, tok0, ntok):
        if name == "XG":
            return [self.dres["XG"]]
        lst = self.XBr if name == "XB" else self.PBr
        return [lst[b] for b in range(tok0 // 128, (tok0 + ntok) // 128)]

    def rope(self, dst, src, H, Q, tbl, sb, R, W, tmp):
        dim = 4 * Q
        xs = src.rearrange("p (h a w q) -> p h a w q", h=H, a=2, w=2)
        xd = dst.rearrange("p (h a w q) -> p h a w q", h=H, a=2, w=2)
        xt = tmp[:, 0:H * dim].rearrange("p (h a w q) -> p h a w q", h=H, a=2, w=2)
        cs = tbl.ap[:, 0, sb, :].rearrange("p (a w q) -> p a w q", a=2, w=2)
        ss = tbl.ap[:, 1, sb, :].rearrange("p (a w q) -> p a w q", a=2, w=2)
        for a in range(2):
            cb = cs[:, a].unsqueeze(1).to_broadcast([128, H, 2, Q])
            self.tt(xd[:, :, a], xs[:, :, a], cb, ALU.mult, R + [tbl.res], W)
            for w in range(2):
                sbb = ss[:, a, w].unsqueeze(1).to_broadcast([128, H, Q])
                self.tt(xt[:, :, a, w], xs[:, :, a, 1 - w], sbb, ALU.mult, R + [tbl.res], [self.tmp_res])
            self.tt(xd[:, :, a], xd[:, :, a], xt[:, :, a], ALU.add, [self.tmp_res], W)

    def rms(self, dst, src, n, gam, R, W, sidx):
        sm = self.small.ap
        sr = self.small_r[sidx]
        c0 = 16 + sidx * 4
        self.tt(self.junk.ap[:, 0:n], src, src, ALU.mult, R, [self.junk.res])
        self.S.op("dve", lambda e: e.reduce_sum(sm[:, c0:c0 + 1], self.junk.ap[:, 0:n], axis=AX.X),
                  reads=[self.junk.res], writes=[sr])
        self.ts(sm[:, c0:c0 + 1], sm[:, c0:c0 + 1], 1.0 / n, RMS_EPS, ALU.mult, ALU.add, [], [sr])
        self.act(sm[:, c0:c0 + 1], sm[:, c0:c0 + 1], AF.Sqrt, [], [sr])
        self.S.op("dve", lambda e: e.reciprocal(sm[:, c0:c0 + 1], sm[:, c0:c0 + 1]), reads=[], writes=[sr])
        self.stt(dst, src, sm[:, c0:c0 + 1], gam.ap, ALU.mult, ALU.mult, R + [sr, gam.res], W)

    def project(self, l):
        d = self.d
        hT = self.atile("hTm", [8, 512], BF16)
        w_in = self.atile("w_in", [8, INW], BF16)
        wq = self.atile("wq", [2, 384], BF16)
        wkv = self.atile("wkv", [512], BF16)
        gq = self.atile("gq", [256], F32)
        gkv = self.atile("gkv", [128], F32)
        r32 = self.atile("r32", [2, NSB, 32], F32)
        r64 = self.atile("r64", [2, NSB, 64], F32)
        tp = self.atile("tp", [INW], F32)
        rq = self.atile("rq", [1184], F32)
        tmp = self.atile("tmp", [640], F32)
        self.tmp_res = tmp.res
        self.junk = self.atile("junk", [256], F32)
        nq = self.atile("nq", [256], BF16)
        nqT = self.atile("nqT", [2, 128], BF16)
        cq = self.atile("cq", [384], F32)
        ckv = self.atile("ckv", [128], F32)
        kpe96 = self.atile("kpe96", [96], F32)
        qA = self.atile("qA", [8, 128], BF16)
        qB = self.atile("qB", [8, 128], BF16)
        qC = self.atile("qC", [4, 128], BF16)
        kst = self.atile("kst", [5, 128], BF16)
        vst = self.atile("vst", [6, 65], BF16)
        wsrc = d["w_in"][l].rearrange("(k p) n -> p k n", p=128)
        self.dmas("pool", [(w_in.ap[:, :, c0:c1], wsrc[:, :, c0:c1]) for (c0, c1) in ((0, 512), (512, 1024), (1024, 1536), (1536, INW))],
                  [], [w_in.res], w_in.res)
        self.dma("pool", wq.ap, d["c_w_q_up"][l].rearrange("(k p) n -> p k n", p=128), [], [wq.res], wq.res)
        self.dma("pool", wkv.ap, d["c_w_kv_up"][l], [], [wkv.res], wkv.res)
        self.dma("sp", gq.ap, d["c_q_norm_g"][l].partition_broadcast(128), [], [gq.res], gq.res)
        self.dma("sp", gkv.ap, d["c_kv_norm_g"][l].partition_broadcast(128), [], [gkv.res], gkv.res)
        self.dma("sp", r32.ap, d["rope32"], [], [r32.res], r32.res)
        self.dma("sp", r64.ap, d["rope64"], [], [r64.res], r64.res)
        self.S.op("dve", lambda e: e.memset(kpe96.ap, 0.0), writes=[kpe96.res])
        self.S.op("dve", lambda e: e.memset(vst.ap, 1.0), writes=[vst.res])
        self.S.op("dve", lambda e: e.memset(qC.ap, 0.0), writes=[qC.res])
        self.S.op("dve", lambda e: e.memset(kst.ap, 0.0), writes=[kst.res])
        self.wkv_t = None
        groups = ((0, 512), (512, 1024), (1024, 1536), (1536, INW))
        rm = self.rmask
        lvl = int(os.environ.get("KLVL", "9"))
        skip = set(os.environ.get("KSKIP", "").split(","))
        stv = d["st"]
        for tile in [2, 3, 4, 5, 0, 1]:
            samp = tile >= 2
            self.pre(tile, hT, [self.ps[4], self.ps[5]])
            for tb in range(4):
                blk = tile * 4 + tb
                sb = blk - NPB
                for gi, (c0, c1) in enumerate(groups):
                    p = self.ps[gi]
                    for k in range(8):
                        self.mm(p.ap[:, 0:c1 - c0], hT.ap[:, k, tb * 128:(tb + 1) * 128], w_in.ap[:, k, c0:c1], k == 0, k == 7,
                                [hT.res, w_in.res], [p.res])
                    self.cp(tp.ap[:, c0:c1], p.ap[:, 0:c1 - c0], [p.res], [tp.res], eng="act")
                if lvl < 2:
                    continue
                if samp:
                    self.rope(rq.ap[:, 0:512], tp.ap[:, 0:512], 16, 8, r32, sb, [tp.res], [rq.res], tmp.ap)
                    self.rope(rq.ap[:, 512:1152], tp.ap[:, 768:1408], 10, 16, r64, sb, [tp.res], [rq.res], tmp.ap)
                    self.rope(rq.ap[:, 1152:1184], tp.ap[:, 1920:1952], 1, 8, r32, sb, [tp.res], [rq.res], tmp.ap)
                    s_aq, s_ak, s_bq, s_bk, s_kpe = (rq.ap[:, 0:256], rq.ap[:, 256:512], rq.ap[:, 512:1024],
                                                     rq.ap[:, 1024:1152], rq.ap[:, 1152:1184])
                    sres = [rq.res]
                else:
                    s_aq, s_ak, s_bq, s_bk, s_kpe = (tp.ap[:, 0:256], tp.ap[:, 256:512], tp.ap[:, 768:1280],
                                                     tp.ap[:, 1280:1408], tp.ap[:, 1920:1952])
                    sres = [tp.res]
                if lvl < 3:
                    continue
                self.rms(nq.ap, tp.ap[:, 1536:1792], 256, gq, [tp.res], [nq.res], 1)
                self.rms(ckv.ap, tp.ap[:, 1792:1920], 128, gkv, [tp.res], [ckv.res], 2)
                if not samp:
                    seq, t0 = blk // 2, (blk % 2) * 128
                    self.dmas("sp", [(stv[seq, l, t0:t0 + 128, 0:512], tp.ap[:, 256:768]),
                                     (stv[seq, l, t0:t0 + 128, 512:768], tp.ap[:, 1280:1536]),
                                     (stv[seq, l, t0:t0 + 128, 896:928], tp.ap[:, 1920:1952]),
                                     (stv[seq, l, t0:t0 + 128, 768:896], ckv.ap)],
                              [tp.res, ckv.res], [], tp.res)
                if lvl < 4:
                    continue
                p4, p5, p6, p7 = self.ps[4], self.ps[5], self.ps[6], self.ps[7]
                p4b = p4.ap[:, 0:128].bitcast(BF16)
                for kk in range(2):
                    self.tr(p4b[:, kk * 128:(kk + 1) * 128], nq.ap[:, kk * 128:(kk + 1) * 128], self.identB.ap,
                            [nq.res, self.identB.res], [p4.res])
                self.cp(nqT.ap, p4b.rearrange("p (a b) -> p a b", a=2), [p4.res], [nqT.res])
                for kk in range(2):
                    self.mm(p5.ap[:, 0:384], nqT.ap[:, kk, :], wq.ap[:, kk, :], kk == 0, kk == 1, [nqT.res, wq.res], [p5.res])
                self.cp(cq.ap, p5.ap[:, 0:384], [p5.res], [cq.res], eng="act")
                if samp:
                    cq4 = cq.ap.rearrange("p (h e) -> p h e", e=96)
                    p54 = p5.ap[:, 0:384].rearrange("p (h e) -> p h e", e=96)
                    self.rope_strided(cq4[:, :, 64:96], p54[:, :, 64:96], 4, 8, r32, sb, [p5.res], [cq.res], tmp.ap)
                if lvl < 5:
                    continue
                if "A" not in skip:
                    for c in range(2):
                        self.tr(p6.ap[:, c * 128:(c + 1) * 128], s_aq[:, c * 128:(c + 1) * 128], self.identF.ap, sres + [self.identF.res], [p6.res])
                        self.tr(p6.ap[:, 256 + c * 128:256 + (c + 1) * 128], s_ak[:, c * 128:(c + 1) * 128], self.identF.ap, sres + [self.identF.res], [p6.res])
                    for c in range(2):
                        for i in range(4):
                            self.ts(qA.ap[:, c * 4 + i, :], p6.ap[:, c * 128:(c + 1) * 128], rm.ap[:, i:i + 1], None, ALU.mult, None,
                                    [p6.res, rm.res], [qA.res])
                    self.cp(kst.ap[:, 0:2, :], p6.ap[:, 256:512].rearrange("p (a b) -> p a b", a=2), [p6.res], [kst.res], eng="act")
                if "B" not in skip:
                    for c in range(4):
                        self.tr(p7.ap[:, c * 128:(c + 1) * 128], s_bq[:, c * 128:(c + 1) * 128], self.identF.ap, sres + [self.identF.res], [p7.res])
                    for hq in range(8):
                        self.ts(qB.ap[:, hq, :], p7.ap[:, (hq // 2) * 128:(hq // 2 + 1) * 128], rm.ap[:, 4 + hq % 2:5 + hq % 2], None,
                                ALU.mult, None, [p7.res, rm.res], [qB.res])
                if "P4" not in skip:
                    self.cp(kpe96.ap[:, 64:96], s_kpe, sres, [kpe96.res])
                    self.tr(p4.ap[:, 0:128], s_bk, self.identF.ap, sres + [self.identF.res], [p4.res])
                    self.tr(p4.ap[:, 128:256], ckv.ap, self.identF.ap, [ckv.res, self.identF.res], [p4.res])
                    self.tr(p4.ap[0:96, 256:384], kpe96.ap, self.identF.ap, [kpe96.res, self.identF.res], [p4.res])
                    self.cp(kst.ap[:, 2:4, :], p4.ap[:, 0:256].rearrange("p (a b) -> p a b", a=2), [p4.res], [kst.res], eng="act")
                    self.cp(kst.ap[64:96, 4, :], p4.ap[64:96, 256:384], [p4.res], [kst.res])
                if "QC" not in skip:
                    for h in range(4):
                        self.tr(p5.ap[0:96, h * 128:(h + 1) * 128], cq.ap[:, h * 96:(h + 1) * 96], self.identF.ap, [cq.res, self.identF.res], [p5.res])
                    self.cp(qC.ap[0:96, :, :], p5.ap[0:96, :].rearrange("p (a b) -> p a b", a=4), [p5.res], [qC.res])
                if "V" not in skip:
                    self.cp(vst.ap[:, 0:4, 0:64], tp.ap[:, 512:768].rearrange("p (h e) -> p h e", e=64), [tp.res], [vst.res])
                    self.cp(vst.ap[:, 4:6, 0:64], tp.ap[:, 1408:1536].rearrange("p (h e) -> p h e", e=64), [tp.res], [vst.res])
                if lvl < 6:
                    continue
                tok = blk * 128
                QS = d["QS"]
                name = "XB" if samp else "PB"
                xt0 = (blk - NPB) * 128 if samp else blk * 128
                pairs = [
                    (QS[0:1024, tok:tok + 128].rearrange("(i p) t -> p i t", p=128), qA.ap),
                    (QS[1024:2048, tok:tok + 128].rearrange("(i p) t -> p i t", p=128), qB.ap),
                    (QS[2048:2560, tok:tok + 128].rearrange("(i p) t -> p i t", p=128), qC.ap),
                    (self.xrows(name, xt0, 128, 0, 256).rearrange("(c p) t -> p c t", p=128), kst.ap[:, 0:2, :]),
                    (self.xrows(name, xt0, 128, 256, 256).rearrange("(c p) t -> p c t", p=128), kst.ap[:, 2:4, :]),
                    (self.xrows(name, xt0, 128, 512, 32), kst.ap[64:96, 4, :]),
                    (self.xv(name, xt0, 128), vst.ap.rearrange("p h e -> p (h e)")),
                ]
                wr = [self.QSr[blk], (self.XBr[blk - NPB] if samp else self.PBr[blk])]
                self.dmas("sp", pairs, [qA.res, qB.res, qC.res, kst.res, vst.res], wr, qA.res)

    def rope_strided(self, dst, src, H, Q, tbl, sb, R, W, tmp):
        dim = 4 * Q
        xs = src.rearrange("p h (a w q) -> p h a w q", a=2, w=2)
        xd = dst.rearrange("p h (a w q) -> p h a w q", a=2, w=2)
        xt = tmp[:, 0:H * dim].rearrange("p (h a w q) -> p h a w q", h=H, a=2, w=2)
        cs = tbl.ap[:, 0, sb, :].rearrange("p (a w q) -> p a w q", a=2, w=2)
        ss = tbl.ap[:, 1, sb, :].rearrange("p (a w q) -> p a w q", a=2, w=2)
        for a in range(2):
            cb = cs[:, a].unsqueeze(1).to_broadcast([128, H, 2, Q])
            self.tt(xd[:, :, a], xs[:, :, a], cb, ALU.mult, R + [tbl.res], W)
            for w in range(2):
                sbb = ss[:, a, w].unsqueeze(1).to_broadcast([128, H, Q])
                self.tt(xt[:, :, a, w], xs[:, :, a, 1 - w], sbb, ALU.mult, R + [tbl.res], [self.tmp_res])
            self.tt(xd[:, :, a], xd[:, :, a], xt[:, :, a], ALU.add, [self.tmp_res], W)

    def gather(self):
        d = self.d
        for pi, (r0, r1) in enumerate(XG_PIECES):
            src = d["XB"][r0:r1, :]
            dst = d["XG%d" % pi]
            self.S.op("pool", lambda e, src=src, dst=dst: e.collective_compute(
                "AllGather", ALU.bypass, replica_groups=[[0, 1], [2, 3], [4, 5], [6, 7]], ins=[src], outs=[dst]),
                reads=list(self.XBr), writes=[self.dres["XG"]], dma=self.dres["XG"], inc=1)

    def layer_consts(self, l):
        d = self.d
        lam_init = 0.8 - 0.6 * float(np.exp(-0.3 * l))
        al = self.atile("al", [128], F32)
        self.gsub = self.atile("gsub", [64], F32)
        self.es = self.atile("es", [8], F32)
        self.lam = self.atile("lamc", [4], F32)
        self.dma("sp", al.ap, d["a_lambda"][l].partition_broadcast(128), [], [al.res], al.res)
        self.dma("sp", self.gsub.ap, d["a_subln_g"][l].partition_broadcast(128), [], [self.gsub.res], self.gsub.res)
        self.dma("sp", self.es.ap, d["b_sink"][l].partition_broadcast(128), [], [self.es.res], self.es.res)
        self.act(self.es.ap, self.es.ap, AF.Exp, [], [self.es.res])
        self.ts(self.gsub.ap, self.gsub.ap, 1.0 - lam_init, None, ALU.mult, None, [], [self.gsub.res])
        lm = self.lam
        for i in range(2):
            self.tt(al.ap[:, i * 64:i * 64 + 32], al.ap[:, i * 64:i * 64 + 32], al.ap[:, i * 64 + 32:i * 64 + 64], ALU.mult, [], [al.res])
            self.S.op("dve", lambda e, i=i: e.reduce_sum(lm.ap[:, i:i + 1], al.ap[:, i * 64:i * 64 + 32], axis=AX.X),
                      reads=[al.res], writes=[lm.res])
        self.act(lm.ap[:, 0:2], lm.ap[:, 0:2], AF.Exp, [], [lm.res])
        self.tt(lm.ap[:, 2:3], lm.ap[:, 0:1], lm.ap[:, 1:2], ALU.subtract, [], [lm.res])
        self.ts(lm.ap[:, 3:4], lm.ap[:, 2:3], lam_init, -1.0, ALU.add, ALU.mult, [], [lm.res])
        self.pt = [self.atile("pt%d" % i, [512], BF16) for i in range(4)]
        self.pti = 0
        self.sci = 0
        self.rec = self.atile("rec", [8], F32)
        self.bmask4 = self.atile("bmask4", [4, 4, 128], BF16)
        for m_ in range(4):
            for g_ in range(4):
                self.cp(self.bmask4.ap[:, m_, g_, :], self.bmask.ap[:, m_, :], [self.bmask.res], [self.bmask4.res])
        self.ctmp = self.atile("ctmp", [2, 256], F32)

    def attend(self, q_ap, q_res, kblocks, nq, scale, obank):
        nqs = nq // 128
        nkb = len(kblocks)
        LA = 2
        pts = [None] * nkb
        for i in range(nkb + LA):
            if i < nkb:
                kT, kres, V, vres, mask = kblocks[i]
                sc = self.ps[self.sci % 4]
                self.sci += 1
                self.mm(sc.ap[:, 0:nq], kT, q_ap, True, mask is None, kres + [q_res], [sc.res])
                if mask is not None:
                    self.mm(sc.ap[:, 0:nq], self.identB.ap, mask, False, True, [self.identB.res, self.bmask4.res], [sc.res])
                pt = self.pt[self.pti % 4]
                self.pti += 1
                self.act(pt.ap[:, 0:nq], sc.ap[:, 0:nq], AF.Exp, [sc.res], [pt.res], scale=float(scale))
                pts[i] = pt
            j = i - LA
            if j >= 0:
                kT, kres, V, vres, mask = kblocks[j]
                pt = pts[j]
                for qs in range(nqs):
                    Vq = V[qs] if isinstance(V, (list, tuple)) else V
                    self.mm(obank.ap[:, qs * 65:(qs + 1) * 65], pt.ap[:, qs * 128:(qs + 1) * 128], Vq, (j == 0 and qs == 0),
                            j == nkb - 1, [pt.res] + vres, [obank.res], skip=True)

    def load_ctxT(self, src_dram, width, dst_fn, dres):
        ct = self.ctmp
        self.dma("sp", ct.ap[:, :, 0:width], src_dram.rearrange("(b p) f -> p b f", p=128), [], [ct.res], ct.res)
        p = self.ps[2]
        nch = width // 128
        for kb in range(2):
            for c in range(nch):
                self.tr(p.ap[:, (kb * nch + c) * 128:(kb * nch + c + 1) * 128], ct.ap[:, kb, c * 128:(c + 1) * 128], self.identF.ap,
                        [ct.res, self.identF.res], [p.res])
        for kb in range(2):
            for c in range(nch):
                self.cp(dst_fn(c, kb), p.ap[:, (kb * nch + c) * 128:(kb * nch + c + 1) * 128], [p.res], [dres])

    def attn_A(self, l, segs, ctx, qtiles, o_t, oblk0, qtok_unused):
        d = self.d
        nk = (256 if ctx else 0) + sum(s[2] for s in segs)
        nkb = nk // 128
        kT = self.atile("kT_A", [2, nk], BF16)
        V = self.atile("V_A", [nkb, 4, 65], BF16)
        qm = [self.atile("qmA%d" % i, [512], BF16) for i in range(2)]
        oa = self.atile("oa", [4, 4, 64], F32)
        o1 = self.atile("o1", [4, 64], F32)
        o2 = self.atile("o2", [64], F32)
        ssq = self.atile("ssq", [16], F32)
        sq = self.atile("sqA", [4, 64], F32)
        koff = 0
        if ctx:
            self.load_ctxT(d["ck_a"][l], 256, lambda c, kb: kT.ap[:, c, kb * 128:(kb + 1) * 128], kT.res)
            self.dmas("pool", [(V.ap[:, b_, :, 0:64], d["cv_a"][l][b_ * 128:(b_ + 1) * 128, :].rearrange("p (h e) -> p h e", e=64))
                               for b_ in range(2)], [], [V.res], V.res)
            self.S.op("dve", lambda e: e.memset(V.ap[:, 0:2, :, 64:65], 1.0), writes=[V.res])
            koff = 256
        pairs = []
        rd = []
        for (name, t0, nt) in segs:
            pairs.append((kT.ap[:, :, koff:koff + nt], self.xrows(name, t0, nt, 0, 256).rearrange("(c p) t -> p c t", p=128)))
            pairs.append((V.ap[:, koff // 128:(koff + nt) // 128, :, :],
                          self.xv(name, t0, nt)[:, 0:260].rearrange("(b p) (h e) -> p b h e", p=128, e=65)))
            rd += self.xres_of(name, t0, nt)
            koff += nt
        self.dmas("sp", pairs, rd, [kT.res, V.res], kT.res)
        QS = d["QS"]
        scale = 32 ** -0.5
        qi = 0
        for ti, (qt0, nq) in enumerate(qtiles):
            nqs = nq // 128
            qres = [self.QSr[b] for b in range(qt0 // 128, (qt0 + nq) // 128)]
            for i in range(8):
                c, h, j = i // 4, (i // 4) * 2 + (i % 4) // 2, i % 2
                q = qm[qi % 2]
                qi += 1
                self.dma("sp", q.ap[:, 0:nq], QS[i * 128:(i + 1) * 128, qt0:qt0 + nq], qres, [q.res], q.res)
                ob = self.ps[4 + (i % 2)]
                kbl = [(kT.ap[:, c, kb * 128:(kb + 1) * 128], [kT.res], V.ap[:, kb, h, :], [V.res], None) for kb in range(nkb)]
                self.attend(q.ap[:, 0:nq], q.res, kbl, nq, scale, ob)
                for qs in range(nqs):
                    self.S.op("dve", lambda e, qs=qs, ob=ob: e.reciprocal(self.rec.ap[:, qs:qs + 1], ob.ap[:, qs * 65 + 64:qs * 65 + 65]),
                              reads=[ob.res], writes=[self.rec.res])
                    if j == 0:
                        self.ts(o1.ap[:, qs, :], ob.ap[:, qs * 65:qs * 65 + 64], self.rec.ap[:, qs:qs + 1], None, ALU.mult, None,
                                [ob.res, self.rec.res], [o1.res])
                    else:
                        self.ts(o2.ap, ob.ap[:, qs * 65:qs * 65 + 64], self.rec.ap[:, qs:qs + 1], None, ALU.mult, None,
                                [ob.res, self.rec.res], [o2.res])
                        self.stt(oa.ap[:, qs, h, :], o2.ap, self.lam.ap[:, 3:4], o1.ap[:, qs, :], ALU.mult, ALU.add,
                                 [o2.res, o1.res, self.lam.res], [oa.res])
            n16 = nqs * 4
            for qs in range(nqs):
                self.tt(sq.ap, oa.ap[:, qs], oa.ap[:, qs], ALU.mult, [oa.res], [sq.res])
                self.S.op("dve", lambda e, qs=qs: e.reduce_sum(ssq.ap[:, qs * 4:qs * 4 + 4], sq.ap, axis=AX.X),
                          reads=[sq.res], writes=[ssq.res])
            self.ts(ssq.ap[:, 0:n16], ssq.ap[:, 0:n16], 1.0 / 64, RMS_EPS, ALU.mult, ALU.add, [], [ssq.res])
            self.act(ssq.ap[:, 0:n16], ssq.ap[:, 0:n16], AF.Sqrt, [], [ssq.res])
            self.S.op("dve", lambda e, n16=n16: e.reciprocal(ssq.ap[:, 0:n16], ssq.ap[:, 0:n16]), reads=[], writes=[ssq.res])
            for qs in range(nqs):
                for h in range(4):
                    ob_ = oblk0 + ti * nqs + qs
                    self.stt(o_t.ap[:, ob_, h * 64:(h + 1) * 64], oa.ap[:, qs, h, :], ssq.ap[:, qs * 4 + h:qs * 4 + h + 1], self.gsub.ap,
                             ALU.mult, ALU.mult, [oa.res, ssq.res, self.gsub.res], [o_t.res])

    def attn_C(self, l, segs, ctx, qtiles, o_t, oblk0, qtok_unused):
        d = self.d
        nk = (256 if ctx else 0) + sum(s[2] for s in segs)
        nkb = nk // 128
        wkv = self.atile("wkvC", [512], BF16)
        self.dma("pool", wkv.ap, d["c_w_kv_up"][l], [], [wkv.res], wkv.res)
        wkv4 = wkv.ap.rearrange("p (h e) -> p h e", e=128)
        ckT = [self.atile("ckT%d" % i, [512], BF16) for i in range(2)]
        cpe = self.atile("cpe", [2, 96], F32)
        qm = [self.atile("qmC%d" % i, [512], BF16) for i in range(2)]
        kT = self.atile("kT_C", [2, nk], BF16)
        V = self.atile("V_C", [nkb, 2, 65], BF16)
        QS = d["QS"]
        scale = 96 ** -0.5
        qi = 0
        ci = 0
        for hp in range(2):
            self.S.op("dve", lambda e: e.memset(V.ap[:, :, :, 64:65], 1.0), writes=[V.res])
            ktiles = []
            koff = 0
            if ctx:
                ktiles.append((None, 0, 256, 0))
                koff = 256
            for (name, t0, nt) in segs:
                for s0 in range(0, nt, 512):
                    n = min(512, nt - s0)
                    ktiles.append((name, t0 + s0, n, koff))
                    koff += n
            if ctx:
                self.S.op("dve", lambda e: e.memset(cpe.ap, 0.0), writes=[cpe.res])
                self.dma("sp", cpe.ap[:, :, 64:96], d["c_kpe"][l].rearrange("(b p) f -> p b f", p=128), [], [cpe.res], cpe.res)
                p = self.ps[2]
                for kb in range(2):
                    self.tr(p.ap[0:96, kb * 128:(kb + 1) * 128], cpe.ap[:, kb, :], self.identF.ap, [cpe.res, self.identF.res], [p.res])
                for hh in range(2):
                    self.cp(kT.ap[64:96, hh, 0:256], p.ap[64:96, 0:256], [p.res], [kT.res])
            pairs = []
            rd = []
            ko2 = 256 if ctx else 0
            for (name, t0, nt) in segs:
                for hh in range(2):
                    pairs.append((kT.ap[64:96, hh, ko2:ko2 + nt], self.xrows(name, t0, nt, 512, 32)))
                rd += self.xres_of(name, t0, nt)
                ko2 += nt
            self.dmas("sp", pairs, rd, [kT.res], kT.res)
            for (name, t0, n, ko) in ktiles:
                ck = ckT[ci % 2]
                ci += 1
                if name is None:
                    self.load_ctxT(d["c_ckv"][l], 128, lambda c, kb: ck.ap[:, kb * 128:(kb + 1) * 128], ck.res)
                else:
                    self.dma("sp", ck.ap[:, 0:n], self.xrows(name, t0, n, 384, 128), self.xres_of(name, t0, n), [ck.res], ck.res)
                for hh in range(2):
                    h = hp * 2 + hh
                    p = self.ps[2 + hh]
                    self.mm(p.ap[:, 0:n], wkv.ap[:, h * 128:(h + 1) * 128], ck.ap[:, 0:n], True, True, [wkv.res, ck.res], [p.res])
                    self.cp(kT.ap[0:64, hh, ko:ko + n], p.ap[0:64, 0:n], [p.res], [kT.res], eng=("act" if hh else "dve"))
                p = self.ps[6]
                for b in range(n // 128):
                    self.mm(p.ap[:, b * 128:(b + 1) * 128], ck.ap[:, b * 128:(b + 1) * 128], wkv4[:, hp * 2:hp * 2 + 2, 64:128], True, True,
                            [ck.res, wkv.res], [p.res])
                self.cp(V.ap[:, ko // 128:(ko + n) // 128, :, 0:64],
                        p.ap[:, 0:n].rearrange("p (b h e) -> p b h e", h=2, e=64), [p.res], [V.res])
            for ti, (qt0, nq) in enumerate(qtiles):
                nqs = nq // 128
                qres = [self.QSr[b] for b in range(qt0 // 128, (qt0 + nq) // 128)]
                for hh in range(2):
                    h = hp * 2 + hh
                    q = qm[qi % 2]
                    qi += 1
                    self.dma("sp", q.ap[0:96, 0:nq], QS[2048 + h * 128:2048 + h * 128 + 96, qt0:qt0 + nq], qres, [q.res], q.res)
                    ob = self.ps[4 + (qi % 2)]
                    kbl = [(kT.ap[0:96, hh, kb * 128:(kb + 1) * 128], [kT.res], V.ap[:, kb, hh, :], [V.res], None) for kb in range(nkb)]
                    self.attend(q.ap[0:96, 0:nq], q.res, kbl, nq, scale, ob)
                    for qs in range(nqs):
                        ob_ = oblk0 + ti * nqs + qs
                        self.S.op("dve", lambda e, qs=qs, ob=ob: e.reciprocal(self.rec.ap[:, qs:qs + 1], ob.ap[:, qs * 65 + 64:qs * 65 + 65]),
                                  reads=[ob.res], writes=[self.rec.res])
                        self.ts(o_t.ap[:, ob_, 768 + h * 64:768 + (h + 1) * 64], ob.ap[:, qs * 65:qs * 65 + 64], self.rec.ap[:, qs:qs + 1], None,
                                ALU.mult, None, [ob.res, self.rec.res], [o_t.res])

    def attn_B(self, l, samp, seq, o_t, oblk0):
        d = self.d
        QS = d["QS"]
        scale = 64 ** -0.5
        if samp:
            nkb = 20
        else:
            nkb = 2
        nk = nkb * 128
        kT = self.atile("kT_B", [2, nk], BF16)
        V = self.atile("V_B", [nkb, 2, 65], BF16)
        qm = [self.atile("qmB%d" % i, [8, 512], BF16) for i in range(2)]
        dup = self.atile("dupB", [2, 64], F32)
        pairs = []
        rd = []
        if samp:
            ct = self.ctmp
            self.dma("sp", ct.ap[:, :, 0:128], d["ck_b"][l].rearrange("(b p) f -> p b f", p=128), [], [ct.res], ct.res)
            p = self.ps[2]
            for kvh in range(2):
                for kb in range(2):
                    for r in range(2):
                        self.cp(dup.ap[:, r, :], ct.ap[:, kb, kvh * 64:(kvh + 1) * 64], [ct.res], [dup.res])
                    self.tr(p.ap[:, (kvh * 2 + kb) * 128:(kvh * 2 + kb + 1) * 128], dup.ap.rearrange("p a b -> p (a b)"), self.identF.ap,
                            [dup.res, self.identF.res], [p.res])
            for kvh in range(2):
                self.cp(kT.ap[:, kvh, 0:256], p.ap[:, kvh * 256:(kvh + 1) * 256], [p.res], [kT.res])
            self.dmas("pool", [(V.ap[:, b_, :, 0:64], d["cv_b"][l][b_ * 128:(b_ + 1) * 128, :].rearrange("p (h e) -> p h e", e=64))
                               for b_ in range(2)], [], [V.res], V.res)
            self.S.op("dve", lambda e: e.memset(V.ap[:, 0:2, :, 64:65], 1.0), writes=[V.res])
            srcs = [("XB", 0, 2048, 256), ("XG", 15 * 128, 128, 18 * 128), ("XG", 16 * 128, 128, 19 * 128)]
        else:
            srcs = [("PB", seq * 256, 256, 0)]
        for (name, t0, nt, ko) in srcs:
            for kvh in range(2):
                for r in range(2):
                    pairs.append((kT.ap[r * 64:(r + 1) * 64, kvh, ko:ko + nt], self.xrows(name, t0, nt, 256 + kvh * 64, 64)))
            pairs.append((V.ap[:, ko // 128:(ko + nt) // 128, :, :],
                          self.xv(name, t0, nt)[:, 260:390].rearrange("(b p) (h e) -> p b h e", p=128, e=65)))
            rd += self.xres_of(name, t0, nt)
        self.dmas("sp", pairs, rd, [kT.res, V.res], kT.res)
        nblk = 16 if samp else 2
        ntile = 4 if samp else 1
        per = nblk // ntile
        for ti in range(ntile):
            tok0 = (1024 + ti * 512) if samp else seq * 256
            nq = per * 128
            q = qm[ti % 2]
            qres = [self.QSr[b] for b in range(tok0 // 128, (tok0 + nq) // 128)]
            self.dma("sp", q.ap[:, :, 0:nq], QS[1024:2048, tok0:tok0 + nq].rearrange("(i p) t -> p i t", p=128), qres, [q.res], q.res)
            for bi in range(per):
                i = ti * per + bi
                for kvh in range(2):
                    ob = self.ps[4 + (kvh % 2)]
                    bm = self.bmask4.ap

                    def kb_(idx, m=None):
                        return (kT.ap[:, kvh, idx * 128:(idx + 1) * 128], [kT.res], V.ap[:, idx, kvh, :], [V.res],
                                None if m is None else bm[:, m].rearrange("p g q -> p (g q)"))
                    if samp:
                        kbl = [kb_(0), kb_(1)]
                        kbl.append(kb_(2 + i - 1, 0) if i > 0 else kb_(18, 2))
                        kbl.append(kb_(2 + i))
                        kbl.append(kb_(2 + i + 1, 1) if i < 15 else kb_(19, 3))
                    else:
                        kbl = [kb_(0), kb_(1)]
                    qv = q.ap[:, 4 * kvh:4 * kvh + 4, bi * 128:(bi + 1) * 128]
                    self.attend(qv, q.res, kbl, 512, scale, ob)
                    for g in range(4):
                        hq = 4 * kvh + g
                        self.ts(self.rec.ap[:, g:g + 1], ob.ap[:, g * 65 + 64:g * 65 + 65], self.es.ap[:, hq:hq + 1], None, ALU.add, None,
                                [ob.res, self.es.res], [self.rec.res])
                    self.S.op("dve", lambda e: e.reciprocal(self.rec.ap[:, 0:4], self.rec.ap[:, 0:4]), reads=[], writes=[self.rec.res])
                    for g in range(4):
                        hq = 4 * kvh + g
                        self.ts(o_t.ap[:, oblk0 + i, 256 + hq * 64:256 + (hq + 1) * 64], ob.ap[:, g * 65:g * 65 + 64], self.rec.ap[:, g:g + 1], None,
                                ALU.mult, None, [ob.res, self.rec.res], [o_t.res])

    def wo_phase(self, l, tiles, o_t, blk0):
        d = self.d
        w_o = self.atile("w_o", [8, D], BF16)
        oT = self.atile("oT", [8, 512], BF16)
        yb = [self.atile("ybm%d" % i, [D], F32) for i in range(4)]
        wsrc = d["w_o"][l].rearrange("(k p) n -> p k n", p=128)
        self.dmas("pool", [(w_o.ap[:, :, 0:512], wsrc[:, :, 0:512]), (w_o.ap[:, :, 512:1024], wsrc[:, :, 512:1024])], [], [w_o.res], w_o.res)
        for tile in tiles:
            cond = 1 if tile < 2 else 0
            for k in range(8):
                p = self.ps[k % 2]
                pb = p.ap[:, 0:256].bitcast(BF16)
                for tb in range(4):
                    ob_ = tile * 4 + tb - blk0
                    self.tr(pb[:, tb * 128:(tb + 1) * 128], o_t.ap[:, ob_, k * 128:(k + 1) * 128], self.identB.ap,
                            [o_t.res, self.identB.res], [p.res])
                self.cp(oT.ap[:, k, :], pb, [p.res], [oT.res], eng=("act" if k % 2 else "dve"))
            for hf in range(2):
                for tb in range(4):
                    p = self.ps[4 + tb] if hf == 0 else self.ps[2 + (tb % 2)]
                    for k in range(8):
                        self.mm(p.ap, oT.ap[:, k, tb * 128:(tb + 1) * 128], w_o.ap[:, k, hf * 512:(hf + 1) * 512], k == 0, k == 7,
                                [oT.res, w_o.res], [p.res])
                    self.tt(yb[tb].ap[:, hf * 512:(hf + 1) * 512], p.ap, self.gate_bc.ap[:, cond, hf * 512:(hf + 1) * 512], ALU.mult,
                            [p.res, self.gate_bc.res], [yb[tb].res])
            for tb in range(4):
                self.post(tile * 4 + tb, cond, yb[tb])


def build_nc():
    b = Builder()
    nc = b.build()
    return nc, list(b.in_names)


def _rope_tables(dim, pos0, n):
    t = np.arange(pos0, pos0 + n)
    row = (t // 64).astype(np.float32)
    col = (t % 64).astype(np.float32)
    a = dim // 2
    inv = np.power(np.float32(10000.0), -np.arange(0, a, 2, dtype=np.float32) / np.float32(a)).astype(np.float32)
    ar = row[:, None] * inv[None, :]
    ac = col[:, None] * inv[None, :]
    ang = np.concatenate([ar, ar, ac, ac], axis=-1).astype(np.float32)
    cos = np.cos(ang).astype(np.float32)
    sin = np.sin(ang).astype(np.float32)
    q = a // 2
    sgn = np.ones(dim, np.float32)
    sgn[0:q] = -1.0
    sgn[a:a + q] = -1.0
    ss = sin * sgn[None, :]
    out = np.stack([cos, ss], 0).reshape(2, n // 128, 128, dim).transpose(2, 0, 1, 3)
    return np.ascontiguousarray(out, dtype=np.float32)


def _consts(core):
    half = core % 2
    ident = np.eye(128, dtype=np.float32)
    k = np.arange(128)[:, None]
    q = np.arange(128)[None, :]
    mL = np.where(k >= q, 0.0, NEG).astype(np.float32)
    mR = np.where(k <= q, 0.0, NEG).astype(np.float32)
    full = np.full((128, 128), NEG, np.float32)
    mLe = full if half == 0 else mL
    mRe = full if half == 1 else mR
    bmask = np.ascontiguousarray(np.stack([mL, mR, mLe, mRe], 1))
    rmask = np.zeros((128, 6), np.float32)
    for i in range(4):
        rmask[32 * i:32 * i + 32, i] = 1.0
    rmask[0:64, 4] = 1.0
    rmask[64:128, 5] = 1.0
    return ident, bmask, rmask


_NC_CACHE = {}


def kernel(**inp):
    f = lambda a: np.ascontiguousarray(np.asarray(a, dtype=np.float32))
    x_prompt = f(inp["x_prompt"]); x_sample = f(inp["x_sample"])
    if "nc" not in _NC_CACHE:
        _NC_CACHE["nc"] = build_nc()
    nc, in_names = _NC_CACHE["nc"]
    shared = {
        "w_mod": f(inp["w_mod"]),
        "bmT": np.ascontiguousarray(f(inp["b_mod"]).reshape(DEPTH, 72, 128).transpose(2, 0, 1)),
        "ln_g": f(inp["ln_g"]), "ln_b": f(inp["ln_b"]),
        "ffn_w1": f(inp["ffn_w1"]), "ffn_w3": f(inp["ffn_w3"]), "ffn_w2": f(inp["ffn_w2"]),
        "w_in": f(inp["w_in"]), "w_o": f(inp["w_o"]),
        "a_lambda": f(inp["a_lambda"]).reshape(DEPTH, 128), "a_subln_g": f(inp["a_subln_g"]),
        "b_sink": f(inp["b_sink"]), "c_q_norm_g": f(inp["c_q_norm_g"]), "c_w_q_up": f(inp["c_w_q_up"]),
        "c_kv_norm_g": f(inp["c_kv_norm_g"]), "c_w_kv_up": f(inp["c_w_kv_up"]).reshape(DEPTH, 128, 512),
    }
    c = f(inp["c"]); c_ctx = f(inp["c_ctx"])
    in_maps = []
    for core in range(8):
        b = core // 2
        half = core % 2
        m = dict(shared)
        m["xin"] = np.ascontiguousarray(np.concatenate(
            [x_prompt[4 * core:4 * core + 4].reshape(1024, D), x_sample[b, half * 2048:(half + 1) * 2048]], 0))
        cond = np.stack([c[b], c_ctx], 0)
        m["condT"] = np.ascontiguousarray(cond.reshape(2, 8, 128).transpose(2, 1, 0))
        m["ck_a"] = f(inp["cache_a_k"])[b].reshape(DEPTH, 256, 256)
        m["cv_a"] = f(inp["cache_a_v"])[b].reshape(DEPTH, 256, 256)
        m["ck_b"] = f(inp["cache_b_k"])[b].reshape(DEPTH, 256, 128)
        m["cv_b"] = f(inp["cache_b_v"])[b].reshape(DEPTH, 256, 128)
        m["c_ckv"] = f(inp["cache_c_kv"])[b]
        m["c_kpe"] = f(inp["cache_c_kpe"])[b]
        ident, bmask, rmask = _consts(core)
        m["ident"] = ident; m["bmask"] = bmask; m["rmask"] = rmask
        m["rope32"] = _rope_tables(32, half * 2048, 2048)
        m["rope64"] = _rope_tables(64, half * 2048, 2048)
        in_maps.append({k: np.ascontiguousarray(m[k]) for k in in_names})
    res = run_bass_kernel_spmd(nc, in_maps, core_ids=list(range(8)))
    ys = [np.asarray(r["y"]) for r in res.results]
    sts = [np.asarray(r["st"]) for r in res.results]
    y_prompt = np.concatenate([y[:1024].reshape(4, 256, D) for y in ys], 0)
    y_sample = np.stack([np.concatenate([ys[2 * b][1024:], ys[2 * b + 1][1024:]], 0) for b in range(4)], 0)
    stt = np.concatenate(sts, 0)
    new_a_k = stt[..., 0:256].reshape(32, DEPTH, 256, 4, 64)
    new_a_v = stt[..., 256:512].reshape(32, DEPTH, 256, 4, 64)
    new_b_k = stt[..., 512:640].reshape(32, DEPTH, 256, 2, 64)
    new_b_v = stt[..., 640:768].reshape(32, DEPTH, 256, 2, 64)
    new_c_kv = stt[..., 768:896]
    new_c_kpe = stt[..., 896:928]
    outs = (y_prompt, y_sample, new_a_k, new_a_v, new_b_k, new_b_v, new_c_kv, new_c_kpe)
    return tuple(np.ascontiguousarray(o, dtype=np.float32) for o in outs)
```

```python
import os
import numpy as np
from contextlib import ExitStack
import concourse.bass as bass
import concourse.mybir as mybir
from concourse.bass_utils import run_bass_kernel_spmd

F32 = mybir.dt.float32
BF16 = mybir.dt.bfloat16
ALU = mybir.AluOpType
AF = mybir.ActivationFunctionType
AX = mybir.AxisListType

D = 1024
DFF = 2816
NJ = DFF // 128
DEPTH = 2
INW = 1952
LN_EPS = 1e-5
RMS_EPS = 1e-6
DN_ALPHA = float((2 * DEPTH) ** 0.25)
NEG = -30000.0
NPB = 8
NSB = 16
NB = NPB + NSB
XK_ROWS = 544
XV_W = 390
X_ROWS = XK_ROWS + XV_W
QS_ROWS = 2560
XG_PIECES = ((0, 256), (256, 512), (512, 739), (739, 934))


class Res:
    __slots__ = ("name", "w", "r", "dsem", "dcnt", "excl")

    def __init__(self, name, excl=False):
        self.name = name
        self.excl = excl
        self.w = None
        self.r = {}
        self.dsem = None
        self.dcnt = 0

    def inherit(self, other):
        if other.w is not None:
            k, v = other.w
            if self.r.get(k, 0) < v:
                self.r[k] = v
        for k, v in other.r.items():
            if self.r.get(k, 0) < v:
                self.r[k] = v


class Sched:
    ENG = ("pe", "act", "dve", "pool", "sp")

    def __init__(self):
        self.q = {e: [] for e in self.ENG}
        self.cnt = {e: 0 for e in self.ENG}
        self.waited = {e: {} for e in self.ENG}
        self.ndsem = 0
        self.semmap = {}

    def _dep(self, eng, k, v):
        if eng == "pe" and k == ("e", "pe"):
            return
        if self.waited[eng].get(k, 0) >= v:
            return
        self.waited[eng][k] = v
        self.q[eng].append(("wait", k, v))

    def op(self, eng, fn, reads=(), writes=(), dma=None, ndma=1, inc=16):
        for r in reads:
            if r.w is not None:
                self._dep(eng, *r.w)
            if r.excl:
                for k, v in r.r.items():
                    if k != ("e", eng):
                        self._dep(eng, k, v)
        for w in writes:
            if w.w is not None:
                self._dep(eng, *w.w)
            for k, v in w.r.items():
                self._dep(eng, k, v)
        if dma is None:
            self.cnt[eng] += 1
            tok = (("e", eng), self.cnt[eng])
            self.q[eng].append(("op", fn, tok[0], 1))
        else:
            ent = self.semmap.get(dma.name)
            if ent is None:
                ent = [("d", self.ndsem), 0, None]
                self.ndsem += 1
                self.semmap[dma.name] = ent
            if ent[2] is not dma:
                if ent[1] > 0:
                    self._dep(eng, ent[0], ent[1])
                ent[2] = dma
            ent[1] += inc * ndma
            tok = (ent[0], ent[1])
            self.q[eng].append(("op", fn, tok[0], inc))
        for r in reads:
            if r.r.get(tok[0], 0) < tok[1]:
                r.r[tok[0]] = tok[1]
        for w in writes:
            w.w = tok
            w.r = {}
        return tok

    def finish(self, eng="sp"):
        for ent in self.semmap.values():
            self._dep(eng, ent[0], ent[1])
        for e in self.ENG:
            if e != eng and self.cnt[e] > 0:
                self._dep(eng, ("e", e), self.cnt[e])

    def emit(self, nc, stack):
        sems = {}
        for e in self.ENG:
            sems[("e", e)] = stack.enter_context(nc.semaphore("s_" + e))
        for i in range(self.ndsem):
            sems[("d", i)] = stack.enter_context(nc.semaphore("d_%d" % i))
        block = stack.enter_context(nc.Block())
        q = self.q

        def run(ename):
            def body(h):
                for it in q[ename]:
                    if it[0] == "wait":
                        h.wait_ge(sems[it[1]], it[2])
                    else:
                        ins = it[1](h)
                        if not isinstance(ins, (list, tuple)):
                            ins = [ins]
                        for i_ in ins:
                            i_.then_inc(sems[it[2]], it[3])
            return body

        block.tensor(run("pe"))
        block.scalar(run("act"))
        block.vector(run("dve"))
        block.gpsimd(run("pool"))
        block.sync(run("sp"))


class T:
    __slots__ = ("ap", "res")

    def __init__(self, ap, res):
        self.ap = ap
        self.res = res


class Builder:
    def __init__(self):
        self.nc = bass.Bass("TRN2", target_bir_lowering=False)
        self.S = Sched()
        self.d = {}
        self.dres = {}
        self.in_names = []

    def din(self, name, shape, dt=F32):
        if os.environ.get("KSTAGE", "full") in ("projonly", "mixonly", "mixprompt") and name.startswith("ffn_w"):
            return
        self.d[name] = self.nc.dram_tensor(name, list(shape), dt, kind="ExternalInput").ap()
        self.dres[name] = Res(name)
        self.in_names.append(name)

    def dout(self, name, shape, dt=F32):
        self.d[name] = self.nc.dram_tensor(name, list(shape), dt, kind="ExternalOutput").ap()
        self.dres[name] = Res(name)

    def dscr(self, name, shape, dt=BF16):
        self.d[name] = self.nc.dram_tensor(name, list(shape), dt).ap()
        self.dres[name] = Res(name)

    def ptile(self, name, free, dt):
        t = self.stack.enter_context(self.nc.sbuf_tensor(name, [128] + list(free), dt))
        return T(t[:], Res(name))

    def arena_reset(self):
        self.acur = 0

    def atile(self, name, free, dt):
        n = int(np.prod(free))
        esz = 4 if dt == F32 else 2
        nb = (n * esz + 63) // 64 * 64
        off = self.acur
        assert off + nb <= self.ABYTES, (name, off, nb, self.ABYTES)
        self.acur += nb
        ap = self.arena[:, off // 2: off // 2 + n * esz // 2]
        if dt == F32:
            ap = ap.bitcast(F32)
        if len(free) == 2:
            ap = ap.rearrange("p (a b) -> p a b", a=free[0])
        elif len(free) == 3:
            ap = ap.rearrange("p (a b c) -> p a b c", a=free[0], b=free[1])
        elif len(free) == 4:
            ap = ap.rearrange("p (a b c d) -> p a b c d", a=free[0], b=free[1], c=free[2])
        res = Res(name)
        keep = []
        for (o, s, r) in self.alive:
            if o < off + nb and off < o + s:
                res.inherit(r)
                if not (off <= o and o + s <= off + nb):
                    keep.append((o, s, r))
            else:
                keep.append((o, s, r))
        keep.append((off, nb, res))
        self.alive = keep
        return T(ap, res)

    def mm(self, out, lhsT, rhs, start, stop, R, W, skip=False):
        self.S.op("pe", lambda e: e.matmul(out, lhsT, rhs, start=start, stop=stop, skip_group_check=skip),
                  reads=R, writes=W)

    def tr(self, out, in_, ident, R, W):
        self.S.op("pe", lambda e: e.transpose(out, in_, ident), reads=R, writes=W)

    def act(self, out, in_, func, R, W, bias=None, scale=None):
        kw = {}
        if bias is not None:
            kw["bias"] = bias
        if scale is not None:
            kw["scale"] = scale
        self.S.op("act", lambda e: e.activation(out, in_, func, **kw), reads=R, writes=W)

    def tt(self, out, in0, in1, op, R, W, eng="dve"):
        self.S.op(eng, lambda e: e.tensor_tensor(out, in0, in1, op), reads=R, writes=W)

    def ts(self, out, in0, s1, s2, op0, op1, R, W, eng="dve"):
        if op1 is None:
            self.S.op(eng, lambda e: e.tensor_scalar(out, in0, s1, None, op0), reads=R, writes=W)
        else:
            self.S.op(eng, lambda e: e.tensor_scalar(out, in0, s1, s2, op0, op1), reads=R, writes=W)

    def stt(self, out, in0, scalar, in1, op0, op1, R, W, eng="dve"):
        self.S.op(eng, lambda e: e.scalar_tensor_tensor(out, in0, scalar, in1, op0, op1), reads=R, writes=W)

    def cp(self, out, in_, R, W, eng="dve"):
        if eng == "act":
            self.S.op("act", lambda e: e.activation(out, in_, AF.Copy), reads=R, writes=W)
        else:
            self.S.op(eng, lambda e: e.tensor_copy(out, in_), reads=R, writes=W)

    def dma(self, q, out, in_, R, W, sem, slow=False):
        if slow:
            self.S.op(q, lambda e: e.dma_start(out=out, in_=in_, allow_slow_non_contiguous=True), reads=R, writes=W, dma=sem)
        else:
            self.S.op(q, lambda e: e.dma_start(out=out, in_=in_), reads=R, writes=W, dma=sem)

    def dmas(self, q, pairs, R, W, sem):
        pairs = list(pairs)
        self.S.op(q, lambda e: [e.dma_start(out=o, in_=i) for (o, i) in pairs], reads=R, writes=W, dma=sem,
                  ndma=len(pairs))

    def build(self):
        nc = self.nc
        self.din("xin", [NB * 128, D])
        self.din("condT", [128, 8, 2])
        self.din("bmT", [128, DEPTH, 72])
        self.din("ck_a", [DEPTH, 256, 256]); self.din("cv_a", [DEPTH, 256, 256])
        self.din("ck_b", [DEPTH, 256, 128]); self.din("cv_b", [DEPTH, 256, 128])
        self.din("c_ckv", [DEPTH, 256, 128]); self.din("c_kpe", [DEPTH, 256, 32])
        self.din("w_mod", [DEPTH, D, 9 * D])
        self.din("ln_g", [DEPTH, 3, D]); self.din("ln_b", [DEPTH, 3, D])
        self.din("ffn_w1", [DEPTH, 2, D, DFF]); self.din("ffn_w3", [DEPTH, 2, D, DFF])
        self.din("ffn_w2", [DEPTH, 2, DFF, D])
        self.din("w_in", [DEPTH, D, INW]); self.din("w_o", [DEPTH, D, D])
        self.din("a_lambda", [DEPTH, 128]); self.din("a_subln_g", [DEPTH, 64]); self.din("b_sink", [DEPTH, 8])
        self.din("c_q_norm_g", [DEPTH, 256]); self.din("c_w_q_up", [DEPTH, 256, 384])
        self.din("c_kv_norm_g", [DEPTH, 128]); self.din("c_w_kv_up", [DEPTH, 128, 512])
        self.din("ident", [128, 128])
        self.din("rope32", [128, 2, NSB, 32]); self.din("rope64", [128, 2, NSB, 64])
        self.din("bmask", [128, 4, 128])
        self.din("rmask", [128, 6])
        self.dout("y", [NB * 128, D])
        self.dout("st", [4, DEPTH, 256, 928])
        self.dscr("XB", [X_ROWS, 2048])
        self.dres["XG"] = Res("XG")
        for pi, (r0, r1) in enumerate(XG_PIECES):
            self.dscr("XG%d" % pi, [2 * (r1 - r0), 2048])
        self.dscr("PB", [X_ROWS, 1024]); self.dscr("QS", [QS_ROWS, NB * 128])
        self.QSr = [Res("QS%d" % i) for i in range(NB)]
        for s_ in range(2):
            self.dscr("W13S%d" % s_, [11, 128, 2, 8, 256]); self.dscr("W2S%d" % s_, [12, 128, 4, 512])
        self.w13s_res = [[Res("w13s%d_%d" % (s_, j)) for j in range(11)] for s_ in range(2)]
        self.w2s_res = [[Res("w2s%d_%d" % (s_, j)) for j in range(12)] for s_ in range(2)]
        self.cast_n = 0
        self.XBr = [Res("XB%d" % i) for i in range(NSB)]
        self.PBr = [Res("PB%d" % i) for i in range(NPB)]

        with ExitStack() as st:
            self.stack = st
            self.xres = self.ptile("xres", [NB, D], F32)
            self.xblk = [Res("xblk%d" % i) for i in range(NB)]
            self.identF = self.ptile("identF", [128], F32)
            self.identB = self.ptile("identB", [128], BF16)
            self.onesF = self.ptile("onesF", [128], F32)
            self.csT = self.ptile("csT", [8, 2], F32)
            self.bmT = self.ptile("bmT_s", [DEPTH, 72], F32)
            self.modT = self.ptile("modT", [24, 2], F32)
            self.gate_bc = self.ptile("gate_bc", [2, D], F32)
            self.lng = self.ptile("lng", [D], F32)
            self.lnb = self.ptile("lnb", [D], F32)
            self.bmask = self.ptile("bmask_s", [4, 128], BF16)
            self.rmask = self.ptile("rmask_s", [6], F32)
            self.small = self.ptile("small", [64], F32)
            self.small_r = [Res("small%d" % i) for i in range(8)]
            self.ABYTES = 88 * 1024
            arena_t = st.enter_context(nc.sbuf_tensor("arena", [128, self.ABYTES // 2], BF16))
            self.arena = arena_t[:]
            self.alive = []
            self.acur = 0
            self.ps = []
            for i in range(8):
                p = st.enter_context(nc.psum_tensor("ps%d" % i, [128, 512], F32))
                self.ps.append(T(p[:], Res("ps%d" % i, excl=True)))

            self.prologue()
            for l in range(DEPTH):
                self.layer(l)
            self.epilogue()
            self.S.finish()
            self.S.emit(nc, st)
        return nc

    def prologue(self):
        d = self.d
        xv = d["xin"].rearrange("(b p) f -> p b f", p=128)
        for i in range(3):
            blks = list(range(i * 8, (i + 1) * 8))
            self.dma("sp", self.xres.ap[:, i * 8:(i + 1) * 8, :], xv[:, i * 8:(i + 1) * 8, :], [],
                     [self.xblk[b] for b in blks], self.xblk[blks[0]])
        self.dma("sp", self.identF.ap, d["ident"], [], [self.identF.res], self.identF.res)
        self.dma("pool", self.identB.ap, d["ident"], [], [self.identB.res], self.identB.res)
        self.dma("pool", self.bmask.ap, d["bmask"], [], [self.bmask.res], self.bmask.res)
        self.dma("sp", self.rmask.ap, d["rmask"], [], [self.rmask.res], self.rmask.res)
        self.dma("sp", self.bmT.ap, d["bmT"], [], [self.bmT.res], self.bmT.res)
        self.dma("sp", self.csT.ap, d["condT"], [], [self.csT.res], self.csT.res)
        self.S.op("dve", lambda e: e.memset(self.onesF.ap, 1.0), writes=[self.onesF.res])
        self.act(self.csT.ap, self.csT.ap, AF.Silu, [], [self.csT.res])

    def epilogue(self):
        yv = self.d["y"].rearrange("(b p) f -> p b f", p=128)
        for i in range(6):
            blks = list(range(i * 4, (i + 1) * 4))
            self.dma("sp", yv[:, i * 4:(i + 1) * 4, :], self.xres.ap[:, i * 4:(i + 1) * 4, :],
                     [self.xblk[b] for b in blks], [], self.xblk[blks[0]])

    def layer(self, l):
        tiles = [2, 3, 4, 5, 0, 1]
        if os.environ.get("KSTAGE", "full") in ("projonly", "mixonly", "mixprompt"):
            if l == 0:
                self.mixer_phase(l)
            return
        self.ffn_phase(l, 0, 0, tiles)
        self.mixer_phase(l)
        self.ffn_phase(l, 1, 2, tiles)

    def mods(self, l, slot, weight):
        d = self.d
        ps = self.ps[6]
        wsrc = d["w_mod"][l].rearrange("(k p) n -> p k n", p=128)
        base = slot * 3 * D
        ring = [self.atile("wm%d" % i, [8, 256], F32) for i in range(2)]
        for jb in range(12):
            w = ring[jb % 2]
            self.dma("sp", w.ap, wsrc[:, :, base + jb * 256: base + (jb + 1) * 256], [], [w.res], w.res)
            for cc in range(2):
                n = jb * 2 + cc
                for k in range(8):
                    self.mm(ps.ap[:, 2 * n:2 * n + 2], w.ap[:, k, cc * 128:(cc + 1) * 128], self.csT.ap[:, k, :],
                            k == 0, k == 7, [w.res, self.csT.res], [ps.res])
        psv = ps.ap[:, 0:48].rearrange("p (n c) -> p n c", c=2)
        for c in range(2):
            self.tt(self.modT.ap[:, :, c], psv[:, :, c], self.bmT.ap[:, l, slot * 24:(slot + 1) * 24], ALU.add,
                    [ps.res, self.bmT.res], [self.modT.res])
        self.ts(self.modT.ap[:, 8:16, :], self.modT.ap[:, 8:16, :], 1.0, None, ALU.add, None, [], [self.modT.res])
        self.ts(self.modT.ap[:, 16:24, :], self.modT.ap[:, 16:24, :], float(weight), None, ALU.mult, None, [],
                [self.modT.res])
        dg = [self.atile("dg%d" % i, [128], F32) for i in range(2)]
        pb = [self.ps[4], self.ps[5]]
        i = 0
        for c in range(2):
            for hf in range(2):
                p = pb[(c * 2 + hf) % 2]
                for kk in range(4):
                    k = hf * 4 + kk
                    g = dg[i % 2]
                    i += 1
                    self.ts(g.ap, self.identF.ap, self.modT.ap[:, 16 + k, c:c + 1], None, ALU.mult, None,
                            [self.identF.res, self.modT.res], [g.res])
                    self.mm(p.ap[:, kk * 128:(kk + 1) * 128], self.onesF.ap, g.ap, True, True,
                            [self.onesF.res, g.res], [p.res])
                self.cp(self.gate_bc.ap[:, c, hf * 512:(hf + 1) * 512], p.ap, [p.res], [self.gate_bc.res], eng="act")

    def load_ln(self, l, idx):
        d = self.d
        self.dma("sp", self.lng.ap, d["ln_g"][l, idx].partition_broadcast(128), [], [self.lng.res], self.lng.res)
        self.dma("sp", self.lnb.ap, d["ln_b"][l, idx].partition_broadcast(128), [], [self.lnb.res], self.lnb.res)

    def pre(self, tile, hT, banks):
        cond = 1 if tile < 2 else 0
        for k in range(8):
            p = banks[k % len(banks)]
            for tb in range(4):
                blk = tile * 4 + tb
                self.tr(p.ap[:, tb * 128:(tb + 1) * 128], self.xres.ap[:, blk, k * 128:(k + 1) * 128], self.identF.ap,
                        [self.xblk[blk], self.identF.res], [p.res])
            self.act(hT.ap[:, k, :], p.ap, AF.Identity, [p.res, self.modT.res], [hT.res],
                     bias=self.modT.ap[:, k, cond:cond + 1], scale=self.modT.ap[:, 8 + k, cond:cond + 1])

    def post(self, blk, cond, ybuf):
        x = self.xres.ap[:, blk, :]
        xr = self.xblk[blk]
        sm = self.small.ap
        sr = self.small_r[0]
        self.stt(ybuf.ap, x, DN_ALPHA, ybuf.ap, ALU.mult, ALU.add, [xr], [ybuf.res])
        st6 = sm[:, 0:12].rearrange("p (a b) -> p a b", a=2)
        for hf in range(2):
            self.S.op("dve", lambda e, hf=hf: e.bn_stats(st6[:, hf, :], ybuf.ap[:, hf * 512:(hf + 1) * 512]),
                      reads=[ybuf.res], writes=[sr])
        self.S.op("dve", lambda e: e.bn_aggr(sm[:, 12:14], st6), reads=[], writes=[sr])
        self.ts(sm[:, 14:15], sm[:, 13:14], LN_EPS, None, ALU.add, None, [], [sr])
        self.act(sm[:, 14:15], sm[:, 14:15], AF.Sqrt, [], [sr])
        self.S.op("dve", lambda e: e.reciprocal(sm[:, 14:15], sm[:, 14:15]), reads=[], writes=[sr])
        self.stt(sm[:, 15:16], sm[:, 12:13], -1.0, sm[:, 14:15], ALU.mult, ALU.mult, [], [sr])
        self.act(ybuf.ap, ybuf.ap, AF.Identity, [sr], [ybuf.res], bias=sm[:, 15:16], scale=sm[:, 14:15])
        self.tt(ybuf.ap, ybuf.ap, self.lng.ap, ALU.mult, [self.lng.res], [ybuf.res])
        self.tt(x, ybuf.ap, self.lnb.ap, ALU.add, [ybuf.res, self.lnb.res], [xr])

    def cast_jobs(self, p):
        d = self.d
        l, half = p // 2, p % 2
        st_ = p % 2
        w1s = d["ffn_w1"][l, half].rearrange("(k p) n -> p k n", p=128)
        w3s = d["ffn_w3"][l, half].rearrange("(k p) n -> p k n", p=128)
        w2s = d["ffn_w2"][l, half].rearrange("(j p) n -> p j n", p=128)
        jobs = []
        for jb in range(11):
            def j13(jb=jb):
                dst = d["W13S%d" % st_][jb]
                self.dmas("pool", [(dst[:, 0], w1s[:, :, jb * 256:(jb + 1) * 256]), (dst[:, 1], w3s[:, :, jb * 256:(jb + 1) * 256])],
                          [], [self.w13s_res[st_][jb]], Res("cast%d" % (self.cast_n % 4)))
                self.cast_n += 1
            jobs.append(j13)
        for r in range(12):
            def j2(r=r):
                hf, jg = r // 6, r % 6
                nj = 4 if jg < 5 else 2
                dst = d["W2S%d" % st_][r]
                self.dmas("pool", [(dst[:, 0:nj, :], w2s[:, jg * 4:jg * 4 + nj, hf * 512:(hf + 1) * 512])],
                          [], [self.w2s_res[st_][r]], Res("cast%d" % (self.cast_n % 4)))
                self.cast_n += 1
            jobs.append(j2)
        return jobs

    def ffn_phase(self, l, half, slot, tiles):
        d = self.d
        p_ = l * 2 + half
        st_ = p_ % 2
        if p_ == 0:
            for j in self.cast_jobs(0):
                j()
        nxt = self.cast_jobs(p_ + 1) if p_ + 1 < 2 * DEPTH else []
        self.arena_reset()
        self.mods(l, slot, 0.5)
        self.load_ln(l, 0 if slot == 0 else 2)
        self.arena_reset()
        hT = self.atile("hT", [8, 512], BF16)
        aT = self.atile("aT", [NJ, 512], BF16)
        su = [self.atile("su%d" % i, [512], F32) for i in range(2)]
        R13, R2 = 3, 3
        w13 = [self.atile("w13_%d" % i, [2, 8, 256], BF16) for i in range(R13)]
        w2r = [self.atile("w2_%d" % i, [4, 512], BF16) for i in range(R2)]
        yb = [self.atile("yb%d" % i, [D], F32) for i in range(4)]
        state = {"i13": 0, "i2": 0}
        total13 = len(tiles) * 11
        total2 = len(tiles) * 12

        def issue13(upto):
            while state["i13"] < min(upto, total13):
                n = state["i13"]
                jb = n % 11
                w = w13[n % R13]
                self.dma("sp", w.ap, d["W13S%d" % st_][jb], [self.w13s_res[st_][jb]], [w.res], w.res)
                state["i13"] += 1
                if nxt:
                    nxt.pop(0)()

        def issue2(upto):
            while state["i2"] < min(upto, total2):
                n = state["i2"]
                r = n % 12
                w = w2r[n % R2]
                self.dma("sp", w.ap, d["W2S%d" % st_][r], [self.w2s_res[st_][r]], [w.res], w.res)
                state["i2"] += 1

        issue13(R13)
        self.pre(tiles[0], hT, [self.ps[4], self.ps[5]])
        for ti, tile in enumerate(tiles):
            cond = 1 if tile < 2 else 0
            for jb in range(11):
                n = ti * 11 + jb
                issue13(n + R13)
                w = w13[n % R13]
                for cc in range(2):
                    j = jb * 2 + cc
                    pu = self.ps[j % 2]
                    pg = self.ps[2 + j % 2]
                    for k in range(8):
                        self.mm(pu.ap, w.ap[:, 0, k, cc * 128:(cc + 1) * 128], hT.ap[:, k, :], k == 0, k == 7,
                                [w.res, hT.res], [pu.res])
                    for k in range(8):
                        self.mm(pg.ap, w.ap[:, 1, k, cc * 128:(cc + 1) * 128], hT.ap[:, k, :], k == 0, k == 7,
                                [w.res, hT.res], [pg.res])
                    s_ = su[j % 2]
                    self.act(s_.ap, pu.ap, AF.Silu, [pu.res], [s_.res])
                    self.tt(aT.ap[:, j, :], s_.ap, pg.ap, ALU.mult, [s_.res, pg.res], [aT.res])
                if jb == 8:
                    issue2(ti * 12 + R2)
            if ti + 1 < len(tiles):
                self.pre(tiles[ti + 1], hT, [self.ps[4], self.ps[5]])
                issue13((ti + 1) * 11 + R13)
            for hf in range(2):
                banks = [self.ps[4 + tb] for tb in range(4)] if hf == 0 else [self.ps[tb] for tb in range(4)]
                for jg in range(6):
                    n = ti * 12 + hf * 6 + jg
                    issue2(n + R2)
                    w = w2r[n % R2]
                    nj = 4 if jg < 5 else 2
                    for jj in range(nj):
                        j = jg * 4 + jj
                        for tb in range(4):
                            self.mm(banks[tb].ap, aT.ap[:, j, tb * 128:(tb + 1) * 128], w.ap[:, jj, :], j == 0, j == NJ - 1,
                                    [aT.res, w.res], [banks[tb].res])
                for tb in range(4):
                    self.tt(yb[tb].ap[:, hf * 512:(hf + 1) * 512], banks[tb].ap,
                            self.gate_bc.ap[:, cond, hf * 512:(hf + 1) * 512], ALU.mult,
                            [banks[tb].res, self.gate_bc.res], [yb[tb].res])
            for tb in range(4):
                self.post(tile * 4 + tb, cond, yb[tb])

    def mixer_phase(self, l):
        self.arena_reset()
        self.mods(l, 1, 1.0)
        self.load_ln(l, 1)
        self.arena_reset()
        stage = os.environ.get("KSTAGE", "full")
        self.project(l)
        if stage in ("proj", "projonly"):
            return
        self.gather()
        if stage == "gather":
            return
        self.arena_reset()
        self.layer_consts(l)
        mark = self.acur
        o_p = self.atile("o_p", [NPB, D], BF16)
        m2 = self.acur
        for s_ in range(4):
            self.acur = m2
            segs = [("PB", s_ * 256, 256)]
            self.attn_A(l, segs, False, [(s_ * 256, 256)], o_p, s_ * 2, 0)
            self.acur = m2
            self.attn_C(l, segs, False, [(s_ * 256, 256)], o_p, s_ * 2, 0)
            self.acur = m2
            self.attn_B(l, False, s_, o_p, s_ * 2)
        self.acur = m2
        self.wo_phase(l, [0, 1], o_p, 0)
        if stage in ("prompt", "mixprompt"):
            return
        self.acur = mark
        o_s = self.atile("o_s", [NSB, D], BF16)
        m2 = self.acur
        segs = [("XG", i * 1024, 1024) for i in range(4)]
        qt = [(1024 + i * 512, 512) for i in range(4)]
        self.attn_A(l, segs, True, qt, o_s, 0, 1024)
        self.acur = m2
        self.attn_C(l, segs, True, qt, o_s, 0, 1024)
        self.acur = m2
        self.attn_B(l, True, 0, o_s, 0)
        self.acur = m2
        self.wo_phase(l, [2, 3, 4, 5], o_s, 8)

    def xrows(self, name, tok0, ntok, r0, nr):
        if name == "XG":
            rk = tok0 // 2048
            t = tok0 % 2048
            for pi, (p0, p1) in enumerate(XG_PIECES):
                if p0 <= r0 and r0 + nr <= p1:
                    a = self.d["XG%d" % pi]
                    n = p1 - p0
                    return a[rk * n + r0 - p0: rk * n + r0 - p0 + nr, t:t + ntok]
            raise AssertionError((r0, nr))
        a = self.d[name]
        return a[r0:r0 + nr, tok0:tok0 + ntok]

    def xv(self, name, tok0, ntok):
        if name == "XG":
            rk = tok0 // 2048
            t = tok0 % 2048
            assert (t // 1024) == ((t + ntok - 1) // 1024)
            if t < 1024:
                a = self.d["XG2"]
                v = a[rk * 227 + 32: rk * 227 + 227, :]
            else:
                a = self.d["XG3"]
                v = a[rk * 195: rk * 195 + 195, :]
                t -= 1024
            v = v.rearrange("r t -> (r t)").rearrange("(t c) -> t c", c=XV_W)
            return v[t:t + ntok, :]
        a = self.d[name]
        v = a[XK_ROWS:X_ROWS, :]
        v = v.rearrange("r t -> (r t)").rearrange("(t c) -> t c", c=XV_W)
        return v[tok0:tok0 + ntok, :]

    def xres_of(self, name, tok0, ntok):
        if name == "XG":
            return [self.dres["XG"]]
        lst = self.XBr if name == "XB" else self.PBr
        return [lst[b] for b in range(tok0 // 128, (tok0 + ntok) // 128)]

    def rope(self, dst, src, H, Q, tbl, sb, R, W, tmp):
        dim = 4 * Q
        xs = src.rearrange("p (h a w q) -> p h a w q", h=H, a=2, w=2)
        xd = dst.rearrange("p (h a w q) -> p h a w q", h=H, a=2, w=2)
        xt = tmp[:, 0:H * dim].rearrange("p (h a w q) -> p h a w q", h=H, a=2, w=2)
        cs = tbl.ap[:, 0, sb, :].rearrange("p (a w q) -> p a w q", a=2, w=2)
        ss = tbl.ap[:, 1, sb, :].rearrange("p (a w q) -> p a w q", a=2, w=2)
        for a in range(2):
            cb = cs[:, a].unsqueeze(1).to_broadcast([128, H, 2, Q])
            self.tt(xd[:, :, a], xs[:, :, a], cb, ALU.mult, R + [tbl.res], W)
            for w in range(2):
                sbb = ss[:, a, w].unsqueeze(1).to_broadcast([128, H, Q])
                self.tt(xt[:, :, a, w], xs[:, :, a, 1 - w], sbb, ALU.mult, R + [tbl.res], [self.tmp_res])
            self.tt(xd[:, :, a], xd[:, :, a], xt[:, :, a], ALU.add, [self.tmp_res], W)

    def rms(self, dst, src, n, gam, R, W, sidx):
        sm = self.small.ap
        sr = self.small_r[sidx]
        c0 = 16 + sidx * 4
        self.tt(self.junk.ap[:, 0:n], src, src, ALU.mult, R, [self.junk.res])
        self.S.op("dve", lambda e: e.reduce_sum(sm[:, c0:c0 + 1], self.junk.ap[:, 0:n], axis=AX.X),
                  reads=[self.junk.res], writes=[sr])
        self.ts(sm[:, c0:c0 + 1], sm[:, c0:c0 + 1], 1.0 / n, RMS_EPS, ALU.mult, ALU.add, [], [sr])
        self.act(sm[:, c0:c0 + 1], sm[:, c0:c0 + 1], AF.Sqrt, [], [sr])
        self.S.op("dve", lambda e: e.reciprocal(sm[:, c0:c0 + 1], sm[:, c0:c0 + 1]), reads=[], writes=[sr])
        self.stt(dst, src, sm[:, c0:c0 + 1], gam.ap, ALU.mult, ALU.mult, R + [sr, gam.res], W)

    def project(self, l):
        d = self.d
        hT = self.atile("hTm", [8, 512], BF16)
        w_in = self.atile("w_in", [8, INW], BF16)
        wq = self.atile("wq", [2, 384], BF16)
        wkv = self.atile("wkv", [512], BF16)
        gq = self.atile("gq", [256], F32)
        gkv = self.atile("gkv", [128], F32)
        r32 = self.atile("r32", [2, NSB, 32], F32)
        r64 = self.atile("r64", [2, NSB, 64], F32)
        tp = self.atile("tp", [INW], F32)
        rq = self.atile("rq", [1184], F32)
        tmp = self.atile("tmp", [640], F32)
        self.tmp_res = tmp.res
        self.junk = self.atile("junk", [256], F32)
        nq = self.atile("nq", [256], BF16)
        nqT = self.atile("nqT", [2, 128], BF16)
        cq = self.atile("cq", [384], F32)
        ckv = self.atile("ckv", [128], F32)
        kpe96 = self.atile("kpe96", [96], F32)
        qA = self.atile("qA", [8, 128], BF16)
        qB = self.atile("qB", [8, 128], BF16)
        qC = self.atile("qC", [4, 128], BF16)
        kst = self.atile("kst", [5, 128], BF16)
        vst = self.atile("vst", [6, 65], BF16)
        wsrc = d["w_in"][l].rearrange("(k p) n -> p k n", p=128)
        self.dmas("pool", [(w_in.ap[:, :, c0:c1], wsrc[:, :, c0:c1]) for (c0, c1) in ((0, 512), (512, 1024), (1024, 1536), (1536, INW))],
                  [], [w_in.res], w_in.res)
        self.dma("pool", wq.ap, d["c_w_q_up"][l].rearrange("(k p) n -> p k n", p=128), [], [wq.res], wq.res)
        self.dma("pool", wkv.ap, d["c_w_kv_up"][l], [], [wkv.res], wkv.res)
        self.dma("sp", gq.ap, d["c_q_norm_g"][l].partition_broadcast(128), [], [gq.res], gq.res)
        self.dma("sp", gkv.ap, d["c_kv_norm_g"][l].partition_broadcast(128), [], [gkv.res], gkv.res)
        self.dma("sp", r32.ap, d["rope32"], [], [r32.res], r32.res)
        self.dma("sp", r64.ap, d["rope64"], [], [r64.res], r64.res)
        self.S.op("dve", lambda e: e.memset(kpe96.ap, 0.0), writes=[kpe96.res])
        self.S.op("dve", lambda e: e.memset(vst.ap, 1.0), writes=[vst.res])
        self.S.op("dve", lambda e: e.memset(qC.ap, 0.0), writes=[qC.res])
        self.S.op("dve", lambda e: e.memset(kst.ap, 0.0), writes=[kst.res])
        self.wkv_t = None
        groups = ((0, 512), (512, 1024), (1024, 1536), (1536, INW))
        rm = self.rmask
        lvl = int(os.environ.get("KLVL", "9"))
        skip = set(os.environ.get("KSKIP", "").split(","))
        stv = d["st"]
        for tile in [2, 3, 4, 5, 0, 1]:
            samp = tile >= 2
            self.pre(tile, hT, [self.ps[4], self.ps[5]])
            for tb in range(4):
                blk = tile * 4 + tb
                sb = blk - NPB
                for gi, (c0, c1) in enumerate(groups):
                    p = self.ps[gi]
                    for k in range(8):
                        self.mm(p.ap[:, 0:c1 - c0], hT.ap[:, k, tb * 128:(tb + 1) * 128], w_in.ap[:, k, c0:c1], k == 0, k == 7,
                                [hT.res, w_in.res], [p.res])
                    self.cp(tp.ap[:, c0:c1], p.ap[:, 0:c1 - c0], [p.res], [tp.res], eng="act")
                if lvl < 2:
                    continue
                if samp:
                    self.rope(rq.ap[:, 0:512], tp.ap[:, 0:512], 16, 8, r32, sb, [tp.res], [rq.res], tmp.ap)
                    self.rope(rq.ap[:, 512:1152], tp.ap[:, 768:1408], 10, 16, r64, sb, [tp.res], [rq.res], tmp.ap)
                    self.rope(rq.ap[:, 1152:1184], tp.ap[:, 1920:1952], 1, 8, r32, sb, [tp.res], [rq.res], tmp.ap)
                    s_aq, s_ak, s_bq, s_bk, s_kpe = (rq.ap[:, 0:256], rq.ap[:, 256:512], rq.ap[:, 512:1024],
                                                     rq.ap[:, 1024:1152], rq.ap[:, 1152:1184])
                    sres = [rq.res]
                else:
                    s_aq, s_ak, s_bq, s_bk, s_kpe = (tp.ap[:, 0:256], tp.ap[:, 256:512], tp.ap[:, 768:1280],
                                                     tp.ap[:, 1280:1408], tp.ap[:, 1920:1952])
                    sres = [tp.res]
                if lvl < 3:
                    continue
                self.rms(nq.ap, tp.ap[:, 1536:1792], 256, gq, [tp.res], [nq.res], 1)
                self.rms(ckv.ap, tp.ap[:, 1792:1920], 128, gkv, [tp.res], [ckv.res], 2)
                if not samp:
                    seq, t0 = blk // 2, (blk % 2) * 128
                    self.dmas("sp", [(stv[seq, l, t0:t0 + 128, 0:512], tp.ap[:, 256:768]),
                                     (stv[seq, l, t0:t0 + 128, 512:768], tp.ap[:, 1280:1536]),
                                     (stv[seq, l, t0:t0 + 128, 896:928], tp.ap[:, 1920:1952]),
                                     (stv[seq, l, t0:t0 + 128, 768:896], ckv.ap)],
                              [tp.res, ckv.res], [], tp.res)
                if lvl < 4:
                    continue
                p4, p5, p6, p7 = self.ps[4], self.ps[5], self.ps[6], self.ps[7]
                p4b = p4.ap[:, 0:128].bitcast(BF16)
                for kk in range(2):
                    self.tr(p4b[:, kk * 128:(kk + 1) * 128], nq.ap[:, kk * 128:(kk + 1) * 128], self.identB.ap,
                            [nq.res, self.identB.res], [p4.res])
                self.cp(nqT.ap, p4b.rearrange("p (a b) -> p a b", a=2), [p4.res], [nqT.res])
                for kk in range(2):
                    self.mm(p5.ap[:, 0:384], nqT.ap[:, kk, :], wq.ap[:, kk, :], kk == 0, kk == 1, [nqT.res, wq.res], [p5.res])
                self.cp(cq.ap, p5.ap[:, 0:384], [p5.res], [cq.res], eng="act")
                if samp:
                    cq4 = cq.ap.rearrange("p (h e) -> p h e", e=96)
                    p54 = p5.ap[:, 0:384].rearrange("p (h e) -> p h e", e=96)
                    self.rope_strided(cq4[:, :, 64:96], p54[:, :, 64:96], 4, 8, r32, sb, [p5.res], [cq.res], tmp.ap)
                if lvl < 5:
                    continue
                if "A" not in skip:
                    for c in range(2):
                        self.tr(p6.ap[:, c * 128:(c + 1) * 128], s_aq[:, c * 128:(c + 1) * 128], self.identF.ap, sres + [self.identF.res], [p6.res])
                        self.tr(p6.ap[:, 256 + c * 128:256 + (c + 1) * 128], s_ak[:, c * 128:(c + 1) * 128], self.identF.ap, sres + [self.identF.res], [p6.res])
                    for c in range(2):
                        for i in range(4):
                            self.ts(qA.ap[:, c * 4 + i, :], p6.ap[:, c * 128:(c + 1) * 128], rm.ap[:, i:i + 1], None, ALU.mult, None,
                                    [p6.res, rm.res], [qA.res])
                    self.cp(kst.ap[:, 0:2, :], p6.ap[:, 256:512].rearrange("p (a b) -> p a b", a=2), [p6.res], [kst.res], eng="act")
                if "B" not in skip:
                    for c in range(4):
                        self.tr(p7.ap[:, c * 128:(c + 1) * 128], s_bq[:, c * 128:(c + 1) * 128], self.identF.ap, sres + [self.identF.res], [p7.res])
                    for hq in range(8):
                        self.ts(qB.ap[:, hq, :], p7.ap[:, (hq // 2) * 128:(hq // 2 + 1) * 128], rm.ap[:, 4 + hq % 2:5 + hq % 2], None,
                                ALU.mult, None, [p7.res, rm.res], [qB.res])
                if "P4" not in skip:
                    self.cp(kpe96.ap[:, 64:96], s_kpe, sres, [kpe96.res])
                    self.tr(p4.ap[:, 0:128], s_bk, self.identF.ap, sres + [self.identF.res], [p4.res])
                    self.tr(p4.ap[:, 128:256], ckv.ap, self.identF.ap, [ckv.res, self.identF.res], [p4.res])
                    self.tr(p4.ap[0:96, 256:384], kpe96.ap, self.identF.ap, [kpe96.res, self.identF.res], [p4.res])
                    self.cp(kst.ap[:, 2:4, :], p4.ap[:, 0:256].rearrange("p (a b) -> p a b", a=2), [p4.res], [kst.res], eng="act")
                    self.cp(kst.ap[64:96, 4, :], p4.ap[64:96, 256:384], [p4.res], [kst.res])
                if "QC" not in skip:
                    for h in range(4):
                        self.tr(p5.ap[0:96, h * 128:(h + 1) * 128], cq.ap[:, h * 96:(h + 1) * 96], self.identF.ap, [cq.res, self.identF.res], [p5.res])
                    self.cp(qC.ap[0:96, :, :], p5.ap[0:96, :].rearrange("p (a b) -> p a b", a=4), [p5.res], [qC.res])
                if "V" not in skip:
                    self.cp(vst.ap[:, 0:4, 0:64], tp.ap[:, 512:768].rearrange("p (h e) -> p h e", e=64), [tp.res], [vst.res])
                    self.cp(vst.ap[:, 4:6, 0:64], tp.ap[:, 1408:1536].rearrange("p (h e) -> p h e", e=64), [tp.res], [vst.res])
                if lvl < 6:
                    continue
                tok = blk * 128
                QS = d["QS"]
                name = "XB" if samp else "PB"
                xt0 = (blk - NPB) * 128 if samp else blk * 128
                pairs = [
                    (QS[0:1024, tok:tok + 128].rearrange("(i p) t -> p i t", p=128), qA.ap),
                    (QS[1024:2048, tok:tok + 128].rearrange("(i p) t -> p i t", p=128), qB.ap),
                    (QS[2048:2560, tok:tok + 128].rearrange("(i p) t -> p i t", p=128), qC.ap),
                    (self.xrows(name, xt0, 128, 0, 256).rearrange("(c p) t -> p c t", p=128), kst.ap[:, 0:2, :]),
                    (self.xrows(name, xt0, 128, 256, 256).rearrange("(c p) t -> p c t", p=128), kst.ap[:, 2:4, :]),
                    (self.xrows(name, xt0, 128, 512, 32), kst.ap[64:96, 4, :]),
                    (self.xv(name, xt0, 128), vst.ap.rearrange("p h e -> p (h e)")),
                ]
                wr = [self.QSr[blk], (self.XBr[blk - NPB] if samp else self.PBr[blk])]
                self.dmas("sp", pairs, [qA.res, qB.res, qC.res, kst.res, vst.res], wr, qA.res)

    def rope_strided(self, dst, src, H, Q, tbl, sb, R, W, tmp):
        dim = 4 * Q
        xs = src.rearrange("p h (a w q) -> p h a w q", a=2, w=2)
        xd = dst.rearrange("p h (a w q) -> p h a w q", a=2, w=2)
        xt = tmp[:, 0:H * dim].rearrange("p (h a w q) -> p h a w q", h=H, a=2, w=2)
        cs = tbl.ap[:, 0, sb, :].rearrange("p (a w q) -> p a w q", a=2, w=2)
        ss = tbl.ap[:, 1, sb, :].rearrange("p (a w q) -> p a w q", a=2, w=2)
        for a in range(2):
            cb = cs[:, a].unsqueeze(1).to_broadcast([128, H, 2, Q])
            self.tt(xd[:, :, a], xs[:, :, a], cb, ALU.mult, R + [tbl.res], W)
            for w in range(2):
                sbb = ss[:, a, w].unsqueeze(1).to_broadcast([128, H, Q])
                self.tt(xt[:, :, a, w], xs[:, :, a, 1 - w], sbb, ALU.mult, R + [tbl.res], [self.tmp_res])
            self.tt(xd[:, :, a], xd[:, :, a], xt[:, :, a], ALU.add, [self.tmp_res], W)

    def gather(self):
        d = self.d
        for pi, (r0, r1) in enumerate(XG_PIECES):
            src = d["XB"][r0:r1, :]
            dst = d["XG%d" % pi]
            self.S.op("pool", lambda e, src=src, dst=dst: e.collective_compute(
                "AllGather", ALU.bypass, replica_groups=[[0, 1], [2, 3], [4, 5], [6, 7]], ins=[src], outs=[dst]),
                reads=list(self.XBr), writes=[self.dres["XG"]], dma=self.dres["XG"], inc=1)

    def layer_consts(self, l):
        d = self.d
        lam_init = 0.8 - 0.6 * float(np.exp(-0.3 * l))
        al = self.atile("al", [128], F32)
        self.gsub = self.atile("gsub", [64], F32)
        self.es = self.atile("es", [8], F32)
        self.lam = self.atile("lamc", [4], F32)
        self.dma("sp", al.ap, d["a_lambda"][l].partition_broadcast(128), [], [al.res], al.res)
        self.dma("sp", self.gsub.ap, d["a_subln_g"][l].partition_broadcast(128), [], [self.gsub.res], self.gsub.res)
        self.dma("sp", self.es.ap, d["b_sink"][l].partition_broadcast(128), [], [self.es.res], self.es.res)
        self.act(self.es.ap, self.es.ap, AF.Exp, [], [self.es.res])
        self.ts(self.gsub.ap, self.gsub.ap, 1.0 - lam_init, None, ALU.mult, None, [], [self.gsub.res])
        lm = self.lam
        for i in range(2):
            self.tt(al.ap[:, i * 64:i * 64 + 32], al.ap[:, i * 64:i * 64 + 32], al.ap[:, i * 64 + 32:i * 64 + 64], ALU.mult, [], [al.res])
            self.S.op("dve", lambda e, i=i: e.reduce_sum(lm.ap[:, i:i + 1], al.ap[:, i * 64:i * 64 + 32], axis=AX.X),
                      reads=[al.res], writes=[lm.res])
        self.act(lm.ap[:, 0:2], lm.ap[:, 0:2], AF.Exp, [], [lm.res])
        self.tt(lm.ap[:, 2:3], lm.ap[:, 0:1], lm.ap[:, 1:2], ALU.subtract, [], [lm.res])
        self.ts(lm.ap[:, 3:4], lm.ap[:, 2:3], lam_init, -1.0, ALU.add, ALU.mult, [], [lm.res])
        self.pt = [self.atile("pt%d" % i, [512], BF16) for i in range(4)]
        self.pti = 0
        self.sci = 0
        self.rec = self.atile("rec", [8], F32)
        self.bmask4 = self.atile("bmask4", [4, 4, 128], BF16)
        for m_ in range(4):
            for g_ in range(4):
                self.cp(self.bmask4.ap[:, m_, g_, :], self.bmask.ap[:, m_, :], [self.bmask.res], [self.bmask4.res])
        self.ctmp = self.atile("ctmp", [2, 256], F32)

    def attend(self, q_ap, q_res, kblocks, nq, scale, obank):
        nqs = nq // 128
        nkb = len(kblocks)
        LA = 2
        pts = [None] * nkb
        for i in range(nkb + LA):
            if i < nkb:
                kT, kres, V, vres, mask = kblocks[i]
                sc = self.ps[self.sci % 4]
                self.sci += 1
                self.mm(sc.ap[:, 0:nq], kT, q_ap, True, mask is None, kres + [q_res], [sc.res])
                if mask is not None:
                    self.mm(sc.ap[:, 0:nq], self.identB.ap, mask, False, True, [self.identB.res, self.bmask4.res], [sc.res])
                pt = self.pt[self.pti % 4]
                self.pti += 1
                self.act(pt.ap[:, 0:nq], sc.ap[:, 0:nq], AF.Exp, [sc.res], [pt.res], scale=float(scale))
                pts[i] = pt
            j = i - LA
            if j >= 0:
                kT, kres, V, vres, mask = kblocks[j]
                pt = pts[j]
                for qs in range(nqs):
                    Vq = V[qs] if isinstance(V, (list, tuple)) else V
                    self.mm(obank.ap[:, qs * 65:(qs + 1) * 65], pt.ap[:, qs * 128:(qs + 1) * 128], Vq, (j == 0 and qs == 0),
                            j == nkb - 1, [pt.res] + vres, [obank.res], skip=True)

    def load_ctxT(self, src_dram, width, dst_fn, dres):
        ct = self.ctmp
        self.dma("sp", ct.ap[:, :, 0:width], src_dram.rearrange("(b p) f -> p b f", p=128), [], [ct.res], ct.res)
        p = self.ps[2]
        nch = width // 128
        for kb in range(2):
            for c in range(nch):
                self.tr(p.ap[:, (kb * nch + c) * 128:(kb * nch + c + 1) * 128], ct.ap[:, kb, c * 128:(c + 1) * 128], self.identF.ap,
                        [ct.res, self.identF.res], [p.res])
        for kb in range(2):
            for c in range(nch):
                self.cp(dst_fn(c, kb), p.ap[:, (kb * nch + c) * 128:(kb * nch + c + 1) * 128], [p.res], [dres])

    def attn_A(self, l, segs, ctx, qtiles, o_t, oblk0, qtok_unused):
        d = self.d
        nk = (256 if ctx else 0) + sum(s[2] for s in segs)
        nkb = nk // 128
        kT = self.atile("kT_A", [2, nk], BF16)
        V = self.atile("V_A", [nkb, 4, 65], BF16)
        qm = [self.atile("qmA%d" % i, [512], BF16) for i in range(2)]
        oa = self.atile("oa", [4, 4, 64], F32)
        o1 = self.atile("o1", [4, 64], F32)
        o2 = self.atile("o2", [64], F32)
        ssq = self.atile("ssq", [16], F32)
        sq = self.atile("sqA", [4, 64], F32)
        koff = 0
        if ctx:
            self.load_ctxT(d["ck_a"][l], 256, lambda c, kb: kT.ap[:, c, kb * 128:(kb + 1) * 128], kT.res)
            self.dmas("pool", [(V.ap[:, b_, :, 0:64], d["cv_a"][l][b_ * 128:(b_ + 1) * 128, :].rearrange("p (h e) -> p h e", e=64))
                               for b_ in range(2)], [], [V.res], V.res)
            self.S.op("dve", lambda e: e.memset(V.ap[:, 0:2, :, 64:65], 1.0), writes=[V.res])
            koff = 256
        pairs = []
        rd = []
        for (name, t0, nt) in segs:
            pairs.append((kT.ap[:, :, koff:koff + nt], self.xrows(name, t0, nt, 0, 256).rearrange("(c p) t -> p c t", p=128)))
            pairs.append((V.ap[:, koff // 128:(koff + nt) // 128, :, :],
                          self.xv(name, t0, nt)[:, 0:260].rearrange("(b p) (h e) -> p b h e", p=128, e=65)))
            rd += self.xres_of(name, t0, nt)
            koff += nt
        self.dmas("sp", pairs, rd, [kT.res, V.res], kT.res)
        QS = d["QS"]
        scale = 32 ** -0.5
        qi = 0
        for ti, (qt0, nq) in enumerate(qtiles):
            nqs = nq // 128
            qres = [self.QSr[b] for b in range(qt0 // 128, (qt0 + nq) // 128)]
            for i in range(8):
                c, h, j = i // 4, (i // 4) * 2 + (i % 4) // 2, i % 2
                q = qm[qi % 2]
                qi += 1
                self.dma("sp", q.ap[:, 0:nq], QS[i * 128:(i + 1) * 128, qt0:qt0 + nq], qres, [q.res], q.res)
                ob = self.ps[4 + (i % 2)]
                kbl = [(kT.ap[:, c, kb * 128:(kb + 1) * 128], [kT.res], V.ap[:, kb, h, :], [V.res], None) for kb in range(nkb)]
                self.attend(q.ap[:, 0:nq], q.res, kbl, nq, scale, ob)
                for qs in range(nqs):
                    self.S.op("dve", lambda e, qs=qs, ob=ob: e.reciprocal(self.rec.ap[:, qs:qs + 1], ob.ap[:, qs * 65 + 64:qs * 65 + 65]),
                              reads=[ob.res], writes=[self.rec.res])
                    if j == 0:
                        self.ts(o1.ap[:, qs, :], ob.ap[:, qs * 65:qs * 65 + 64], self.rec.ap[:, qs:qs + 1], None, ALU.mult, None,
                                [ob.res, self.rec.res], [o1.res])
                    else:
                        self.ts(o2.ap, ob.ap[:, qs * 65:qs * 65 + 64], self.rec.ap[:, qs:qs + 1], None, ALU.mult, None,
                                [ob.res, self.rec.res], [o2.res])
                        self.stt(oa.ap[:, qs, h, :], o2.ap, self.lam.ap[:, 3:4], o1.ap[:, qs, :], ALU.mult, ALU.add,
                                 [o2.res, o1.res, self.lam.res], [oa.res])
            n16 = nqs * 4
            for qs in range(nqs):
                self.tt(sq.ap, oa.ap[:, qs], oa.ap[:, qs], ALU.mult, [oa.res], [sq.res])
                self.S.op("dve", lambda e, qs=qs: e.reduce_sum(ssq.ap[:, qs * 4:qs * 4 + 4], sq.ap, axis=AX.X),
                          reads=[sq.res], writes=[ssq.res])
            self.ts(ssq.ap[:, 0:n16], ssq.ap[:, 0:n16], 1.0 / 64, RMS_EPS, ALU.mult, ALU.add, [], [ssq.res])
            self.act(ssq.ap[:, 0:n16], ssq.ap[:, 0:n16], AF.Sqrt, [], [ssq.res])
            self.S.op("dve", lambda e, n16=n16: e.reciprocal(ssq.ap[:, 0:n16], ssq.ap[:, 0:n16]), reads=[], writes=[ssq.res])
            for qs in range(nqs):
                for h in range(4):
                    ob_ = oblk0 + ti * nqs + qs
                    self.stt(o_t.ap[:, ob_, h * 64:(h + 1) * 64], oa.ap[:, qs, h, :], ssq.ap[:, qs * 4 + h:qs * 4 + h + 1], self.gsub.ap,
                             ALU.mult, ALU.mult, [oa.res, ssq.res, self.gsub.res], [o_t.res])

    def attn_C(self, l, segs, ctx, qtiles, o_t, oblk0, qtok_unused):
        d = self.d
        nk = (256 if ctx else 0) + sum(s[2] for s in segs)
        nkb = nk // 128
        wkv = self.atile("wkvC", [512], BF16)
        self.dma("pool", wkv.ap, d["c_w_kv_up"][l], [], [wkv.res], wkv.res)
        wkv4 = wkv.ap.rearrange("p (h e) -> p h e", e=128)
        ckT = [self.atile("ckT%d" % i, [512], BF16) for i in range(2)]
        cpe = self.atile("cpe", [2, 96], F32)
        qm = [self.atile("qmC%d" % i, [512], BF16) for i in range(2)]
        kT = self.atile("kT_C", [2, nk], BF16)
        V = self.atile("V_C", [nkb, 2, 65], BF16)
        QS = d["QS"]
        scale = 96 ** -0.5
        qi = 0
        ci = 0
        for hp in range(2):
            self.S.op("dve", lambda e: e.memset(V.ap[:, :, :, 64:65], 1.0), writes=[V.res])
            ktiles = []
            koff = 0
            if ctx:
                ktiles.append((None, 0, 256, 0))
                koff = 256
            for (name, t0, nt) in segs:
                for s0 in range(0, nt, 512):
                    n = min(512, nt - s0)
                    ktiles.append((name, t0 + s0, n, koff))
                    koff += n
            if ctx:
                self.S.op("dve", lambda e: e.memset(cpe.ap, 0.0), writes=[cpe.res])
                self.dma("sp", cpe.ap[:, :, 64:96], d["c_kpe"][l].rearrange("(b p) f -> p b f", p=128), [], [cpe.res], cpe.res)
                p = self.ps[2]
                for kb in range(2):
                    self.tr(p.ap[0:96, kb * 128:(kb + 1) * 128], cpe.ap[:, kb, :], self.identF.ap, [cpe.res, self.identF.res], [p.res])
                for hh in range(2):
                    self.cp(kT.ap[64:96, hh, 0:256], p.ap[64:96, 0:256], [p.res], [kT.res])
            pairs = []
            rd = []
            ko2 = 256 if ctx else 0
            for (name, t0, nt) in segs:
                for hh in range(2):
                    pairs.append((kT.ap[64:96, hh, ko2:ko2 + nt], self.xrows(name, t0, nt, 512, 32)))
                rd += self.xres_of(name, t0, nt)
                ko2 += nt
            self.dmas("sp", pairs, rd, [kT.res], kT.res)
            for (name, t0, n, ko) in ktiles:
                ck = ckT[ci % 2]
                ci += 1
                if name is None:
                    self.load_ctxT(d["c_ckv"][l], 128, lambda c, kb: ck.ap[:, kb * 128:(kb + 1) * 128], ck.res)
                else:
                    self.dma("sp", ck.ap[:, 0:n], self.xrows(name, t0, n, 384, 128), self.xres_of(name, t0, n), [ck.res], ck.res)
                for hh in range(2):
                    h = hp * 2 + hh
                    p = self.ps[2 + hh]
                    self.mm(p.ap[:, 0:n], wkv.ap[:, h * 128:(h + 1) * 128], ck.ap[:, 0:n], True, True, [wkv.res, ck.res], [p.res])
                    self.cp(kT.ap[0:64, hh, ko:ko + n], p.ap[0:64, 0:n], [p.res], [kT.res], eng=("act" if hh else "dve"))
                p = self.ps[6]
                for b in range(n // 128):
                    self.mm(p.ap[:, b * 128:(b + 1) * 128], ck.ap[:, b * 128:(b + 1) * 128], wkv4[:, hp * 2:hp * 2 + 2, 64:128], True, True,
                            [ck.res, wkv.res], [p.res])
                self.cp(V.ap[:, ko // 128:(ko + n) // 128, :, 0:64],
                        p.ap[:, 0:n].rearrange("p (b h e) -> p b h e", h=2, e=64), [p.res], [V.res])
            for ti, (qt0, nq) in enumerate(qtiles):
                nqs = nq // 128
                qres = [self.QSr[b] for b in range(qt0 // 128, (qt0 + nq) // 128)]
                for hh in range(2):
                    h = hp * 2 + hh
                    q = qm[qi % 2]
                    qi += 1
                    self.dma("sp", q.ap[0:96, 0:nq], QS[2048 + h * 128:2048 + h * 128 + 96, qt0:qt0 + nq], qres, [q.res], q.res)
                    ob = self.ps[4 + (qi % 2)]
                    kbl = [(kT.ap[0:96, hh, kb * 128:(kb + 1) * 128], [kT.res], V.ap[:, kb, hh, :], [V.res], None) for kb in range(nkb)]
                    self.attend(q.ap[0:96, 0:nq], q.res, kbl, nq, scale, ob)
                    for qs in range(nqs):
                        ob_ = oblk0 + ti * nqs + qs
                        self.S.op("dve", lambda e, qs=qs, ob=ob: e.reciprocal(self.rec.ap[:, qs:qs + 1], ob.ap[:, qs * 65 + 64:qs * 65 + 65]),
                                  reads=[ob.res], writes=[self.rec.res])
                        self.ts(o_t.ap[:, ob_, 768 + h * 64:768 + (h + 1) * 64], ob.ap[:, qs * 65:qs * 65 + 64], self.rec.ap[:, qs:qs + 1], None,
                                ALU.mult, None, [ob.res, self.rec.res], [o_t.res])

    def attn_B(self, l, samp, seq, o_t, oblk0):
        d = self.d
        QS = d["QS"]
        scale = 64 ** -0.5
        if samp:
            nkb = 20
        else:
            nkb = 2
        nk = nkb * 128
        kT = self.atile("kT_B", [2, nk], BF16)
        V = self.atile("V_B", [nkb, 2, 65], BF16)
        qm = [self.atile("qmB%d" % i, [8, 512], BF16) for i in range(2)]
        dup = self.atile("dupB", [2, 64], F32)
        pairs = []
        rd = []
        if samp:
            ct = self.ctmp
            self.dma("sp", ct.ap[:, :, 0:128], d["ck_b"][l].rearrange("(b p) f -> p b f", p=128), [], [ct.res], ct.res)
            p = self.ps[2]
            for kvh in range(2):
                for kb in range(2):
                    for r in range(2):
                        self.cp(dup.ap[:, r, :], ct.ap[:, kb, kvh * 64:(kvh + 1) * 64], [ct.res], [dup.res])
                    self.tr(p.ap[:, (kvh * 2 + kb) * 128:(kvh * 2 + kb + 1) * 128], dup.ap.rearrange("p a b -> p (a b)"), self.identF.ap,
                            [dup.res, self.identF.res], [p.res])
            for kvh in range(2):
                self.cp(kT.ap[:, kvh, 0:256], p.ap[:, kvh * 256:(kvh + 1) * 256], [p.res], [kT.res])
            self.dmas("pool", [(V.ap[:, b_, :, 0:64], d["cv_b"][l][b_ * 128:(b_ + 1) * 128, :].rearrange("p (h e) -> p h e", e=64))
                               for b_ in range(2)], [], [V.res], V.res)
            self.S.op("dve", lambda e: e.memset(V.ap[:, 0:2, :, 64:65], 1.0), writes=[V.res])
            srcs = [("XB", 0, 2048, 256), ("XG", 15 * 128, 128, 18 * 128), ("XG", 16 * 128, 128, 19 * 128)]
        else:
            srcs = [("PB", seq * 256, 256, 0)]
        for (name, t0, nt, ko) in srcs:
            for kvh in range(2):
                for r in range(2):
                    pairs.append((kT.ap[r * 64:(r + 1) * 64, kvh, ko:ko + nt], self.xrows(name, t0, nt, 256 + kvh * 64, 64)))
            pairs.append((V.ap[:, ko // 128:(ko + nt) // 128, :, :],
                          self.xv(name, t0, nt)[:, 260:390].rearrange("(b p) (h e) -> p b h e", p=128, e=65)))
            rd += self.xres_of(name, t0, nt)
        self.dmas("sp", pairs, rd, [kT.res, V.res], kT.res)
        nblk = 16 if samp else 2
        ntile = 4 if samp else 1
        per = nblk // ntile
        for ti in range(ntile):
            tok0 = (1024 + ti * 512) if samp else seq * 256
            nq = per * 128
            q = qm[ti % 2]
            qres = [self.QSr[b] for b in range(tok0 // 128, (tok0 + nq) // 128)]
            self.dma("sp", q.ap[:, :, 0:nq], QS[1024:2048, tok0:tok0 + nq].rearrange("(i p) t -> p i t", p=128), qres, [q.res], q.res)
            for bi in range(per):
                i = ti * per + bi
                for kvh in range(2):
                    ob = self.ps[4 + (kvh % 2)]
                    bm = self.bmask4.ap

                    def kb_(idx, m=None):
                        return (kT.ap[:, kvh, idx * 128:(idx + 1) * 128], [kT.res], V.ap[:, idx, kvh, :], [V.res],
                                None if m is None else bm[:, m].rearrange("p g q -> p (g q)"))
                    if samp:
                        kbl = [kb_(0), kb_(1)]
                        kbl.append(kb_(2 + i - 1, 0) if i > 0 else kb_(18, 2))
                        kbl.append(kb_(2 + i))
                        kbl.append(kb_(2 + i + 1, 1) if i < 15 else kb_(19, 3))
                    else:
                        kbl = [kb_(0), kb_(1)]
                    qv = q.ap[:, 4 * kvh:4 * kvh + 4, bi * 128:(bi + 1) * 128]
                    self.attend(qv, q.res, kbl, 512, scale, ob)
                    for g in range(4):
                        hq = 4 * kvh + g
                        self.ts(self.rec.ap[:, g:g + 1], ob.ap[:, g * 65 + 64:g * 65 + 65], self.es.ap[:, hq:hq + 1], None, ALU.add, None,
                                [ob.res, self.es.res], [self.rec.res])
                    self.S.op("dve", lambda e: e.reciprocal(self.rec.ap[:, 0:4], self.rec.ap[:, 0:4]), reads=[], writes=[self.rec.res])
                    for g in range(4):
                        hq = 4 * kvh + g
                        self.ts(o_t.ap[:, oblk0 + i, 256 + hq * 64:256 + (hq + 1) * 64], ob.ap[:, g * 65:g * 65 + 64], self.rec.ap[:, g:g + 1], None,
                                ALU.mult, None, [ob.res, self.rec.res], [o_t.res])

    def wo_phase(self, l, tiles, o_t, blk0):
        d = self.d
        w_o = self.atile("w_o", [8, D], BF16)
        oT = self.atile("oT", [8, 512], BF16)
        yb = [self.atile("ybm%d" % i, [D], F32) for i in range(4)]
        wsrc = d["w_o"][l].rearrange("(k p) n -> p k n", p=128)
        self.dmas("pool", [(w_o.ap[:, :, 0:512], wsrc[:, :, 0:512]), (w_o.ap[:, :, 512:1024], wsrc[:, :, 512:1024])], [], [w_o.res], w_o.res)
        for tile in tiles:
            cond = 1 if tile < 2 else 0
            for k in range(8):
                p = self.ps[k % 2]
                pb = p.ap[:, 0:256].bitcast(BF16)
                for tb in range(4):
                    ob_ = tile * 4 + tb - blk0
                    self.tr(pb[:, tb * 128:(tb + 1) * 128], o_t.ap[:, ob_, k * 128:(k + 1) * 128], self.identB.ap,
                            [o_t.res, self.identB.res], [p.res])
                self.cp(oT.ap[:, k, :], pb, [p.res], [oT.res], eng=("act" if k % 2 else "dve"))
            for hf in range(2):
                for tb in range(4):
                    p = self.ps[4 + tb] if hf == 0 else self.ps[2 + (tb % 2)]
                    for k in range(8):
                        self.mm(p.ap, oT.ap[:, k, tb * 128:(tb + 1) * 128], w_o.ap[:, k, hf * 512:(hf + 1) * 512], k == 0, k == 7,
                                [oT.res, w_o.res], [p.res])
                    self.tt(yb[tb].ap[:, hf * 512:(hf + 1) * 512], p.ap, self.gate_bc.ap[:, cond, hf * 512:(hf + 1) * 512], ALU.mult,
                            [p.res, self.gate_bc.res], [yb[tb].res])
            for tb in range(4):
                self.post(tile * 4 + tb, cond, yb[tb])


def build_nc():
    b = Builder()
    nc = b.build()
    return nc, list(b.in_names)


def _rope_tables(dim, pos0, n):
    t = np.arange(pos0, pos0 + n)
    row = (t // 64).astype(np.float32)
    col = (t % 64).astype(np.float32)
    a = dim // 2
    inv = np.power(np.float32(10000.0), -np.arange(0, a, 2, dtype=np.float32) / np.float32(a)).astype(np.float32)
    ar = row[:, None] * inv[None, :]
    ac = col[:, None] * inv[None, :]
    ang = np.concatenate([ar, ar, ac, ac], axis=-1).astype(np.float32)
    cos = np.cos(ang).astype(np.float32)
    sin = np.sin(ang).astype(np.float32)
    q = a // 2
    sgn = np.ones(dim, np.float32)
    sgn[0:q] = -1.0
    sgn[a:a + q] = -1.0
    ss = sin * sgn[None, :]
    out = np.stack([cos, ss], 0).reshape(2, n // 128, 128, dim).transpose(2, 0, 1, 3)
    return np.ascontiguousarray(out, dtype=np.float32)


def _consts(core):
    half = core % 2
    ident = np.eye(128, dtype=np.float32)
    k = np.arange(128)[:, None]
    q = np.arange(128)[None, :]
    mL = np.where(k >= q, 0.0, NEG).astype(np.float32)
    mR = np.where(k <= q, 0.0, NEG).astype(np.float32)
    full = np.full((128, 128), NEG, np.float32)
    mLe = full if half == 0 else mL
    mRe = full if half == 1 else mR
    bmask = np.ascontiguousarray(np.stack([mL, mR, mLe, mRe], 1))
    rmask = np.zeros((128, 6), np.float32)
    for i in range(4):
        rmask[32 * i:32 * i + 32, i] = 1.0
    rmask[0:64, 4] = 1.0
    rmask[64:128, 5] = 1.0
    return ident, bmask, rmask


_NC_CACHE = {}


def kernel(**inp):
    f = lambda a: np.ascontiguousarray(np.asarray(a, dtype=np.float32))
    x_prompt = f(inp["x_prompt"]); x_sample = f(inp["x_sample"])
    if "nc" not in _NC_CACHE:
        _NC_CACHE["nc"] = build_nc()
    nc, in_names = _NC_CACHE["nc"]
    shared = {
        "w_mod": f(inp["w_mod"]),
        "bmT": np.ascontiguousarray(f(inp["b_mod"]).reshape(DEPTH, 72, 128).transpose(2, 0, 1)),
        "ln_g": f(inp["ln_g"]), "ln_b": f(inp["ln_b"]),
        "ffn_w1": f(inp["ffn_w1"]), "ffn_w3": f(inp["ffn_w3"]), "ffn_w2": f(inp["ffn_w2"]),
        "w_in": f(inp["w_in"]), "w_o": f(inp["w_o"]),
        "a_lambda": f(inp["a_lambda"]).reshape(DEPTH, 128), "a_subln_g": f(inp["a_subln_g"]),
        "b_sink": f(inp["b_sink"]), "c_q_norm_g": f(inp["c_q_norm_g"]), "c_w_q_up": f(inp["c_w_q_up"]),
        "c_kv_norm_g": f(inp["c_kv_norm_g"]), "c_w_kv_up": f(inp["c_w_kv_up"]).reshape(DEPTH, 128, 512),
    }
    c = f(inp["c"]); c_ctx = f(inp["c_ctx"])
    in_maps = []
    for core in range(8):
        b = core // 2
        half = core % 2
        m = dict(shared)
        m["xin"] = np.ascontiguousarray(np.concatenate(
            [x_prompt[4 * core:4 * core + 4].reshape(1024, D), x_sample[b, half * 2048:(half + 1) * 2048]], 0))
        cond = np.stack([c[b], c_ctx], 0)
        m["condT"] = np.ascontiguousarray(cond.reshape(2, 8, 128).transpose(2, 1, 0))
        m["ck_a"] = f(inp["cache_a_k"])[b].reshape(DEPTH, 256, 256)
        m["cv_a"] = f(inp["cache_a_v"])[b].reshape(DEPTH, 256, 256)
        m["ck_b"] = f(inp["cache_b_k"])[b].reshape(DEPTH, 256, 128)
        m["cv_b"] = f(inp["cache_b_v"])[b].reshape(DEPTH, 256, 128)
        m["c_ckv"] = f(inp["cache_c_kv"])[b]
        m["c_kpe"] = f(inp["cache_c_kpe"])[b]
        ident, bmask, rmask = _consts(core)
        m["ident"] = ident; m["bmask"] = bmask; m["rmask"] = rmask
        m["rope32"] = _rope_tables(32, half * 2048, 2048)
        m["rope64"] = _rope_tables(64, half * 2048, 2048)
        in_maps.append({k: np.ascontiguousarray(m[k]) for k in in_names})
    res = run_bass_kernel_spmd(nc, in_maps, core_ids=list(range(8)))
    ys = [np.asarray(r["y"]) for r in res.results]
    sts = [np.asarray(r["st"]) for r in res.results]
    y_prompt = np.concatenate([y[:1024].reshape(4, 256, D) for y in ys], 0)
    y_sample = np.stack([np.concatenate([ys[2 * b][1024:], ys[2 * b + 1][1024:]], 0) for b in range(4)], 0)
    stt = np.concatenate(sts, 0)
    new_a_k = stt[..., 0:256].reshape(32, DEPTH, 256, 4, 64)
    new_a_v = stt[..., 256:512].reshape(32, DEPTH, 256, 4, 64)
    new_b_k = stt[..., 512:640].reshape(32, DEPTH, 256, 2, 64)
    new_b_v = stt[..., 640:768].reshape(32, DEPTH, 256, 2, 64)
    new_c_kv = stt[..., 768:896]
    new_c_kpe = stt[..., 896:928]
    outs = (y_prompt, y_sample, new_a_k, new_a_v, new_b_k, new_b_v, new_c_kv, new_c_kpe)
    return tuple(np.ascontiguousarray(o, dtype=np.float32) for o in outs)
```

```python
import os
import numpy as np
from contextlib import ExitStack
import concourse.bass as bass
import concourse.mybir as mybir
from concourse.bass_utils import run_bass_kernel_spmd

F32 = mybir.dt.float32
BF16 = mybir.dt.bfloat16
ALU = mybir.AluOpType
AF = mybir.ActivationFunctionType
AX = mybir.AxisListType

D = 1024
DFF = 2816
NJ = DFF // 128
DEPTH = 2
INW = 1952
LN_EPS = 1e-5
RMS_EPS = 1e-6
DN_ALPHA = float((2 * DEPTH) ** 0.25)
NEG = -30000.0
NPB = 8
NSB = 16
NB = NPB + NSB
XK_ROWS = 544
XV_W = 390
X_ROWS = XK_ROWS + XV_W
QS_ROWS = 2560
XG_PIECES = ((0, 256), (256, 512), (512, 739), (739, 934))


class Res:
    __slots__ = ("name", "w", "r", "dsem", "dcnt", "excl")

    def __init__(self, name, excl=False):
        self.name = name
        self.excl = excl
        self.w = None
        self.r = {}
        self.dsem = None
        self.dcnt = 0

    def inherit(self, other):
        if other.w is not None:
            k, v = other.w
            if self.r.get(k, 0) < v:
                self.r[k] = v
        for k, v in other.r.items():
            if self.r.get(k, 0) < v:
                self.r[k] = v


class Sched:
    ENG = ("pe", "act", "dve", "pool", "sp")

    def __init__(self):
        self.q = {e: [] for e in self.ENG}
        self.cnt = {e: 0 for e in self.ENG}
        self.waited = {e: {} for e in self.ENG}
        self.ndsem = 0
        self.semmap = {}

    def _dep(self, eng, k, v):
        if eng == "pe" and k == ("e", "pe"):
            return
        if self.waited[eng].get(k, 0) >= v:
            return
        self.waited[eng][k] = v
        self.q[eng].append(("wait", k, v))

    def op(self, eng, fn, reads=(), writes=(), dma=None, ndma=1, inc=16):
        for r in reads:
            if r.w is not None:
                self._dep(eng, *r.w)
            if r.excl:
                for k, v in r.r.items():
                    if k != ("e", eng):
                        self._dep(eng, k, v)
        for w in writes:
            if w.w is not None:
                self._dep(eng, *w.w)
            for k, v in w.r.items():
                self._dep(eng, k, v)
        if dma is None:
            self.cnt[eng] += 1
            tok = (("e", eng), self.cnt[eng])
            self.q[eng].append(("op", fn, tok[0], 1))
        else:
            ent = self.semmap.get(dma.name)
            if ent is None:
                ent = [("d", self.ndsem), 0, None]
                self.ndsem += 1
                self.semmap[dma.name] = ent
            if ent[2] is not dma:
                if ent[1] > 0:
                    self._dep(eng, ent[0], ent[1])
                ent[2] = dma
            ent[1] += inc * ndma
            tok = (ent[0], ent[1])
            self.q[eng].append(("op", fn, tok[0], inc))
        for r in reads:
            if r.r.get(tok[0], 0) < tok[1]:
                r.r[tok[0]] = tok[1]
        for w in writes:
            w.w = tok
            w.r = {}
        return tok

    def finish(self, eng="sp"):
        for ent in self.semmap.values():
            self._dep(eng, ent[0], ent[1])
        for e in self.ENG:
            if e != eng and self.cnt[e] > 0:
                self._dep(eng, ("e", e), self.cnt[e])

    def emit(self, nc, stack):
        sems = {}
        for e in self.ENG:
            sems[("e", e)] = stack.enter_context(nc.semaphore("s_" + e))
        for i in range(self.ndsem):
            sems[("d", i)] = stack.enter_context(nc.semaphore("d_%d" % i))
        block = stack.enter_context(nc.Block())
        q = self.q

        def run(ename):
            def body(h):
                for it in q[ename]:
                    if it[0] == "wait":
                        h.wait_ge(sems[it[1]], it[2])
                    else:
                        ins = it[1](h)
                        if not isinstance(ins, (list, tuple)):
                            ins = [ins]
                        for i_ in ins:
                            i_.then_inc(sems[it[2]], it[3])
            return body

        block.tensor(run("pe"))
        block.scalar(run("act"))
        block.vector(run("dve"))
        block.gpsimd(run("pool"))
        block.sync(run("sp"))


class T:
    __slots__ = ("ap", "res")

    def __init__(self, ap, res):
        self.ap = ap
        self.res = res


class Builder:
    def __init__(self):
        self.nc = bass.Bass("TRN2", target_bir_lowering=False)
        self.S = Sched()
        self.d = {}
        self.dres = {}
        self.in_names = []

    def din(self, name, shape, dt=F32):
        if os.environ.get("KSTAGE", "full") in ("projonly", "mixonly", "mixprompt") and name.startswith("ffn_w"):
            return
        self.d[name] = self.nc.dram_tensor(name, list(shape), dt, kind="ExternalInput").ap()
        self.dres[name] = Res(name)
        self.in_names.append(name)

    def dout(self, name, shape, dt=F32):
        self.d[name] = self.nc.dram_tensor(name, list(shape), dt, kind="ExternalOutput").ap()
        self.dres[name] = Res(name)

    def dscr(self, name, shape, dt=BF16):
        self.d[name] = self.nc.dram_tensor(name, list(shape), dt).ap()
        self.dres[name] = Res(name)

    def ptile(self, name, free, dt):
        t = self.stack.enter_context(self.nc.sbuf_tensor(name, [128] + list(free), dt))
        return T(t[:], Res(name))

    def arena_reset(self):
        self.acur = 0

    def atile(self, name, free, dt):
        n = int(np.prod(free))
        esz = 4 if dt == F32 else 2
        nb = (n * esz + 63) // 64 * 64
        off = self.acur
        assert off + nb <= self.ABYTES, (name, off, nb, self.ABYTES)
        self.acur += nb
        ap = self.arena[:, off // 2: off // 2 + n * esz // 2]
        if dt == F32:
            ap = ap.bitcast(F32)
        if len(free) == 2:
            ap = ap.rearrange("p (a b) -> p a b", a=free[0])
        elif len(free) == 3:
            ap = ap.rearrange("p (a b c) -> p a b c", a=free[0], b=free[1])
        elif len(free) == 4:
            ap = ap.rearrange("p (a b c d) -> p a b c d", a=free[0], b=free[1], c=free[2])
        res = Res(name)
        keep = []
        for (o, s, r) in self.alive:
            if o < off + nb and off < o + s:
                res.inherit(r)
                if not (off <= o and o + s <= off + nb):
                    keep.append((o, s, r))
            else:
                keep.append((o, s, r))
        keep.append((off, nb, res))
        self.alive = keep
        return T(ap, res)

    def mm(self, out, lhsT, rhs, start, stop, R, W, skip=False):
        self.S.op("pe", lambda e: e.matmul(out, lhsT, rhs, start=start, stop=stop, skip_group_check=skip),
                  reads=R, writes=W)

    def tr(self, out, in_, ident, R, W):
        self.S.op("pe", lambda e: e.transpose(out, in_, ident), reads=R, writes=W)

    def act(self, out, in_, func, R, W, bias=None, scale=None):
        kw = {}
        if bias is not None:
            kw["bias"] = bias
        if scale is not None:
            kw["scale"] = scale
        self.S.op("act", lambda e: e.activation(out, in_, func, **kw), reads=R, writes=W)

    def tt(self, out, in0, in1, op, R, W, eng="dve"):
        self.S.op(eng, lambda e: e.tensor_tensor(out, in0, in1, op), reads=R, writes=W)

    def ts(self, out, in0, s1, s2, op0, op1, R, W, eng="dve"):
        if op1 is None:
            self.S.op(eng, lambda e: e.tensor_scalar(out, in0, s1, None, op0), reads=R, writes=W)
        else:
            self.S.op(eng, lambda e: e.tensor_scalar(out, in0, s1, s2, op0, op1), reads=R, writes=W)

    def stt(self, out, in0, scalar, in1, op0, op1, R, W, eng="dve"):
        self.S.op(eng, lambda e: e.scalar_tensor_tensor(out, in0, scalar, in1, op0, op1), reads=R, writes=W)

    def cp(self, out, in_, R, W, eng="dve"):
        if eng == "act":
            self.S.op("act", lambda e: e.activation(out, in_, AF.Copy), reads=R, writes=W)
        else:
            self.S.op(eng, lambda e: e.tensor_copy(out, in_), reads=R, writes=W)

    def dma(self, q, out, in_, R, W, sem, slow=False):
        if slow:
            self.S.op(q, lambda e: e.dma_start(out=out, in_=in_, allow_slow_non_contiguous=True), reads=R, writes=W, dma=sem)
        else:
            self.S.op(q, lambda e: e.dma_start(out=out, in_=in_), reads=R, writes=W, dma=sem)

    def dmas(self, q, pairs, R, W, sem):
        pairs = list(pairs)
        self.S.op(q, lambda e: [e.dma_start(out=o, in_=i) for (o, i) in pairs], reads=R, writes=W, dma=sem,
                  ndma=len(pairs))

    def build(self):
        nc = self.nc
        self.din("xin", [NB * 128, D])
        self.din("condT", [128, 8, 2])
        self.din("bmT", [128, DEPTH, 72])
        self.din("ck_a", [DEPTH, 256, 256]); self.din("cv_a", [DEPTH, 256, 256])
        self.din("ck_b", [DEPTH, 256, 128]); self.din("cv_b", [DEPTH, 256, 128])
        self.din("c_ckv", [DEPTH, 256, 128]); self.din("c_kpe", [DEPTH, 256, 32])
        self.din("w_mod", [DEPTH, D, 9 * D])
        self.din("ln_g", [DEPTH, 3, D]); self.din("ln_b", [DEPTH, 3, D])
        self.din("ffn_w1", [DEPTH, 2, D, DFF]); self.din("ffn_w3", [DEPTH, 2, D, DFF])
        self.din("ffn_w2", [DEPTH, 2, DFF, D])
        self.din("w_in", [DEPTH, D, INW]); self.din("w_o", [DEPTH, D, D])
        self.din("a_lambda", [DEPTH, 128]); self.din("a_subln_g", [DEPTH, 64]); self.din("b_sink", [DEPTH, 8])
        self.din("c_q_norm_g", [DEPTH, 256]); self.din("c_w_q_up", [DEPTH, 256, 384])
        self.din("c_kv_norm_g", [DEPTH, 128]); self.din("c_w_kv_up", [DEPTH, 128, 512])
        self.din("ident", [128, 128])
        self.din("rope32", [128, 2, NSB, 32]); self.din("rope64", [128, 2, NSB, 64])
        self.din("bmask", [128, 4, 128])
        self.din("rmask", [128, 6])
        self.dout("y", [NB * 128, D])
        self.dout("st", [4, DEPTH, 256, 928])
        self.dscr("XB", [X_ROWS, 2048])
        self.dres["XG"] = Res("XG")
        for pi, (r0, r1) in enumerate(XG_PIECES):
            self.dscr("XG%d" % pi, [2 * (r1 - r0), 2048])
        self.dscr("PB", [X_ROWS, 1024]); self.dscr("QS", [QS_ROWS, NB * 128])
        self.QSr = [Res("QS%d" % i) for i in range(NB)]
        for s_ in range(2):
            self.dscr("W13S%d" % s_, [11, 128, 2, 8, 256]); self.dscr("W2S%d" % s_, [12, 128, 4, 512])
        self.w13s_res = [[Res("w13s%d_%d" % (s_, j)) for j in range(11)] for s_ in range(2)]
        self.w2s_res = [[Res("w2s%d_%d" % (s_, j)) for j in range(12)] for s_ in range(2)]
        self.cast_n = 0
        self.XBr = [Res("XB%d" % i) for i in range(NSB)]
        self.PBr = [Res("PB%d" % i) for i in range(NPB)]

        with ExitStack() as st:
            self.stack = st
            self.xres = self.ptile("xres", [NB, D], F32)
            self.xblk = [Res("xblk%d" % i) for i in range(NB)]
            self.identF = self.ptile("identF", [128], F32)
            self.identB = self.ptile("identB", [128], BF16)
            self.onesF = self.ptile("onesF", [128], F32)
            self.csT = self.ptile("csT", [8, 2], F32)
            self.bmT = self.ptile("bmT_s", [DEPTH, 72], F32)
            self.modT = self.ptile("modT", [24, 2], F32)
            self.gate_bc = self.ptile("gate_bc", [2, D], F32)
            self.lng = self.ptile("lng", [D], F32)
            self.lnb = self.ptile("lnb", [D], F32)
            self.bmask = self.ptile("bmask_s", [4, 128], BF16)
            self.rmask = self.ptile("rmask_s", [6], F32)
            self.small = self.ptile("small", [128], F32)
            self.small_r = [Res("small%d" % i) for i in range(8)]
            self.ABYTES = 88 * 1024
            arena_t = st.enter_context(nc.sbuf_tensor("arena", [128, self.ABYTES // 2], BF16))
            self.arena = arena_t[:]
            self.alive = []
            self.acur = 0
            self.ps = []
            for i in range(8):
                p = st.enter_context(nc.psum_tensor("ps%d" % i, [128, 512], F32))
                self.ps.append(T(p[:], Res("ps%d" % i, excl=True)))

            self.prologue()
            for l in range(DEPTH):
                self.layer(l)
            self.epilogue()
            self.S.finish()
            self.S.emit(nc, st)
        return nc

    def prologue(self):
        d = self.d
        xv = d["xin"].rearrange("(b p) f -> p b f", p=128)
        for i in range(3):
            blks = list(range(i * 8, (i + 1) * 8))
            self.dma("sp", self.xres.ap[:, i * 8:(i + 1) * 8, :], xv[:, i * 8:(i + 1) * 8, :], [],
                     [self.xblk[b] for b in blks], self.xblk[blks[0]])
        self.dma("sp", self.identF.ap, d["ident"], [], [self.identF.res], self.identF.res)
        self.dma("pool", self.identB.ap, d["ident"], [], [self.identB.res], self.identB.res)
        self.dma("pool", self.bmask.ap, d["bmask"], [], [self.bmask.res], self.bmask.res)
        self.dma("sp", self.rmask.ap, d["rmask"], [], [self.rmask.res], self.rmask.res)
        self.dma("sp", self.bmT.ap, d["bmT"], [], [self.bmT.res], self.bmT.res)
        self.dma("sp", self.csT.ap, d["condT"], [], [self.csT.res], self.csT.res)
        self.S.op("dve", lambda e: e.memset(self.onesF.ap, 1.0), writes=[self.onesF.res])
        self.act(self.csT.ap, self.csT.ap, AF.Silu, [], [self.csT.res])

    def epilogue(self):
        yv = self.d["y"].rearrange("(b p) f -> p b f", p=128)
        for i in range(6):
            blks = list(range(i * 4, (i + 1) * 4))
            self.dma("sp", yv[:, i * 4:(i + 1) * 4, :], self.xres.ap[:, i * 4:(i + 1) * 4, :],
                     [self.xblk[b] for b in blks], [], self.xblk[blks[0]])

    def layer(self, l):
        tiles = [2, 3, 4, 5, 0, 1]
        if os.environ.get("KSTAGE", "full") in ("projonly", "mixonly", "mixprompt"):
            if l == 0:
                self.mixer_phase(l)
            return
        self.ffn_phase(l, 0, 0, tiles)
        self.mixer_phase(l)
        self.ffn_phase(l, 1, 2, tiles)

    def mods(self, l, slot, weight):
        d = self.d
        ps = self.ps[6]
        wsrc = d["w_mod"][l].rearrange("(k p) n -> p k n", p=128)
        base = slot * 3 * D
        ring = [self.atile("wm%d" % i, [8, 256], F32) for i in range(2)]
        for jb in range(12):
            w = ring[jb % 2]
            self.dma("sp", w.ap, wsrc[:, :, base + jb * 256: base + (jb + 1) * 256], [], [w.res], w.res)
            for cc in range(2):
                n = jb * 2 + cc
                for k in range(8):
                    self.mm(ps.ap[:, 2 * n:2 * n + 2], w.ap[:, k, cc * 128:(cc + 1) * 128], self.csT.ap[:, k, :],
                            k == 0, k == 7, [w.res, self.csT.res], [ps.res])
        psv = ps.ap[:, 0:48].rearrange("p (n c) -> p n c", c=2)
        for c in range(2):
            self.tt(self.modT.ap[:, :, c], psv[:, :, c], self.bmT.ap[:, l, slot * 24:(slot + 1) * 24], ALU.add,
                    [ps.res, self.bmT.res], [self.modT.res])
        self.ts(self.modT.ap[:, 8:16, :], self.modT.ap[:, 8:16, :], 1.0, None, ALU.add, None, [], [self.modT.res])
        self.ts(self.modT.ap[:, 16:24, :], self.modT.ap[:, 16:24, :], float(weight), None, ALU.mult, None, [],
                [self.modT.res])
        dg = [self.atile("dg%d" % i, [128], F32) for i in range(2)]
        pb = [self.ps[4], self.ps[5]]
        i = 0
        for c in range(2):
            for hf in range(2):
                p = pb[(c * 2 + hf) % 2]
                for kk in range(4):
                    k = hf * 4 + kk
                    g = dg[i % 2]
                    i += 1
                    self.ts(g.ap, self.identF.ap, self.modT.ap[:, 16 + k, c:c + 1], None, ALU.mult, None,
                            [self.identF.res, self.modT.res], [g.res])
                    self.mm(p.ap[:, kk * 128:(kk + 1) * 128], self.onesF.ap, g.ap, True, True,
                            [self.onesF.res, g.res], [p.res])
                self.cp(self.gate_bc.ap[:, c, hf * 512:(hf + 1) * 512], p.ap, [p.res], [self.gate_bc.res], eng="act")

    def load_ln(self, l, idx):
        d = self.d
        self.dma("sp", self.lng.ap, d["ln_g"][l, idx].partition_broadcast(128), [], [self.lng.res], self.lng.res)
        self.dma("sp", self.lnb.ap, d["ln_b"][l, idx].partition_broadcast(128), [], [self.lnb.res], self.lnb.res)

    def pre(self, tile, hT, banks):
        cond = 1 if tile < 2 else 0
        for k in range(8):
            p = banks[k % len(banks)]
            for tb in range(4):
                blk = tile * 4 + tb
                self.tr(p.ap[:, tb * 128:(tb + 1) * 128], self.xres.ap[:, blk, k * 128:(k + 1) * 128], self.identF.ap,
                        [self.xblk[blk], self.identF.res], [p.res])
            self.act(hT.ap[:, k, :], p.ap, AF.Identity, [p.res, self.modT.res], [hT.res],
                     bias=self.modT.ap[:, k, cond:cond + 1], scale=self.modT.ap[:, 8 + k, cond:cond + 1])

    def post_stages(self, blk, cond, ybuf, slot_i=0):
        x = self.xres.ap[:, blk, :]
        xr = self.xblk[blk]
        c0 = 32 + slot_i * 16
        sm = self.small.ap[:, c0:c0 + 16]
        sr = self.small_r[4 + slot_i]

        def stage_a():
            self.stt(ybuf.ap, x, DN_ALPHA, ybuf.ap, ALU.mult, ALU.add, [xr], [ybuf.res])
            st6 = sm[:, 0:12].rearrange("p (a b) -> p a b", a=2)
            for hf in range(2):
                self.S.op("dve", lambda e, hf=hf: e.bn_stats(st6[:, hf, :], ybuf.ap[:, hf * 512:(hf + 1) * 512]),
                          reads=[ybuf.res], writes=[sr])
            self.S.op("dve", lambda e: e.bn_aggr(sm[:, 12:14], st6), reads=[], writes=[sr])
            self.ts(sm[:, 14:15], sm[:, 13:14], LN_EPS, None, ALU.add, None, [], [sr])
            self.act(sm[:, 14:15], sm[:, 14:15], AF.Sqrt, [], [sr])

        def stage_b():
            self.S.op("dve", lambda e: e.reciprocal(sm[:, 14:15], sm[:, 14:15]), reads=[], writes=[sr])
            self.stt(sm[:, 15:16], sm[:, 12:13], -1.0, sm[:, 14:15], ALU.mult, ALU.mult, [], [sr])
            self.act(ybuf.ap, ybuf.ap, AF.Identity, [sr], [ybuf.res], bias=sm[:, 15:16], scale=sm[:, 14:15])

        def stage_c():
            self.tt(ybuf.ap, ybuf.ap, self.lng.ap, ALU.mult, [self.lng.res], [ybuf.res])
            self.tt(x, ybuf.ap, self.lnb.ap, ALU.add, [ybuf.res, self.lnb.res], [xr])
        return [stage_a, stage_b, stage_c]

    def post(self, blk, cond, ybuf, slot_i=0):
        for f_ in self.post_stages(blk, cond, ybuf, slot_i):
            f_()

    def cast_jobs(self, p):
        d = self.d
        l, half = p // 2, p % 2
        st_ = p % 2
        w1s = d["ffn_w1"][l, half].rearrange("(k p) n -> p k n", p=128)
        w3s = d["ffn_w3"][l, half].rearrange("(k p) n -> p k n", p=128)
        w2s = d["ffn_w2"][l, half].rearrange("(j p) n -> p j n", p=128)
        jobs = []
        for jb in range(11):
            def j13(jb=jb):
                dst = d["W13S%d" % st_][jb]
                self.dmas("pool", [(dst[:, 0], w1s[:, :, jb * 256:(jb + 1) * 256]), (dst[:, 1], w3s[:, :, jb * 256:(jb + 1) * 256])],
                          [], [self.w13s_res[st_][jb]], Res("cast%d" % (self.cast_n % 4)))
                self.cast_n += 1
            jobs.append(j13)
        for r in range(12):
            def j2(r=r):
                hf, jg = r // 6, r % 6
                nj = 4 if jg < 5 else 2
                dst = d["W2S%d" % st_][r]
                self.dmas("pool", [(dst[:, 0:nj, :], w2s[:, jg * 4:jg * 4 + nj, hf * 512:(hf + 1) * 512])],
                          [], [self.w2s_res[st_][r]], Res("cast%d" % (self.cast_n % 4)))
                self.cast_n += 1
            jobs.append(j2)
        return jobs

    def ffn_phase(self, l, half, slot, tiles):
        d = self.d
        p_ = l * 2 + half
        st_ = p_ % 2
        if p_ == 0:
            for j in self.cast_jobs(0):
                j()
        nxt = self.cast_jobs(p_ + 1) if p_ + 1 < 2 * DEPTH else []
        self.arena_reset()
        self.mods(l, slot, 0.5)
        self.load_ln(l, 0 if slot == 0 else 2)
        self.arena_reset()
        hT = self.atile("hT", [8, 512], BF16)
        aT = self.atile("aT", [NJ, 512], BF16)
        su = [self.atile("su%d" % i, [512], F32) for i in range(2)]
        R13, R2 = 3, 3
        w13 = [self.atile("w13_%d" % i, [2, 8, 256], BF16) for i in range(R13)]
        w2r = [self.atile("w2_%d" % i, [4, 512], BF16) for i in range(R2)]
        yb = [self.atile("yb%d" % i, [D], F32) for i in range(4)]
        state = {"i13": 0, "i2": 0}
        total13 = len(tiles) * 11
        total2 = len(tiles) * 12

        def issue13(upto):
            while state["i13"] < min(upto, total13):
                n = state["i13"]
                jb = n % 11
                w = w13[n % R13]
                self.dma("sp", w.ap, d["W13S%d" % st_][jb], [self.w13s_res[st_][jb]], [w.res], w.res)
                state["i13"] += 1
                if nxt:
                    nxt.pop(0)()

        def issue2(upto):
            while state["i2"] < min(upto, total2):
                n = state["i2"]
                r = n % 12
                w = w2r[n % R2]
                self.dma("sp", w.ap, d["W2S%d" % st_][r], [self.w2s_res[st_][r]], [w.res], w.res)
                state["i2"] += 1

        issue13(R13)
        pending = []
        self.pre(tiles[0], hT, [self.ps[4], self.ps[5]])
        for ti, tile in enumerate(tiles):
            cond = 1 if tile < 2 else 0
            for jb in range(11):
                n = ti * 11 + jb
                issue13(n + R13)
                w = w13[n % R13]
                for cc in range(2):
                    j = jb * 2 + cc
                    pu = self.ps[j % 2]
                    pg = self.ps[2 + j % 2]
                    for k in range(8):
                        self.mm(pu.ap, w.ap[:, 0, k, cc * 128:(cc + 1) * 128], hT.ap[:, k, :], k == 0, k == 7,
                                [w.res, hT.res], [pu.res])
                    for k in range(8):
                        self.mm(pg.ap, w.ap[:, 1, k, cc * 128:(cc + 1) * 128], hT.ap[:, k, :], k == 0, k == 7,
                                [w.res, hT.res], [pg.res])
                    s_ = su[j % 2]
                    self.act(s_.ap, pu.ap, AF.Silu, [pu.res], [s_.res])
                    self.tt(aT.ap[:, j, :], s_.ap, pg.ap, ALU.mult, [s_.res, pg.res], [aT.res])
                    if pending and j >= 2:
                        pending.pop(0)()
                if jb == 8:
                    issue2(ti * 12 + R2)
            if ti + 1 < len(tiles):
                self.pre(tiles[ti + 1], hT, [self.ps[4], self.ps[5]])
                issue13((ti + 1) * 11 + R13)
            for hf in range(2):
                banks = [self.ps[4 + tb] for tb in range(4)] if hf == 0 else [self.ps[tb] for tb in range(4)]
                for jg in range(6):
                    n = ti * 12 + hf * 6 + jg
                    issue2(n + R2)
                    w = w2r[n % R2]
                    nj = 4 if jg < 5 else 2
                    for jj in range(nj):
                        j = jg * 4 + jj
                        for tb in range(4):
                            self.mm(banks[tb].ap, aT.ap[:, j, tb * 128:(tb + 1) * 128], w.ap[:, jj, :], j == 0, j == NJ - 1,
                                    [aT.res, w.res], [banks[tb].res])
                for tb in range(4):
                    self.tt(yb[tb].ap[:, hf * 512:(hf + 1) * 512], banks[tb].ap,
                            self.gate_bc.ap[:, cond, hf * 512:(hf + 1) * 512], ALU.mult,
                            [banks[tb].res, self.gate_bc.res], [yb[tb].res])
            stg = [self.post_stages(tile * 4 + tb, cond, yb[tb], tb) for tb in range(4)]
            pending.extend([stg[tb][k_] for k_ in range(3) for tb in range(4)])
        while pending:
            pending.pop(0)()

    def mixer_phase(self, l):
        self.arena_reset()
        self.mods(l, 1, 1.0)
        self.load_ln(l, 1)
        self.arena_reset()
        stage = os.environ.get("KSTAGE", "full")
        self.project(l)
        if stage in ("proj", "projonly"):
            return
        self.gather()
        if stage == "gather":
            return
        self.arena_reset()
        self.layer_consts(l)
        mark = self.acur
        o_p = self.atile("o_p", [NPB, D], BF16)
        m2 = self.acur
        for s_ in range(4):
            self.acur = m2
            segs = [("PB", s_ * 256, 256)]
            self.attn_A(l, segs, False, [(s_ * 256, 256)], o_p, s_ * 2, 0)
            self.acur = m2
            self.attn_C(l, segs, False, [(s_ * 256, 256)], o_p, s_ * 2, 0)
            self.acur = m2
            self.attn_B(l, False, s_, o_p, s_ * 2)
        self.acur = m2
        self.wo_phase(l, [0, 1], o_p, 0)
        if stage in ("prompt", "mixprompt"):
            return
        self.acur = mark
        o_s = self.atile("o_s", [NSB, D], BF16)
        m2 = self.acur
        segs = [("XG", i * 1024, 1024) for i in range(4)]
        qt = [(1024 + i * 512, 512) for i in range(4)]
        self.attn_A(l, segs, True, qt, o_s, 0, 1024)
        self.acur = m2
        self.attn_C(l, segs, True, qt, o_s, 0, 1024)
        self.acur = m2
        self.attn_B(l, True, 0, o_s, 0)
        self.acur = m2
        self.wo_phase(l, [2, 3, 4, 5], o_s, 8)

    def xrows(self, name, tok0, ntok, r0, nr):
        if name == "XG":
            rk = tok0 // 2048
            t = tok0 % 2048
            for pi, (p0, p1) in enumerate(XG_PIECES):
                if p0 <= r0 and r0 + nr <= p1:
                    a = self.d["XG%d" % pi]
                    n = p1 - p0
                    return a[rk * n + r0 - p0: rk * n + r0 - p0 + nr, t:t + ntok]
            raise AssertionError((r0, nr))
        a = self.d[name]
        return a[r0:r0 + nr, tok0:tok0 + ntok]

    def xv(self, name, tok0, ntok):
        if name == "XG":
            rk = tok0 // 2048
            t = tok0 % 2048
            assert (t // 1024) == ((t + ntok - 1) // 1024)
            if t < 1024:
                a = self.d["XG2"]
                v = a[rk * 227 + 32: rk * 227 + 227, :]
            else:
                a = self.d["XG3"]
                v = a[rk * 195: rk * 195 + 195, :]
                t -= 1024
            v = v.rearrange("r t -> (r t)").rearrange("(t c) -> t c", c=XV_W)
            return v[t:t + ntok, :]
        a = self.d[name]
        v = a[XK_ROWS:X_ROWS, :]
        v = v.rearrange("r t -> (r t)").rearrange("(t c) -> t c", c=XV_W)
        return v[tok0:tok0 + ntok, :]

    def xres_of(self, name, tok0, ntok):
        if name == "XG":
            return [self.dres["XG"]]
        lst = self.XBr if name == "XB" else self.PBr
        return [lst[b] for b in range(tok0 // 128, (tok0 + ntok) // 128)]

    def rope(self, dst, src, H, Q, tbl, sb, R, W, tmp):
        dim = 4 * Q
        xs = src.rearrange("p (h a w q) -> p h a w q", h=H, a=2, w=2)
        xd = dst.rearrange("p (h a w q) -> p h a w q", h=H, a=2, w=2)
        xt = tmp[:, 0:H * dim].rearrange("p (h a w q) -> p h a w q", h=H, a=2, w=2)
        cs = tbl.ap[:, 0, sb, :].rearrange("p (a w q) -> p a w q", a=2, w=2)
        ss = tbl.ap[:, 1, sb, :].rearrange("p (a w q) -> p a w q", a=2, w=2)
        for a in range(2):
            cb = cs[:, a].unsqueeze(1).to_broadcast([128, H, 2, Q])
            self.tt(xd[:, :, a], xs[:, :, a], cb, ALU.mult, R + [tbl.res], W)
            for w in range(2):
                sbb = ss[:, a, w].unsqueeze(1).to_broadcast([128, H, Q])
                self.tt(xt[:, :, a, w], xs[:, :, a, 1 - w], sbb, ALU.mult, R + [tbl.res], [self.tmp_res])
            self.tt(xd[:, :, a], xd[:, :, a], xt[:, :, a], ALU.add, [self.tmp_res], W)

    def rms(self, dst, src, n, gam, R, W, sidx):
        sm = self.small.ap
        sr = self.small_r[sidx]
        c0 = 16 + sidx * 4
        self.tt(self.junk.ap[:, 0:n], src, src, ALU.mult, R, [self.junk.res])
        self.S.op("dve", lambda e: e.reduce_sum(sm[:, c0:c0 + 1], self.junk.ap[:, 0:n], axis=AX.X),
                  reads=[self.junk.res], writes=[sr])
        self.ts(sm[:, c0:c0 + 1], sm[:, c0:c0 + 1], 1.0 / n, RMS_EPS, ALU.mult, ALU.add, [], [sr])
        self.act(sm[:, c0:c0 + 1], sm[:, c0:c0 + 1], AF.Sqrt, [], [sr])
        self.S.op("dve", lambda e: e.reciprocal(sm[:, c0:c0 + 1], sm[:, c0:c0 + 1]), reads=[], writes=[sr])
        self.stt(dst, src, sm[:, c0:c0 + 1], gam.ap, ALU.mult, ALU.mult, R + [sr, gam.res], W)

    def project(self, l):
        d = self.d
        hT = self.atile("hTm", [8, 512], BF16)
        w_in = self.atile("w_in", [8, INW], BF16)
        wq = self.atile("wq", [2, 384], BF16)
        wkv = self.atile("wkv", [512], BF16)
        gq = self.atile("gq", [256], F32)
        gkv = self.atile("gkv", [128], F32)
        r32 = self.atile("r32", [2, NSB, 32], F32)
        r64 = self.atile("r64", [2, NSB, 64], F32)
        tp = self.atile("tp", [INW], F32)
        rq = self.atile("rq", [1184], F32)
        tmp = self.atile("tmp", [640], F32)
        self.tmp_res = tmp.res
        self.junk = self.atile("junk", [256], F32)
        nq = self.atile("nq", [256], BF16)
        nqT = self.atile("nqT", [2, 128], BF16)
        cq = self.atile("cq", [384], F32)
        ckv = self.atile("ckv", [128], F32)
        kpe96 = self.atile("kpe96", [96], F32)
        qA = self.atile("qA", [8, 128], BF16)
        qB = self.atile("qB", [8, 128], BF16)
        qC = self.atile("qC", [4, 128], BF16)
        kst = self.atile("kst", [5, 128], BF16)
        vst = self.atile("vst", [6, 65], BF16)
        wsrc = d["w_in"][l].rearrange("(k p) n -> p k n", p=128)
        self.dmas("pool", [(w_in.ap[:, :, c0:c1], wsrc[:, :, c0:c1]) for (c0, c1) in ((0, 512), (512, 1024), (1024, 1536), (1536, INW))],
                  [], [w_in.res], w_in.res)
        self.dma("pool", wq.ap, d["c_w_q_up"][l].rearrange("(k p) n -> p k n", p=128), [], [wq.res], wq.res)
        self.dma("pool", wkv.ap, d["c_w_kv_up"][l], [], [wkv.res], wkv.res)
        self.dma("sp", gq.ap, d["c_q_norm_g"][l].partition_broadcast(128), [], [gq.res], gq.res)
        self.dma("sp", gkv.ap, d["c_kv_norm_g"][l].partition_broadcast(128), [], [gkv.res], gkv.res)
        self.dma("sp", r32.ap, d["rope32"], [], [r32.res], r32.res)
        self.dma("sp", r64.ap, d["rope64"], [], [r64.res], r64.res)
        self.S.op("dve", lambda e: e.memset(kpe96.ap, 0.0), writes=[kpe96.res])
        self.S.op("dve", lambda e: e.memset(vst.ap, 1.0), writes=[vst.res])
        self.S.op("dve", lambda e: e.memset(qC.ap, 0.0), writes=[qC.res])
        self.S.op("dve", lambda e: e.memset(kst.ap, 0.0), writes=[kst.res])
        self.wkv_t = None
        groups = ((0, 512), (512, 1024), (1024, 1536), (1536, INW))
        rm = self.rmask
        lvl = int(os.environ.get("KLVL", "9"))
        skip = set(os.environ.get("KSKIP", "").split(","))
        stv = d["st"]
        for tile in [2, 3, 4, 5, 0, 1]:
            samp = tile >= 2
            self.pre(tile, hT, [self.ps[4], self.ps[5]])
            for tb in range(4):
                blk = tile * 4 + tb
                sb = blk - NPB
                for gi, (c0, c1) in enumerate(groups):
                    p = self.ps[gi]
                    for k in range(8):
                        self.mm(p.ap[:, 0:c1 - c0], hT.ap[:, k, tb * 128:(tb + 1) * 128], w_in.ap[:, k, c0:c1], k == 0, k == 7,
                                [hT.res, w_in.res], [p.res])
                    self.cp(tp.ap[:, c0:c1], p.ap[:, 0:c1 - c0], [p.res], [tp.res], eng="act")
                if lvl < 2:
                    continue
                if samp:
                    self.rope(rq.ap[:, 0:512], tp.ap[:, 0:512], 16, 8, r32, sb, [tp.res], [rq.res], tmp.ap)
                    self.rope(rq.ap[:, 512:1152], tp.ap[:, 768:1408], 10, 16, r64, sb, [tp.res], [rq.res], tmp.ap)
                    self.rope(rq.ap[:, 1152:1184], tp.ap[:, 1920:1952], 1, 8, r32, sb, [tp.res], [rq.res], tmp.ap)
                    s_aq, s_ak, s_bq, s_bk, s_kpe = (rq.ap[:, 0:256], rq.ap[:, 256:512], rq.ap[:, 512:1024],
                                                     rq.ap[:, 1024:1152], rq.ap[:, 1152:1184])
                    sres = [rq.res]
                else:
                    s_aq, s_ak, s_bq, s_bk, s_kpe = (tp.ap[:, 0:256], tp.ap[:, 256:512], tp.ap[:, 768:1280],
                                                     tp.ap[:, 1280:1408], tp.ap[:, 1920:1952])
                    sres = [tp.res]
                if lvl < 3:
                    continue
                self.rms(nq.ap, tp.ap[:, 1536:1792], 256, gq, [tp.res], [nq.res], 1)
                self.rms(ckv.ap, tp.ap[:, 1792:1920], 128, gkv, [tp.res], [ckv.res], 2)
                if not samp:
                    seq, t0 = blk // 2, (blk % 2) * 128
                    self.dmas("sp", [(stv[seq, l, t0:t0 + 128, 0:512], tp.ap[:, 256:768]),
                                     (stv[seq, l, t0:t0 + 128, 512:768], tp.ap[:, 1280:1536]),
                                     (stv[seq, l, t0:t0 + 128, 896:928], tp.ap[:, 1920:1952]),
                                     (stv[seq, l, t0:t0 + 128, 768:896], ckv.ap)],
                              [tp.res, ckv.res], [], tp.res)
                if lvl < 4:
                    continue
                p4, p5, p6, p7 = self.ps[4], self.ps[5], self.ps[6], self.ps[7]
                p4b = p4.ap[:, 0:128].bitcast(BF16)
                for kk in range(2):
                    self.tr(p4b[:, kk * 128:(kk + 1) * 128], nq.ap[:, kk * 128:(kk + 1) * 128], self.identB.ap,
                            [nq.res, self.identB.res], [p4.res])
                self.cp(nqT.ap, p4b.rearrange("p (a b) -> p a b", a=2), [p4.res], [nqT.res])
                for kk in range(2):
                    self.mm(p5.ap[:, 0:384], nqT.ap[:, kk, :], wq.ap[:, kk, :], kk == 0, kk == 1, [nqT.res, wq.res], [p5.res])
                self.cp(cq.ap, p5.ap[:, 0:384], [p5.res], [cq.res], eng="act")
                if samp:
                    cq4 = cq.ap.rearrange("p (h e) -> p h e", e=96)
                    p54 = p5.ap[:, 0:384].rearrange("p (h e) -> p h e", e=96)
                    self.rope_strided(cq4[:, :, 64:96], p54[:, :, 64:96], 4, 8, r32, sb, [p5.res], [cq.res], tmp.ap)
                if lvl < 5:
                    continue
                if "A" not in skip:
                    for c in range(2):
                        self.tr(p6.ap[:, c * 128:(c + 1) * 128], s_aq[:, c * 128:(c + 1) * 128], self.identF.ap, sres + [self.identF.res], [p6.res])
                        self.tr(p6.ap[:, 256 + c * 128:256 + (c + 1) * 128], s_ak[:, c * 128:(c + 1) * 128], self.identF.ap, sres + [self.identF.res], [p6.res])
                    for c in range(2):
                        for i in range(4):
                            self.ts(qA.ap[:, c * 4 + i, :], p6.ap[:, c * 128:(c + 1) * 128], rm.ap[:, i:i + 1], None, ALU.mult, None,
                                    [p6.res, rm.res], [qA.res])
                    self.cp(kst.ap[:, 0:2, :], p6.ap[:, 256:512].rearrange("p (a b) -> p a b", a=2), [p6.res], [kst.res], eng="act")
                if "B" not in skip:
                    for c in range(4):
                        self.tr(p7.ap[:, c * 128:(c + 1) * 128], s_bq[:, c * 128:(c + 1) * 128], self.identF.ap, sres + [self.identF.res], [p7.res])
                    for hq in range(8):
                        self.ts(qB.ap[:, hq, :], p7.ap[:, (hq // 2) * 128:(hq // 2 + 1) * 128], rm.ap[:, 4 + hq % 2:5 + hq % 2], None,
                                ALU.mult, None, [p7.res, rm.res], [qB.res])
                if "P4" not in skip:
                    self.cp(kpe96.ap[:, 64:96], s_kpe, sres, [kpe96.res])
                    self.tr(p4.ap[:, 0:128], s_bk, self.identF.ap, sres + [self.identF.res], [p4.res])
                    self.tr(p4.ap[:, 128:256], ckv.ap, self.identF.ap, [ckv.res, self.identF.res], [p4.res])
                    self.tr(p4.ap[0:96, 256:384], kpe96.ap, self.identF.ap, [kpe96.res, self.identF.res], [p4.res])
                    self.cp(kst.ap[:, 2:4, :], p4.ap[:, 0:256].rearrange("p (a b) -> p a b", a=2), [p4.res], [kst.res], eng="act")
                    self.cp(kst.ap[64:96, 4, :], p4.ap[64:96, 256:384], [p4.res], [kst.res])
                if "QC" not in skip:
                    for h in range(4):
                        self.tr(p5.ap[0:96, h * 128:(h + 1) * 128], cq.ap[:, h * 96:(h + 1) * 96], self.identF.ap, [cq.res, self.identF.res], [p5.res])
                    self.cp(qC.ap[0:96, :, :], p5.ap[0:96, :].rearrange("p (a b) -> p a b", a=4), [p5.res], [qC.res])
                if "V" not in skip:
                    self.cp(vst.ap[:, 0:4, 0:64], tp.ap[:, 512:768].rearrange("p (h e) -> p h e", e=64), [tp.res], [vst.res])
                    self.cp(vst.ap[:, 4:6, 0:64], tp.ap[:, 1408:1536].rearrange("p (h e) -> p h e", e=64), [tp.res], [vst.res])
                if lvl < 6:
                    continue
                tok = blk * 128
                QS = d["QS"]
                name = "XB" if samp else "PB"
                xt0 = (blk - NPB) * 128 if samp else blk * 128
                pairs = [
                    (QS[0:1024, tok:tok + 128].rearrange("(i p) t -> p i t", p=128), qA.ap),
                    (QS[1024:2048, tok:tok + 128].rearrange("(i p) t -> p i t", p=128), qB.ap),
                    (QS[2048:2560, tok:tok + 128].rearrange("(i p) t -> p i t", p=128), qC.ap),
                    (self.xrows(name, xt0, 128, 0, 256).rearrange("(c p) t -> p c t", p=128), kst.ap[:, 0:2, :]),
                    (self.xrows(name, xt0, 128, 256, 256).rearrange("(c p) t -> p c t", p=128), kst.ap[:, 2:4, :]),
                    (self.xrows(name, xt0, 128, 512, 32), kst.ap[64:96, 4, :]),
                    (self.xv(name, xt0, 128), vst.ap.rearrange("p h e -> p (h e)")),
                ]
                wr = [self.QSr[blk], (self.XBr[blk - NPB] if samp else self.PBr[blk])]
                self.dmas("sp", pairs, [qA.res, qB.res, qC.res, kst.res, vst.res], wr, qA.res)

    def rope_strided(self, dst, src, H, Q, tbl, sb, R, W, tmp):
        dim = 4 * Q
        xs = src.rearrange("p h (a w q) -> p h a w q", a=2, w=2)
        xd = dst.rearrange("p h (a w q) -> p h a w q", a=2, w=2)
        xt = tmp[:, 0:H * dim].rearrange("p (h a w q) -> p h a w q", h=H, a=2, w=2)
        cs = tbl.ap[:, 0, sb, :].rearrange("p (a w q) -> p a w q", a=2, w=2)
        ss = tbl.ap[:, 1, sb, :].rearrange("p (a w q) -> p a w q", a=2, w=2)
        for a in range(2):
            cb = cs[:, a].unsqueeze(1).to_broadcast([128, H, 2, Q])
            self.tt(xd[:, :, a], xs[:, :, a], cb, ALU.mult, R + [tbl.res], W)
            for w in range(2):
                sbb = ss[:, a, w].unsqueeze(1).to_broadcast([128, H, Q])
                self.tt(xt[:, :, a, w], xs[:, :, a, 1 - w], sbb, ALU.mult, R + [tbl.res], [self.tmp_res])
            self.tt(xd[:, :, a], xd[:, :, a], xt[:, :, a], ALU.add, [self.tmp_res], W)

    def gather(self):
        d = self.d
        for pi, (r0, r1) in enumerate(XG_PIECES):
            src = d["XB"][r0:r1, :]
            dst = d["XG%d" % pi]
            self.S.op("pool", lambda e, src=src, dst=dst: e.collective_compute(
                "AllGather", ALU.bypass, replica_groups=[[0, 1], [2, 3], [4, 5], [6, 7]], ins=[src], outs=[dst]),
                reads=list(self.XBr), writes=[self.dres["XG"]], dma=self.dres["XG"], inc=1)

    def layer_consts(self, l):
        d = self.d
        lam_init = 0.8 - 0.6 * float(np.exp(-0.3 * l))
        al = self.atile("al", [128], F32)
        self.gsub = self.atile("gsub", [64], F32)
        self.es = self.atile("es", [8], F32)
        self.lam = self.atile("lamc", [4], F32)
        self.dma("sp", al.ap, d["a_lambda"][l].partition_broadcast(128), [], [al.res], al.res)
        self.dma("sp", self.gsub.ap, d["a_subln_g"][l].partition_broadcast(128), [], [self.gsub.res], self.gsub.res)
        self.dma("sp", self.es.ap, d["b_sink"][l].partition_broadcast(128), [], [self.es.res], self.es.res)
        self.act(self.es.ap, self.es.ap, AF.Exp, [], [self.es.res])
        self.ts(self.gsub.ap, self.gsub.ap, 1.0 - lam_init, None, ALU.mult, None, [], [self.gsub.res])
        lm = self.lam
        for i in range(2):
            self.tt(al.ap[:, i * 64:i * 64 + 32], al.ap[:, i * 64:i * 64 + 32], al.ap[:, i * 64 + 32:i * 64 + 64], ALU.mult, [], [al.res])
            self.S.op("dve", lambda e, i=i: e.reduce_sum(lm.ap[:, i:i + 1], al.ap[:, i * 64:i * 64 + 32], axis=AX.X),
                      reads=[al.res], writes=[lm.res])
        self.act(lm.ap[:, 0:2], lm.ap[:, 0:2], AF.Exp, [], [lm.res])
        self.tt(lm.ap[:, 2:3], lm.ap[:, 0:1], lm.ap[:, 1:2], ALU.subtract, [], [lm.res])
        self.ts(lm.ap[:, 3:4], lm.ap[:, 2:3], lam_init, -1.0, ALU.add, ALU.mult, [], [lm.res])
        self.pt = [self.atile("pt%d" % i, [512], BF16) for i in range(4)]
        self.pti = 0
        self.sci = 0
        self.rec = self.atile("rec", [8], F32)
        self.bmask4 = self.atile("bmask4", [4, 4, 128], BF16)
        for m_ in range(4):
            for g_ in range(4):
                self.cp(self.bmask4.ap[:, m_, g_, :], self.bmask.ap[:, m_, :], [self.bmask.res], [self.bmask4.res])
        self.ctmp = self.atile("ctmp", [2, 256], F32)

    def attend(self, q_ap, q_res, kblocks, nq, scale, obank):
        nqs = nq // 128
        nkb = len(kblocks)
        LA = 2
        pts = [None] * nkb
        for i in range(nkb + LA):
            if i < nkb:
                kT, kres, V, vres, mask = kblocks[i]
                sc = self.ps[self.sci % 4]
                self.sci += 1
                self.mm(sc.ap[:, 0:nq], kT, q_ap, True, mask is None, kres + [q_res], [sc.res])
                if mask is not None:
                    self.mm(sc.ap[:, 0:nq], self.identB.ap, mask, False, True, [self.identB.res, self.bmask4.res], [sc.res])
                pt = self.pt[self.pti % 4]
                self.pti += 1
                self.act(pt.ap[:, 0:nq], sc.ap[:, 0:nq], AF.Exp, [sc.res], [pt.res], scale=float(scale))
                pts[i] = pt
            j = i - LA
            if j >= 0:
                kT, kres, V, vres, mask = kblocks[j]
                pt = pts[j]
                for qs in range(nqs):
                    Vq = V[qs] if isinstance(V, (list, tuple)) else V
                    self.mm(obank.ap[:, qs * 65:(qs + 1) * 65], pt.ap[:, qs * 128:(qs + 1) * 128], Vq, (j == 0 and qs == 0),
                            j == nkb - 1, [pt.res] + vres, [obank.res], skip=True)

    def load_ctxT(self, src_dram, width, dst_fn, dres):
        ct = self.ctmp
        self.dma("sp", ct.ap[:, :, 0:width], src_dram.rearrange("(b p) f -> p b f", p=128), [], [ct.res], ct.res)
        p = self.ps[2]
        nch = width // 128
        for kb in range(2):
            for c in range(nch):
                self.tr(p.ap[:, (kb * nch + c) * 128:(kb * nch + c + 1) * 128], ct.ap[:, kb, c * 128:(c + 1) * 128], self.identF.ap,
                        [ct.res, self.identF.res], [p.res])
        for kb in range(2):
            for c in range(nch):
                self.cp(dst_fn(c, kb), p.ap[:, (kb * nch + c) * 128:(kb * nch + c + 1) * 128], [p.res], [dres])

    def attn_A(self, l, segs, ctx, qtiles, o_t, oblk0, qtok_unused):
        d = self.d
        nk = (256 if ctx else 0) + sum(s[2] for s in segs)
        nkb = nk // 128
        kT = self.atile("kT_A", [2, nk], BF16)
        V = self.atile("V_A", [nkb, 4, 65], BF16)
        qm = [self.atile("qmA%d" % i, [512], BF16) for i in range(2)]
        oa = self.atile("oa", [4, 4, 64], F32)
        o1 = self.atile("o1", [4, 64], F32)
        o2 = self.atile("o2", [64], F32)
        ssq = self.atile("ssq", [16], F32)
        sq = self.atile("sqA", [4, 64], F32)
        koff = 0
        if ctx:
            self.load_ctxT(d["ck_a"][l], 256, lambda c, kb: kT.ap[:, c, kb * 128:(kb + 1) * 128], kT.res)
            self.dmas("pool", [(V.ap[:, b_, :, 0:64], d["cv_a"][l][b_ * 128:(b_ + 1) * 128, :].rearrange("p (h e) -> p h e", e=64))
                               for b_ in range(2)], [], [V.res], V.res)
            self.S.op("dve", lambda e: e.memset(V.ap[:, 0:2, :, 64:65], 1.0), writes=[V.res])
            koff = 256
        pairs = []
        rd = []
        for (name, t0, nt) in segs:
            pairs.append((kT.ap[:, :, koff:koff + nt], self.xrows(name, t0, nt, 0, 256).rearrange("(c p) t -> p c t", p=128)))
            pairs.append((V.ap[:, koff // 128:(koff + nt) // 128, :, :],
                          self.xv(name, t0, nt)[:, 0:260].rearrange("(b p) (h e) -> p b h e", p=128, e=65)))
            rd += self.xres_of(name, t0, nt)
            koff += nt
        self.dmas("sp", pairs, rd, [kT.res, V.res], kT.res)
        QS = d["QS"]
        scale = 32 ** -0.5
        qi = 0
        for ti, (qt0, nq) in enumerate(qtiles):
            nqs = nq // 128
            qres = [self.QSr[b] for b in range(qt0 // 128, (qt0 + nq) // 128)]
            for i in range(8):
                c, h, j = i // 4, (i // 4) * 2 + (i % 4) // 2, i % 2
                q = qm[qi % 2]
                qi += 1
                self.dma("sp", q.ap[:, 0:nq], QS[i * 128:(i + 1) * 128, qt0:qt0 + nq], qres, [q.res], q.res)
                ob = self.ps[4 + (i % 2)]
                kbl = [(kT.ap[:, c, kb * 128:(kb + 1) * 128], [kT.res], V.ap[:, kb, h, :], [V.res], None) for kb in range(nkb)]
                self.attend(q.ap[:, 0:nq], q.res, kbl, nq, scale, ob)
                for qs in range(nqs):
                    self.S.op("dve", lambda e, qs=qs, ob=ob: e.reciprocal(self.rec.ap[:, qs:qs + 1], ob.ap[:, qs * 65 + 64:qs * 65 + 65]),
                              reads=[ob.res], writes=[self.rec.res])
                    if j == 0:
                        self.ts(o1.ap[:, qs, :], ob.ap[:, qs * 65:qs * 65 + 64], self.rec.ap[:, qs:qs + 1], None, ALU.mult, None,
                                [ob.res, self.rec.res], [o1.res])
                    else:
                        self.ts(o2.ap, ob.ap[:, qs * 65:qs * 65 + 64], self.rec.ap[:, qs:qs + 1], None, ALU.mult, None,
                                [ob.res, self.rec.res], [o2.res])
                        self.stt(oa.ap[:, qs, h, :], o2.ap, self.lam.ap[:, 3:4], o1.ap[:, qs, :], ALU.mult, ALU.add,
                                 [o2.res, o1.res, self.lam.res], [oa.res])
            n16 = nqs * 4
            for qs in range(nqs):
                self.tt(sq.ap, oa.ap[:, qs], oa.ap[:, qs], ALU.mult, [oa.res], [sq.res])
                self.S.op("dve", lambda e, qs=qs: e.reduce_sum(ssq.ap[:, qs * 4:qs * 4 + 4], sq.ap, axis=AX.X),
                          reads=[sq.res], writes=[ssq.res])
            self.ts(ssq.ap[:, 0:n16], ssq.ap[:, 0:n16], 1.0 / 64, RMS_EPS, ALU.mult, ALU.add, [], [ssq.res])
            self.act(ssq.ap[:, 0:n16], ssq.ap[:, 0:n16], AF.Sqrt, [], [ssq.res])
            self.S.op("dve", lambda e, n16=n16: e.reciprocal(ssq.ap[:, 0:n16], ssq.ap[:, 0:n16]), reads=[], writes=[ssq.res])
            for qs in range(nqs):
                for h in range(4):
                    ob_ = oblk0 + ti * nqs + qs
                    self.stt(o_t.ap[:, ob_, h * 64:(h + 1) * 64], oa.ap[:, qs, h, :], ssq.ap[:, qs * 4 + h:qs * 4 + h + 1], self.gsub.ap,
                             ALU.mult, ALU.mult, [oa.res, ssq.res, self.gsub.res], [o_t.res])

    def attn_C(self, l, segs, ctx, qtiles, o_t, oblk0, qtok_unused):
        d = self.d
        nk = (256 if ctx else 0) + sum(s[2] for s in segs)
        nkb = nk // 128
        wkv = self.atile("wkvC", [512], BF16)
        self.dma("pool", wkv.ap, d["c_w_kv_up"][l], [], [wkv.res], wkv.res)
        wkv4 = wkv.ap.rearrange("p (h e) -> p h e", e=128)
        ckT = [self.atile("ckT%d" % i, [512], BF16) for i in range(2)]
        cpe = self.atile("cpe", [2, 96], F32)
        qm = [self.atile("qmC%d" % i, [512], BF16) for i in range(2)]
        kT = self.atile("kT_C", [2, nk], BF16)
        V = self.atile("V_C", [nkb, 2, 65], BF16)
        QS = d["QS"]
        scale = 96 ** -0.5
        qi = 0
        ci = 0
        for hp in range(2):
            self.S.op("dve", lambda e: e.memset(V.ap[:, :, :, 64:65], 1.0), writes=[V.res])
            ktiles = []
            koff = 0
            if ctx:
                ktiles.append((None, 0, 256, 0))
                koff = 256
            for (name, t0, nt) in segs:
                for s0 in range(0, nt, 512):
                    n = min(512, nt - s0)
                    ktiles.append((name, t0 + s0, n, koff))
                    koff += n
            if ctx:
                self.S.op("dve", lambda e: e.memset(cpe.ap, 0.0), writes=[cpe.res])
                self.dma("sp", cpe.ap[:, :, 64:96], d["c_kpe"][l].rearrange("(b p) f -> p b f", p=128), [], [cpe.res], cpe.res)
                p = self.ps[2]
                for kb in range(2):
                    self.tr(p.ap[0:96, kb * 128:(kb + 1) * 128], cpe.ap[:, kb, :], self.identF.ap, [cpe.res, self.identF.res], [p.res])
                for hh in range(2):
                    self.cp(kT.ap[64:96, hh, 0:256], p.ap[64:96, 0:256], [p.res], [kT.res])
            pairs = []
            rd = []
            ko2 = 256 if ctx else 0
            for (name, t0, nt) in segs:
                for hh in range(2):
                    pairs.append((kT.ap[64:96, hh, ko2:ko2 + nt], self.xrows(name, t0, nt, 512, 32)))
                rd += self.xres_of(name, t0, nt)
                ko2 += nt
            self.dmas("sp", pairs, rd, [kT.res], kT.res)
            for (name, t0, n, ko) in ktiles:
                ck = ckT[ci % 2]
                ci += 1
                if name is None:
                    self.load_ctxT(d["c_ckv"][l], 128, lambda c, kb: ck.ap[:, kb * 128:(kb + 1) * 128], ck.res)
                else:
                    self.dma("sp", ck.ap[:, 0:n], self.xrows(name, t0, n, 384, 128), self.xres_of(name, t0, n), [ck.res], ck.res)
                for hh in range(2):
                    h = hp * 2 + hh
                    p = self.ps[2 + hh]
                    self.mm(p.ap[:, 0:n], wkv.ap[:, h * 128:(h + 1) * 128], ck.ap[:, 0:n], True, True, [wkv.res, ck.res], [p.res])
                    self.cp(kT.ap[0:64, hh, ko:ko + n], p.ap[0:64, 0:n], [p.res], [kT.res], eng=("act" if hh else "dve"))
                p = self.ps[6]
                for b in range(n // 128):
                    self.mm(p.ap[:, b * 128:(b + 1) * 128], ck.ap[:, b * 128:(b + 1) * 128], wkv4[:, hp * 2:hp * 2 + 2, 64:128], True, True,
                            [ck.res, wkv.res], [p.res])
                self.cp(V.ap[:, ko // 128:(ko + n) // 128, :, 0:64],
                        p.ap[:, 0:n].rearrange("p (b h e) -> p b h e", h=2, e=64), [p.res], [V.res])
            for ti, (qt0, nq) in enumerate(qtiles):
                nqs = nq // 128
                qres = [self.QSr[b] for b in range(qt0 // 128, (qt0 + nq) // 128)]
                for hh in range(2):
                    h = hp * 2 + hh
                    q = qm[qi % 2]
                    qi += 1
                    self.dma("sp", q.ap[0:96, 0:nq], QS[2048 + h * 128:2048 + h * 128 + 96, qt0:qt0 + nq], qres, [q.res], q.res)
                    ob = self.ps[4 + (qi % 2)]
                    kbl = [(kT.ap[0:96, hh, kb * 128:(kb + 1) * 128], [kT.res], V.ap[:, kb, hh, :], [V.res], None) for kb in range(nkb)]
                    self.attend(q.ap[0:96, 0:nq], q.res, kbl, nq, scale, ob)
                    for qs in range(nqs):
                        ob_ = oblk0 + ti * nqs + qs
                        self.S.op("dve", lambda e, qs=qs, ob=ob: e.reciprocal(self.rec.ap[:, qs:qs + 1], ob.ap[:, qs * 65 + 64:qs * 65 + 65]),
                                  reads=[ob.res], writes=[self.rec.res])
                        self.ts(o_t.ap[:, ob_, 768 + h * 64:768 + (h + 1) * 64], ob.ap[:, qs * 65:qs * 65 + 64], self.rec.ap[:, qs:qs + 1], None,
                                ALU.mult, None, [ob.res, self.rec.res], [o_t.res])

    def attn_B(self, l, samp, seq, o_t, oblk0):
        d = self.d
        QS = d["QS"]
        scale = 64 ** -0.5
        if samp:
            nkb = 20
        else:
            nkb = 2
        nk = nkb * 128
        kT = self.atile("kT_B", [2, nk], BF16)
        V = self.atile("V_B", [nkb, 2, 65], BF16)
        qm = [self.atile("qmB%d" % i, [8, 512], BF16) for i in range(2)]
        dup = self.atile("dupB", [2, 64], F32)
        pairs = []
        rd = []
        if samp:
            ct = self.ctmp
            self.dma("sp", ct.ap[:, :, 0:128], d["ck_b"][l].rearrange("(b p) f -> p b f", p=128), [], [ct.res], ct.res)
            p = self.ps[2]
            for kvh in range(2):
                for kb in range(2):
                    for r in range(2):
                        self.cp(dup.ap[:, r, :], ct.ap[:, kb, kvh * 64:(kvh + 1) * 64], [ct.res], [dup.res])
                    self.tr(p.ap[:, (kvh * 2 + kb) * 128:(kvh * 2 + kb + 1) * 128], dup.ap.rearrange("p a b -> p (a b)"), self.identF.ap,
                            [dup.res, self.identF.res], [p.res])
            for kvh in range(2):
                self.cp(kT.ap[:, kvh, 0:256], p.ap[:, kvh * 256:(kvh + 1) * 256], [p.res], [kT.res])
            self.dmas("pool", [(V.ap[:, b_, :, 0:64], d["cv_b"][l][b_ * 128:(b_ + 1) * 128, :].rearrange("p (h e) -> p h e", e=64))
                               for b_ in range(2)], [], [V.res], V.res)
            self.S.op("dve", lambda e: e.memset(V.ap[:, 0:2, :, 64:65], 1.0), writes=[V.res])
            srcs = [("XB", 0, 2048, 256), ("XG", 15 * 128, 128, 18 * 128), ("XG", 16 * 128, 128, 19 * 128)]
        else:
            srcs = [("PB", seq * 256, 256, 0)]
        for (name, t0, nt, ko) in srcs:
            for kvh in range(2):
                for r in range(2):
                    pairs.append((kT.ap[r * 64:(r + 1) * 64, kvh, ko:ko + nt], self.xrows(name, t0, nt, 256 + kvh * 64, 64)))
            pairs.append((V.ap[:, ko // 128:(ko + nt) // 128, :, :],
                          self.xv(name, t0, nt)[:, 260:390].rearrange("(b p) (h e) -> p b h e", p=128, e=65)))
            rd += self.xres_of(name, t0, nt)
        self.dmas("sp", pairs, rd, [kT.res, V.res], kT.res)
        nblk = 16 if samp else 2
        ntile = 4 if samp else 1
        per = nblk // ntile
        for ti in range(ntile):
            tok0 = (1024 + ti * 512) if samp else seq * 256
            nq = per * 128
            q = qm[ti % 2]
            qres = [self.QSr[b] for b in range(tok0 // 128, (tok0 + nq) // 128)]
            self.dma("sp", q.ap[:, :, 0:nq], QS[1024:2048, tok0:tok0 + nq].rearrange("(i p) t -> p i t", p=128), qres, [q.res], q.res)
            for bi in range(per):
                i = ti * per + bi
                for kvh in range(2):
                    ob = self.ps[4 + (kvh % 2)]
                    bm = self.bmask4.ap

                    def kb_(idx, m=None):
                        return (kT.ap[:, kvh, idx * 128:(idx + 1) * 128], [kT.res], V.ap[:, idx, kvh, :], [V.res],
                                None if m is None else bm[:, m].rearrange("p g q -> p (g q)"))
                    if samp:
                        kbl = [kb_(0), kb_(1)]
                        kbl.append(kb_(2 + i - 1, 0) if i > 0 else kb_(18, 2))
                        kbl.append(kb_(2 + i))
                        kbl.append(kb_(2 + i + 1, 1) if i < 15 else kb_(19, 3))
                    else:
                        kbl = [kb_(0), kb_(1)]
                    qv = q.ap[:, 4 * kvh:4 * kvh + 4, bi * 128:(bi + 1) * 128]
                    self.attend(qv, q.res, kbl, 512, scale, ob)
                    for g in range(4):
                        hq = 4 * kvh + g
                        self.ts(self.rec.ap[:, g:g + 1], ob.ap[:, g * 65 + 64:g * 65 + 65], self.es.ap[:, hq:hq + 1], None, ALU.add, None,
                                [ob.res, self.es.res], [self.rec.res])
                    self.S.op("dve", lambda e: e.reciprocal(self.rec.ap[:, 0:4], self.rec.ap[:, 0:4]), reads=[], writes=[self.rec.res])
                    for g in range(4):
                        hq = 4 * kvh + g
                        self.ts(o_t.ap[:, oblk0 + i, 256 + hq * 64:256 + (hq + 1) * 64], ob.ap[:, g * 65:g * 65 + 64], self.rec.ap[:, g:g + 1], None,
                                ALU.mult, None, [ob.res, self.rec.res], [o_t.res])

    def wo_phase(self, l, tiles, o_t, blk0):
        d = self.d
        w_o = self.atile("w_o", [8, D], BF16)
        oT = self.atile("oT", [8, 512], BF16)
        yb = [self.atile("ybm%d" % i, [D], F32) for i in range(4)]
        wsrc = d["w_o"][l].rearrange("(k p) n -> p k n", p=128)
        self.dmas("pool", [(w_o.ap[:, :, 0:512], wsrc[:, :, 0:512]), (w_o.ap[:, :, 512:1024], wsrc[:, :, 512:1024])], [], [w_o.res], w_o.res)
        for tile in tiles:
            cond = 1 if tile < 2 else 0
            for k in range(8):
                p = self.ps[k % 2]
                pb = p.ap[:, 0:256].bitcast(BF16)
                for tb in range(4):
                    ob_ = tile * 4 + tb - blk0
                    self.tr(pb[:, tb * 128:(tb + 1) * 128], o_t.ap[:, ob_, k * 128:(k + 1) * 128], self.identB.ap,
                            [o_t.res, self.identB.res], [p.res])
                self.cp(oT.ap[:, k, :], pb, [p.res], [oT.res], eng=("act" if k % 2 else "dve"))
            for hf in range(2):
                for tb in range(4):
                    p = self.ps[4 + tb] if hf == 0 else self.ps[2 + (tb % 2)]
                    for k in range(8):
                        self.mm(p.ap, oT.ap[:, k, tb * 128:(tb + 1) * 128], w_o.ap[:, k, hf * 512:(hf + 1) * 512], k == 0, k == 7,
                                [oT.res, w_o.res], [p.res])
                    self.tt(yb[tb].ap[:, hf * 512:(hf + 1) * 512], p.ap, self.gate_bc.ap[:, cond, hf * 512:(hf + 1) * 512], ALU.mult,
                            [p.res, self.gate_bc.res], [yb[tb].res])
            for tb in range(4):
                self.post(tile * 4 + tb, cond, yb[tb])


def build_nc():
    b = Builder()
    nc = b.build()
    return nc, list(b.in_names)


def _rope_tables(dim, pos0, n):
    t = np.arange(pos0, pos0 + n)
    row = (t // 64).astype(np.float32)
    col = (t % 64).astype(np.float32)
    a = dim // 2
    inv = np.power(np.float32(10000.0), -np.arange(0, a, 2, dtype=np.float32) / np.float32(a)).astype(np.float32)
    ar = row[:, None] * inv[None, :]
    ac = col[:, None] * inv[None, :]
    ang = np.concatenate([ar, ar, ac, ac], axis=-1).astype(np.float32)
    cos = np.cos(ang).astype(np.float32)
    sin = np.sin(ang).astype(np.float32)
    q = a // 2
    sgn = np.ones(dim, np.float32)
    sgn[0:q] = -1.0
    sgn[a:a + q] = -1.0
    ss = sin * sgn[None, :]
    out = np.stack([cos, ss], 0).reshape(2, n // 128, 128, dim).transpose(2, 0, 1, 3)
    return np.ascontiguousarray(out, dtype=np.float32)


def _consts(core):
    half = core % 2
    ident = np.eye(128, dtype=np.float32)
    k = np.arange(128)[:, None]
    q = np.arange(128)[None, :]
    mL = np.where(k >= q, 0.0, NEG).astype(np.float32)
    mR = np.where(k <= q, 0.0, NEG).astype(np.float32)
    full = np.full((128, 128), NEG, np.float32)
    mLe = full if half == 0 else mL
    mRe = full if half == 1 else mR
    bmask = np.ascontiguousarray(np.stack([mL, mR, mLe, mRe], 1))
    rmask = np.zeros((128, 6), np.float32)
    for i in range(4):
        rmask[32 * i:32 * i + 32, i] = 1.0
    rmask[0:64, 4] = 1.0
    rmask[64:128, 5] = 1.0
    return ident, bmask, rmask


_NC_CACHE = {}


def kernel(**inp):
    f = lambda a: np.ascontiguousarray(np.asarray(a, dtype=np.float32))
    x_prompt = f(inp["x_prompt"]); x_sample = f(inp["x_sample"])
    if "nc" not in _NC_CACHE:
        _NC_CACHE["nc"] = build_nc()
    nc, in_names = _NC_CACHE["nc"]
    shared = {
        "w_mod": f(inp["w_mod"]),
        "bmT": np.ascontiguousarray(f(inp["b_mod"]).reshape(DEPTH, 72, 128).transpose(2, 0, 1)),
        "ln_g": f(inp["ln_g"]), "ln_b": f(inp["ln_b"]),
        "ffn_w1": f(inp["ffn_w1"]), "ffn_w3": f(inp["ffn_w3"]), "ffn_w2": f(inp["ffn_w2"]),
        "w_in": f(inp["w_in"]), "w_o": f(inp["w_o"]),
        "a_lambda": f(inp["a_lambda"]).reshape(DEPTH, 128), "a_subln_g": f(inp["a_subln_g"]),
        "b_sink": f(inp["b_sink"]), "c_q_norm_g": f(inp["c_q_norm_g"]), "c_w_q_up": f(inp["c_w_q_up"]),
        "c_kv_norm_g": f(inp["c_kv_norm_g"]), "c_w_kv_up": f(inp["c_w_kv_up"]).reshape(DEPTH, 128, 512),
    }
    c = f(inp["c"]); c_ctx = f(inp["c_ctx"])
    in_maps = []
    for core in range(8):
        b = core // 2
        half = core % 2
        m = dict(shared)
        m["xin"] = np.ascontiguousarray(np.concatenate(
            [x_prompt[4 * core:4 * core + 4].reshape(1024, D), x_sample[b, half * 2048:(half + 1) * 2048]], 0))
        cond = np.stack([c[b], c_ctx], 0)
        m["condT"] = np.ascontiguousarray(cond.reshape(2, 8, 128).transpose(2, 1, 0))
        m["ck_a"] = f(inp["cache_a_k"])[b].reshape(DEPTH, 256, 256)
        m["cv_a"] = f(inp["cache_a_v"])[b].reshape(DEPTH, 256, 256)
        m["ck_b"] = f(inp["cache_b_k"])[b].reshape(DEPTH, 256, 128)
        m["cv_b"] = f(inp["cache_b_v"])[b].reshape(DEPTH, 256, 128)
        m["c_ckv"] = f(inp["cache_c_kv"])[b]
        m["c_kpe"] = f(inp["cache_c_kpe"])[b]
        ident, bmask, rmask = _consts(core)
        m["ident"] = ident; m["bmask"] = bmask; m["rmask"] = rmask
        m["rope32"] = _rope_tables(32, half * 2048, 2048)
        m["rope64"] = _rope_tables(64, half * 2048, 2048)
        in_maps.append({k: np.ascontiguousarray(m[k]) for k in in_names})
    res = run_bass_kernel_spmd(nc, in_maps, core_ids=list(range(8)))
    ys = [np.asarray(r["y"]) for r in res.results]
    sts = [np.asarray(r["st"]) for r in res.results]
    y_prompt = np.concatenate([y[:1024].reshape(4, 256, D) for y in ys], 0)
    y_sample = np.stack([np.concatenate([ys[2 * b][1024:], ys[2 * b + 1][1024:]], 0) for b in range(4)], 0)
    stt = np.concatenate(sts, 0)
    new_a_k = stt[..., 0:256].reshape(32, DEPTH, 256, 4, 64)
    new_a_v = stt[..., 256:512].reshape(32, DEPTH, 256, 4, 64)
    new_b_k = stt[..., 512:640].reshape(32, DEPTH, 256, 2, 64)
    new_b_v = stt[..., 640:768].reshape(32, DEPTH, 256, 2, 64)
    new_c_kv = stt[..., 768:896]
    new_c_kpe = stt[..., 896:928]
    outs = (y_prompt, y_sample, new_a_k, new_a_v, new_b_k, new_b_v, new_c_kv, new_c_kpe)
    return tuple(np.ascontiguousarray(o, dtype=np.float32) for o in outs)
```

```python
import os
import numpy as np
from contextlib import ExitStack
import concourse.bass as bass
import concourse.mybir as mybir
from concourse.bass_utils import run_bass_kernel_spmd

F32 = mybir.dt.float32
BF16 = mybir.dt.bfloat16
ALU = mybir.AluOpType
AF = mybir.ActivationFunctionType
AX = mybir.AxisListType

D = 1024
DFF = 2816
NJ = DFF // 128
DEPTH = 2
INW = 1952
LN_EPS = 1e-5
RMS_EPS = 1e-6
DN_ALPHA = float((2 * DEPTH) ** 0.25)
NEG = -30000.0
NPB = 8
NSB = 16
NB = NPB + NSB
XK_ROWS = 544
XV_W = 390
X_ROWS = XK_ROWS + XV_W
QS_ROWS = 2560
XG_PIECES = ((0, 256), (256, 512), (512, 739), (739, 934))


class Res:
    __slots__ = ("name", "w", "r", "dsem", "dcnt", "excl")

    def __init__(self, name, excl=False):
        self.name = name
        self.excl = excl
        self.w = None
        self.r = {}
        self.dsem = None
        self.dcnt = 0

    def inherit(self, other):
        if other.w is not None:
            k, v = other.w
            if self.r.get(k, 0) < v:
                self.r[k] = v
        for k, v in other.r.items():
            if self.r.get(k, 0) < v:
                self.r[k] = v


class Sched:
    ENG = ("pe", "act", "dve", "pool", "sp")

    def __init__(self):
        self.q = {e: [] for e in self.ENG}
        self.cnt = {e: 0 for e in self.ENG}
        self.waited = {e: {} for e in self.ENG}
        self.ndsem = 0
        self.semmap = {}

    def _dep(self, eng, k, v):
        if eng == "pe" and k == ("e", "pe"):
            return
        if self.waited[eng].get(k, 0) >= v:
            return
        self.waited[eng][k] = v
        self.q[eng].append(("wait", k, v))

    def op(self, eng, fn, reads=(), writes=(), dma=None, ndma=1, inc=16):
        for r in reads:
            if r.w is not None:
                self._dep(eng, *r.w)
            if r.excl:
                for k, v in r.r.items():
                    if k != ("e", eng):
                        self._dep(eng, k, v)
        for w in writes:
            if w.w is not None:
                self._dep(eng, *w.w)
            for k, v in w.r.items():
                self._dep(eng, k, v)
        if dma is None:
            self.cnt[eng] += 1
            tok = (("e", eng), self.cnt[eng])
            self.q[eng].append(("op", fn, tok[0], 1))
        else:
            ent = self.semmap.get(dma.name)
            if ent is None:
                ent = [("d", self.ndsem), 0, None]
                self.ndsem += 1
                self.semmap[dma.name] = ent
            if ent[2] is not dma:
                if ent[1] > 0:
                    self._dep(eng, ent[0], ent[1])
                ent[2] = dma
            ent[1] += inc * ndma
            tok = (ent[0], ent[1])
            self.q[eng].append(("op", fn, tok[0], inc))
        for r in reads:
            if r.r.get(tok[0], 0) < tok[1]:
                r.r[tok[0]] = tok[1]
        for w in writes:
            w.w = tok
            w.r = {}
        return tok

    def finish(self, eng="sp"):
        for ent in self.semmap.values():
            self._dep(eng, ent[0], ent[1])
        for e in self.ENG:
            if e != eng and self.cnt[e] > 0:
                self._dep(eng, ("e", e), self.cnt[e])

    def emit(self, nc, stack):
        sems = {}
        for e in self.ENG:
            sems[("e", e)] = stack.enter_context(nc.semaphore("s_" + e))
        for i in range(self.ndsem):
            sems[("d", i)] = stack.enter_context(nc.semaphore("d_%d" % i))
        block = stack.enter_context(nc.Block())
        q = self.q

        def run(ename):
            def body(h):
                for it in q[ename]:
                    if it[0] == "wait":
                        h.wait_ge(sems[it[1]], it[2])
                    else:
                        ins = it[1](h)
                        if not isinstance(ins, (list, tuple)):
                            ins = [ins]
                        for i_ in ins:
                            i_.then_inc(sems[it[2]], it[3])
            return body

        block.tensor(run("pe"))
        block.scalar(run("act"))
        block.vector(run("dve"))
        block.gpsimd(run("pool"))
        block.sync(run("sp"))


class T:
    __slots__ = ("ap", "res")

    def __init__(self, ap, res):
        self.ap = ap
        self.res = res


class Builder:
    def __init__(self):
        self.nc = bass.Bass("TRN2", target_bir_lowering=False)
        self.S = Sched()
        self.d = {}
        self.dres = {}
        self.in_names = []

    def din(self, name, shape, dt=F32):
        if os.environ.get("KSTAGE", "full") in ("projonly", "mixonly", "mixprompt") and name.startswith("ffn_w"):
            return
        self.d[name] = self.nc.dram_tensor(name, list(shape), dt, kind="ExternalInput").ap()
        self.dres[name] = Res(name)
        self.in_names.append(name)

    def dout(self, name, shape, dt=F32):
        self.d[name] = self.nc.dram_tensor(name, list(shape), dt, kind="ExternalOutput").ap()
        self.dres[name] = Res(name)

    def dscr(self, name, shape, dt=BF16):
        self.d[name] = self.nc.dram_tensor(name, list(shape), dt).ap()
        self.dres[name] = Res(name)

    def ptile(self, name, free, dt):
        t = self.stack.enter_context(self.nc.sbuf_tensor(name, [128] + list(free), dt))
        return T(t[:], Res(name))

    def arena_reset(self):
        self.acur = 0

    def atile(self, name, free, dt):
        n = int(np.prod(free))
        esz = 4 if dt == F32 else 2
        nb = (n * esz + 63) // 64 * 64
        off = self.acur
        assert off + nb <= self.ABYTES, (name, off, nb, self.ABYTES)
        self.acur += nb
        ap = self.arena[:, off // 2: off // 2 + n * esz // 2]
        if dt == F32:
            ap = ap.bitcast(F32)
        if len(free) == 2:
            ap = ap.rearrange("p (a b) -> p a b", a=free[0])
        elif len(free) == 3:
            ap = ap.rearrange("p (a b c) -> p a b c", a=free[0], b=free[1])
        elif len(free) == 4:
            ap = ap.rearrange("p (a b c d) -> p a b c d", a=free[0], b=free[1], c=free[2])
        res = Res(name)
        keep = []
        for (o, s, r) in self.alive:
            if o < off + nb and off < o + s:
                res.inherit(r)
                if not (off <= o and o + s <= off + nb):
                    keep.append((o, s, r))
            else:
                keep.append((o, s, r))
        keep.append((off, nb, res))
        self.alive = keep
        return T(ap, res)

    def mm(self, out, lhsT, rhs, start, stop, R, W, skip=False):
        self.S.op("pe", lambda e: e.matmul(out, lhsT, rhs, start=start, stop=stop, skip_group_check=skip),
                  reads=R, writes=W)

    def tr(self, out, in_, ident, R, W):
        self.S.op("pe", lambda e: e.transpose(out, in_, ident), reads=R, writes=W)

    def act(self, out, in_, func, R, W, bias=None, scale=None):
        kw = {}
        if bias is not None:
            kw["bias"] = bias
        if scale is not None:
            kw["scale"] = scale
        self.S.op("act", lambda e: e.activation(out, in_, func, **kw), reads=R, writes=W)

    def tt(self, out, in0, in1, op, R, W, eng="dve"):
        self.S.op(eng, lambda e: e.tensor_tensor(out, in0, in1, op), reads=R, writes=W)

    def ts(self, out, in0, s1, s2, op0, op1, R, W, eng="dve"):
        if op1 is None:
            self.S.op(eng, lambda e: e.tensor_scalar(out, in0, s1, None, op0), reads=R, writes=W)
        else:
            self.S.op(eng, lambda e: e.tensor_scalar(out, in0, s1, s2, op0, op1), reads=R, writes=W)

    def stt(self, out, in0, scalar, in1, op0, op1, R, W, eng="dve"):
        self.S.op(eng, lambda e: e.scalar_tensor_tensor(out, in0, scalar, in1, op0, op1), reads=R, writes=W)

    def cp(self, out, in_, R, W, eng="dve"):
        if eng == "act":
            self.S.op("act", lambda e: e.activation(out, in_, AF.Copy), reads=R, writes=W)
        else:
            self.S.op(eng, lambda e: e.tensor_copy(out, in_), reads=R, writes=W)

    def dma(self, q, out, in_, R, W, sem, slow=False):
        if slow:
            self.S.op(q, lambda e: e.dma_start(out=out, in_=in_, allow_slow_non_contiguous=True), reads=R, writes=W, dma=sem)
        else:
            self.S.op(q, lambda e: e.dma_start(out=out, in_=in_), reads=R, writes=W, dma=sem)

    def dmas(self, q, pairs, R, W, sem):
        pairs = list(pairs)
        self.S.op(q, lambda e: [e.dma_start(out=o, in_=i) for (o, i) in pairs], reads=R, writes=W, dma=sem,
                  ndma=len(pairs))

    def build(self):
        nc = self.nc
        self.din("xin", [NB * 128, D])
        self.din("condT", [128, 8, 2])
        self.din("bmT", [128, DEPTH, 72])
        self.din("ck_a", [DEPTH, 256, 256]); self.din("cv_a", [DEPTH, 256, 256])
        self.din("ck_b", [DEPTH, 256, 128]); self.din("cv_b", [DEPTH, 256, 128])
        self.din("c_ckv", [DEPTH, 256, 128]); self.din("c_kpe", [DEPTH, 256, 32])
        self.din("w_mod", [DEPTH, D, 9 * D])
        self.din("ln_g", [DEPTH, 3, D]); self.din("ln_b", [DEPTH, 3, D])
        self.din("ffn_w1", [DEPTH, 2, D, DFF]); self.din("ffn_w3", [DEPTH, 2, D, DFF])
        self.din("ffn_w2", [DEPTH, 2, DFF, D])
        self.din("w_in", [DEPTH, D, INW]); self.din("w_o", [DEPTH, D, D])
        self.din("a_lambda", [DEPTH, 128]); self.din("a_subln_g", [DEPTH, 64]); self.din("b_sink", [DEPTH, 8])
        self.din("c_q_norm_g", [DEPTH, 256]); self.din("c_w_q_up", [DEPTH, 256, 384])
        self.din("c_kv_norm_g", [DEPTH, 128]); self.din("c_w_kv_up", [DEPTH, 128, 512])
        self.din("ident", [128, 128])
        self.din("rope32", [128, 2, NSB, 32]); self.din("rope64", [128, 2, NSB, 64])
        self.din("bmask", [128, 4, 128])
        self.din("rmask", [128, 6])
        self.dout("y", [NB * 128, D])
        self.dout("st", [4, DEPTH, 256, 928])
        self.dscr("XB", [X_ROWS, 2048])
        self.dres["XG"] = Res("XG")
        for pi, (r0, r1) in enumerate(XG_PIECES):
            self.dscr("XG%d" % pi, [2 * (r1 - r0), 2048])
        self.dscr("PB", [X_ROWS, 1024]); self.dscr("QS", [QS_ROWS, NB * 128])
        self.QSr = [Res("QS%d" % i) for i in range(NB)]
        for s_ in range(2):
            self.dscr("W13S%d" % s_, [11, 128, 2, 8, 256]); self.dscr("W2S%d" % s_, [12, 128, 4, 512])
        self.w13s_res = [[Res("w13s%d_%d" % (s_, j)) for j in range(11)] for s_ in range(2)]
        self.w2s_res = [[Res("w2s%d_%d" % (s_, j)) for j in range(12)] for s_ in range(2)]
        self.cast_n = 0
        self.XBr = [Res("XB%d" % i) for i in range(NSB)]
        self.PBr = [Res("PB%d" % i) for i in range(NPB)]

        with ExitStack() as st:
            self.stack = st
            self.xres = self.ptile("xres", [NB, D], F32)
            self.xblk = [Res("xblk%d" % i) for i in range(NB)]
            self.identF = self.ptile("identF", [128], F32)
            self.identB = self.ptile("identB", [128], BF16)
            self.onesF = self.ptile("onesF", [128], F32)
            self.csT = self.ptile("csT", [8, 2], F32)
            self.bmT = self.ptile("bmT_s", [DEPTH, 72], F32)
            self.modT = self.ptile("modT", [24, 2], F32)
            self.gate_bc = self.ptile("gate_bc", [2, D], F32)
            self.lng = self.ptile("lng", [D], F32)
            self.lnb = self.ptile("lnb", [D], F32)
            self.bmask = self.ptile("bmask_s", [4, 128], BF16)
            self.rmask = self.ptile("rmask_s", [6], F32)
            self.small = self.ptile("small", [128], F32)
            self.small_r = [Res("small%d" % i) for i in range(8)]
            self.ABYTES = 88 * 1024
            arena_t = st.enter_context(nc.sbuf_tensor("arena", [128, self.ABYTES // 2], BF16))
            self.arena = arena_t[:]
            self.alive = []
            self.acur = 0
            self.ps = []
            for i in range(8):
                p = st.enter_context(nc.psum_tensor("ps%d" % i, [128, 512], F32))
                self.ps.append(T(p[:], Res("ps%d" % i, excl=True)))

            self.prologue()
            for l in range(DEPTH):
                self.layer(l)
            self.epilogue()
            self.S.finish()
            self.S.emit(nc, st)
        return nc

    def prologue(self):
        d = self.d
        xv = d["xin"].rearrange("(b p) f -> p b f", p=128)
        for i in range(3):
            blks = list(range(i * 8, (i + 1) * 8))
            self.dma("sp", self.xres.ap[:, i * 8:(i + 1) * 8, :], xv[:, i * 8:(i + 1) * 8, :], [],
                     [self.xblk[b] for b in blks], self.xblk[blks[0]])
        self.dma("sp", self.identF.ap, d["ident"], [], [self.identF.res], self.identF.res)
        self.dma("pool", self.identB.ap, d["ident"], [], [self.identB.res], self.identB.res)
        self.dma("pool", self.bmask.ap, d["bmask"], [], [self.bmask.res], self.bmask.res)
        self.dma("sp", self.rmask.ap, d["rmask"], [], [self.rmask.res], self.rmask.res)
        self.dma("sp", self.bmT.ap, d["bmT"], [], [self.bmT.res], self.bmT.res)
        self.dma("sp", self.csT.ap, d["condT"], [], [self.csT.res], self.csT.res)
        self.S.op("dve", lambda e: e.memset(self.onesF.ap, 1.0), writes=[self.onesF.res])
        self.act(self.csT.ap, self.csT.ap, AF.Silu, [], [self.csT.res])

    def epilogue(self):
        yv = self.d["y"].rearrange("(b p) f -> p b f", p=128)
        for i in range(6):
            blks = list(range(i * 4, (i + 1) * 4))
            self.dma("sp", yv[:, i * 4:(i + 1) * 4, :], self.xres.ap[:, i * 4:(i + 1) * 4, :],
                     [self.xblk[b] for b in blks], [], self.xblk[blks[0]])

    def layer(self, l):
        tiles = [2, 3, 4, 5, 0, 1]
        if os.environ.get("KSTAGE", "full") in ("projonly", "mixonly", "mixprompt"):
            if l == 0:
                self.mixer_phase(l)
            return
        self.ffn_phase(l, 0, 0, tiles)
        self.mixer_phase(l)
        self.ffn_phase(l, 1, 2, tiles)

    def mods(self, l, slot, weight):
        d = self.d
        ps = self.ps[6]
        wsrc = d["w_mod"][l].rearrange("(k p) n -> p k n", p=128)
        base = slot * 3 * D
        ring = [self.atile("wm%d" % i, [8, 256], F32) for i in range(2)]
        for jb in range(12):
            w = ring[jb % 2]
            self.dma("sp", w.ap, wsrc[:, :, base + jb * 256: base + (jb + 1) * 256], [], [w.res], w.res)
            for cc in range(2):
                n = jb * 2 + cc
                for k in range(8):
                    self.mm(ps.ap[:, 2 * n:2 * n + 2], w.ap[:, k, cc * 128:(cc + 1) * 128], self.csT.ap[:, k, :],
                            k == 0, k == 7, [w.res, self.csT.res], [ps.res])
        psv = ps.ap[:, 0:48].rearrange("p (n c) -> p n c", c=2)
        for c in range(2):
            self.tt(self.modT.ap[:, :, c], psv[:, :, c], self.bmT.ap[:, l, slot * 24:(slot + 1) * 24], ALU.add,
                    [ps.res, self.bmT.res], [self.modT.res])
        self.ts(self.modT.ap[:, 8:16, :], self.modT.ap[:, 8:16, :], 1.0, None, ALU.add, None, [], [self.modT.res])
        self.ts(self.modT.ap[:, 16:24, :], self.modT.ap[:, 16:24, :], float(weight), None, ALU.mult, None, [],
                [self.modT.res])
        dg = [self.atile("dg%d" % i, [128], F32) for i in range(2)]
        pb = [self.ps[4], self.ps[5]]
        i = 0
        for c in range(2):
            for hf in range(2):
                p = pb[(c * 2 + hf) % 2]
                for kk in range(4):
                    k = hf * 4 + kk
                    g = dg[i % 2]
                    i += 1
                    self.ts(g.ap, self.identF.ap, self.modT.ap[:, 16 + k, c:c + 1], None, ALU.mult, None,
                            [self.identF.res, self.modT.res], [g.res])
                    self.mm(p.ap[:, kk * 128:(kk + 1) * 128], self.onesF.ap, g.ap, True, True,
                            [self.onesF.res, g.res], [p.res])
                self.cp(self.gate_bc.ap[:, c, hf * 512:(hf + 1) * 512], p.ap, [p.res], [self.gate_bc.res], eng="act")

    def load_ln(self, l, idx):
        d = self.d
        self.dma("sp", self.lng.ap, d["ln_g"][l, idx].partition_broadcast(128), [], [self.lng.res], self.lng.res)
        self.dma("sp", self.lnb.ap, d["ln_b"][l, idx].partition_broadcast(128), [], [self.lnb.res], self.lnb.res)

    def pre(self, tile, hT, banks):
        cond = 1 if tile < 2 else 0
        for k in range(8):
            p = banks[k % len(banks)]
            for tb in range(4):
                blk = tile * 4 + tb
                self.tr(p.ap[:, tb * 128:(tb + 1) * 128], self.xres.ap[:, blk, k * 128:(k + 1) * 128], self.identF.ap,
                        [self.xblk[blk], self.identF.res], [p.res])
            self.act(hT.ap[:, k, :], p.ap, AF.Identity, [p.res, self.modT.res], [hT.res],
                     bias=self.modT.ap[:, k, cond:cond + 1], scale=self.modT.ap[:, 8 + k, cond:cond + 1])

    def post_stages(self, blk, cond, ybuf, slot_i=0):
        x = self.xres.ap[:, blk, :]
        xr = self.xblk[blk]
        c0 = 32 + slot_i * 16
        sm = self.small.ap[:, c0:c0 + 16]
        sr = self.small_r[4 + slot_i]

        def stage_a():
            self.stt(ybuf.ap, x, DN_ALPHA, ybuf.ap, ALU.mult, ALU.add, [xr], [ybuf.res])
            st6 = sm[:, 0:12].rearrange("p (a b) -> p a b", a=2)
            for hf in range(2):
                self.S.op("dve", lambda e, hf=hf: e.bn_stats(st6[:, hf, :], ybuf.ap[:, hf * 512:(hf + 1) * 512]),
                          reads=[ybuf.res], writes=[sr])
            self.S.op("dve", lambda e: e.bn_aggr(sm[:, 12:14], st6), reads=[], writes=[sr])
            self.ts(sm[:, 14:15], sm[:, 13:14], LN_EPS, None, ALU.add, None, [], [sr])
            self.act(sm[:, 14:15], sm[:, 14:15], AF.Sqrt, [], [sr])

        def stage_b():
            self.S.op("dve", lambda e: e.reciprocal(sm[:, 14:15], sm[:, 14:15]), reads=[], writes=[sr])
            self.stt(sm[:, 15:16], sm[:, 12:13], -1.0, sm[:, 14:15], ALU.mult, ALU.mult, [], [sr])
            self.act(ybuf.ap, ybuf.ap, AF.Identity, [sr], [ybuf.res], bias=sm[:, 15:16], scale=sm[:, 14:15])

        def stage_c():
            self.tt(ybuf.ap, ybuf.ap, self.lng.ap, ALU.mult, [self.lng.res], [ybuf.res])
            self.tt(x, ybuf.ap, self.lnb.ap, ALU.add, [ybuf.res, self.lnb.res], [xr])
        return [stage_a, stage_b, stage_c]

    def post(self, blk, cond, ybuf, slot_i=0):
        for f_ in self.post_stages(blk, cond, ybuf, slot_i):
            f_()

    def cast_jobs(self, p):
        d = self.d
        l, half = p // 2, p % 2
        st_ = p % 2
        w1s = d["ffn_w1"][l, half].rearrange("(k p) n -> p k n", p=128)
        w3s = d["ffn_w3"][l, half].rearrange("(k p) n -> p k n", p=128)
        w2s = d["ffn_w2"][l, half].rearrange("(j p) n -> p j n", p=128)
        jobs = []
        for jb in range(11):
            def j13(jb=jb):
                dst = d["W13S%d" % st_][jb]
                self.dmas("pool", [(dst[:, 0], w1s[:, :, jb * 256:(jb + 1) * 256]), (dst[:, 1], w3s[:, :, jb * 256:(jb + 1) * 256])],
                          [], [self.w13s_res[st_][jb]], Res("cast%d" % (self.cast_n % 4)))
                self.cast_n += 1
            jobs.append(j13)
        for r in range(12):
            def j2(r=r):
                hf, jg = r // 6, r % 6
                nj = 4 if jg < 5 else 2
                dst = d["W2S%d" % st_][r]
                self.dmas("pool", [(dst[:, 0:nj, :], w2s[:, jg * 4:jg * 4 + nj, hf * 512:(hf + 1) * 512])],
                          [], [self.w2s_res[st_][r]], Res("cast%d" % (self.cast_n % 4)))
                self.cast_n += 1
            jobs.append(j2)
        return jobs

    def ffn_phase(self, l, half, slot, tiles):
        d = self.d
        p_ = l * 2 + half
        st_ = p_ % 2
        if p_ == 0:
            for j in self.cast_jobs(0):
                j()
        nxt = self.cast_jobs(p_ + 1) if p_ + 1 < 2 * DEPTH else []
        self.arena_reset()
        self.mods(l, slot, 0.5)
        self.load_ln(l, 0 if slot == 0 else 2)
        self.arena_reset()
        hT = self.atile("hT", [8, 512], BF16)
        aT = self.atile("aT", [NJ, 512], BF16)
        su = [self.atile("su%d" % i, [512], F32) for i in range(2)]
        R13, R2 = 3, 3
        w13 = [self.atile("w13_%d" % i, [2, 8, 256], BF16) for i in range(R13)]
        w2r = [self.atile("w2_%d" % i, [4, 512], BF16) for i in range(R2)]
        yb = [self.atile("yb%d" % i, [D], F32) for i in range(4)]
        state = {"i13": 0, "i2": 0}
        total13 = len(tiles) * 11
        total2 = len(tiles) * 12

        def issue13(upto):
            while state["i13"] < min(upto, total13):
                n = state["i13"]
                jb = n % 11
                w = w13[n % R13]
                self.dma("sp", w.ap, d["W13S%d" % st_][jb], [self.w13s_res[st_][jb]], [w.res], w.res)
                state["i13"] += 1
                if nxt:
                    nxt.pop(0)()

        def issue2(upto):
            while state["i2"] < min(upto, total2):
                n = state["i2"]
                r = n % 12
                w = w2r[n % R2]
                self.dma("sp", w.ap, d["W2S%d" % st_][r], [self.w2s_res[st_][r]], [w.res], w.res)
                state["i2"] += 1

        issue13(R13)
        pending = []
        self.pre(tiles[0], hT, [self.ps[4], self.ps[5]])
        for ti, tile in enumerate(tiles):
            cond = 1 if tile < 2 else 0
            for jb in range(11):
                n = ti * 11 + jb
                issue13(n + R13)
                w = w13[n % R13]
                for cc in range(2):
                    j = jb * 2 + cc
                    pu = self.ps[j % 2]
                    pg = self.ps[2 + j % 2]
                    for k in range(8):
                        self.mm(pu.ap, w.ap[:, 0, k, cc * 128:(cc + 1) * 128], hT.ap[:, k, :], k == 0, k == 7,
                                [w.res, hT.res], [pu.res])
                    for k in range(8):
                        self.mm(pg.ap, w.ap[:, 1, k, cc * 128:(cc + 1) * 128], hT.ap[:, k, :], k == 0, k == 7,
                                [w.res, hT.res], [pg.res])
                    s_ = su[j % 2]
                    self.act(s_.ap, pu.ap, AF.Silu, [pu.res], [s_.res])
                    self.tt(aT.ap[:, j, :], s_.ap, pg.ap, ALU.mult, [s_.res, pg.res], [aT.res])
                    if pending and j >= 2:
                        pending.pop(0)()
                if jb == 8:
                    issue2(ti * 12 + R2)
            if ti + 1 < len(tiles):
                self.pre(tiles[ti + 1], hT, [self.ps[4], self.ps[5]])
                issue13((ti + 1) * 11 + R13)
            for hf in range(2):
                banks = [self.ps[4 + tb] for tb in range(4)] if hf == 0 else [self.ps[tb] for tb in range(4)]
                for jg in range(6):
                    n = ti * 12 + hf * 6 + jg
                    issue2(n + R2)
                    w = w2r[n % R2]
                    nj = 4 if jg < 5 else 2
                    for jj in range(nj):
                        j = jg * 4 + jj
                        for tb in range(4):
                            self.mm(banks[tb].ap, aT.ap[:, j, tb * 128:(tb + 1) * 128], w.ap[:, jj, :], j == 0, j == NJ - 1,
                                    [aT.res, w.res], [banks[tb].res])
                for tb in range(4):
                    self.tt(yb[tb].ap[:, hf * 512:(hf + 1) * 512], banks[tb].ap,
                            self.gate_bc.ap[:, cond, hf * 512:(hf + 1) * 512], ALU.mult,
                            [banks[tb].res, self.gate_bc.res], [yb[tb].res])
            stg = [self.post_stages(tile * 4 + tb, cond, yb[tb], tb) for tb in range(4)]
            pending.extend([stg[tb][k_] for k_ in range(3) for tb in range(4)])
        while pending:
            pending.pop(0)()

    def mixer_phase(self, l):
        self.arena_reset()
        self.mods(l, 1, 1.0)
        self.load_ln(l, 1)
        self.arena_reset()
        stage = os.environ.get("KSTAGE", "full")
        self.project(l)
        if stage in ("proj", "projonly"):
            return
        self.gather()
        if stage == "gather":
            return
        self.arena_reset()
        self.layer_consts(l)
        mark = self.acur
        o_p = self.atile("o_p", [NPB, D], BF16)
        m2 = self.acur
        for s_ in range(4):
            self.acur = m2
            segs = [("PB", s_ * 256, 256)]
            self.attn_A(l, segs, False, [(s_ * 256, 256)], o_p, s_ * 2, 0)
            self.acur = m2
            self.attn_C(l, segs, False, [(s_ * 256, 256)], o_p, s_ * 2, 0)
            self.acur = m2
            self.attn_B(l, False, s_, o_p, s_ * 2)
        self.acur = m2
        self.wo_phase(l, [0, 1], o_p, 0)
        if stage in ("prompt", "mixprompt"):
            return
        self.acur = mark
        o_s = self.atile("o_s", [NSB, D], BF16)
        m2 = self.acur
        segs = [("XG", i * 1024, 1024) for i in range(4)]
        qt = [(1024 + i * 512, 512) for i in range(4)]
        self.attn_A(l, segs, True, qt, o_s, 0, 1024)
        self.acur = m2
        self.attn_C(l, segs, True, qt, o_s, 0, 1024)
        self.acur = m2
        self.attn_B(l, True, 0, o_s, 0)
        self.acur = m2
        self.wo_phase(l, [2, 3, 4, 5], o_s, 8)

    def xrows(self, name, tok0, ntok, r0, nr):
        if name == "XG":
            rk = tok0 // 2048
            t = tok0 % 2048
            for pi, (p0, p1) in enumerate(XG_PIECES):
                if p0 <= r0 and r0 + nr <= p1:
                    a = self.d["XG%d" % pi]
                    n = p1 - p0
                    return a[rk * n + r0 - p0: rk * n + r0 - p0 + nr, t:t + ntok]
            raise AssertionError((r0, nr))
        a = self.d[name]
        return a[r0:r0 + nr, tok0:tok0 + ntok]

    def xv(self, name, tok0, ntok):
        if name == "XG":
            rk = tok0 // 2048
            t = tok0 % 2048
            assert (t // 1024) == ((t + ntok - 1) // 1024)
            if t < 1024:
                a = self.d["XG2"]
                v = a[rk * 227 + 32: rk * 227 + 227, :]
            else:
                a = self.d["XG3"]
                v = a[rk * 195: rk * 195 + 195, :]
                t -= 1024
            v = v.rearrange("r t -> (r t)").rearrange("(t c) -> t c", c=XV_W)
            return v[t:t + ntok, :]
        a = self.d[name]
        v = a[XK_ROWS:X_ROWS, :]
        v = v.rearrange("r t -> (r t)").rearrange("(t c) -> t c", c=XV_W)
        return v[tok0:tok0 + ntok, :]

    def xres_of(self, name, tok0, ntok):
        if name == "XG":
            return [self.dres["XG"]]
        lst = self.XBr if name == "XB" else self.PBr
        return [lst[b] for b in range(tok0 // 128, (tok0 + ntok) // 128)]

    def rope(self, dst, src, H, Q, tbl, sb, R, W, tmp):
        dim = 4 * Q
        xs = src.rearrange("p (h a w q) -> p h a w q", h=H, a=2, w=2)
        xd = dst.rearrange("p (h a w q) -> p h a w q", h=H, a=2, w=2)
        xt = tmp[:, 0:H * dim].rearrange("p (h a w q) -> p h a w q", h=H, a=2, w=2)
        cs = tbl.ap[:, 0, sb, :].rearrange("p (a w q) -> p a w q", a=2, w=2)
        ss = tbl.ap[:, 1, sb, :].rearrange("p (a w q) -> p a w q", a=2, w=2)
        for a in range(2):
            cb = cs[:, a].unsqueeze(1).to_broadcast([128, H, 2, Q])
            self.tt(xd[:, :, a], xs[:, :, a], cb, ALU.mult, R + [tbl.res], W)
            for w in range(2):
                sbb = ss[:, a, w].unsqueeze(1).to_broadcast([128, H, Q])
                self.tt(xt[:, :, a, w], xs[:, :, a, 1 - w], sbb, ALU.mult, R + [tbl.res], [self.tmp_res])
            self.tt(xd[:, :, a], xd[:, :, a], xt[:, :, a], ALU.add, [self.tmp_res], W)

    def rms(self, dst, src, n, gam, R, W, sidx):
        sm = self.small.ap
        sr = self.small_r[sidx]
        c0 = 16 + sidx * 4
        self.tt(self.junk.ap[:, 0:n], src, src, ALU.mult, R, [self.junk.res])
        self.S.op("dve", lambda e: e.reduce_sum(sm[:, c0:c0 + 1], self.junk.ap[:, 0:n], axis=AX.X),
                  reads=[self.junk.res], writes=[sr])
        self.ts(sm[:, c0:c0 + 1], sm[:, c0:c0 + 1], 1.0 / n, RMS_EPS, ALU.mult, ALU.add, [], [sr])
        self.act(sm[:, c0:c0 + 1], sm[:, c0:c0 + 1], AF.Sqrt, [], [sr])
        self.S.op("dve", lambda e: e.reciprocal(sm[:, c0:c0 + 1], sm[:, c0:c0 + 1]), reads=[], writes=[sr])
        self.stt(dst, src, sm[:, c0:c0 + 1], gam.ap, ALU.mult, ALU.mult, R + [sr, gam.res], W)

    def project(self, l):
        d = self.d
        hT = self.atile("hTm", [8, 512], BF16)
        w_in = self.atile("w_in", [8, INW], BF16)
        wq = self.atile("wq", [2, 384], BF16)
        gq = self.atile("gq", [256], F32)
        gkv = self.atile("gkv", [128], F32)
        r32 = [self.atile("r32_%d" % i, [2, 1, 32], F32) for i in range(2)]
        r64 = [self.atile("r64_%d" % i, [2, 1, 64], F32) for i in range(2)]
        tp2 = [self.atile("tp%d" % i, [INW], F32) for i in range(2)]
        rq2 = [self.atile("rq%d" % i, [1184], F32) for i in range(2)]
        tmp = self.atile("tmp", [640], F32)
        self.tmp_res = tmp.res
        self.junk = self.atile("junk", [256], F32)
        nq2 = [self.atile("nq%d" % i, [256], BF16) for i in range(2)]
        nqT = self.atile("nqT", [2, 128], BF16)
        cq2 = [self.atile("cq%d" % i, [384], F32) for i in range(2)]
        ckv2 = [self.atile("ckv%d" % i, [128], F32) for i in range(2)]
        kpe2 = [self.atile("kpe96_%d" % i, [96], F32) for i in range(2)]
        qA = self.atile("qA", [8, 128], BF16)
        qB = self.atile("qB", [8, 128], BF16)
        qC = self.atile("qC", [4, 128], BF16)
        kst = self.atile("kst", [5, 128], BF16)
        vst = self.atile("vst", [6, 65], BF16)
        wsrc = d["w_in"][l].rearrange("(k p) n -> p k n", p=128)
        self.dmas("pool", [(w_in.ap[:, :, c0:c1], wsrc[:, :, c0:c1]) for (c0, c1) in ((0, 512), (512, 1024), (1024, 1536), (1536, INW))],
                  [], [w_in.res], w_in.res)
        self.dma("pool", wq.ap, d["c_w_q_up"][l].rearrange("(k p) n -> p k n", p=128), [], [wq.res], wq.res)
        self.dma("sp", gq.ap, d["c_q_norm_g"][l].partition_broadcast(128), [], [gq.res], gq.res)
        self.dma("sp", gkv.ap, d["c_kv_norm_g"][l].partition_broadcast(128), [], [gkv.res], gkv.res)
        for i in range(2):
            self.S.op("dve", lambda e, i=i: e.memset(kpe2[i].ap, 0.0), writes=[kpe2[i].res])
        self.S.op("dve", lambda e: e.memset(vst.ap, 1.0), writes=[vst.res])
        self.S.op("dve", lambda e: e.memset(qC.ap, 0.0), writes=[qC.res])
        self.S.op("dve", lambda e: e.memset(kst.ap, 0.0), writes=[kst.res])
        groups = ((0, 512), (512, 1024), (1024, 1536), (1536, INW))
        rm = self.rmask
        stv = d["st"]
        blocks = [(tile, tb) for tile in [2, 3, 4, 5, 0, 1] for tb in range(4)]
        nblk = len(blocks)

        def front_pe(i):
            tile, tb = blocks[i]
            tp = tp2[i % 2]
            if tb == 0:
                self.pre(tile, hT, [self.ps[4], self.ps[5]])
            for gi, (c0, c1) in enumerate(groups):
                p = self.ps[gi]
                for k in range(8):
                    self.mm(p.ap[:, 0:c1 - c0], hT.ap[:, k, tb * 128:(tb + 1) * 128], w_in.ap[:, k, c0:c1], k == 0, k == 7,
                            [hT.res, w_in.res], [p.res])
                self.cp(tp.ap[:, c0:c1], p.ap[:, 0:c1 - c0], [p.res], [tp.res], eng="act")
            if tile >= 2:
                sb = tile * 4 + tb - NPB
                self.dmas("sp", [(r32[i % 2].ap, d["rope32"][:, :, sb:sb + 1, :]), (r64[i % 2].ap, d["rope64"][:, :, sb:sb + 1, :])],
                          [], [r32[i % 2].res, r64[i % 2].res], r32[i % 2].res)

        def front_dve(i):
            tile, tb = blocks[i]
            samp = tile >= 2
            blk = tile * 4 + tb
            tp, rq, nq, ckv = tp2[i % 2], rq2[i % 2], nq2[i % 2], ckv2[i % 2]
            if samp:
                self.rope(rq.ap[:, 0:512], tp.ap[:, 0:512], 16, 8, r32[i % 2], 0, [tp.res], [rq.res], tmp.ap)
                self.rope(rq.ap[:, 512:1152], tp.ap[:, 768:1408], 10, 16, r64[i % 2], 0, [tp.res], [rq.res], tmp.ap)
                self.rope(rq.ap[:, 1152:1184], tp.ap[:, 1920:1952], 1, 8, r32[i % 2], 0, [tp.res], [rq.res], tmp.ap)
            self.rms(nq.ap, tp.ap[:, 1536:1792], 256, gq, [tp.res], [nq.res], 1)
            self.rms(ckv.ap, tp.ap[:, 1792:1920], 128, gkv, [tp.res], [ckv.res], 2)
            if not samp:
                seq, t0 = blk // 2, (blk % 2) * 128
                self.dmas("sp", [(stv[seq, l, t0:t0 + 128, 0:512], tp.ap[:, 256:768]),
                                 (stv[seq, l, t0:t0 + 128, 512:768], tp.ap[:, 1280:1536]),
                                 (stv[seq, l, t0:t0 + 128, 896:928], tp.ap[:, 1920:1952]),
                                 (stv[seq, l, t0:t0 + 128, 768:896], ckv.ap)],
                          [tp.res, ckv.res], [], tp.res)

        def back(i):
            tile, tb = blocks[i]
            samp = tile >= 2
            blk = tile * 4 + tb
            tp, rq, nq, ckv, cq, kpe96 = tp2[i % 2], rq2[i % 2], nq2[i % 2], ckv2[i % 2], cq2[i % 2], kpe2[i % 2]
            if samp:
                s_aq, s_ak, s_bq, s_bk, s_kpe = (rq.ap[:, 0:256], rq.ap[:, 256:512], rq.ap[:, 512:1024],
                                                 rq.ap[:, 1024:1152], rq.ap[:, 1152:1184])
                sres = [rq.res]
            else:
                s_aq, s_ak, s_bq, s_bk, s_kpe = (tp.ap[:, 0:256], tp.ap[:, 256:512], tp.ap[:, 768:1280],
                                                 tp.ap[:, 1280:1408], tp.ap[:, 1920:1952])
                sres = [tp.res]
            p4, p5, p6, p7 = self.ps[4], self.ps[5], self.ps[6], self.ps[7]
            p4b = p4.ap[:, 0:128].bitcast(BF16)
            for kk in range(2):
                self.tr(p4b[:, kk * 128:(kk + 1) * 128], nq.ap[:, kk * 128:(kk + 1) * 128], self.identB.ap,
                        [nq.res, self.identB.res], [p4.res])
            self.cp(nqT.ap, p4b.rearrange("p (a b) -> p a b", a=2), [p4.res], [nqT.res])
            for kk in range(2):
                self.mm(p5.ap[:, 0:384], nqT.ap[:, kk, :], wq.ap[:, kk, :], kk == 0, kk == 1, [nqT.res, wq.res], [p5.res])
            self.cp(cq.ap, p5.ap[:, 0:384], [p5.res], [cq.res], eng="act")
            if samp:
                cq4 = cq.ap.rearrange("p (h e) -> p h e", e=96)
                p54 = p5.ap[:, 0:384].rearrange("p (h e) -> p h e", e=96)
                self.rope_strided(cq4[:, :, 64:96], p54[:, :, 64:96], 4, 8, r32[i % 2], 0, [p5.res], [cq.res], tmp.ap)
            for c in range(2):
                self.tr(p6.ap[:, c * 128:(c + 1) * 128], s_aq[:, c * 128:(c + 1) * 128], self.identF.ap, sres + [self.identF.res], [p6.res])
                self.tr(p6.ap[:, 256 + c * 128:256 + (c + 1) * 128], s_ak[:, c * 128:(c + 1) * 128], self.identF.ap, sres + [self.identF.res], [p6.res])
            for c in range(4):
                self.tr(p7.ap[:, c * 128:(c + 1) * 128], s_bq[:, c * 128:(c + 1) * 128], self.identF.ap, sres + [self.identF.res], [p7.res])
            for c in range(2):
                for i_ in range(4):
                    self.ts(qA.ap[:, c * 4 + i_, :], p6.ap[:, c * 128:(c + 1) * 128], rm.ap[:, i_:i_ + 1], None, ALU.mult, None,
                            [p6.res, rm.res], [qA.res])
            for hq in range(8):
                self.act(qB.ap[:, hq, :], p7.ap[:, (hq // 2) * 128:(hq // 2 + 1) * 128], AF.Identity, [p7.res, rm.res], [qB.res],
                         scale=rm.ap[:, 4 + hq % 2:5 + hq % 2])
            self.cp(kst.ap[:, 0:2, :], p6.ap[:, 256:512].rearrange("p (a b) -> p a b", a=2), [p6.res], [kst.res])
            self.cp(kpe96.ap[:, 64:96], s_kpe, sres, [kpe96.res])
            self.tr(p4.ap[:, 0:128], s_bk, self.identF.ap, sres + [self.identF.res], [p4.res])
            self.tr(p4.ap[:, 128:256], ckv.ap, self.identF.ap, [ckv.res, self.identF.res], [p4.res])
            self.tr(p4.ap[0:96, 256:384], kpe96.ap, self.identF.ap, [kpe96.res, self.identF.res], [p4.res])
            self.cp(kst.ap[:, 2:4, :], p4.ap[:, 0:256].rearrange("p (a b) -> p a b", a=2), [p4.res], [kst.res])
            self.cp(kst.ap[64:96, 4, :], p4.ap[64:96, 256:384], [p4.res], [kst.res])
            for h in range(4):
                self.tr(p5.ap[0:96, h * 128:(h + 1) * 128], cq.ap[:, h * 96:(h + 1) * 96], self.identF.ap, [cq.res, self.identF.res], [p5.res])
            self.cp(qC.ap[0:96, :, :], p5.ap[0:96, :].rearrange("p (a b) -> p a b", a=4), [p5.res], [qC.res], eng="act")
            self.cp(vst.ap[:, 0:4, 0:64], tp.ap[:, 512:768].rearrange("p (h e) -> p h e", e=64), [tp.res], [vst.res])
            self.cp(vst.ap[:, 4:6, 0:64], tp.ap[:, 1408:1536].rearrange("p (h e) -> p h e", e=64), [tp.res], [vst.res])
            tok = blk * 128
            QS = d["QS"]
            name = "XB" if samp else "PB"
            xt0 = (blk - NPB) * 128 if samp else blk * 128
            pairs = [
                (QS[0:1024, tok:tok + 128].rearrange("(i p) t -> p i t", p=128), qA.ap),
                (QS[1024:2048, tok:tok + 128].rearrange("(i p) t -> p i t", p=128), qB.ap),
                (QS[2048:2560, tok:tok + 128].rearrange("(i p) t -> p i t", p=128), qC.ap),
                (self.xrows(name, xt0, 128, 0, 256).rearrange("(c p) t -> p c t", p=128), kst.ap[:, 0:2, :]),
                (self.xrows(name, xt0, 128, 256, 256).rearrange("(c p) t -> p c t", p=128), kst.ap[:, 2:4, :]),
                (self.xrows(name, xt0, 128, 512, 32), kst.ap[64:96, 4, :]),
                (self.xv(name, xt0, 128), vst.ap.rearrange("p h e -> p (h e)")),
            ]
            wr = [self.QSr[blk], (self.XBr[blk - NPB] if samp else self.PBr[blk])]
            self.dmas("sp", pairs, [qA.res, qB.res, qC.res, kst.res, vst.res], wr, qA.res)

        front_pe(0)
        for i in range(nblk):
            front_dve(i)
            if i + 1 < nblk:
                front_pe(i + 1)
            back(i)

    def rope_strided(self, dst, src, H, Q, tbl, sb, R, W, tmp):
        dim = 4 * Q
        xs = src.rearrange("p h (a w q) -> p h a w q", a=2, w=2)
        xd = dst.rearrange("p h (a w q) -> p h a w q", a=2, w=2)
        xt = tmp[:, 0:H * dim].rearrange("p (h a w q) -> p h a w q", h=H, a=2, w=2)
        cs = tbl.ap[:, 0, sb, :].rearrange("p (a w q) -> p a w q", a=2, w=2)
        ss = tbl.ap[:, 1, sb, :].rearrange("p (a w q) -> p a w q", a=2, w=2)
        for a in range(2):
            cb = cs[:, a].unsqueeze(1).to_broadcast([128, H, 2, Q])
            self.tt(xd[:, :, a], xs[:, :, a], cb, ALU.mult, R + [tbl.res], W)
            for w in range(2):
                sbb = ss[:, a, w].unsqueeze(1).to_broadcast([128, H, Q])
                self.tt(xt[:, :, a, w], xs[:, :, a, 1 - w], sbb, ALU.mult, R + [tbl.res], [self.tmp_res])
            self.tt(xd[:, :, a], xd[:, :, a], xt[:, :, a], ALU.add, [self.tmp_res], W)

    def gather(self):
        d = self.d
        for pi, (r0, r1) in enumerate(XG_PIECES):
            src = d["XB"][r0:r1, :]
            dst = d["XG%d" % pi]
            self.S.op("pool", lambda e, src=src, dst=dst: e.collective_compute(
                "AllGather", ALU.bypass, replica_groups=[[0, 1], [2, 3], [4, 5], [6, 7]], ins=[src], outs=[dst]),
                reads=list(self.XBr), writes=[self.dres["XG"]], dma=self.dres["XG"], inc=1)

    def layer_consts(self, l):
        d = self.d
        lam_init = 0.8 - 0.6 * float(np.exp(-0.3 * l))
        al = self.atile("al", [128], F32)
        self.gsub = self.atile("gsub", [64], F32)
        self.es = self.atile("es", [8], F32)
        self.lam = self.atile("lamc", [4], F32)
        self.dma("sp", al.ap, d["a_lambda"][l].partition_broadcast(128), [], [al.res], al.res)
        self.dma("sp", self.gsub.ap, d["a_subln_g"][l].partition_broadcast(128), [], [self.gsub.res], self.gsub.res)
        self.dma("sp", self.es.ap, d["b_sink"][l].partition_broadcast(128), [], [self.es.res], self.es.res)
        self.act(self.es.ap, self.es.ap, AF.Exp, [], [self.es.res])
        self.ts(self.gsub.ap, self.gsub.ap, 1.0 - lam_init, None, ALU.mult, None, [], [self.gsub.res])
        lm = self.lam
        for i in range(2):
            self.tt(al.ap[:, i * 64:i * 64 + 32], al.ap[:, i * 64:i * 64 + 32], al.ap[:, i * 64 + 32:i * 64 + 64], ALU.mult, [], [al.res])
            self.S.op("dve", lambda e, i=i: e.reduce_sum(lm.ap[:, i:i + 1], al.ap[:, i * 64:i * 64 + 32], axis=AX.X),
                      reads=[al.res], writes=[lm.res])
        self.act(lm.ap[:, 0:2], lm.ap[:, 0:2], AF.Exp, [], [lm.res])
        self.tt(lm.ap[:, 2:3], lm.ap[:, 0:1], lm.ap[:, 1:2], ALU.subtract, [], [lm.res])
        self.ts(lm.ap[:, 3:4], lm.ap[:, 2:3], lam_init, -1.0, ALU.add, ALU.mult, [], [lm.res])
        self.pt = [self.atile("pt%d" % i, [512], BF16) for i in range(4)]
        self.pti = 0
        self.sci = 0
        self.rec = self.atile("rec", [8], F32)
        self.bmask4 = self.atile("bmask4", [4, 4, 128], BF16)
        for m_ in range(4):
            for g_ in range(4):
                self.cp(self.bmask4.ap[:, m_, g_, :], self.bmask.ap[:, m_, :], [self.bmask.res], [self.bmask4.res])
        self.ctmp = self.atile("ctmp", [2, 256], F32)

    def attend(self, q_ap, q_res, kblocks, nq, scale, obank):
        nqs = nq // 128
        nkb = len(kblocks)
        LA = 2
        pts = [None] * nkb
        for i in range(nkb + LA):
            if i < nkb:
                kT, kres, V, vres, mask = kblocks[i]
                sc = self.ps[self.sci % 4]
                self.sci += 1
                self.mm(sc.ap[:, 0:nq], kT, q_ap, True, mask is None, kres + [q_res], [sc.res])
                if mask is not None:
                    self.mm(sc.ap[:, 0:nq], self.identB.ap, mask, False, True, [self.identB.res, self.bmask4.res], [sc.res])
                pt = self.pt[self.pti % 4]
                self.pti += 1
                self.act(pt.ap[:, 0:nq], sc.ap[:, 0:nq], AF.Exp, [sc.res], [pt.res], scale=float(scale))
                pts[i] = pt
            j = i - LA
            if j >= 0:
                kT, kres, V, vres, mask = kblocks[j]
                pt = pts[j]
                for qs in range(nqs):
                    Vq = V[qs] if isinstance(V, (list, tuple)) else V
                    self.mm(obank.ap[:, qs * 65:(qs + 1) * 65], pt.ap[:, qs * 128:(qs + 1) * 128], Vq, (j == 0 and qs == 0),
                            j == nkb - 1, [pt.res] + vres, [obank.res], skip=True)

    def load_ctxT(self, src_dram, width, dst_fn, dres):
        ct = self.ctmp
        self.dma("sp", ct.ap[:, :, 0:width], src_dram.rearrange("(b p) f -> p b f", p=128), [], [ct.res], ct.res)
        p = self.ps[2]
        nch = width // 128
        for kb in range(2):
            for c in range(nch):
                self.tr(p.ap[:, (kb * nch + c) * 128:(kb * nch + c + 1) * 128], ct.ap[:, kb, c * 128:(c + 1) * 128], self.identF.ap,
                        [ct.res, self.identF.res], [p.res])
        for kb in range(2):
            for c in range(nch):
                self.cp(dst_fn(c, kb), p.ap[:, (kb * nch + c) * 128:(kb * nch + c + 1) * 128], [p.res], [dres])

    def attn_A(self, l, segs, ctx, qtiles, o_t, oblk0, qtok_unused):
        d = self.d
        nk = (256 if ctx else 0) + sum(s[2] for s in segs)
        nkb = nk // 128
        kT = self.atile("kT_A", [2, nk], BF16)
        V = self.atile("V_A", [nkb, 4, 65], BF16)
        qm = [self.atile("qmA%d" % i, [512], BF16) for i in range(2)]
        oa = self.atile("oa", [4, 4, 64], F32)
        o1 = self.atile("o1", [4, 64], F32)
        o2 = self.atile("o2", [64], F32)
        ssq = self.atile("ssq", [16], F32)
        sq = self.atile("sqA", [4, 64], F32)
        koff = 0
        if ctx:
            self.load_ctxT(d["ck_a"][l], 256, lambda c, kb: kT.ap[:, c, kb * 128:(kb + 1) * 128], kT.res)
            self.dmas("pool", [(V.ap[:, b_, :, 0:64], d["cv_a"][l][b_ * 128:(b_ + 1) * 128, :].rearrange("p (h e) -> p h e", e=64))
                               for b_ in range(2)], [], [V.res], V.res)
            self.S.op("dve", lambda e: e.memset(V.ap[:, 0:2, :, 64:65], 1.0), writes=[V.res])
            koff = 256
        pairs = []
        rd = []
        for (name, t0, nt) in segs:
            pairs.append((kT.ap[:, :, koff:koff + nt], self.xrows(name, t0, nt, 0, 256).rearrange("(c p) t -> p c t", p=128)))
            pairs.append((V.ap[:, koff // 128:(koff + nt) // 128, :, :],
                          self.xv(name, t0, nt)[:, 0:260].rearrange("(b p) (h e) -> p b h e", p=128, e=65)))
            rd += self.xres_of(name, t0, nt)
            koff += nt
        self.dmas("sp", pairs, rd, [kT.res, V.res], kT.res)
        QS = d["QS"]
        scale = 32 ** -0.5
        qi = 0
        for ti, (qt0, nq) in enumerate(qtiles):
            nqs = nq // 128
            qres = [self.QSr[b] for b in range(qt0 // 128, (qt0 + nq) // 128)]
            for i in range(8):
                c, h, j = i // 4, (i // 4) * 2 + (i % 4) // 2, i % 2
                q = qm[qi % 2]
                qi += 1
                self.dma("sp", q.ap[:, 0:nq], QS[i * 128:(i + 1) * 128, qt0:qt0 + nq], qres, [q.res], q.res)
                ob = self.ps[4 + (i % 2)]
                kbl = [(kT.ap[:, c, kb * 128:(kb + 1) * 128], [kT.res], V.ap[:, kb, h, :], [V.res], None) for kb in range(nkb)]
                self.attend(q.ap[:, 0:nq], q.res, kbl, nq, scale, ob)
                for qs in range(nqs):
                    self.S.op("dve", lambda e, qs=qs, ob=ob: e.reciprocal(self.rec.ap[:, qs:qs + 1], ob.ap[:, qs * 65 + 64:qs * 65 + 65]),
                              reads=[ob.res], writes=[self.rec.res])
                    if j == 0:
                        self.ts(o1.ap[:, qs, :], ob.ap[:, qs * 65:qs * 65 + 64], self.rec.ap[:, qs:qs + 1], None, ALU.mult, None,
                                [ob.res, self.rec.res], [o1.res])
                    else:
                        self.ts(o2.ap, ob.ap[:, qs * 65:qs * 65 + 64], self.rec.ap[:, qs:qs + 1], None, ALU.mult, None,
                                [ob.res, self.rec.res], [o2.res])
                        self.stt(oa.ap[:, qs, h, :], o2.ap, self.lam.ap[:, 3:4], o1.ap[:, qs, :], ALU.mult, ALU.add,
                                 [o2.res, o1.res, self.lam.res], [oa.res])
            n16 = nqs * 4
            for qs in range(nqs):
                self.tt(sq.ap, oa.ap[:, qs], oa.ap[:, qs], ALU.mult, [oa.res], [sq.res])
                self.S.op("dve", lambda e, qs=qs: e.reduce_sum(ssq.ap[:, qs * 4:qs * 4 + 4], sq.ap, axis=AX.X),
                          reads=[sq.res], writes=[ssq.res])
            self.ts(ssq.ap[:, 0:n16], ssq.ap[:, 0:n16], 1.0 / 64, RMS_EPS, ALU.mult, ALU.add, [], [ssq.res])
            self.act(ssq.ap[:, 0:n16], ssq.ap[:, 0:n16], AF.Sqrt, [], [ssq.res])
            self.S.op("dve", lambda e, n16=n16: e.reciprocal(ssq.ap[:, 0:n16], ssq.ap[:, 0:n16]), reads=[], writes=[ssq.res])
            for qs in range(nqs):
                for h in range(4):
                    ob_ = oblk0 + ti * nqs + qs
                    self.stt(o_t.ap[:, ob_, h * 64:(h + 1) * 64], oa.ap[:, qs, h, :], ssq.ap[:, qs * 4 + h:qs * 4 + h + 1], self.gsub.ap,
                             ALU.mult, ALU.mult, [oa.res, ssq.res, self.gsub.res], [o_t.res])

    def attn_C(self, l, segs, ctx, qtiles, o_t, oblk0, qtok_unused):
        d = self.d
        nk = (256 if ctx else 0) + sum(s[2] for s in segs)
        nkb = nk // 128
        wkv = self.atile("wkvC", [512], BF16)
        self.dma("pool", wkv.ap, d["c_w_kv_up"][l], [], [wkv.res], wkv.res)
        wkv4 = wkv.ap.rearrange("p (h e) -> p h e", e=128)
        ckT = [self.atile("ckT%d" % i, [512], BF16) for i in range(2)]
        cpe = self.atile("cpe", [2, 96], F32)
        qm = [self.atile("qmC%d" % i, [512], BF16) for i in range(2)]
        kT = self.atile("kT_C", [2, nk], BF16)
        V = self.atile("V_C", [nkb, 2, 65], BF16)
        QS = d["QS"]
        scale = 96 ** -0.5
        qi = 0
        ci = 0
        for hp in range(2):
            self.S.op("dve", lambda e: e.memset(V.ap[:, :, :, 64:65], 1.0), writes=[V.res])
            ktiles = []
            koff = 0
            if ctx:
                ktiles.append((None, 0, 256, 0))
                koff = 256
            for (name, t0, nt) in segs:
                for s0 in range(0, nt, 512):
                    n = min(512, nt - s0)
                    ktiles.append((name, t0 + s0, n, koff))
                    koff += n
            if ctx:
                self.S.op("dve", lambda e: e.memset(cpe.ap, 0.0), writes=[cpe.res])
                self.dma("sp", cpe.ap[:, :, 64:96], d["c_kpe"][l].rearrange("(b p) f -> p b f", p=128), [], [cpe.res], cpe.res)
                p = self.ps[2]
                for kb in range(2):
                    self.tr(p.ap[0:96, kb * 128:(kb + 1) * 128], cpe.ap[:, kb, :], self.identF.ap, [cpe.res, self.identF.res], [p.res])
                for hh in range(2):
                    self.cp(kT.ap[64:96, hh, 0:256], p.ap[64:96, 0:256], [p.res], [kT.res])
            pairs = []
            rd = []
            ko2 = 256 if ctx else 0
            for (name, t0, nt) in segs:
                for hh in range(2):
                    pairs.append((kT.ap[64:96, hh, ko2:ko2 + nt], self.xrows(name, t0, nt, 512, 32)))
                rd += self.xres_of(name, t0, nt)
                ko2 += nt
            self.dmas("sp", pairs, rd, [kT.res], kT.res)
            for (name, t0, n, ko) in ktiles:
                ck = ckT[ci % 2]
                ci += 1
                if name is None:
                    self.load_ctxT(d["c_ckv"][l], 128, lambda c, kb: ck.ap[:, kb * 128:(kb + 1) * 128], ck.res)
                else:
                    self.dma("sp", ck.ap[:, 0:n], self.xrows(name, t0, n, 384, 128), self.xres_of(name, t0, n), [ck.res], ck.res)
                for hh in range(2):
                    h = hp * 2 + hh
                    p = self.ps[2 + hh]
                    self.mm(p.ap[:, 0:n], wkv.ap[:, h * 128:(h + 1) * 128], ck.ap[:, 0:n], True, True, [wkv.res, ck.res], [p.res])
                    self.cp(kT.ap[0:64, hh, ko:ko + n], p.ap[0:64, 0:n], [p.res], [kT.res], eng=("act" if hh else "dve"))
                p = self.ps[6]
                for b in range(n // 128):
                    self.mm(p.ap[:, b * 128:(b + 1) * 128], ck.ap[:, b * 128:(b + 1) * 128], wkv4[:, hp * 2:hp * 2 + 2, 64:128], True, True,
                            [ck.res, wkv.res], [p.res])
                self.cp(V.ap[:, ko // 128:(ko + n) // 128, :, 0:64],
                        p.ap[:, 0:n].rearrange("p (b h e) -> p b h e", h=2, e=64), [p.res], [V.res])
            for ti, (qt0, nq) in enumerate(qtiles):
                nqs = nq // 128
                qres = [self.QSr[b] for b in range(qt0 // 128, (qt0 + nq) // 128)]
                for hh in range(2):
                    h = hp * 2 + hh
                    q = qm[qi % 2]
                    qi += 1
                    self.dma("sp", q.ap[0:96, 0:nq], QS[2048 + h * 128:2048 + h * 128 + 96, qt0:qt0 + nq], qres, [q.res], q.res)
                    ob = self.ps[4 + (qi % 2)]
                    kbl = [(kT.ap[0:96, hh, kb * 128:(kb + 1) * 128], [kT.res], V.ap[:, kb, hh, :], [V.res], None) for kb in range(nkb)]
                    self.attend(q.ap[0:96, 0:nq], q.res, kbl, nq, scale, ob)
                    for qs in range(nqs):
                        ob_ = oblk0 + ti * nqs + qs
                        self.S.op("dve", lambda e, qs=qs, ob=ob: e.reciprocal(self.rec.ap[:, qs:qs + 1], ob.ap[:, qs * 65 + 64:qs * 65 + 65]),
                                  reads=[ob.res], writes=[self.rec.res])
                        self.ts(o_t.ap[:, ob_, 768 + h * 64:768 + (h + 1) * 64], ob.ap[:, qs * 65:qs * 65 + 64], self.rec.ap[:, qs:qs + 1], None,
                                ALU.mult, None, [ob.res, self.rec.res], [o_t.res])

    def attn_B(self, l, samp, seq, o_t, oblk0):
        d = self.d
        QS = d["QS"]
        scale = 64 ** -0.5
        if samp:
            nkb = 20
        else:
            nkb = 2
        nk = nkb * 128
        kT = self.atile("kT_B", [2, nk], BF16)
        V = self.atile("V_B", [nkb, 2, 65], BF16)
        qm = [self.atile("qmB%d" % i, [8, 512], BF16) for i in range(2)]
        dup = self.atile("dupB", [2, 64], F32)
        pairs = []
        rd = []
        if samp:
            ct = self.ctmp
            self.dma("sp", ct.ap[:, :, 0:128], d["ck_b"][l].rearrange("(b p) f -> p b f", p=128), [], [ct.res], ct.res)
            p = self.ps[2]
            for kvh in range(2):
                for kb in range(2):
                    for r in range(2):
                        self.cp(dup.ap[:, r, :], ct.ap[:, kb, kvh * 64:(kvh + 1) * 64], [ct.res], [dup.res])
                    self.tr(p.ap[:, (kvh * 2 + kb) * 128:(kvh * 2 + kb + 1) * 128], dup.ap.rearrange("p a b -> p (a b)"), self.identF.ap,
                            [dup.res, self.identF.res], [p.res])
            for kvh in range(2):
                self.cp(kT.ap[:, kvh, 0:256], p.ap[:, kvh * 256:(kvh + 1) * 256], [p.res], [kT.res])
            self.dmas("pool", [(V.ap[:, b_, :, 0:64], d["cv_b"][l][b_ * 128:(b_ + 1) * 128, :].rearrange("p (h e) -> p h e", e=64))
                               for b_ in range(2)], [], [V.res], V.res)
            self.S.op("dve", lambda e: e.memset(V.ap[:, 0:2, :, 64:65], 1.0), writes=[V.res])
            srcs = [("XB", 0, 2048, 256), ("XG", 15 * 128, 128, 18 * 128), ("XG", 16 * 128, 128, 19 * 128)]
        else:
            srcs = [("PB", seq * 256, 256, 0)]
        for (name, t0, nt, ko) in srcs:
            for kvh in range(2):
                for r in range(2):
                    pairs.append((kT.ap[r * 64:(r + 1) * 64, kvh, ko:ko + nt], self.xrows(name, t0, nt, 256 + kvh * 64, 64)))
            pairs.append((V.ap[:, ko // 128:(ko + nt) // 128, :, :],
                          self.xv(name, t0, nt)[:, 260:390].rearrange("(b p) (h e) -> p b h e", p=128, e=65)))
            rd += self.xres_of(name, t0, nt)
        self.dmas("sp", pairs, rd, [kT.res, V.res], kT.res)
        nblk = 16 if samp else 2
        ntile = 4 if samp else 1
        per = nblk // ntile
        for ti in range(ntile):
            tok0 = (1024 + ti * 512) if samp else seq * 256
            nq = per * 128
            q = qm[ti % 2]
            qres = [self.QSr[b] for b in range(tok0 // 128, (tok0 + nq) // 128)]
            self.dma("sp", q.ap[:, :, 0:nq], QS[1024:2048, tok0:tok0 + nq].rearrange("(i p) t -> p i t", p=128), qres, [q.res], q.res)
            for bi in range(per):
                i = ti * per + bi
                for kvh in range(2):
                    ob = self.ps[4 + (kvh % 2)]
                    bm = self.bmask4.ap

                    def kb_(idx, m=None):
                        return (kT.ap[:, kvh, idx * 128:(idx + 1) * 128], [kT.res], V.ap[:, idx, kvh, :], [V.res],
                                None if m is None else bm[:, m].rearrange("p g q -> p (g q)"))
                    if samp:
                        kbl = [kb_(0), kb_(1)]
                        kbl.append(kb_(2 + i - 1, 0) if i > 0 else kb_(18, 2))
                        kbl.append(kb_(2 + i))
                        kbl.append(kb_(2 + i + 1, 1) if i < 15 else kb_(19, 3))
                    else:
                        kbl = [kb_(0), kb_(1)]
                    qv = q.ap[:, 4 * kvh:4 * kvh + 4, bi * 128:(bi + 1) * 128]
                    self.attend(qv, q.res, kbl, 512, scale, ob)
                    for g in range(4):
                        hq = 4 * kvh + g
                        self.ts(self.rec.ap[:, g:g + 1], ob.ap[:, g * 65 + 64:g * 65 + 65], self.es.ap[:, hq:hq + 1], None, ALU.add, None,
                                [ob.res, self.es.res], [self.rec.res])
                    self.S.op("dve", lambda e: e.reciprocal(self.rec.ap[:, 0:4], self.rec.ap[:, 0:4]), reads=[], writes=[self.rec.res])
                    for g in range(4):
                        hq = 4 * kvh + g
                        self.ts(o_t.ap[:, oblk0 + i, 256 + hq * 64:256 + (hq + 1) * 64], ob.ap[:, g * 65:g * 65 + 64], self.rec.ap[:, g:g + 1], None,
                                ALU.mult, None, [ob.res, self.rec.res], [o_t.res])

    def wo_phase(self, l, tiles, o_t, blk0):
        d = self.d
        w_o = self.atile("w_o", [8, D], BF16)
        oT = self.atile("oT", [8, 512], BF16)
        yb = [self.atile("ybm%d" % i, [D], F32) for i in range(4)]
        wsrc = d["w_o"][l].rearrange("(k p) n -> p k n", p=128)
        self.dmas("pool", [(w_o.ap[:, :, 0:512], wsrc[:, :, 0:512]), (w_o.ap[:, :, 512:1024], wsrc[:, :, 512:1024])], [], [w_o.res], w_o.res)
        for tile in tiles:
            cond = 1 if tile < 2 else 0
            for k in range(8):
                p = self.ps[k % 2]
                pb = p.ap[:, 0:256].bitcast(BF16)
                for tb in range(4):
                    ob_ = tile * 4 + tb - blk0
                    self.tr(pb[:, tb * 128:(tb + 1) * 128], o_t.ap[:, ob_, k * 128:(k + 1) * 128], self.identB.ap,
                            [o_t.res, self.identB.res], [p.res])
                self.cp(oT.ap[:, k, :], pb, [p.res], [oT.res], eng=("act" if k % 2 else "dve"))
            for hf in range(2):
                for tb in range(4):
                    p = self.ps[4 + tb] if hf == 0 else self.ps[2 + (tb % 2)]
                    for k in range(8):
                        self.mm(p.ap, oT.ap[:, k, tb * 128:(tb + 1) * 128], w_o.ap[:, k, hf * 512:(hf + 1) * 512], k == 0, k == 7,
                                [oT.res, w_o.res], [p.res])
                    self.tt(yb[tb].ap[:, hf * 512:(hf + 1) * 512], p.ap, self.gate_bc.ap[:, cond, hf * 512:(hf + 1) * 512], ALU.mult,
                            [p.res, self.gate_bc.res], [yb[tb].res])
            for tb in range(4):
                self.post(tile * 4 + tb, cond, yb[tb])


def build_nc():
    b = Builder()
    nc = b.build()
    return nc, list(b.in_names)


def _rope_tables(dim, pos0, n):
    t = np.arange(pos0, pos0 + n)
    row = (t // 64).astype(np.float32)
    col = (t % 64).astype(np.float32)
    a = dim // 2
    inv = np.power(np.float32(10000.0), -np.arange(0, a, 2, dtype=np.float32) / np.float32(a)).astype(np.float32)
    ar = row[:, None] * inv[None, :]
    ac = col[:, None] * inv[None, :]
    ang = np.concatenate([ar, ar, ac, ac], axis=-1).astype(np.float32)
    cos = np.cos(ang).astype(np.float32)
    sin = np.sin(ang).astype(np.float32)
    q = a // 2
    sgn = np.ones(dim, np.float32)
    sgn[0:q] = -1.0
    sgn[a:a + q] = -1.0
    ss = sin * sgn[None, :]
    out = np.stack([cos, ss], 0).reshape(2, n // 128, 128, dim).transpose(2, 0, 1, 3)
    return np.ascontiguousarray(out, dtype=np.float32)


def _consts(core):
    half = core % 2
    ident = np.eye(128, dtype=np.float32)
    k = np.arange(128)[:, None]
    q = np.arange(128)[None, :]
    mL = np.where(k >= q, 0.0, NEG).astype(np.float32)
    mR = np.where(k <= q, 0.0, NEG).astype(np.float32)
    full = np.full((128, 128), NEG, np.float32)
    mLe = full if half == 0 else mL
    mRe = full if half == 1 else mR
    bmask = np.ascontiguousarray(np.stack([mL, mR, mLe, mRe], 1))
    rmask = np.zeros((128, 6), np.float32)
    for i in range(4):
        rmask[32 * i:32 * i + 32, i] = 1.0
    rmask[0:64, 4] = 1.0
    rmask[64:128, 5] = 1.0
    return ident, bmask, rmask


_NC_CACHE = {}


def kernel(**inp):
    f = lambda a: np.ascontiguousarray(np.asarray(a, dtype=np.float32))
    x_prompt = f(inp["x_prompt"]); x_sample = f(inp["x_sample"])
    if "nc" not in _NC_CACHE:
        _NC_CACHE["nc"] = build_nc()
    nc, in_names = _NC_CACHE["nc"]
    shared = {
        "w_mod": f(inp["w_mod"]),
        "bmT": np.ascontiguousarray(f(inp["b_mod"]).reshape(DEPTH, 72, 128).transpose(2, 0, 1)),
        "ln_g": f(inp["ln_g"]), "ln_b": f(inp["ln_b"]),
        "ffn_w1": f(inp["ffn_w1"]), "ffn_w3": f(inp["ffn_w3"]), "ffn_w2": f(inp["ffn_w2"]),
        "w_in": f(inp["w_in"]), "w_o": f(inp["w_o"]),
        "a_lambda": f(inp["a_lambda"]).reshape(DEPTH, 128), "a_subln_g": f(inp["a_subln_g"]),
        "b_sink": f(inp["b_sink"]), "c_q_norm_g": f(inp["c_q_norm_g"]), "c_w_q_up": f(inp["c_w_q_up"]),
        "c_kv_norm_g": f(inp["c_kv_norm_g"]), "c_w_kv_up": f(inp["c_w_kv_up"]).reshape(DEPTH, 128, 512),
    }
    c = f(inp["c"]); c_ctx = f(inp["c_ctx"])
    in_maps = []
    for core in range(8):
        b = core // 2
        half = core % 2
        m = dict(shared)
        m["xin"] = np.ascontiguousarray(np.concatenate(
            [x_prompt[4 * core:4 * core + 4].reshape(1024, D), x_sample[b, half * 2048:(half + 1) * 2048]], 0))
        cond = np.stack([c[b], c_ctx], 0)
        m["condT"] = np.ascontiguousarray(cond.reshape(2, 8, 128).transpose(2, 1, 0))
        m["ck_a"] = f(inp["cache_a_k"])[b].reshape(DEPTH, 256, 256)
        m["cv_a"] = f(inp["cache_a_v"])[b].reshape(DEPTH, 256, 256)
        m["ck_b"] = f(inp["cache_b_k"])[b].reshape(DEPTH, 256, 128)
        m["cv_b"] = f(inp["cache_b_v"])[b].reshape(DEPTH, 256, 128)
        m["c_ckv"] = f(inp["cache_c_kv"])[b]
        m["c_kpe"] = f(inp["cache_c_kpe"])[b]
        ident, bmask, rmask = _consts(core)
        m["ident"] = ident; m["bmask"] = bmask; m["rmask"] = rmask
        m["rope32"] = _rope_tables(32, half * 2048, 2048)
        m["rope64"] = _rope_tables(64, half * 2048, 2048)
        in_maps.append({k: np.ascontiguousarray(m[k]) for k in in_names})
    res = run_bass_kernel_spmd(nc, in_maps, core_ids=list(range(8)))
    ys = [np.asarray(r["y"]) for r in res.results]
    sts = [np.asarray(r["st"]) for r in res.results]
    y_prompt = np.concatenate([y[:1024].reshape(4, 256, D) for y in ys], 0)
    y_sample = np.stack([np.concatenate([ys[2 * b][1024:], ys[2 * b + 1][1024:]], 0) for b in range(4)], 0)
    stt = np.concatenate(sts, 0)
    new_a_k = stt[..., 0:256].reshape(32, DEPTH, 256, 4, 64)
    new_a_v = stt[..., 256:512].reshape(32, DEPTH, 256, 4, 64)
    new_b_k = stt[..., 512:640].reshape(32, DEPTH, 256, 2, 64)
    new_b_v = stt[..., 640:768].reshape(32, DEPTH, 256, 2, 64)
    new_c_kv = stt[..., 768:896]
    new_c_kpe = stt[..., 896:928]
    outs = (y_prompt, y_sample, new_a_k, new_a_v, new_b_k, new_b_v, new_c_kv, new_c_kpe)
    return tuple(np.ascontiguousarray(o, dtype=np.float32) for o in outs)
```

```python
import os
import numpy as np
from contextlib import ExitStack
import concourse.bass as bass
import concourse.mybir as mybir
from concourse.bass_utils import run_bass_kernel_spmd

F32 = mybir.dt.float32
BF16 = mybir.dt.bfloat16
ALU = mybir.AluOpType
AF = mybir.ActivationFunctionType
AX = mybir.AxisListType

D = 1024
DFF = 2816
NJ = DFF // 128
DEPTH = 2
INW = 1952
LN_EPS = 1e-5
RMS_EPS = 1e-6
DN_ALPHA = float((2 * DEPTH) ** 0.25)
NEG = -30000.0
NPB = 8
NSB = 16
NB = NPB + NSB
XK_ROWS = 544
XV_W = 390
X_ROWS = XK_ROWS + XV_W
QS_ROWS = 2560
XG_PIECES = ((0, 256), (256, 512), (512, 739), (739, 934))


class Res:
    __slots__ = ("name", "w", "r", "dsem", "dcnt", "excl")

    def __init__(self, name, excl=False):
        self.name = name
        self.excl = excl
        self.w = None
        self.r = {}
        self.dsem = None
        self.dcnt = 0

    def inherit(self, other):
        if other.w is not None:
            k, v = other.w
            if self.r.get(k, 0) < v:
                self.r[k] = v
        for k, v in other.r.items():
            if self.r.get(k, 0) < v:
                self.r[k] = v


class Sched:
    ENG = ("pe", "act", "dve", "pool", "sp")

    def __init__(self):
        self.q = {e: [] for e in self.ENG}
        self.cnt = {e: 0 for e in self.ENG}
        self.waited = {e: {} for e in self.ENG}
        self.ndsem = 0
        self.semmap = {}

    def _dep(self, eng, k, v):
        if eng == "pe" and k == ("e", "pe"):
            return
        if self.waited[eng].get(k, 0) >= v:
            return
        self.waited[eng][k] = v
        self.q[eng].append(("wait", k, v))

    def op(self, eng, fn, reads=(), writes=(), dma=None, ndma=1, inc=16):
        for r in reads:
            if r.w is not None:
                self._dep(eng, *r.w)
            if r.excl:
                for k, v in r.r.items():
                    if k != ("e", eng):
                        self._dep(eng, k, v)
        for w in writes:
            if w.w is not None:
                self._dep(eng, *w.w)
            for k, v in w.r.items():
                self._dep(eng, k, v)
        if dma is None:
            self.cnt[eng] += 1
            tok = (("e", eng), self.cnt[eng])
            self.q[eng].append(("op", fn, tok[0], 1))
        else:
            ent = self.semmap.get(dma.name)
            if ent is None:
                ent = [("d", self.ndsem), 0, None]
                self.ndsem += 1
                self.semmap[dma.name] = ent
            if ent[2] is not dma:
                if ent[1] > 0:
                    self._dep(eng, ent[0], ent[1])
                ent[2] = dma
            ent[1] += inc * ndma
            tok = (ent[0], ent[1])
            self.q[eng].append(("op", fn, tok[0], inc))
        for r in reads:
            if r.r.get(tok[0], 0) < tok[1]:
                r.r[tok[0]] = tok[1]
        for w in writes:
            w.w = tok
            w.r = {}
        return tok

    def finish(self, eng="sp"):
        for ent in self.semmap.values():
            self._dep(eng, ent[0], ent[1])
        for e in self.ENG:
            if e != eng and self.cnt[e] > 0:
                self._dep(eng, ("e", e), self.cnt[e])

    def emit(self, nc, stack):
        sems = {}
        for e in self.ENG:
            sems[("e", e)] = stack.enter_context(nc.semaphore("s_" + e))
        for i in range(self.ndsem):
            sems[("d", i)] = stack.enter_context(nc.semaphore("d_%d" % i))
        block = stack.enter_context(nc.Block())
        q = self.q

        def run(ename):
            def body(h):
                for it in q[ename]:
                    if it[0] == "wait":
                        h.wait_ge(sems[it[1]], it[2])
                    else:
                        ins = it[1](h)
                        if not isinstance(ins, (list, tuple)):
                            ins = [ins]
                        for i_ in ins:
                            i_.then_inc(sems[it[2]], it[3])
            return body

        block.tensor(run("pe"))
        block.scalar(run("act"))
        block.vector(run("dve"))
        block.gpsimd(run("pool"))
        block.sync(run("sp"))


class T:
    __slots__ = ("ap", "res")

    def __init__(self, ap, res):
        self.ap = ap
        self.res = res


class Builder:
    def __init__(self):
        self.nc = bass.Bass("TRN2", target_bir_lowering=False)
        self.S = Sched()
        self.d = {}
        self.dres = {}
        self.in_names = []

    def din(self, name, shape, dt=F32):
        if os.environ.get("KSTAGE", "full") in ("projonly", "mixonly", "mixprompt") and name.startswith("ffn_w"):
            return
        self.d[name] = self.nc.dram_tensor(name, list(shape), dt, kind="ExternalInput").ap()
        self.dres[name] = Res(name)
        self.in_names.append(name)

    def dout(self, name, shape, dt=F32):
        self.d[name] = self.nc.dram_tensor(name, list(shape), dt, kind="ExternalOutput").ap()
        self.dres[name] = Res(name)

    def dscr(self, name, shape, dt=BF16):
        self.d[name] = self.nc.dram_tensor(name, list(shape), dt).ap()
        self.dres[name] = Res(name)

    def ptile(self, name, free, dt):
        t = self.stack.enter_context(self.nc.sbuf_tensor(name, [128] + list(free), dt))
        return T(t[:], Res(name))

    def arena_reset(self):
        self.acur = 0

    def atile(self, name, free, dt):
        n = int(np.prod(free))
        esz = 4 if dt == F32 else 2
        nb = (n * esz + 63) // 64 * 64
        off = self.acur
        assert off + nb <= self.ABYTES, (name, off, nb, self.ABYTES)
        self.acur += nb
        ap = self.arena[:, off // 2: off // 2 + n * esz // 2]
        if dt == F32:
            ap = ap.bitcast(F32)
        if len(free) == 2:
            ap = ap.rearrange("p (a b) -> p a b", a=free[0])
        elif len(free) == 3:
            ap = ap.rearrange("p (a b c) -> p a b c", a=free[0], b=free[1])
        elif len(free) == 4:
            ap = ap.rearrange("p (a b c d) -> p a b c d", a=free[0], b=free[1], c=free[2])
        res = Res(name)
        keep = []
        for (o, s, r) in self.alive:
            if o < off + nb and off < o + s:
                res.inherit(r)
                if not (off <= o and o + s <= off + nb):
                    keep.append((o, s, r))
            else:
                keep.append((o, s, r))
        keep.append((off, nb, res))
        self.alive = keep
        return T(ap, res)

    def mm(self, out, lhsT, rhs, start, stop, R, W, skip=False):
        self.S.op("pe", lambda e: e.matmul(out, lhsT, rhs, start=start, stop=stop, skip_group_check=skip),
                  reads=R, writes=W)

    def tr(self, out, in_, ident, R, W):
        self.S.op("pe", lambda e: e.transpose(out, in_, ident), reads=R, writes=W)

    def act(self, out, in_, func, R, W, bias=None, scale=None):
        kw = {}
        if bias is not None:
            kw["bias"] = bias
        if scale is not None:
            kw["scale"] = scale
        self.S.op("act", lambda e: e.activation(out, in_, func, **kw), reads=R, writes=W)

    def tt(self, out, in0, in1, op, R, W, eng="dve"):
        self.S.op(eng, lambda e: e.tensor_tensor(out, in0, in1, op), reads=R, writes=W)

    def ts(self, out, in0, s1, s2, op0, op1, R, W, eng="dve"):
        if op1 is None:
            self.S.op(eng, lambda e: e.tensor_scalar(out, in0, s1, None, op0), reads=R, writes=W)
        else:
            self.S.op(eng, lambda e: e.tensor_scalar(out, in0, s1, s2, op0, op1), reads=R, writes=W)

    def stt(self, out, in0, scalar, in1, op0, op1, R, W, eng="dve"):
        self.S.op(eng, lambda e: e.scalar_tensor_tensor(out, in0, scalar, in1, op0, op1), reads=R, writes=W)

    def cp(self, out, in_, R, W, eng="dve"):
        if eng == "act":
            self.S.op("act", lambda e: e.activation(out, in_, AF.Copy), reads=R, writes=W)
        else:
            self.S.op(eng, lambda e: e.tensor_copy(out, in_), reads=R, writes=W)

    def dma(self, q, out, in_, R, W, sem, slow=False):
        if slow:
            self.S.op(q, lambda e: e.dma_start(out=out, in_=in_, allow_slow_non_contiguous=True), reads=R, writes=W, dma=sem)
        else:
            self.S.op(q, lambda e: e.dma_start(out=out, in_=in_), reads=R, writes=W, dma=sem)

    def dmas(self, q, pairs, R, W, sem):
        pairs = list(pairs)
        self.S.op(q, lambda e: [e.dma_start(out=o, in_=i) for (o, i) in pairs], reads=R, writes=W, dma=sem,
                  ndma=len(pairs))

    def build(self):
        nc = self.nc
        self.din("xin", [NB * 128, D])
        self.din("condT", [128, 8, 2])
        self.din("bmT", [128, DEPTH, 72])
        self.din("ck_a", [DEPTH, 256, 256]); self.din("cv_a", [DEPTH, 256, 256])
        self.din("ck_b", [DEPTH, 256, 128]); self.din("cv_b", [DEPTH, 256, 128])
        self.din("c_ckv", [DEPTH, 256, 128]); self.din("c_kpe", [DEPTH, 256, 32])
        self.din("w_mod", [DEPTH, D, 9 * D])
        self.din("ln_g", [DEPTH, 3, D]); self.din("ln_b", [DEPTH, 3, D])
        self.din("ffn_w1", [DEPTH, 2, D, DFF]); self.din("ffn_w3", [DEPTH, 2, D, DFF])
        self.din("ffn_w2", [DEPTH, 2, DFF, D])
        self.din("w_in", [DEPTH, D, INW]); self.din("w_o", [DEPTH, D, D])
        self.din("a_lambda", [DEPTH, 128]); self.din("a_subln_g", [DEPTH, 64]); self.din("b_sink", [DEPTH, 8])
        self.din("c_q_norm_g", [DEPTH, 256]); self.din("c_w_q_up", [DEPTH, 256, 384])
        self.din("c_kv_norm_g", [DEPTH, 128]); self.din("c_w_kv_up", [DEPTH, 128, 512])
        self.din("ident", [128, 128])
        self.din("rope32", [128, 2, NSB, 32]); self.din("rope64", [128, 2, NSB, 64])
        self.din("bmask", [128, 4, 128])
        self.din("rmask", [128, 6])
        self.dout("y", [NB * 128, D])
        self.dout("st", [4, DEPTH, 256, 928])
        self.dscr("XB", [X_ROWS, 2048])
        self.dres["XG"] = Res("XG")
        for pi, (r0, r1) in enumerate(XG_PIECES):
            self.dscr("XG%d" % pi, [2 * (r1 - r0), 2048])
        self.dscr("PB", [X_ROWS, 1024]); self.dscr("QS", [QS_ROWS, NB * 128])
        self.QSr = [Res("QS%d" % i) for i in range(NB)]
        for s_ in range(2):
            self.dscr("W13S%d" % s_, [11, 128, 2, 8, 256]); self.dscr("W2S%d" % s_, [12, 128, 4, 512])
        self.w13s_res = [[Res("w13s%d_%d" % (s_, j)) for j in range(11)] for s_ in range(2)]
        self.w2s_res = [[Res("w2s%d_%d" % (s_, j)) for j in range(12)] for s_ in range(2)]
        self.cast_n = 0
        self.XBr = [Res("XB%d" % i) for i in range(NSB)]
        self.PBr = [Res("PB%d" % i) for i in range(NPB)]

        with ExitStack() as st:
            self.stack = st
            self.xres = self.ptile("xres", [NB, D], F32)
            self.xblk = [Res("xblk%d" % i) for i in range(NB)]
            self.identF = self.ptile("identF", [128], F32)
            self.identB = self.ptile("identB", [128], BF16)
            self.onesF = self.ptile("onesF", [128], F32)
            self.csT = self.ptile("csT", [8, 2], F32)
            self.bmT = self.ptile("bmT_s", [DEPTH, 72], F32)
            self.modT = self.ptile("modT", [24, 2], F32)
            self.gate_bc = self.ptile("gate_bc", [2, D], F32)
            self.lng = self.ptile("lng", [D], F32)
            self.lnb = self.ptile("lnb", [D], F32)
            self.bmask = self.ptile("bmask_s", [4, 128], BF16)
            self.rmask = self.ptile("rmask_s", [6], F32)
            self.small = self.ptile("small", [128], F32)
            self.small_r = [Res("small%d" % i) for i in range(8)]
            self.ABYTES = 88 * 1024
            arena_t = st.enter_context(nc.sbuf_tensor("arena", [128, self.ABYTES // 2], BF16))
            self.arena = arena_t[:]
            self.alive = []
            self.acur = 0
            self.ps = []
            for i in range(8):
                p = st.enter_context(nc.psum_tensor("ps%d" % i, [128, 512], F32))
                self.ps.append(T(p[:], Res("ps%d" % i, excl=True)))

            self.prologue()
            for l in range(DEPTH):
                self.layer(l)
            self.epilogue()
            self.S.finish()
            self.S.emit(nc, st)
        return nc

    def prologue(self):
        d = self.d
        xv = d["xin"].rearrange("(b p) f -> p b f", p=128)
        for i in range(3):
            blks = list(range(i * 8, (i + 1) * 8))
            self.dma("sp", self.xres.ap[:, i * 8:(i + 1) * 8, :], xv[:, i * 8:(i + 1) * 8, :], [],
                     [self.xblk[b] for b in blks], self.xblk[blks[0]])
        self.dma("sp", self.identF.ap, d["ident"], [], [self.identF.res], self.identF.res)
        self.dma("pool", self.identB.ap, d["ident"], [], [self.identB.res], self.identB.res)
        self.dma("pool", self.bmask.ap, d["bmask"], [], [self.bmask.res], self.bmask.res)
        self.dma("sp", self.rmask.ap, d["rmask"], [], [self.rmask.res], self.rmask.res)
        self.dma("sp", self.bmT.ap, d["bmT"], [], [self.bmT.res], self.bmT.res)
        self.dma("sp", self.csT.ap, d["condT"], [], [self.csT.res], self.csT.res)
        self.S.op("dve", lambda e: e.memset(self.onesF.ap, 1.0), writes=[self.onesF.res])
        self.act(self.csT.ap, self.csT.ap, AF.Silu, [], [self.csT.res])

    def epilogue(self):
        yv = self.d["y"].rearrange("(b p) f -> p b f", p=128)
        for i in range(6):
            blks = list(range(i * 4, (i + 1) * 4))
            self.dma("sp", yv[:, i * 4:(i + 1) * 4, :], self.xres.ap[:, i * 4:(i + 1) * 4, :],
                     [self.xblk[b] for b in blks], [], self.xblk[blks[0]])

    def layer(self, l):
        tiles = [2, 3, 4, 5, 0, 1]
        if os.environ.get("KSTAGE", "full") in ("projonly", "mixonly", "mixprompt"):
            if l == 0:
                self.mixer_phase(l)
            return
        self.ffn_phase(l, 0, 0, tiles)
        self.mixer_phase(l)
        self.ffn_phase(l, 1, 2, tiles)

    def mods(self, l, slot, weight):
        d = self.d
        ps = self.ps[6]
        wsrc = d["w_mod"][l].rearrange("(k p) n -> p k n", p=128)
        base = slot * 3 * D
        ring = [self.atile("wm%d" % i, [8, 256], F32) for i in range(2)]
        for jb in range(12):
            w = ring[jb % 2]
            self.dma("sp", w.ap, wsrc[:, :, base + jb * 256: base + (jb + 1) * 256], [], [w.res], w.res)
            for cc in range(2):
                n = jb * 2 + cc
                for k in range(8):
                    self.mm(ps.ap[:, 2 * n:2 * n + 2], w.ap[:, k, cc * 128:(cc + 1) * 128], self.csT.ap[:, k, :],
                            k == 0, k == 7, [w.res, self.csT.res], [ps.res])
        psv = ps.ap[:, 0:48].rearrange("p (n c) -> p n c", c=2)
        for c in range(2):
            self.tt(self.modT.ap[:, :, c], psv[:, :, c], self.bmT.ap[:, l, slot * 24:(slot + 1) * 24], ALU.add,
                    [ps.res, self.bmT.res], [self.modT.res])
        self.ts(self.modT.ap[:, 8:16, :], self.modT.ap[:, 8:16, :], 1.0, None, ALU.add, None, [], [self.modT.res])
        self.ts(self.modT.ap[:, 16:24, :], self.modT.ap[:, 16:24, :], float(weight), None, ALU.mult, None, [],
                [self.modT.res])
        dg = [self.atile("dg%d" % i, [128], F32) for i in range(2)]
        pb = [self.ps[4], self.ps[5]]
        i = 0
        for c in range(2):
            for hf in range(2):
                p = pb[(c * 2 + hf) % 2]
                for kk in range(4):
                    k = hf * 4 + kk
                    g = dg[i % 2]
                    i += 1
                    self.ts(g.ap, self.identF.ap, self.modT.ap[:, 16 + k, c:c + 1], None, ALU.mult, None,
                            [self.identF.res, self.modT.res], [g.res])
                    self.mm(p.ap[:, kk * 128:(kk + 1) * 128], self.onesF.ap, g.ap, True, True,
                            [self.onesF.res, g.res], [p.res])
                self.cp(self.gate_bc.ap[:, c, hf * 512:(hf + 1) * 512], p.ap, [p.res], [self.gate_bc.res], eng="act")

    def load_ln(self, l, idx):
        d = self.d
        self.dma("sp", self.lng.ap, d["ln_g"][l, idx].partition_broadcast(128), [], [self.lng.res], self.lng.res)
        self.dma("sp", self.lnb.ap, d["ln_b"][l, idx].partition_broadcast(128), [], [self.lnb.res], self.lnb.res)

    def pre(self, tile, hT, banks):
        cond = 1 if tile < 2 else 0
        for k in range(8):
            p = banks[k % len(banks)]
            for tb in range(4):
                blk = tile * 4 + tb
                self.tr(p.ap[:, tb * 128:(tb + 1) * 128], self.xres.ap[:, blk, k * 128:(k + 1) * 128], self.identF.ap,
                        [self.xblk[blk], self.identF.res], [p.res])
            self.act(hT.ap[:, k, :], p.ap, AF.Identity, [p.res, self.modT.res], [hT.res],
                     bias=self.modT.ap[:, k, cond:cond + 1], scale=self.modT.ap[:, 8 + k, cond:cond + 1])

    def post_stages(self, blk, cond, ybuf, slot_i=0):
        x = self.xres.ap[:, blk, :]
        xr = self.xblk[blk]
        c0 = 32 + slot_i * 16
        sm = self.small.ap[:, c0:c0 + 16]
        sr = self.small_r[4 + slot_i]

        def stage_a():
            self.stt(ybuf.ap, x, DN_ALPHA, ybuf.ap, ALU.mult, ALU.add, [xr], [ybuf.res])
            st6 = sm[:, 0:12].rearrange("p (a b) -> p a b", a=2)
            for hf in range(2):
                self.S.op("dve", lambda e, hf=hf: e.bn_stats(st6[:, hf, :], ybuf.ap[:, hf * 512:(hf + 1) * 512]),
                          reads=[ybuf.res], writes=[sr])
            self.S.op("dve", lambda e: e.bn_aggr(sm[:, 12:14], st6), reads=[], writes=[sr])
            self.ts(sm[:, 14:15], sm[:, 13:14], LN_EPS, None, ALU.add, None, [], [sr])
            self.act(sm[:, 14:15], sm[:, 14:15], AF.Sqrt, [], [sr])

        def stage_b():
            self.S.op("dve", lambda e: e.reciprocal(sm[:, 14:15], sm[:, 14:15]), reads=[], writes=[sr])
            self.stt(sm[:, 15:16], sm[:, 12:13], -1.0, sm[:, 14:15], ALU.mult, ALU.mult, [], [sr])
            self.act(ybuf.ap, ybuf.ap, AF.Identity, [sr], [ybuf.res], bias=sm[:, 15:16], scale=sm[:, 14:15])

        def stage_c():
            self.tt(ybuf.ap, ybuf.ap, self.lng.ap, ALU.mult, [self.lng.res], [ybuf.res])
            self.tt(x, ybuf.ap, self.lnb.ap, ALU.add, [ybuf.res, self.lnb.res], [xr])
        return [stage_a, stage_b, stage_c]

    def post(self, blk, cond, ybuf, slot_i=0):
        for f_ in self.post_stages(blk, cond, ybuf, slot_i):
            f_()

    def cast_jobs(self, p):
        d = self.d
        l, half = p // 2, p % 2
        st_ = p % 2
        w1s = d["ffn_w1"][l, half].rearrange("(k p) n -> p k n", p=128)
        w3s = d["ffn_w3"][l, half].rearrange("(k p) n -> p k n", p=128)
        w2s = d["ffn_w2"][l, half].rearrange("(j p) n -> p j n", p=128)
        jobs = []
        for jb in range(11):
            def j13(jb=jb):
                dst = d["W13S%d" % st_][jb]
                self.dmas("pool", [(dst[:, 0], w1s[:, :, jb * 256:(jb + 1) * 256]), (dst[:, 1], w3s[:, :, jb * 256:(jb + 1) * 256])],
                          [], [self.w13s_res[st_][jb]], Res("cast%d" % (self.cast_n % 4)))
                self.cast_n += 1
            jobs.append(j13)
        for r in range(12):
            def j2(r=r):
                hf, jg = r // 6, r % 6
                nj = 4 if jg < 5 else 2
                dst = d["W2S%d" % st_][r]
                self.dmas("pool", [(dst[:, 0:nj, :], w2s[:, jg * 4:jg * 4 + nj, hf * 512:(hf + 1) * 512])],
                          [], [self.w2s_res[st_][r]], Res("cast%d" % (self.cast_n % 4)))
                self.cast_n += 1
            jobs.append(j2)
        return jobs

    def ffn_phase(self, l, half, slot, tiles):
        d = self.d
        p_ = l * 2 + half
        st_ = p_ % 2
        if p_ == 0:
            for j in self.cast_jobs(0):
                j()
        nxt = self.cast_jobs(p_ + 1) if p_ + 1 < 2 * DEPTH else []
        self.arena_reset()
        self.mods(l, slot, 0.5)
        self.load_ln(l, 0 if slot == 0 else 2)
        self.arena_reset()
        hT = self.atile("hT", [8, 512], BF16)
        aT = self.atile("aT", [NJ, 512], BF16)
        su = [self.atile("su%d" % i, [512], F32) for i in range(2)]
        R13, R2 = 3, 3
        w13 = [self.atile("w13_%d" % i, [2, 8, 256], BF16) for i in range(R13)]
        w2r = [self.atile("w2_%d" % i, [4, 512], BF16) for i in range(R2)]
        yb = [self.atile("yb%d" % i, [D], F32) for i in range(4)]
        state = {"i13": 0, "i2": 0}
        total13 = len(tiles) * 11
        total2 = len(tiles) * 12

        def issue13(upto):
            while state["i13"] < min(upto, total13):
                n = state["i13"]
                jb = n % 11
                w = w13[n % R13]
                self.dma("sp", w.ap, d["W13S%d" % st_][jb], [self.w13s_res[st_][jb]], [w.res], w.res)
                state["i13"] += 1
                if nxt:
                    nxt.pop(0)()

        def issue2(upto):
            while state["i2"] < min(upto, total2):
                n = state["i2"]
                r = n % 12
                w = w2r[n % R2]
                nj_ = 4 if (r % 6) < 5 else 2
                self.dma("sp", w.ap[:, 0:nj_, :], d["W2S%d" % st_][r][:, 0:nj_, :], [self.w2s_res[st_][r]], [w.res], w.res)
                state["i2"] += 1

        issue13(R13)
        pending = []
        self.pre(tiles[0], hT, [self.ps[4], self.ps[5]])
        for ti, tile in enumerate(tiles):
            cond = 1 if tile < 2 else 0
            for jb in range(11):
                n = ti * 11 + jb
                issue13(n + R13)
                w = w13[n % R13]
                for cc in range(2):
                    j = jb * 2 + cc
                    pu = self.ps[j % 2]
                    pg = self.ps[2 + j % 2]
                    for k in range(8):
                        self.mm(pu.ap, w.ap[:, 0, k, cc * 128:(cc + 1) * 128], hT.ap[:, k, :], k == 0, k == 7,
                                [w.res, hT.res], [pu.res])
                    for k in range(8):
                        self.mm(pg.ap, w.ap[:, 1, k, cc * 128:(cc + 1) * 128], hT.ap[:, k, :], k == 0, k == 7,
                                [w.res, hT.res], [pg.res])
                    s_ = su[j % 2]
                    self.act(s_.ap, pu.ap, AF.Silu, [pu.res], [s_.res])
                    self.tt(aT.ap[:, j, :], s_.ap, pg.ap, ALU.mult, [s_.res, pg.res], [aT.res])
                    if pending and j >= 2:
                        pending.pop(0)()
                if jb == 8:
                    issue2(ti * 12 + R2)
            if ti + 1 < len(tiles):
                self.pre(tiles[ti + 1], hT, [self.ps[4], self.ps[5]])
                issue13((ti + 1) * 11 + R13)
            for hf in range(2):
                banks = [self.ps[4 + tb] for tb in range(4)] if hf == 0 else [self.ps[tb] for tb in range(4)]
                for jg in range(6):
                    n = ti * 12 + hf * 6 + jg
                    issue2(n + R2)
                    w = w2r[n % R2]
                    nj = 4 if jg < 5 else 2
                    for jj in range(nj):
                        j = jg * 4 + jj
                        for tb in range(4):
                            self.mm(banks[tb].ap, aT.ap[:, j, tb * 128:(tb + 1) * 128], w.ap[:, jj, :], j == 0, j == NJ - 1,
                                    [aT.res, w.res], [banks[tb].res])
                for tb in range(4):
                    self.tt(yb[tb].ap[:, hf * 512:(hf + 1) * 512], banks[tb].ap,
                            self.gate_bc.ap[:, cond, hf * 512:(hf + 1) * 512], ALU.mult,
                            [banks[tb].res, self.gate_bc.res], [yb[tb].res])
            stg = [self.post_stages(tile * 4 + tb, cond, yb[tb], tb) for tb in range(4)]
            pending.extend([stg[tb][k_] for k_ in range(3) for tb in range(4)])
        while pending:
            pending.pop(0)()

    def mixer_phase(self, l):
        self.arena_reset()
        self.mods(l, 1, 1.0)
        self.load_ln(l, 1)
        self.arena_reset()
        stage = os.environ.get("KSTAGE", "full")
        self.project(l)
        if stage in ("proj", "projonly"):
            return
        self.gather()
        if stage == "gather":
            return
        self.arena_reset()
        self.layer_consts(l)
        mark = self.acur
        o_p = self.atile("o_p", [NPB, D], BF16)
        m2 = self.acur
        for s_ in range(4):
            self.acur = m2
            segs = [("PB", s_ * 256, 256)]
            self.attn_A(l, segs, False, [(s_ * 256, 256)], o_p, s_ * 2, 0)
            self.acur = m2
            self.attn_C(l, segs, False, [(s_ * 256, 256)], o_p, s_ * 2, 0)
            self.acur = m2
            self.attn_B(l, False, s_, o_p, s_ * 2)
        self.acur = m2
        self.wo_phase(l, [0, 1], o_p, 0)
        if stage in ("prompt", "mixprompt"):
            return
        self.acur = mark
        o_s = self.atile("o_s", [NSB, D], BF16)
        m2 = self.acur
        segs = [("XG", i * 1024, 1024) for i in range(4)]
        qt = [(1024 + i * 512, 512) for i in range(4)]
        self.attn_A(l, segs, True, qt, o_s, 0, 1024)
        self.acur = m2
        self.attn_C(l, segs, True, qt, o_s, 0, 1024)
        self.acur = m2
        self.attn_B(l, True, 0, o_s, 0)
        self.acur = m2
        self.wo_phase(l, [2, 3, 4, 5], o_s, 8)

    def xrows(self, name, tok0, ntok, r0, nr):
        if name == "XG":
            rk = tok0 // 2048
            t = tok0 % 2048
            for pi, (p0, p1) in enumerate(XG_PIECES):
                if p0 <= r0 and r0 + nr <= p1:
                    a = self.d["XG%d" % pi]
                    n = p1 - p0
                    return a[rk * n + r0 - p0: rk * n + r0 - p0 + nr, t:t + ntok]
            raise AssertionError((r0, nr))
        a = self.d[name]
        return a[r0:r0 + nr, tok0:tok0 + ntok]

    def xv(self, name, tok0, ntok):
        if name == "XG":
            rk = tok0 // 2048
            t = tok0 % 2048
            assert (t // 1024) == ((t + ntok - 1) // 1024)
            if t < 1024:
                a = self.d["XG2"]
                v = a[rk * 227 + 32: rk * 227 + 227, :]
            else:
                a = self.d["XG3"]
                v = a[rk * 195: rk * 195 + 195, :]
                t -= 1024
            v = v.rearrange("r t -> (r t)").rearrange("(t c) -> t c", c=XV_W)
            return v[t:t + ntok, :]
        a = self.d[name]
        v = a[XK_ROWS:X_ROWS, :]
        v = v.rearrange("r t -> (r t)").rearrange("(t c) -> t c", c=XV_W)
        return v[tok0:tok0 + ntok, :]

    def xres_of(self, name, tok0, ntok):
        if name == "XG":
            return [self.dres["XG"]]
        lst = self.XBr if name == "XB" else self.PBr
        return [lst[b] for b in range(tok0 // 128, (tok0 + ntok) // 128)]

    def rope(self, dst, src, H, Q, tbl, sb, R, W, tmp):
        dim = 4 * Q
        xs = src.rearrange("p (h a w q) -> p h a w q", h=H, a=2, w=2)
        xd = dst.rearrange("p (h a w q) -> p h a w q", h=H, a=2, w=2)
        xt = tmp[:, 0:H * dim].rearrange("p (h a w q) -> p h a w q", h=H, a=2, w=2)
        cs = tbl.ap[:, 0, sb, :].rearrange("p (a w q) -> p a w q", a=2, w=2)
        ss = tbl.ap[:, 1, sb, :].rearrange("p (a w q) -> p a w q", a=2, w=2)
        for a in range(2):
            cb = cs[:, a].unsqueeze(1).to_broadcast([128, H, 2, Q])
            self.tt(xd[:, :, a], xs[:, :, a], cb, ALU.mult, R + [tbl.res], W)
            for w in range(2):
                sbb = ss[:, a, w].unsqueeze(1).to_broadcast([128, H, Q])
                self.tt(xt[:, :, a, w], xs[:, :, a, 1 - w], sbb, ALU.mult, R + [tbl.res], [self.tmp_res])
            self.tt(xd[:, :, a], xd[:, :, a], xt[:, :, a], ALU.add, [self.tmp_res], W)

    def rms(self, dst, src, n, gam, R, W, sidx):
        sm = self.small.ap
        sr = self.small_r[sidx]
        c0 = 16 + sidx * 4
        self.tt(self.junk.ap[:, 0:n], src, src, ALU.mult, R, [self.junk.res])
        self.S.op("dve", lambda e: e.reduce_sum(sm[:, c0:c0 + 1], self.junk.ap[:, 0:n], axis=AX.X),
                  reads=[self.junk.res], writes=[sr])
        self.ts(sm[:, c0:c0 + 1], sm[:, c0:c0 + 1], 1.0 / n, RMS_EPS, ALU.mult, ALU.add, [], [sr])
        self.act(sm[:, c0:c0 + 1], sm[:, c0:c0 + 1], AF.Sqrt, [], [sr])
        self.S.op("dve", lambda e: e.reciprocal(sm[:, c0:c0 + 1], sm[:, c0:c0 + 1]), reads=[], writes=[sr])
        self.stt(dst, src, sm[:, c0:c0 + 1], gam.ap, ALU.mult, ALU.mult, R + [sr, gam.res], W)

    def project(self, l):
        d = self.d
        hT = self.atile("hTm", [8, 512], BF16)
        w_in = self.atile("w_in", [8, INW], BF16)
        wq = self.atile("wq", [2, 384], BF16)
        gq = self.atile("gq", [256], F32)
        gkv = self.atile("gkv", [128], F32)
        r32 = [self.atile("r32_%d" % i, [2, 1, 32], F32) for i in range(2)]
        r64 = [self.atile("r64_%d" % i, [2, 1, 64], F32) for i in range(2)]
        tp2 = [self.atile("tp%d" % i, [INW], F32) for i in range(2)]
        rq2 = [self.atile("rq%d" % i, [1184], F32) for i in range(2)]
        tmp = self.atile("tmp", [640], F32)
        self.tmp_res = tmp.res
        self.junk = self.atile("junk", [256], F32)
        nq2 = [self.atile("nq%d" % i, [256], BF16) for i in range(2)]
        nqT = self.atile("nqT", [2, 128], BF16)
        cq2 = [self.atile("cq%d" % i, [384], F32) for i in range(2)]
        ckv2 = [self.atile("ckv%d" % i, [128], F32) for i in range(2)]
        kpe2 = [self.atile("kpe96_%d" % i, [96], F32) for i in range(2)]
        qA = self.atile("qA", [8, 128], BF16)
        qB = self.atile("qB", [8, 128], BF16)
        qC = self.atile("qC", [4, 128], BF16)
        kst = self.atile("kst", [5, 128], BF16)
        vst = self.atile("vst", [6, 65], BF16)
        wsrc = d["w_in"][l].rearrange("(k p) n -> p k n", p=128)
        self.dmas("pool", [(w_in.ap[:, :, c0:c1], wsrc[:, :, c0:c1]) for (c0, c1) in ((0, 512), (512, 1024), (1024, 1536), (1536, INW))],
                  [], [w_in.res], w_in.res)
        self.dma("pool", wq.ap, d["c_w_q_up"][l].rearrange("(k p) n -> p k n", p=128), [], [wq.res], wq.res)
        self.dma("sp", gq.ap, d["c_q_norm_g"][l].partition_broadcast(128), [], [gq.res], gq.res)
        self.dma("sp", gkv.ap, d["c_kv_norm_g"][l].partition_broadcast(128), [], [gkv.res], gkv.res)
        for i in range(2):
            self.S.op("dve", lambda e, i=i: e.memset(kpe2[i].ap, 0.0), writes=[kpe2[i].res])
        self.S.op("dve", lambda e: e.memset(vst.ap, 1.0), writes=[vst.res])
        self.S.op("dve", lambda e: e.memset(qC.ap, 0.0), writes=[qC.res])
        self.S.op("dve", lambda e: e.memset(kst.ap, 0.0), writes=[kst.res])
        groups = ((0, 512), (512, 1024), (1024, 1536), (1536, INW))
        rm = self.rmask
        stv = d["st"]
        blocks = [(tile, tb) for tile in [2, 3, 4, 5, 0, 1] for tb in range(4)]
        nblk = len(blocks)

        def front_pe(i):
            tile, tb = blocks[i]
            tp = tp2[i % 2]
            if tb == 0:
                self.pre(tile, hT, [self.ps[4], self.ps[5]])
            for gi, (c0, c1) in enumerate(groups):
                p = self.ps[gi]
                for k in range(8):
                    self.mm(p.ap[:, 0:c1 - c0], hT.ap[:, k, tb * 128:(tb + 1) * 128], w_in.ap[:, k, c0:c1], k == 0, k == 7,
                            [hT.res, w_in.res], [p.res])
                self.cp(tp.ap[:, c0:c1], p.ap[:, 0:c1 - c0], [p.res], [tp.res], eng="act")
            if tile >= 2:
                sb = tile * 4 + tb - NPB
                self.dmas("sp", [(r32[i % 2].ap, d["rope32"][:, :, sb:sb + 1, :]), (r64[i % 2].ap, d["rope64"][:, :, sb:sb + 1, :])],
                          [], [r32[i % 2].res, r64[i % 2].res], r32[i % 2].res)

        def front_dve(i):
            tile, tb = blocks[i]
            samp = tile >= 2
            blk = tile * 4 + tb
            tp, rq, nq, ckv = tp2[i % 2], rq2[i % 2], nq2[i % 2], ckv2[i % 2]
            if samp:
                self.rope(rq.ap[:, 0:512], tp.ap[:, 0:512], 16, 8, r32[i % 2], 0, [tp.res], [rq.res], tmp.ap)
                self.rope(rq.ap[:, 512:1152], tp.ap[:, 768:1408], 10, 16, r64[i % 2], 0, [tp.res], [rq.res], tmp.ap)
                self.rope(rq.ap[:, 1152:1184], tp.ap[:, 1920:1952], 1, 8, r32[i % 2], 0, [tp.res], [rq.res], tmp.ap)
            self.rms(nq.ap, tp.ap[:, 1536:1792], 256, gq, [tp.res], [nq.res], 1)
            self.rms(ckv.ap, tp.ap[:, 1792:1920], 128, gkv, [tp.res], [ckv.res], 2)
            if not samp:
                seq, t0 = blk // 2, (blk % 2) * 128
                self.dmas("sp", [(stv[seq, l, t0:t0 + 128, 0:512], tp.ap[:, 256:768]),
                                 (stv[seq, l, t0:t0 + 128, 512:768], tp.ap[:, 1280:1536]),
                                 (stv[seq, l, t0:t0 + 128, 896:928], tp.ap[:, 1920:1952]),
                                 (stv[seq, l, t0:t0 + 128, 768:896], ckv.ap)],
                          [tp.res, ckv.res], [], tp.res)

        def back(i):
            tile, tb = blocks[i]
            samp = tile >= 2
            blk = tile * 4 + tb
            tp, rq, nq, ckv, cq, kpe96 = tp2[i % 2], rq2[i % 2], nq2[i % 2], ckv2[i % 2], cq2[i % 2], kpe2[i % 2]
            if samp:
                s_aq, s_ak, s_bq, s_bk, s_kpe = (rq.ap[:, 0:256], rq.ap[:, 256:512], rq.ap[:, 512:1024],
                                                 rq.ap[:, 1024:1152], rq.ap[:, 1152:1184])
                sres = [rq.res]
            else:
                s_aq, s_ak, s_bq, s_bk, s_kpe = (tp.ap[:, 0:256], tp.ap[:, 256:512], tp.ap[:, 768:1280],
                                                 tp.ap[:, 1280:1408], tp.ap[:, 1920:1952])
                sres = [tp.res]
            p4, p5, p6, p7 = self.ps[4], self.ps[5], self.ps[6], self.ps[7]
            p4b = p4.ap[:, 0:128].bitcast(BF16)
            for kk in range(2):
                self.tr(p4b[:, kk * 128:(kk + 1) * 128], nq.ap[:, kk * 128:(kk + 1) * 128], self.identB.ap,
                        [nq.res, self.identB.res], [p4.res])
            self.cp(nqT.ap, p4b.rearrange("p (a b) -> p a b", a=2), [p4.res], [nqT.res])
            for kk in range(2):
                self.mm(p5.ap[:, 0:384], nqT.ap[:, kk, :], wq.ap[:, kk, :], kk == 0, kk == 1, [nqT.res, wq.res], [p5.res])
            self.cp(cq.ap, p5.ap[:, 0:384], [p5.res], [cq.res], eng="act")
            if samp:
                cq4 = cq.ap.rearrange("p (h e) -> p h e", e=96)
                p54 = p5.ap[:, 0:384].rearrange("p (h e) -> p h e", e=96)
                self.rope_strided(cq4[:, :, 64:96], p54[:, :, 64:96], 4, 8, r32[i % 2], 0, [p5.res], [cq.res], tmp.ap)
            for c in range(2):
                self.tr(p6.ap[:, c * 128:(c + 1) * 128], s_aq[:, c * 128:(c + 1) * 128], self.identF.ap, sres + [self.identF.res], [p6.res])
                self.tr(p6.ap[:, 256 + c * 128:256 + (c + 1) * 128], s_ak[:, c * 128:(c + 1) * 128], self.identF.ap, sres + [self.identF.res], [p6.res])
            for c in range(4):
                self.tr(p7.ap[:, c * 128:(c + 1) * 128], s_bq[:, c * 128:(c + 1) * 128], self.identF.ap, sres + [self.identF.res], [p7.res])
            for c in range(2):
                for i_ in range(4):
                    self.ts(qA.ap[:, c * 4 + i_, :], p6.ap[:, c * 128:(c + 1) * 128], rm.ap[:, i_:i_ + 1], None, ALU.mult, None,
                            [p6.res, rm.res], [qA.res])
            for hq in range(8):
                self.act(qB.ap[:, hq, :], p7.ap[:, (hq // 2) * 128:(hq // 2 + 1) * 128], AF.Identity, [p7.res, rm.res], [qB.res],
                         scale=rm.ap[:, 4 + hq % 2:5 + hq % 2])
            self.cp(kst.ap[:, 0:2, :], p6.ap[:, 256:512].rearrange("p (a b) -> p a b", a=2), [p6.res], [kst.res])
            self.cp(kpe96.ap[:, 64:96], s_kpe, sres, [kpe96.res])
            self.tr(p4.ap[:, 0:128], s_bk, self.identF.ap, sres + [self.identF.res], [p4.res])
            self.tr(p4.ap[:, 128:256], ckv.ap, self.identF.ap, [ckv.res, self.identF.res], [p4.res])
            self.tr(p4.ap[0:96, 256:384], kpe96.ap, self.identF.ap, [kpe96.res, self.identF.res], [p4.res])
            self.cp(kst.ap[:, 2:4, :], p4.ap[:, 0:256].rearrange("p (a b) -> p a b", a=2), [p4.res], [kst.res])
            self.cp(kst.ap[64:96, 4, :], p4.ap[64:96, 256:384], [p4.res], [kst.res])
            for h in range(4):
                self.tr(p5.ap[0:96, h * 128:(h + 1) * 128], cq.ap[:, h * 96:(h + 1) * 96], self.identF.ap, [cq.res, self.identF.res], [p5.res])
            self.cp(qC.ap[0:96, :, :], p5.ap[0:96, :].rearrange("p (a b) -> p a b", a=4), [p5.res], [qC.res], eng="act")
            self.cp(vst.ap[:, 0:4, 0:64], tp.ap[:, 512:768].rearrange("p (h e) -> p h e", e=64), [tp.res], [vst.res])
            self.cp(vst.ap[:, 4:6, 0:64], tp.ap[:, 1408:1536].rearrange("p (h e) -> p h e", e=64), [tp.res], [vst.res])
            tok = blk * 128
            QS = d["QS"]
            name = "XB" if samp else "PB"
            xt0 = (blk - NPB) * 128 if samp else blk * 128
            pairs = [
                (QS[0:1024, tok:tok + 128].rearrange("(i p) t -> p i t", p=128), qA.ap),
                (QS[1024:2048, tok:tok + 128].rearrange("(i p) t -> p i t", p=128), qB.ap),
                (QS[2048:2560, tok:tok + 128].rearrange("(i p) t -> p i t", p=128), qC.ap),
                (self.xrows(name, xt0, 128, 0, 256).rearrange("(c p) t -> p c t", p=128), kst.ap[:, 0:2, :]),
                (self.xrows(name, xt0, 128, 256, 256).rearrange("(c p) t -> p c t", p=128), kst.ap[:, 2:4, :]),
                (self.xrows(name, xt0, 128, 512, 32), kst.ap[64:96, 4, :]),
                (self.xv(name, xt0, 128), vst.ap.rearrange("p h e -> p (h e)")),
            ]
            wr = [self.QSr[blk], (self.XBr[blk - NPB] if samp else self.PBr[blk])]
            self.dmas("sp", pairs, [qA.res, qB.res, qC.res, kst.res, vst.res], wr, qA.res)

        front_pe(0)
        for i in range(nblk):
            front_dve(i)
            if i + 1 < nblk:
                front_pe(i + 1)
            back(i)

    def rope_strided(self, dst, src, H, Q, tbl, sb, R, W, tmp):
        dim = 4 * Q
        xs = src.rearrange("p h (a w q) -> p h a w q", a=2, w=2)
        xd = dst.rearrange("p h (a w q) -> p h a w q", a=2, w=2)
        xt = tmp[:, 0:H * dim].rearrange("p (h a w q) -> p h a w q", h=H, a=2, w=2)
        cs = tbl.ap[:, 0, sb, :].rearrange("p (a w q) -> p a w q", a=2, w=2)
        ss = tbl.ap[:, 1, sb, :].rearrange("p (a w q) -> p a w q", a=2, w=2)
        for a in range(2):
            cb = cs[:, a].unsqueeze(1).to_broadcast([128, H, 2, Q])
            self.tt(xd[:, :, a], xs[:, :, a], cb, ALU.mult, R + [tbl.res], W)
            for w in range(2):
                sbb = ss[:, a, w].unsqueeze(1).to_broadcast([128, H, Q])
                self.tt(xt[:, :, a, w], xs[:, :, a, 1 - w], sbb, ALU.mult, R + [tbl.res], [self.tmp_res])
            self.tt(xd[:, :, a], xd[:, :, a], xt[:, :, a], ALU.add, [self.tmp_res], W)

    def gather(self):
        d = self.d
        for pi, (r0, r1) in enumerate(XG_PIECES):
            src = d["XB"][r0:r1, :]
            dst = d["XG%d" % pi]
            self.S.op("pool", lambda e, src=src, dst=dst: e.collective_compute(
                "AllGather", ALU.bypass, replica_groups=[[0, 1], [2, 3], [4, 5], [6, 7]], ins=[src], outs=[dst]),
                reads=list(self.XBr), writes=[self.dres["XG"]], dma=self.dres["XG"], inc=1)

    def layer_consts(self, l):
        d = self.d
        lam_init = 0.8 - 0.6 * float(np.exp(-0.3 * l))
        al = self.atile("al", [128], F32)
        self.gsub = self.atile("gsub", [64], F32)
        self.es = self.atile("es", [8], F32)
        self.lam = self.atile("lamc", [4], F32)
        self.dma("sp", al.ap, d["a_lambda"][l].partition_broadcast(128), [], [al.res], al.res)
        self.dma("sp", self.gsub.ap, d["a_subln_g"][l].partition_broadcast(128), [], [self.gsub.res], self.gsub.res)
        self.dma("sp", self.es.ap, d["b_sink"][l].partition_broadcast(128), [], [self.es.res], self.es.res)
        self.act(self.es.ap, self.es.ap, AF.Exp, [], [self.es.res])
        self.ts(self.gsub.ap, self.gsub.ap, 1.0 - lam_init, None, ALU.mult, None, [], [self.gsub.res])
        lm = self.lam
        for i in range(2):
            self.tt(al.ap[:, i * 64:i * 64 + 32], al.ap[:, i * 64:i * 64 + 32], al.ap[:, i * 64 + 32:i * 64 + 64], ALU.mult, [], [al.res])
            self.S.op("dve", lambda e, i=i: e.reduce_sum(lm.ap[:, i:i + 1], al.ap[:, i * 64:i * 64 + 32], axis=AX.X),
                      reads=[al.res], writes=[lm.res])
        self.act(lm.ap[:, 0:2], lm.ap[:, 0:2], AF.Exp, [], [lm.res])
        self.tt(lm.ap[:, 2:3], lm.ap[:, 0:1], lm.ap[:, 1:2], ALU.subtract, [], [lm.res])
        self.ts(lm.ap[:, 3:4], lm.ap[:, 2:3], lam_init, -1.0, ALU.add, ALU.mult, [], [lm.res])
        self.pt = [self.atile("pt%d" % i, [512], BF16) for i in range(4)]
        self.pti = 0
        self.sci = 0
        self.rec = self.atile("rec", [8], F32)
        self.bmask4 = self.atile("bmask4", [4, 4, 128], BF16)
        for m_ in range(4):
            for g_ in range(4):
                self.cp(self.bmask4.ap[:, m_, g_, :], self.bmask.ap[:, m_, :], [self.bmask.res], [self.bmask4.res])
        self.ctmp = self.atile("ctmp", [2, 256], F32)

    def attend(self, q_ap, q_res, kblocks, nq, scale, obank):
        nqs = nq // 128
        nkb = len(kblocks)
        LA = 2
        pts = [None] * nkb
        for i in range(nkb + LA):
            if i < nkb:
                kT, kres, V, vres, mask = kblocks[i]
                sc = self.ps[self.sci % 4]
                self.sci += 1
                self.mm(sc.ap[:, 0:nq], kT, q_ap, True, mask is None, kres + [q_res], [sc.res])
                if mask is not None:
                    self.mm(sc.ap[:, 0:nq], self.identB.ap, mask, False, True, [self.identB.res, self.bmask4.res], [sc.res])
                pt = self.pt[self.pti % 4]
                self.pti += 1
                self.act(pt.ap[:, 0:nq], sc.ap[:, 0:nq], AF.Exp, [sc.res], [pt.res], scale=float(scale))
                pts[i] = pt
            j = i - LA
            if j >= 0:
                kT, kres, V, vres, mask = kblocks[j]
                pt = pts[j]
                for qs in range(nqs):
                    Vq = V[qs] if isinstance(V, (list, tuple)) else V
                    self.mm(obank.ap[:, qs * 65:(qs + 1) * 65], pt.ap[:, qs * 128:(qs + 1) * 128], Vq, (j == 0 and qs == 0),
                            j == nkb - 1, [pt.res] + vres, [obank.res], skip=True)

    def load_ctxT(self, src_dram, width, dst_fn, dres):
        ct = self.ctmp
        self.dma("sp", ct.ap[:, :, 0:width], src_dram.rearrange("(b p) f -> p b f", p=128), [], [ct.res], ct.res)
        p = self.ps[2]
        nch = width // 128
        for kb in range(2):
            for c in range(nch):
                self.tr(p.ap[:, (kb * nch + c) * 128:(kb * nch + c + 1) * 128], ct.ap[:, kb, c * 128:(c + 1) * 128], self.identF.ap,
                        [ct.res, self.identF.res], [p.res])
        for kb in range(2):
            for c in range(nch):
                self.cp(dst_fn(c, kb), p.ap[:, (kb * nch + c) * 128:(kb * nch + c + 1) * 128], [p.res], [dres])

    def attn_A(self, l, segs, ctx, qtiles, o_t, oblk0, qtok_unused):
        d = self.d
        nk = (256 if ctx else 0) + sum(s[2] for s in segs)
        nkb = nk // 128
        kT = self.atile("kT_A", [2, nk], BF16)
        V = self.atile("V_A", [nkb, 4, 65], BF16)
        qm = [self.atile("qmA%d" % i, [512], BF16) for i in range(2)]
        oa = self.atile("oa", [4, 4, 64], F32)
        o1 = self.atile("o1", [4, 64], F32)
        o2 = self.atile("o2", [64], F32)
        ssq = self.atile("ssq", [16], F32)
        sq = self.atile("sqA", [4, 64], F32)
        koff = 0
        if ctx:
            self.load_ctxT(d["ck_a"][l], 256, lambda c, kb: kT.ap[:, c, kb * 128:(kb + 1) * 128], kT.res)
            self.dmas("pool", [(V.ap[:, b_, :, 0:64], d["cv_a"][l][b_ * 128:(b_ + 1) * 128, :].rearrange("p (h e) -> p h e", e=64))
                               for b_ in range(2)], [], [V.res], V.res)
            self.S.op("dve", lambda e: e.memset(V.ap[:, 0:2, :, 64:65], 1.0), writes=[V.res])
            koff = 256
        pairs = []
        rd = []
        for (name, t0, nt) in segs:
            pairs.append((kT.ap[:, :, koff:koff + nt], self.xrows(name, t0, nt, 0, 256).rearrange("(c p) t -> p c t", p=128)))
            pairs.append((V.ap[:, koff // 128:(koff + nt) // 128, :, :],
                          self.xv(name, t0, nt)[:, 0:260].rearrange("(b p) (h e) -> p b h e", p=128, e=65)))
            rd += self.xres_of(name, t0, nt)
            koff += nt
        self.dmas("sp", pairs, rd, [kT.res, V.res], kT.res)
        QS = d["QS"]
        scale = 32 ** -0.5
        qi = 0
        for ti, (qt0, nq) in enumerate(qtiles):
            nqs = nq // 128
            qres = [self.QSr[b] for b in range(qt0 // 128, (qt0 + nq) // 128)]
            for i in range(8):
                c, h, j = i // 4, (i // 4) * 2 + (i % 4) // 2, i % 2
                q = qm[qi % 2]
                qi += 1
                self.dma("sp", q.ap[:, 0:nq], QS[i * 128:(i + 1) * 128, qt0:qt0 + nq], qres, [q.res], q.res)
                ob = self.ps[4 + (i % 2)]
                kbl = [(kT.ap[:, c, kb * 128:(kb + 1) * 128], [kT.res], V.ap[:, kb, h, :], [V.res], None) for kb in range(nkb)]
                self.attend(q.ap[:, 0:nq], q.res, kbl, nq, scale, ob)
                for qs in range(nqs):
                    self.S.op("dve", lambda e, qs=qs, ob=ob: e.reciprocal(self.rec.ap[:, qs:qs + 1], ob.ap[:, qs * 65 + 64:qs * 65 + 65]),
                              reads=[ob.res], writes=[self.rec.res])
                    if j == 0:
                        self.ts(o1.ap[:, qs, :], ob.ap[:, qs * 65:qs * 65 + 64], self.rec.ap[:, qs:qs + 1], None, ALU.mult, None,
                                [ob.res, self.rec.res], [o1.res])
                    else:
                        self.ts(o2.ap, ob.ap[:, qs * 65:qs * 65 + 64], self.rec.ap[:, qs:qs + 1], None, ALU.mult, None,
                                [ob.res, self.rec.res], [o2.res])
                        self.stt(oa.ap[:, qs, h, :], o2.ap, self.lam.ap[:, 3:4], o1.ap[:, qs, :], ALU.mult, ALU.add,
                                 [o2.res, o1.res, self.lam.res], [oa.res])
            n16 = nqs * 4
            for qs in range(nqs):
                self.tt(sq.ap, oa.ap[:, qs], oa.ap[:, qs], ALU.mult, [oa.res], [sq.res])
                self.S.op("dve", lambda e, qs=qs: e.reduce_sum(ssq.ap[:, qs * 4:qs * 4 + 4], sq.ap, axis=AX.X),
                          reads=[sq.res], writes=[ssq.res])
            self.ts(ssq.ap[:, 0:n16], ssq.ap[:, 0:n16], 1.0 / 64, RMS_EPS, ALU.mult, ALU.add, [], [ssq.res])
            self.act(ssq.ap[:, 0:n16], ssq.ap[:, 0:n16], AF.Sqrt, [], [ssq.res])
            self.S.op("dve", lambda e, n16=n16: e.reciprocal(ssq.ap[:, 0:n16], ssq.ap[:, 0:n16]), reads=[], writes=[ssq.res])
            for qs in range(nqs):
                for h in range(4):
                    ob_ = oblk0 + ti * nqs + qs
                    self.stt(o_t.ap[:, ob_, h * 64:(h + 1) * 64], oa.ap[:, qs, h, :], ssq.ap[:, qs * 4 + h:qs * 4 + h + 1], self.gsub.ap,
                             ALU.mult, ALU.mult, [oa.res, ssq.res, self.gsub.res], [o_t.res])

    def attn_C(self, l, segs, ctx, qtiles, o_t, oblk0, qtok_unused):
        d = self.d
        nk = (256 if ctx else 0) + sum(s[2] for s in segs)
        nkb = nk // 128
        wkv = self.atile("wkvC", [512], BF16)
        self.dma("pool", wkv.ap, d["c_w_kv_up"][l], [], [wkv.res], wkv.res)
        wkv4 = wkv.ap.rearrange("p (h e) -> p h e", e=128)
        ckT = [self.atile("ckT%d" % i, [512], BF16) for i in range(2)]
        cpe = self.atile("cpe", [2, 96], F32)
        qm = [self.atile("qmC%d" % i, [512], BF16) for i in range(2)]
        kT = self.atile("kT_C", [2, nk], BF16)
        V = self.atile("V_C", [nkb, 2, 65], BF16)
        QS = d["QS"]
        scale = 96 ** -0.5
        qi = 0
        ci = 0
        for hp in range(2):
            self.S.op("dve", lambda e: e.memset(V.ap[:, :, :, 64:65], 1.0), writes=[V.res])
            ktiles = []
            koff = 0
            if ctx:
                ktiles.append((None, 0, 256, 0))
                koff = 256
            for (name, t0, nt) in segs:
                for s0 in range(0, nt, 512):
                    n = min(512, nt - s0)
                    ktiles.append((name, t0 + s0, n, koff))
                    koff += n
            if ctx:
                self.S.op("dve", lambda e: e.memset(cpe.ap, 0.0), writes=[cpe.res])
                self.dma("sp", cpe.ap[:, :, 64:96], d["c_kpe"][l].rearrange("(b p) f -> p b f", p=128), [], [cpe.res], cpe.res)
                p = self.ps[2]
                for kb in range(2):
                    self.tr(p.ap[0:96, kb * 128:(kb + 1) * 128], cpe.ap[:, kb, :], self.identF.ap, [cpe.res, self.identF.res], [p.res])
                for hh in range(2):
                    self.cp(kT.ap[64:96, hh, 0:256], p.ap[64:96, 0:256], [p.res], [kT.res])
            pairs = []
            rd = []
            ko2 = 256 if ctx else 0
            for (name, t0, nt) in segs:
                for hh in range(2):
                    pairs.append((kT.ap[64:96, hh, ko2:ko2 + nt], self.xrows(name, t0, nt, 512, 32)))
                rd += self.xres_of(name, t0, nt)
                ko2 += nt
            self.dmas("sp", pairs, rd, [kT.res], kT.res)
            for (name, t0, n, ko) in ktiles:
                ck = ckT[ci % 2]
                ci += 1
                if name is None:
                    self.load_ctxT(d["c_ckv"][l], 128, lambda c, kb: ck.ap[:, kb * 128:(kb + 1) * 128], ck.res)
                else:
                    self.dma("sp", ck.ap[:, 0:n], self.xrows(name, t0, n, 384, 128), self.xres_of(name, t0, n), [ck.res], ck.res)
                for hh in range(2):
                    h = hp * 2 + hh
                    p = self.ps[2 + hh]
                    self.mm(p.ap[:, 0:n], wkv.ap[:, h * 128:(h + 1) * 128], ck.ap[:, 0:n], True, True, [wkv.res, ck.res], [p.res])
                    self.cp(kT.ap[0:64, hh, ko:ko + n], p.ap[0:64, 0:n], [p.res], [kT.res], eng=("act" if hh else "dve"))
                p = self.ps[6]
                for b in range(n // 128):
                    self.mm(p.ap[:, b * 128:(b + 1) * 128], ck.ap[:, b * 128:(b + 1) * 128], wkv4[:, hp * 2:hp * 2 + 2, 64:128], True, True,
                            [ck.res, wkv.res], [p.res])
                self.cp(V.ap[:, ko // 128:(ko + n) // 128, :, 0:64],
                        p.ap[:, 0:n].rearrange("p (b h e) -> p b h e", h=2, e=64), [p.res], [V.res])
            for ti, (qt0, nq) in enumerate(qtiles):
                nqs = nq // 128
                qres = [self.QSr[b] for b in range(qt0 // 128, (qt0 + nq) // 128)]
                for hh in range(2):
                    h = hp * 2 + hh
                    q = qm[qi % 2]
                    qi += 1
                    self.dma("sp", q.ap[0:96, 0:nq], QS[2048 + h * 128:2048 + h * 128 + 96, qt0:qt0 + nq], qres, [q.res], q.res)
                    ob = self.ps[4 + (qi % 2)]
                    kbl = [(kT.ap[0:96, hh, kb * 128:(kb + 1) * 128], [kT.res], V.ap[:, kb, hh, :], [V.res], None) for kb in range(nkb)]
                    self.attend(q.ap[0:96, 0:nq], q.res, kbl, nq, scale, ob)
                    for qs in range(nqs):
                        ob_ = oblk0 + ti * nqs + qs
                        self.S.op("dve", lambda e, qs=qs, ob=ob: e.reciprocal(self.rec.ap[:, qs:qs + 1], ob.ap[:, qs * 65 + 64:qs * 65 + 65]),
                                  reads=[ob.res], writes=[self.rec.res])
                        self.ts(o_t.ap[:, ob_, 768 + h * 64:768 + (h + 1) * 64], ob.ap[:, qs * 65:qs * 65 + 64], self.rec.ap[:, qs:qs + 1], None,
                                ALU.mult, None, [ob.res, self.rec.res], [o_t.res])

    def attn_B(self, l, samp, seq, o_t, oblk0):
        d = self.d
        QS = d["QS"]
        scale = 64 ** -0.5
        if samp:
            nkb = 20
        else:
            nkb = 2
        nk = nkb * 128
        kT = self.atile("kT_B", [2, nk], BF16)
        V = self.atile("V_B", [nkb, 2, 65], BF16)
        qm = [self.atile("qmB%d" % i, [8, 512], BF16) for i in range(2)]
        dup = self.atile("dupB", [2, 64], F32)
        pairs = []
        rd = []
        if samp:
            ct = self.ctmp
            self.dma("sp", ct.ap[:, :, 0:128], d["ck_b"][l].rearrange("(b p) f -> p b f", p=128), [], [ct.res], ct.res)
            p = self.ps[2]
            for kvh in range(2):
                for kb in range(2):
                    for r in range(2):
                        self.cp(dup.ap[:, r, :], ct.ap[:, kb, kvh * 64:(kvh + 1) * 64], [ct.res], [dup.res])
                    self.tr(p.ap[:, (kvh * 2 + kb) * 128:(kvh * 2 + kb + 1) * 128], dup.ap.rearrange("p a b -> p (a b)"), self.identF.ap,
                            [dup.res, self.identF.res], [p.res])
            for kvh in range(2):
                self.cp(kT.ap[:, kvh, 0:256], p.ap[:, kvh * 256:(kvh + 1) * 256], [p.res], [kT.res])
            self.dmas("pool", [(V.ap[:, b_, :, 0:64], d["cv_b"][l][b_ * 128:(b_ + 1) * 128, :].rearrange("p (h e) -> p h e", e=64))
                               for b_ in range(2)], [], [V.res], V.res)
            self.S.op("dve", lambda e: e.memset(V.ap[:, 0:2, :, 64:65], 1.0), writes=[V.res])
            srcs = [("XB", 0, 2048, 256), ("XG", 15 * 128, 128, 18 * 128), ("XG", 16 * 128, 128, 19 * 128)]
        else:
            srcs = [("PB", seq * 256, 256, 0)]
        for (name, t0, nt, ko) in srcs:
            for kvh in range(2):
                for r in range(2):
                    pairs.append((kT.ap[r * 64:(r + 1) * 64, kvh, ko:ko + nt], self.xrows(name, t0, nt, 256 + kvh * 64, 64)))
            pairs.append((V.ap[:, ko // 128:(ko + nt) // 128, :, :],
                          self.xv(name, t0, nt)[:, 260:390].rearrange("(b p) (h e) -> p b h e", p=128, e=65)))
            rd += self.xres_of(name, t0, nt)
        self.dmas("sp", pairs, rd, [kT.res, V.res], kT.res)
        nblk = 16 if samp else 2
        ntile = 4 if samp else 1
        per = nblk // ntile
        for ti in range(ntile):
            tok0 = (1024 + ti * 512) if samp else seq * 256
            nq = per * 128
            q = qm[ti % 2]
            qres = [self.QSr[b] for b in range(tok0 // 128, (tok0 + nq) // 128)]
            self.dma("sp", q.ap[:, :, 0:nq], QS[1024:2048, tok0:tok0 + nq].rearrange("(i p) t -> p i t", p=128), qres, [q.res], q.res)
            for bi in range(per):
                i = ti * per + bi
                for kvh in range(2):
                    ob = self.ps[4 + (kvh % 2)]
                    bm = self.bmask4.ap

                    def kb_(idx, m=None):
                        return (kT.ap[:, kvh, idx * 128:(idx + 1) * 128], [kT.res], V.ap[:, idx, kvh, :], [V.res],
                                None if m is None else bm[:, m].rearrange("p g q -> p (g q)"))
                    if samp:
                        kbl = [kb_(0), kb_(1)]
                        kbl.append(kb_(2 + i - 1, 0) if i > 0 else kb_(18, 2))
                        kbl.append(kb_(2 + i))
                        kbl.append(kb_(2 + i + 1, 1) if i < 15 else kb_(19, 3))
                    else:
                        kbl = [kb_(0), kb_(1)]
                    qv = q.ap[:, 4 * kvh:4 * kvh + 4, bi * 128:(bi + 1) * 128]
                    self.attend(qv, q.res, kbl, 512, scale, ob)
                    for g in range(4):
                        hq = 4 * kvh + g
                        self.ts(self.rec.ap[:, g:g + 1], ob.ap[:, g * 65 + 64:g * 65 + 65], self.es.ap[:, hq:hq + 1], None, ALU.add, None,
                                [ob.res, self.es.res], [self.rec.res])
                    self.S.op("dve", lambda e: e.reciprocal(self.rec.ap[:, 0:4], self.rec.ap[:, 0:4]), reads=[], writes=[self.rec.res])
                    for g in range(4):
                        hq = 4 * kvh + g
                        self.ts(o_t.ap[:, oblk0 + i, 256 + hq * 64:256 + (hq + 1) * 64], ob.ap[:, g * 65:g * 65 + 64], self.rec.ap[:, g:g + 1], None,
                                ALU.mult, None, [ob.res, self.rec.res], [o_t.res])

    def wo_phase(self, l, tiles, o_t, blk0):
        d = self.d
        w_o = self.atile("w_o", [8, D], BF16)
        oT = self.atile("oT", [8, 512], BF16)
        yb = [self.atile("ybm%d" % i, [D], F32) for i in range(4)]
        wsrc = d["w_o"][l].rearrange("(k p) n -> p k n", p=128)
        self.dmas("pool", [(w_o.ap[:, :, 0:512], wsrc[:, :, 0:512]), (w_o.ap[:, :, 512:1024], wsrc[:, :, 512:1024])], [], [w_o.res], w_o.res)
        for tile in tiles:
            cond = 1 if tile < 2 else 0
            for k in range(8):
                p = self.ps[k % 2]
                pb = p.ap[:, 0:256].bitcast(BF16)
                for tb in range(4):
                    ob_ = tile * 4 + tb - blk0
                    self.tr(pb[:, tb * 128:(tb + 1) * 128], o_t.ap[:, ob_, k * 128:(k + 1) * 128], self.identB.ap,
                            [o_t.res, self.identB.res], [p.res])
                self.cp(oT.ap[:, k, :], pb, [p.res], [oT.res], eng=("act" if k % 2 else "dve"))
            for hf in range(2):
                for tb in range(4):
                    p = self.ps[4 + tb] if hf == 0 else self.ps[2 + (tb % 2)]
                    for k in range(8):
                        self.mm(p.ap, oT.ap[:, k, tb * 128:(tb + 1) * 128], w_o.ap[:, k, hf * 512:(hf + 1) * 512], k == 0, k == 7,
                                [oT.res, w_o.res], [p.res])
                    self.tt(yb[tb].ap[:, hf * 512:(hf + 1) * 512], p.ap, self.gate_bc.ap[:, cond, hf * 512:(hf + 1) * 512], ALU.mult,
                            [p.res, self.gate_bc.res], [yb[tb].res])
            for tb in range(4):
                self.post(tile * 4 + tb, cond, yb[tb])


def build_nc():
    b = Builder()
    nc = b.build()
    return nc, list(b.in_names)


def _rope_tables(dim, pos0, n):
    t = np.arange(pos0, pos0 + n)
    row = (t // 64).astype(np.float32)
    col = (t % 64).astype(np.float32)
    a = dim // 2
    inv = np.power(np.float32(10000.0), -np.arange(0, a, 2, dtype=np.float32) / np.float32(a)).astype(np.float32)
    ar = row[:, None] * inv[None, :]
    ac = col[:, None] * inv[None, :]
    ang = np.concatenate([ar, ar, ac, ac], axis=-1).astype(np.float32)
    cos = np.cos(ang).astype(np.float32)
    sin = np.sin(ang).astype(np.float32)
    q = a // 2
    sgn = np.ones(dim, np.float32)
    sgn[0:q] = -1.0
    sgn[a:a + q] = -1.0
    ss = sin * sgn[None, :]
    out = np.stack([cos, ss], 0).reshape(2, n // 128, 128, dim).transpose(2, 0, 1, 3)
    return np.ascontiguousarray(out, dtype=np.float32)


def _consts(core):
    half = core % 2
    ident = np.eye(128, dtype=np.float32)
    k = np.arange(128)[:, None]
    q = np.arange(128)[None, :]
    mL = np.where(k >= q, 0.0, NEG).astype(np.float32)
    mR = np.where(k <= q, 0.0, NEG).astype(np.float32)
    full = np.full((128, 128), NEG, np.float32)
    mLe = full if half == 0 else mL
    mRe = full if half == 1 else mR
    bmask = np.ascontiguousarray(np.stack([mL, mR, mLe, mRe], 1))
    rmask = np.zeros((128, 6), np.float32)
    for i in range(4):
        rmask[32 * i:32 * i + 32, i] = 1.0
    rmask[0:64, 4] = 1.0
    rmask[64:128, 5] = 1.0
    return ident, bmask, rmask


_NC_CACHE = {}


def kernel(**inp):
    f = lambda a: np.ascontiguousarray(np.asarray(a, dtype=np.float32))
    x_prompt = f(inp["x_prompt"]); x_sample = f(inp["x_sample"])
    if "nc" not in _NC_CACHE:
        _NC_CACHE["nc"] = build_nc()
    nc, in_names = _NC_CACHE["nc"]
    shared = {
        "w_mod": f(inp["w_mod"]),
        "bmT": np.ascontiguousarray(f(inp["b_mod"]).reshape(DEPTH, 72, 128).transpose(2, 0, 1)),
        "ln_g": f(inp["ln_g"]), "ln_b": f(inp["ln_b"]),
        "ffn_w1": f(inp["ffn_w1"]), "ffn_w3": f(inp["ffn_w3"]), "ffn_w2": f(inp["ffn_w2"]),
        "w_in": f(inp["w_in"]), "w_o": f(inp["w_o"]),
        "a_lambda": f(inp["a_lambda"]).reshape(DEPTH, 128), "a_subln_g": f(inp["a_subln_g"]),
        "b_sink": f(inp["b_sink"]), "c_q_norm_g": f(inp["c_q_norm_g"]), "c_w_q_up": f(inp["c_w_q_up"]),
        "c_kv_norm_g": f(inp["c_kv_norm_g"]), "c_w_kv_up": f(inp["c_w_kv_up"]).reshape(DEPTH, 128, 512),
    }
    c = f(inp["c"]); c_ctx = f(inp["c_ctx"])
    in_maps = []
    for core in range(8):
        b = core // 2
        half = core % 2
        m = dict(shared)
        m["xin"] = np.ascontiguousarray(np.concatenate(
            [x_prompt[4 * core:4 * core + 4].reshape(1024, D), x_sample[b, half * 2048:(half + 1) * 2048]], 0))
        cond = np.stack([c[b], c_ctx], 0)
        m["condT"] = np.ascontiguousarray(cond.reshape(2, 8, 128).transpose(2, 1, 0))
        m["ck_a"] = f(inp["cache_a_k"])[b].reshape(DEPTH, 256, 256)
        m["cv_a"] = f(inp["cache_a_v"])[b].reshape(DEPTH, 256, 256)
        m["ck_b"] = f(inp["cache_b_k"])[b].reshape(DEPTH, 256, 128)
        m["cv_b"] = f(inp["cache_b_v"])[b].reshape(DEPTH, 256, 128)
        m["c_ckv"] = f(inp["cache_c_kv"])[b]
        m["c_kpe"] = f(inp["cache_c_kpe"])[b]
        ident, bmask, rmask = _consts(core)
        m["ident"] = ident; m["bmask"] = bmask; m["rmask"] = rmask
        m["rope32"] = _rope_tables(32, half * 2048, 2048)
        m["rope64"] = _rope_tables(64, half * 2048, 2048)
        in_maps.append({k: np.ascontiguousarray(m[k]) for k in in_names})
    res = run_bass_kernel_spmd(nc, in_maps, core_ids=list(range(8)))
    ys = [np.asarray(r["y"]) for r in res.results]
    sts = [np.asarray(r["st"]) for r in res.results]
    y_prompt = np.concatenate([y[:1024].reshape(4, 256, D) for y in ys], 0)
    y_sample = np.stack([np.concatenate([ys[2 * b][1024:], ys[2 * b + 1][1024:]], 0) for b in range(4)], 0)
    stt = np.concatenate(sts, 0)
    new_a_k = stt[..., 0:256].reshape(32, DEPTH, 256, 4, 64)
    new_a_v = stt[..., 256:512].reshape(32, DEPTH, 256, 4, 64)
    new_b_k = stt[..., 512:640].reshape(32, DEPTH, 256, 2, 64)
    new_b_v = stt[..., 640:768].reshape(32, DEPTH, 256, 2, 64)
    new_c_kv = stt[..., 768:896]
    new_c_kpe = stt[..., 896:928]
    outs = (y_prompt, y_sample, new_a_k, new_a_v, new_b_k, new_b_v, new_c_kv, new_c_kpe)
    return tuple(np.ascontiguousarray(o, dtype=np.float32) for o in outs)
```
